# Optimizing a Trainium2 kernel written in Bass

```python
import math
import jax, jax.numpy as jnp
from jax import lax
import numpy as np

D_MODEL = 1024
BATCH = 4
SEQ = 4096
DEPTH = 2
DEC_BATCH = 32
DEC_SEQ = 64
PAST_LEN = 2048

CHUNK = 64
N_META = 16
N_BRANCH = 3
RW_HEADS = 12
RW_HEAD_DIM = 64
RW_WIDTH = RW_HEADS * RW_HEAD_DIM
RW_LORA_W = 64
RW_LORA_A = 64
RW_SHIFT = 3 * RW_WIDTH + RW_LORA_W + RW_LORA_A
S5_GROUPS = 32
S5_GROUP_CH = 16
S5_WIDTH = S5_GROUPS * S5_GROUP_CH
S5_STATE = 64
ML_HEADS = 4
ML_HEAD_DIM = 192
ML_WIDTH = ML_HEADS * ML_HEAD_DIM
ML_CONV = 4
IN_SIZES = (RW_SHIFT, RW_WIDTH, S5_WIDTH, S5_WIDTH, 2 * ML_WIDTH, ML_WIDTH, 2 * ML_HEADS, ML_WIDTH, ML_WIDTH, N_BRANCH * D_MODEL)
IN_COLS = sum(IN_SIZES)
DN_ALPHA = (2 * DEPTH) ** 0.25
DN_BETA = (8 * DEPTH) ** -0.25
LN_EPS = 1e-5
RW_GN_EPS = 64e-5
F32 = jnp.float32

kernel_name = 'hybrid_rwkv7_s5_mlstm_stream_step'


def split_cols(x, sizes):
    return jnp.split(x, np.cumsum(sizes)[:-1].tolist(), axis=-1)


def layer_norm(x, g, b):
    xf = x.astype(F32)
    mu = jnp.mean(xf, -1, keepdims=True)
    var = jnp.mean(jnp.square(xf - mu), -1, keepdims=True)
    return (xf - mu) * lax.rsqrt(var + LN_EPS) * g + b


def head_norm(y, g, eps):
    mu = jnp.mean(y, -1, keepdims=True)
    var = jnp.mean(jnp.square(y - mu), -1, keepdims=True)
    return (y - mu) * lax.rsqrt(var + eps) * g.reshape(y.shape[-2:])


def token_shift(u, prev, mu):
    u_prev = jnp.concatenate([prev[:, None].astype(u.dtype), u[:, :-1]], axis=1)
    return u + (u_prev - u) * mu, u[:, -1]


def rwkv7_branch(cols, gate, shift_prev, wkv0, p):
    bsz, L, _ = cols.shape
    xs, shift_new = token_shift(cols, shift_prev, p['rw_mu'])
    r, k, v, xw, xa = split_cols(xs, (RW_WIDTH, RW_WIDTH, RW_WIDTH, RW_LORA_W, RW_LORA_A))
    w_log = -jax.nn.softplus(-(p['rw_w0'] + jnp.tanh(xw) @ p['rw_w2'])) - 0.5
    decay = jnp.exp(-jnp.exp(w_log))
    a = jax.nn.sigmoid(p['rw_a0'] + xa @ p['rw_a2'])
    hd = lambda t: t.reshape(bsz, L, RW_HEADS, RW_HEAD_DIM)
    kk = hd(k * p['rw_kk'])
    kk = kk * lax.rsqrt(jnp.sum(kk * kk, -1, keepdims=True) + 1e-12)
    k = k * (1.0 + (a - 1.0) * p['rw_ka'])
    r, k, v, a, decay = hd(r), hd(k), hd(v), hd(a), hd(decay)

    def step(S, inp):
        r_t, k_t, v_t, kk_t, a_t, w_t = inp
        sa = jnp.einsum('bhvk,bhk->bhv', S, -kk_t)
        S = S * w_t[:, :, None, :] + sa[..., None] * (kk_t * a_t)[:, :, None, :] + v_t[..., None] * k_t[:, :, None, :]
        return S, jnp.einsum('bhvk,bhk->bhv', S, r_t)

    seq = tuple(jnp.moveaxis(t, 1, 0) for t in (r, k, v, kk, a, decay))
    wkv_new, y = lax.scan(step, wkv0.astype(F32), seq)
    y = jnp.moveaxis(y, 0, 1)
    y = head_norm(y, p['rw_ln_g'], RW_GN_EPS) + p['rw_ln_b'].reshape(RW_HEADS, RW_HEAD_DIM)
    y = y + jnp.sum(r * k * p['rw_rk'], -1, keepdims=True) * v
    y = y.reshape(bsz, L, RW_WIDTH) * jax.nn.silu(gate)
    return y, shift_new, wkv_new


def _lin_combine(e1, e2):
    a1, b1 = e1
    a2, b2 = e2
    return a1 * a2, a2 * b1 + b2


def s5_branch(u, gate, h0_re, h0_im, p):
    bsz, L, _ = u.shape
    lam = lax.complex(p['s5_a_re'].astype(F32), p['s5_a_im'].astype(F32))
    dt = jnp.exp(p['s5_log_dt'].astype(F32))[:, None]
    lam_bar = jnp.exp(lam * dt)
    b_mat = lax.complex(p['s5_b_re'].astype(F32), p['s5_b_im'].astype(F32))
    b_bar = ((lam_bar - 1.0) / lam)[..., None] * b_mat
    c_mat = lax.complex(p['s5_c_re'].astype(F32), p['s5_c_im'].astype(F32))
    ug = u.reshape(bsz, L, S5_GROUPS, S5_GROUP_CH).astype(jnp.complex64)
    bu = jnp.einsum('gph,blgh->blgp', b_bar, ug)
    lam_seq = jnp.broadcast_to(lam_bar, bu.shape)
    lam_pow, hs = lax.associative_scan(_lin_combine, (lam_seq, bu), axis=1)
    h0 = lax.complex(h0_re.astype(F32), h0_im.astype(F32))
    hs = hs + lam_pow * h0[:, None]
    y = jnp.einsum('ghp,blgp->blgh', c_mat, hs).real.reshape(bsz, L, S5_WIDTH) + p['s5_d'] * u
    y = jax.nn.gelu(y)
    y = y * jax.nn.sigmoid(y @ p['s5_w_glu'] + p['s5_b_glu'])
    y = y * jax.nn.silu(gate)
    h_last = hs[:, -1]
    return y, jnp.real(h_last), jnp.imag(h_last)


def mlstm_chunkwise(q, k, v, log_i, log_f, state, blk):
    bsz, L, H, Dh = q.shape
    nb = L // blk
    blocks = lambda t: jnp.moveaxis(t.reshape((bsz, nb, blk) + t.shape[2:]), 1, 0)
    causal = jnp.tril(jnp.ones((blk, blk), dtype=bool))

    def step(carry, inp):
        C, n, m = carry
        qc, kc, vc, lic, lfc = inp
        b = jnp.cumsum(lfc, axis=1)
        g = b + m[:, None, :]
        d = b[:, :, None, :] - b[:, None, :, :] + lic[:, None, :, :]
        d = jnp.where(causal[None, :, :, None], d, -jnp.inf)
        m_row = jnp.maximum(g, jnp.max(d, axis=2))
        s = jnp.einsum('bthd,bshd->btsh', qc, kc) * jnp.exp(d - m_row[:, :, None, :])
        w_inter = jnp.exp(g - m_row)
        num = jnp.einsum('btsh,bshv->bthv', s, vc) + w_inter[..., None] * jnp.einsum('bthk,bhkv->bthv', qc, C)
        den = jnp.sum(s, axis=2) + w_inter * jnp.einsum('bthk,bhk->bth', qc, n)
        h = num / jnp.maximum(jnp.abs(den), jnp.exp(-m_row))[..., None]
        b_end = b[:, -1]
        e_log = b_end[:, None, :] - b + lic
        m_new = jnp.maximum(b_end + m, jnp.max(e_log, axis=1))
        we = jnp.exp(e_log - m_new[:, None, :])
        keep = jnp.exp(b_end + m - m_new)
        C_new = keep[..., None, None] * C + jnp.einsum('bsh,bshk,bshv->bhkv', we, kc, vc)
        n_new = keep[..., None] * n + jnp.einsum('bsh,bshk->bhk', we, kc)
        return (C_new, n_new, m_new), h

    state, h = lax.scan(step, state, tuple(blocks(t) for t in (q, k, v, log_i, log_f)))
    h = jnp.moveaxis(h, 0, 1).reshape(bsz, L, H, Dh)
    return h, state


def mlstm_branch(qk_in, v, if_pre, o_pre, z, conv_prev, C0, n0, m0, p, segments):
    bsz, L, _ = qk_in.shape
    xp = jnp.concatenate([conv_prev.astype(F32), qk_in], axis=1)
    conv = p['ml_conv_b'] + xp[:, 0:L] * p['ml_conv_w'][0]
    for j in range(1, ML_CONV):
        conv = conv + xp[:, j:j + L] * p['ml_conv_w'][j]
    conv_new = xp[:, L:]
    q, k = jnp.split(jax.nn.silu(conv), 2, axis=-1)
    hd = lambda t: t.reshape(bsz, L, ML_HEADS, ML_HEAD_DIM)
    q, k, vh = hd(q), hd(k) / math.sqrt(ML_HEAD_DIM), hd(v)
    log_i, f_pre = jnp.split(if_pre + p['ml_b_if'], 2, axis=-1)
    log_f = jax.nn.log_sigmoid(f_pre)
    state = (C0.astype(F32), n0.astype(F32), m0.astype(F32))
    outs = []
    start = 0
    for seg_len, blk in segments:
        sl = slice(start, start + seg_len)
        h_seg, state = mlstm_chunkwise(q[:, sl], k[:, sl], vh[:, sl], log_i[:, sl], log_f[:, sl], state, blk)
        outs.append(h_seg)
        start += seg_len
    h = jnp.concatenate(outs, axis=1)
    h = head_norm(h, p['ml_ln_g'], LN_EPS).reshape(bsz, L, ML_WIDTH)
    y = jax.nn.sigmoid(o_pre) * h * jax.nn.silu(z)
    return y, conv_new, state[0], state[1], state[2]


def trunk_layer(x, st, p, segments):
    rw_shift0, rw_wkv0, s5_re0, s5_im0, ml_conv0, ml_c0, ml_n0, ml_m0 = st
    proj = (x @ p['w_in']).astype(F32)
    rw_cols, rw_gate, s5_u, s5_gate, ml_qk, ml_v, ml_if, ml_o, ml_z, merge = split_cols(proj, IN_SIZES)
    y_rw, rw_shift1, rw_wkv1 = rwkv7_branch(rw_cols, rw_gate, rw_shift0, rw_wkv0, p)
    y_s5, s5_re1, s5_im1 = s5_branch(s5_u, s5_gate, s5_re0, s5_im0, p)
    y_ml, ml_conv1, ml_c1, ml_n1, ml_m1 = mlstm_branch(ml_qk, ml_v, ml_if, ml_o, ml_z, ml_conv0, ml_c0, ml_n0, ml_m0, p, segments)
    gates = jax.nn.sigmoid(merge + p['b_merge']).reshape(x.shape[:2] + (N_BRANCH, D_MODEL))
    merged = (gates[..., 0, :] * (y_rw @ p['w_br_rw'])
              + gates[..., 1, :] * (y_s5 @ p['w_br_s5'])
              + gates[..., 2, :] * (y_ml @ p['w_br_ml']))
    out = merged @ p['w_out']
    x_new = layer_norm(DN_ALPHA * x.astype(F32) + out, p['ln_g'], p['ln_b']).astype(x.dtype)
    return x_new, (rw_shift1, rw_wkv1, s5_re1, s5_im1, ml_conv1, ml_c1, ml_n1, ml_m1)


def setup_inputs(seed: int = 0) -> dict:
    key = jax.random.key(seed)
    ks = iter(jax.random.split(key, 64))
    nrm = lambda shape, scale: scale * jax.random.normal(next(ks), shape, F32)
    uni = lambda shape, lo, hi: jax.random.uniform(next(ks), shape, F32, lo, hi)
    L = DEPTH
    return {
        'x_prompt': nrm((BATCH, SEQ, D_MODEL), 1.0),
        'x_sample': nrm((DEC_BATCH, DEC_SEQ, D_MODEL), 1.0),
        'state_rwkv_shift': nrm((L, DEC_BATCH, RW_SHIFT), 1.0),
        'state_rwkv_wkv': nrm((L, DEC_BATCH, RW_HEADS, RW_HEAD_DIM, RW_HEAD_DIM), 0.1),
        'state_s5_re': nrm((L, DEC_BATCH, S5_GROUPS, S5_STATE), 0.5),
        'state_s5_im': nrm((L, DEC_BATCH, S5_GROUPS, S5_STATE), 0.5),
        'state_mlstm_conv': nrm((L, DEC_BATCH, ML_CONV - 1, 2 * ML_WIDTH), 1.0),
        'state_mlstm_c': nrm((L, DEC_BATCH, ML_HEADS, ML_HEAD_DIM, ML_HEAD_DIM), 0.1),
        'state_mlstm_n': nrm((L, DEC_BATCH, ML_HEADS, ML_HEAD_DIM), 0.1),
        'state_mlstm_m': nrm((L, DEC_BATCH, ML_HEADS), 0.5),
        'meta': nrm((N_META, D_MODEL), 1.0),
        'in_ln_g': 1.0 + nrm((D_MODEL,), 0.02),
        'in_ln_b': nrm((D_MODEL,), 0.02),
        'w_in': nrm((L, D_MODEL, IN_COLS), D_MODEL ** -0.5),
        'rw_mu': uni((L, RW_SHIFT), 0.0, 1.0),
        'rw_w0': uni((L, RW_WIDTH), -3.0, 1.0),
        'rw_w2': nrm((L, RW_LORA_W, RW_WIDTH), 0.1 * RW_LORA_W ** -0.5),
        'rw_a0': nrm((L, RW_WIDTH), 0.1),
        'rw_a2': nrm((L, RW_LORA_A, RW_WIDTH), RW_LORA_A ** -0.5),
        'rw_kk': 0.85 + nrm((L, RW_WIDTH), 0.02),
        'rw_ka': 1.0 + nrm((L, RW_WIDTH), 0.02),
        'rw_rk': nrm((L, RW_HEADS, RW_HEAD_DIM), 0.1),
        'rw_ln_g': 1.0 + nrm((L, RW_WIDTH), 0.02),
        'rw_ln_b': nrm((L, RW_WIDTH), 0.02),
        's5_a_re': -0.5 + nrm((L, S5_GROUPS, S5_STATE), 0.01),
        's5_a_im': jnp.pi * jnp.arange(S5_STATE, dtype=F32) + nrm((L, S5_GROUPS, S5_STATE), 0.01),
        's5_b_re': nrm((L, S5_GROUPS, S5_STATE, S5_GROUP_CH), (2 * S5_GROUP_CH) ** -0.5),
        's5_b_im': nrm((L, S5_GROUPS, S5_STATE, S5_GROUP_CH), (2 * S5_GROUP_CH) ** -0.5),
        's5_c_re': nrm((L, S5_GROUPS, S5_GROUP_CH, S5_STATE), (2 * S5_STATE) ** -0.5),
        's5_c_im': nrm((L, S5_GROUPS, S5_GROUP_CH, S5_STATE), (2 * S5_STATE) ** -0.5),
        's5_d': nrm((L, S5_WIDTH), 1.0),
        's5_log_dt': uni((L, S5_GROUPS), math.log(0.001), math.log(0.1)),
        's5_w_glu': nrm((L, S5_WIDTH, S5_WIDTH), S5_WIDTH ** -0.5),
        's5_b_glu': nrm((L, S5_WIDTH), 0.02),
        'ml_conv_w': nrm((L, ML_CONV, 2 * ML_WIDTH), 0.5),
        'ml_conv_b': nrm((L, 2 * ML_WIDTH), 0.02),
        'ml_b_if': jnp.concatenate([nrm((L, ML_HEADS), 0.1), uni((L, ML_HEADS), 3.0, 6.0)], axis=-1),
        'ml_ln_g': 1.0 + nrm((L, ML_WIDTH), 0.02),
        'b_merge': nrm((L, N_BRANCH * D_MODEL), 0.02),
        'w_br_rw': nrm((L, RW_WIDTH, D_MODEL), DN_BETA * RW_WIDTH ** -0.5),
        'w_br_s5': nrm((L, S5_WIDTH, D_MODEL), DN_BETA * S5_WIDTH ** -0.5),
        'w_br_ml': nrm((L, ML_WIDTH, D_MODEL), DN_BETA * ML_WIDTH ** -0.5),
        'w_out': nrm((L, D_MODEL, D_MODEL), DN_BETA * D_MODEL ** -0.5),
        'ln_g': 1.0 + nrm((L, D_MODEL), 0.02),
        'ln_b': nrm((L, D_MODEL), 0.02),
    }


def reference(x_prompt, x_sample, state_rwkv_shift, state_rwkv_wkv, state_s5_re, state_s5_im,
              state_mlstm_conv, state_mlstm_c, state_mlstm_n, state_mlstm_m,
              meta, in_ln_g, in_ln_b, w_in, rw_mu, rw_w0, rw_w2, rw_a0, rw_a2, rw_kk, rw_ka, rw_rk,
              rw_ln_g, rw_ln_b, s5_a_re, s5_a_im, s5_b_re, s5_b_im, s5_c_re, s5_c_im, s5_d, s5_log_dt,
              s5_w_glu, s5_b_glu, ml_conv_w, ml_conv_b, ml_b_if, ml_ln_g, b_merge, w_br_rw, w_br_s5,
              w_br_ml, w_out, ln_g, ln_b):
    def layer_params(l):
        return dict(w_in=w_in[l], rw_mu=rw_mu[l], rw_w0=rw_w0[l], rw_w2=rw_w2[l], rw_a0=rw_a0[l],
                    rw_a2=rw_a2[l], rw_kk=rw_kk[l], rw_ka=rw_ka[l], rw_rk=rw_rk[l], rw_ln_g=rw_ln_g[l],
                    rw_ln_b=rw_ln_b[l], s5_a_re=s5_a_re[l], s5_a_im=s5_a_im[l], s5_b_re=s5_b_re[l],
                    s5_b_im=s5_b_im[l], s5_c_re=s5_c_re[l], s5_c_im=s5_c_im[l], s5_d=s5_d[l],
                    s5_log_dt=s5_log_dt[l], s5_w_glu=s5_w_glu[l], s5_b_glu=s5_b_glu[l],
                    ml_conv_w=ml_conv_w[l], ml_conv_b=ml_conv_b[l], ml_b_if=ml_b_if[l], ml_ln_g=ml_ln_g[l],
                    b_merge=b_merge[l], w_br_rw=w_br_rw[l], w_br_s5=w_br_s5[l], w_br_ml=w_br_ml[l],
                    w_out=w_out[l], ln_g=ln_g[l], ln_b=ln_b[l])

    bp, sp = x_prompt.shape[0], x_prompt.shape[1]
    xp = jnp.concatenate([jnp.broadcast_to(meta[None], (bp, N_META, D_MODEL)).astype(x_prompt.dtype), x_prompt], axis=1)
    xp = layer_norm(xp, in_ln_g, in_ln_b).astype(x_prompt.dtype)
    seg_prompt = ((N_META, N_META), (sp, CHUNK))
    z0 = (jnp.zeros((bp, RW_SHIFT), F32), jnp.zeros((bp, RW_HEADS, RW_HEAD_DIM, RW_HEAD_DIM), F32),
          jnp.zeros((bp, S5_GROUPS, S5_STATE), F32), jnp.zeros((bp, S5_GROUPS, S5_STATE), F32),
          jnp.zeros((bp, ML_CONV - 1, 2 * ML_WIDTH), F32), jnp.zeros((bp, ML_HEADS, ML_HEAD_DIM, ML_HEAD_DIM), F32),
          jnp.zeros((bp, ML_HEADS, ML_HEAD_DIM), F32), jnp.zeros((bp, ML_HEADS), F32))
    p_states = []
    for l in range(DEPTH):
        xp, st = trunk_layer(xp, z0, layer_params(l), seg_prompt)
        p_states.append(st)
    y_prompt = xp[:, N_META:]

    ds = x_sample.shape[1]
    xs = layer_norm(x_sample, in_ln_g, in_ln_b).astype(x_sample.dtype)
    seg_sample = ((ds, ds),)
    s_states = []
    for l in range(DEPTH):
        st0 = (state_rwkv_shift[l], state_rwkv_wkv[l], state_s5_re[l], state_s5_im[l],
               state_mlstm_conv[l], state_mlstm_c[l], state_mlstm_n[l], state_mlstm_m[l])
        xs, st = trunk_layer(xs, st0, layer_params(l), seg_sample)
        s_states.append(st)
    y_sample = xs

    (p_rw_shift, p_rw_wkv, p_s5_re, p_s5_im, p_ml_conv, p_ml_c, p_ml_n, p_ml_m) = [jnp.stack(s, 0) for s in zip(*p_states)]
    (s_rw_shift, s_rw_wkv, s_s5_re, s_s5_im, s_ml_conv, s_ml_c, s_ml_n, s_ml_m) = [jnp.stack(s, 0) for s in zip(*s_states)]
    return (y_prompt, y_sample,
            p_rw_shift, p_rw_wkv, p_s5_re, p_s5_im, p_ml_conv, p_ml_c, p_ml_n, p_ml_m,
            s_rw_shift, s_rw_wkv, s_s5_re, s_s5_im, s_ml_conv, s_ml_c, s_ml_n, s_ml_m)
```

```python
import contextlib
import os
import math
import numpy as np
import concourse.bass as bass
import concourse.mybir as mybir
from concourse.bass_utils import run_bass_kernel_spmd

F32 = mybir.dt.float32
BF16 = mybir.dt.bfloat16
ALU = mybir.AluOpType
AF = mybir.ActivationFunctionType
AX = mybir.AxisListType

ENGS = ("pe", "act", "dve", "pool", "sp")
D = 1024
DEPTH = 2
NT = 128
SEQ = 4096
NMETA = 16
NSAMP = 4
RWW = 768
RWS = 2432
S5W = 512
MLW = 768
INC = 11144
DN_ALPHA = (2 * DEPTH) ** 0.25
LN_EPS = 1e-5
RW_GN_EPS = 64e-5
C_RW = 0
C_RWG = 2432
C_S5U = 3200
C_S5G = 3712
C_MLQK = 4224
C_MLV = 5760
C_MLIF = 6528
C_MLO = 6536
C_MLZ = 7304
C_MRG = 8072
EXPM05 = math.exp(-0.5)


class StopBuild(Exception):
    pass


class _FirstHook:
    def __init__(self, eng, wait):
        self._e = eng
        self._w = wait

    def _wrap(self, f):
        def g(*a, **k):
            ins = f(*a, **k)
            if self._w is not None:
                ins._wait_ge(*self._w)
                self._w = None
            return ins
        return g

    def __getattr__(self, name):
        v = getattr(self._e, name)
        if name in ("matmul", "transpose"):
            return self._wrap(v)
        return v


class Sched:
    limit = None
    resched = True
    def __init__(self, nc, n_dma_sems=12):
        self.nc = nc
        self.ops = []
        self.last_w = {}
        self.readers = {}
        self.n_dma_sems = n_dma_sems
        self.arena = {}

    def xl(self, keys):
        out = []
        for k in keys:
            r = self.arena.get(k)
            if r is None:
                r = self.arena.get(k.rstrip('0123456789'))
            if r is None:
                assert not ('_' in k and k.rsplit('_', 1)[1].isdigit() and k.rsplit('_', 1)[0] in self.arena), k
                out.append(k)
            else:
                out.extend("ar%d" % u for u in range(r[0] // 256, (r[1] + 255) // 256))
        return out

    def op(self, eng, fn, reads=(), writes=(), dma=False, single=False):
        if self.limit is not None and len(self.ops) >= self.limit:
            raise StopBuild()
        reads = self.xl(reads)
        writes = self.xl(writes)
        i = len(self.ops)
        deps = set()
        for k in reads:
            w = self.last_w.get(k)
            if w is not None:
                deps.add(w)
        for k in writes:
            w = self.last_w.get(k)
            if w is not None:
                deps.add(w)
            for r in self.readers.get(k, ()):
                deps.add(r)
        deps.discard(i)
        odeps = set()
        if eng == "pe" and not dma:
            odeps = {d for d in deps if (self.ops[d]["eng"] == "pe" and not self.ops[d]["dma"])}
            deps = deps - odeps
        self.ops.append(dict(eng=eng, fn=fn, deps=deps, odeps=odeps, dma=dma, used=False, single=single, label=getattr(self, 'label', ''),
                             dur=getattr(self, 'dur', None), tbl=getattr(self, 'tbl', None)))
        for d in deps:
            self.ops[d]["used"] = True
        for k in writes:
            self.last_w[k] = i
            self.readers[k] = []
        for k in reads:
            if k not in writes:
                self.readers.setdefault(k, []).append(i)
        return i

    def reschedule(self, final_wait_ops):
        ops = self.ops
        n = len(ops)
        DUR = {"pe": 0.35, "act": 0.3, "dve": 0.3, "pool": 0.45, "sp": 0.2}
        succ = [[] for _ in range(n)]
        indeg = [0] * n
        for i, o in enumerate(ops):
            for d in (o["deps"] | o["odeps"]):
                succ[d].append(i)
                indeg[i] += 1
        lastq = {}
        for i, o in enumerate(ops):
            if o["dma"]:
                q = o["eng"]
                if q in lastq:
                    succ[lastq[q]].append(i)
                    indeg[i] += 1
                lastq[q] = i
        fin = [0.0] * n
        ready_t = [0.0] * n
        cur = {e: 0.0 for e in ENGS}
        ready = {e: [] for e in ENGS}
        for i in range(n):
            if indeg[i] == 0:
                ready[ops[i]["eng"]].append(i)
        order = []
        acttbl = [None]
        done = 0
        while done < n:
            best = None
            for e in ENGS:
                lst = ready[e]
                if not lst:
                    continue
                bi_, bs_ = None, None
                for i in lst[:24]:
                    st = max(ready_t[i], cur[e])
                    if e == "act" and ops[i]["tbl"] is not None and ops[i]["tbl"] != acttbl[0]:
                        st += 1.3
                    key = (st, i)
                    if bs_ is None or key < bs_:
                        bi_, bs_ = i, key
                if best is None or bs_ < best[0]:
                    best = (bs_, bi_, e)
            (st, _), i, e = best
            ready[e].remove(i)
            o = ops[i]
            if o["dma"]:
                cur[e] = st + 0.1
                fin[i] = st + 2.5
            else:
                if e == "act" and o["tbl"] is not None:
                    acttbl[0] = o["tbl"]
                fin[i] = st + (o["dur"] or DUR[e])
                cur[e] = fin[i]
            order.append(i)
            done += 1
            for j in succ[i]:
                ready_t[j] = max(ready_t[j], fin[i] + 0.15)
                indeg[j] -= 1
                if indeg[j] == 0:
                    ready[ops[j]["eng"]].append(j)
        assert len(order) == n
        pos = {old: new for new, old in enumerate(order)}
        newops = []
        for old_i in order:
            o = ops[old_i]
            o["deps"] = {pos[d] for d in o["deps"]}
            o["odeps"] = {pos[d] for d in o["odeps"]}
            newops.append(o)
        self.ops = newops
        return [pos[i] for i in final_wait_ops]

    def emit(self, final_wait_ops=()):
        nc = self.nc
        if self.resched:
            final_wait_ops = self.reschedule(list(final_wait_ops))
        ops = self.ops
        for i in final_wait_ops:
            ops[i]["used"] = True
        cnt = {e: 0 for e in ENGS}
        dma_cnt = {}
        dma_rr = {e: 0 for e in ENGS}
        for o in ops:
            if o["dma"]:
                q = o["eng"]
                s = (q, dma_rr[q] % self.n_dma_sems)
                dma_rr[q] += 1
                dma_cnt[s] = dma_cnt.get(s, 0) + 1
                o["sig"] = ("dma", s, 16 * dma_cnt[s])
            elif o["used"]:
                cnt[o["eng"]] += 1
                o["sig"] = ("eng", o["eng"], cnt[o["eng"]])
            else:
                o["sig"] = None
        with contextlib.ExitStack() as st:
            esem = {e: st.enter_context(nc.semaphore("s_" + e)) for e in ENGS}
            dsem = {}
            for s in dma_cnt:
                dsem[s] = st.enter_context(nc.semaphore("d_%s_%d" % s))
            block = st.enter_context(nc.Block())

            def semof(sig):
                return esem[sig[1]] if sig[0] == "eng" else dsem[sig[1]]

            def run(e, engobj):
                seen = {}
                for o in ops:
                    if o["eng"] != e:
                        continue
                    need = {}
                    for d in o["deps"]:
                        sg = ops[d]["sig"]
                        key = (sg[0], sg[1])
                        need[key] = max(need.get(key, 0), sg[2])
                    if o["dma"]:
                        sg = o["sig"]
                        key = (sg[0], sg[1])
                        if sg[2] > 16:
                            need[key] = max(need.get(key, 0), sg[2] - 16)
                    waits = []
                    for key, v in need.items():
                        if seen.get(key, 0) < v:
                            waits.append((esem[key[1]] if key[0] == "eng" else dsem[key[1]], v))
                            seen[key] = v
                    emb = []
                    if not o["dma"] and (o["single"] or e == "pe"):
                        emb = waits[-1:]
                        waits = waits[:-1]
                    for sm, v in waits:
                        engobj.wait_ge(sm, v)
                    if emb and not o["single"]:
                        ins = o["fn"](_FirstHook(engobj, emb[0]))
                    else:
                        ins = o["fn"](engobj)
                        for sm, v in emb:
                            ins._wait_ge(sm, v)
                    sg = o["sig"]
                    if sg is not None:
                        ins.then_inc(semof(sg), 16 if sg[0] == "dma" else 1)
                if e == "sp":
                    for i in final_wait_ops:
                        sg = ops[i]["sig"]
                        engobj.wait_ge(semof(sg), sg[2])

            @block.tensor
            def _(eng):
                run("pe", eng)

            @block.scalar
            def _(eng):
                run("act", eng)

            @block.vector
            def _(eng):
                run("dve", eng)

            @block.gpsimd
            def _(eng):
                run("pool", eng)

            @block.sync
            def _(eng):
                run("sp", eng)


def w_groups():
    g = []
    g.append(("rwx", "w_in", 1024, 2304, 128))
    g.append(("rwk0", "w_in", 1024, 768, 512))
    g.append(("rwk1", "w_in", 1024, 1280, 256))
    g.append(("rwr0", "w_in", 1024, 0, 512))
    g.append(("rwr1", "w_in", 1024, 512, 256))
    g.append(("rwv0", "w_in", 1024, 1536, 512))
    g.append(("rwv1", "w_in", 1024, 2048, 256))
    g.append(("rwg0", "w_in", 1024, C_RWG, 512))
    g.append(("rwg1", "w_in", 1024, C_RWG + 512, 256))
    g.append(("s5u", "w_in", 1024, C_S5U, 512))
    g.append(("s5g", "w_in", 1024, C_S5G, 512))
    for i in range(4):
        g.append(("mlqk%d" % i, "w_in", 1024, C_MLQK + 384 * i, 384))
    g.append(("mlv0", "w_in", 1024, C_MLV, 512))
    g.append(("mlv1", "w_in", 1024, C_MLV + 512, 264))
    g.append(("mlo0", "w_in", 1024, C_MLO, 512))
    g.append(("mlo1", "w_in", 1024, C_MLO + 512, 256))
    g.append(("mlz0", "w_in", 1024, C_MLZ, 512))
    g.append(("mlz1", "w_in", 1024, C_MLZ + 512, 256))
    for jg in range(2):
        for b, (nm, k) in enumerate((("w_br_rw", 768), ("w_br_s5", 512), ("w_br_ml", 768))):
            g.append(("mg%d%d" % (b, jg), "w_in", 1024, C_MRG + b * 1024 + jg * 512, 512))
            g.append(("br%d%d" % (b, jg), nm, k, jg * 512, 512))
    for jg in range(2):
        g.append(("wo%d" % jg, "w_out", 1024, jg * 512, 512))
    return g


GROUPS = w_groups()
GIDX = {g[0]: i for i, g in enumerate(GROUPS)}

PP = {}
_o = 0
for _n, _w in (("mu", 19), ("w0", 6), ("a0", 6), ("kk", 6), ("ka", 6), ("rk", 6), ("s5d", 4), ("bglu", 4),
               ("mlg", 6), ("bmrg", 24), ("are", 16), ("aim", 16), ("ldt", 16)):
    PP[_n] = (_o, _w)
    _o += _w
NPP = _o


def build(stage=99):
    nc = bass.Bass("TRN2", target_bir_lowering=False)
    es = contextlib.ExitStack()
    S = Sched(nc)

    def din(name, shape):
        return nc.dram_tensor(name, list(shape), F32, kind="ExternalInput").ap()

    def dout(name, shape):
        return nc.dram_tensor(name, list(shape), F32, kind="ExternalOutput").ap()

    def dscr(name, shape, dt):
        return nc.dram_tensor(name, list(shape), dt, kind="Internal").ap()

    xp = din("xp", [SEQ, D])
    xs = din("xs", [NSAMP * 64, D])
    meta = din("meta", [NMETA, D])
    st_shift = din("st_shift", [DEPTH, NSAMP, RWS])
    st_wkv = din("st_wkv", [DEPTH, NSAMP, 12, 64, 64])
    st_s5re = din("st_s5re", [DEPTH, NSAMP, 32, 64])
    st_s5im = din("st_s5im", [DEPTH, NSAMP, 32, 64])
    st_conv = din("st_conv", [DEPTH, NSAMP, 3, 1536])
    st_c = din("st_c", [DEPTH, NSAMP, 4, 192, 192])
    st_n = din("st_n", [DEPTH, NSAMP, 4, 192])
    st_m = din("st_m", [DEPTH, NSAMP, 4])
    w_in = din("w_in", [DEPTH, D, INC])
    wsrc = dict(w_in=w_in, w_br_rw=din("w_br_rw", [DEPTH, 768, D]), w_br_s5=din("w_br_s5", [DEPTH, 512, D]),
                w_br_ml=din("w_br_ml", [DEPTH, 768, D]), w_out=din("w_out", [DEPTH, D, D]))
    w_glu = din("s5_w_glu", [DEPTH, 512, 512])
    rw_w2 = din("rw_w2", [DEPTH, 64, 768])
    rw_a2 = din("rw_a2", [DEPTH, 64, 768])
    pp_d = din("pp", [DEPTH, 128, NPP])
    pq_d = din("pq", [DEPTH, 96, 16 * 5])
    bc_d = din("bc", [6, D])
    rwln_d = din("rwln", [DEPTH, 2, 2, 384])
    bif_d = din("bif", [DEPTH, 8])
    bblk_d = din("bblk", [DEPTH, 128, 4, 1024])
    cpad_d = din("cpad", [DEPTH, 128, 16, 2, 64])

    yp = dout("yp", [SEQ, D])
    ys = dout("ys", [NSAMP * 64, D])
    o_shift = {"p": dout("p_shift", [DEPTH, 1, RWS]), "s": dout("s_shift", [DEPTH, NSAMP, RWS])}
    o_wkv = {"p": dout("p_wkv", [DEPTH, 1, 12, 64, 64]), "s": dout("s_wkv", [DEPTH, NSAMP, 12, 64, 64])}
    o_s5re = {"p": dout("p_s5re", [DEPTH, 1, 32, 64]), "s": dout("s_s5re", [DEPTH, NSAMP, 32, 64])}
    o_s5im = {"p": dout("p_s5im", [DEPTH, 1, 32, 64]), "s": dout("s_s5im", [DEPTH, NSAMP, 32, 64])}
    o_conv = {"p": dout("p_conv", [DEPTH, 1, 3, 1536]), "s": dout("s_conv", [DEPTH, NSAMP, 3, 1536])}
    o_c = {"p": dout("p_c", [DEPTH, 1, 4, 192, 192]), "s": dout("s_c", [DEPTH, NSAMP, 4, 192, 192])}
    o_n = {"p": dout("p_n", [DEPTH, 1, 4, 192]), "s": dout("s_n", [DEPTH, NSAMP, 4, 192])}
    o_m = {"p": dout("p_m", [DEPTH, 1, 4]), "s": dout("s_m", [DEPTH, NSAMP, 4])}

    wq = [[dscr("wq%d_%d" % (l, i), [128, g[2] // 128, g[4]], BF16) for i, g in enumerate(GROUPS)]
          for l in range(DEPTH)]

    def sb(name, shape, dt=F32):
        return es.enter_context(nc.sbuf_tensor(name, list(shape), dt))

    def psum(name, shape, dt):
        return es.enter_context(nc.psum_tensor(name, list(shape), dt))

    final_ops = []

    def ACT(fn, r, w):
        tbl = None
        names = fn.__code__.co_names
        for nm_, t_ in (("Sigmoid", "sig"), ("Exp", "exp"), ("Ln", "ln"), ("Sqrt", "sqrt"), ("Silu", "silu"), ("Sin", "silu")):
            if nm_ in names:
                tbl = t_
        S.tbl = tbl
        i_ = S.op("act", fn, r, w, single=True)
        S.tbl = None
        return i_

    def DVE(fn, r, w):
        return S.op("dve", fn, r, w, single=True)

    def POOL(fn, r, w):
        return S.op("pool", fn, r, w, single=True)

    def PE(fn, r, w):
        return S.op("pe", fn, r, w)

    def LOAD(out, in_, r, w, slow=False):
        return S.op("sp", lambda e: e.dma_start(out=out, in_=in_, allow_slow_non_contiguous=slow), r, w, dma=True)

    def STORE(out, in_, r, slow=False):
        i = S.op("pool", lambda e: e.dma_start(out=out, in_=in_, allow_slow_non_contiguous=slow), r, (), dma=True)
        final_ops.append(i)
        return i

    def MM(out, pairs, r, w):
        S.dur = 0.1 + 0.07 * len(pairs)
        def fn(e):
            n = len(pairs)
            for i, (l_, r_) in enumerate(pairs):
                ins = e.matmul(out, lhsT=l_, rhs=r_, start=(i == 0), stop=(i == n - 1))
            return ins
        i_ = PE(fn, r, w)
        S.dur = None
        return i_

    def TR(out, in_, ident, r, w):
        return S.op("pe", lambda e: e.transpose(out=out, in_=in_, identity=ident), r, w, single=True)

    PSF = [psum("psf%d" % i, [128, 2, 512], F32) for i in range(3)]
    PSB = psum("psb", [128, 2, 1024], BF16)
    rr = {"b": 0, "p": 0, "t": 0}

    def fbank():
        i = rr["b"] % 6
        rr["b"] += 1
        return PSF[i // 2][:, i % 2, :], ["pf%d" % i]

    def fpair():
        i = rr["p"] % 3
        rr["p"] += 1
        return PSF[i], ["pf%d" % (2 * i), "pf%d" % (2 * i + 1)]

    def bbank():
        i = rr["t"] % 2
        rr["t"] += 1
        return PSB[:, i, :], ["pb%d" % i]

    ident_b = sb("ident_b", [128, 128], BF16)
    ident_f = sb("ident_f", [128, 128], F32)
    m_strict = sb("m_strict", [128, 6, 64], F32)
    m_incl = sb("m_incl", [128, 6, 64], F32)
    m_lower = sb("m_lower", [128, 6, 64], F32)
    tri_b = sb("tri_b", [64, 64], BF16)
    tri_f = sb("tri_f", [64, 64], F32)
    blk1 = sb("blk1", [128, 128], BF16)
    bsel = sb("bsel", [128, 2], BF16)
    ones_f = sb("ones_f", [128, 128], F32)
    onesb = sb("onesb", [128, 1], BF16)
    scanm = sb("scanm", [128, NT], F32)

    def mk_sel(t, pattern, base, cm, op, key):
        POOL(lambda e: e.memset(t, 1.0), [], [key])
        POOL(lambda e: e.affine_select(out=t, in_=t, pattern=pattern, compare_op=op, fill=0.0, base=base,
                                       channel_multiplier=cm), [key], [key])

    mk_sel(ident_f[:], [[-1, 128]], 0, 1, ALU.is_equal, "ident_f")
    POOL(lambda e: e.tensor_copy(out=ident_b[:], in_=ident_f[:]), ["ident_f"], ["ident_b"])
    for hf_ in range(2):
        ps_ = slice(hf_ * 64, hf_ * 64 + 64)
        mk_sel(m_strict[ps_], [[0, 6], [1, 64]], -1, -1, ALU.is_ge, "m_strict")
        mk_sel(m_incl[ps_], [[0, 6], [1, 64]], 0, -1, ALU.is_ge, "m_incl")
        mk_sel(m_lower[ps_], [[0, 6], [-1, 64]], -1, 1, ALU.is_ge, "m_lower")
    mk_sel(tri_f[:], [[1, 64]], 0, -1, ALU.is_ge, "tri_f")
    POOL(lambda e: e.tensor_copy(out=tri_b[:], in_=tri_f[:]), ["tri_f"], ["tri_b"])
    POOL(lambda e: e.memset(ones_f[:], 1.0), [], ["ones_f"])
    POOL(lambda e: e.memset(onesb[:], 1.0), [], ["onesb"])
    POOL(lambda e: e.memset(blk1[:], 0.0), [], ["blk1"])
    POOL(lambda e: e.memset(blk1[0:64, 0:64], 1.0), ["blk1"], ["blk1"])
    POOL(lambda e: e.memset(blk1[64:128, 64:128], 1.0), ["blk1"], ["blk1"])
    POOL(lambda e: e.memset(bsel[:], 0.0), [], ["bsel"])
    POOL(lambda e: e.memset(bsel[0:64, 0:1], 1.0), ["bsel"], ["bsel"])
    POOL(lambda e: e.memset(bsel[64:128, 1:2], 1.0), ["bsel"], ["bsel"])
    POOL(lambda e: e.memset(scanm[:], 1.0), [], ["scanm"])
    POOL(lambda e: e.memset(scanm[:].rearrange("p (c t) -> p c t", t=64)[:, :, 0:1], 0.0), ["scanm"], ["scanm"])

    if stage == -4:
        S.emit(final_wait_ops=final_ops); es.close(); return nc
    for l in range(DEPTH):
        for i, (nm, src, K, c0, wd) in enumerate(GROUPS):
            srcap = wsrc[src][l, :, c0:c0 + wd].rearrange("(kc p) c -> p kc c", p=128)
            _i = S.op("pool", lambda e, o_=wq[l][i], s_=srcap: e.dma_start(out=o_, in_=s_), [], ["wq%d_%d" % (l, i)],
                      dma=True)
            if os.environ.get('DBG_WAITPRE') and (l * 100 + i) < int(os.environ.get('DBG_WAITPRE')):
                final_ops.append(_i)

    if stage == -3:
        S.emit(final_wait_ops=final_ops); es.close(); return nc
    ARW = 15360
    arena_t = sb("arena", [128, ARW], F32)
    arp = {"o": 0}

    def ar_reset(o=0):
        arp["o"] = o

    def ar(name, shape, dt=F32):
        P = shape[0]
        n = int(np.prod(shape[1:]))
        words = n if dt == F32 else (n + 1) // 2
        words = (words + 15) // 16 * 16
        o = arp["o"]
        assert o + words <= ARW, (name, o, words)
        arp["o"] = o + words
        v = arena_t[0:P, o:o + words]
        if dt != F32:
            v = v.bitcast(BF16)
        v = v[:, 0:n]
        if len(shape) == 3:
            v = v.rearrange("p (a b) -> p a b", b=shape[2])
        elif len(shape) == 4:
            v = v.rearrange("p (a b c) -> p a b c", b=shape[2], c=shape[3])
        S.arena[name] = (o * 4, (o + words) * 4)
        if len(shape) >= 3:
            esz = 4 if dt == F32 else 2
            sub = int(np.prod(shape[2:])) * esz
            for a_ in range(shape[1]):
                S.arena["%s_%d" % (name, a_)] = (o * 4 + a_ * sub, o * 4 + (a_ + 1) * sub)
        return v

    pp = [sb("pp%d" % l, [128, NPP]) for l in range(DEPTH)]
    pq = [sb("pq%d" % l, [96, 16, 5]) for l in range(DEPTH)]
    omka = [sb("omka%d" % l, [128, 6]) for l in range(DEPTH)]
    lora = [sb("lora%d" % l, [128, 768], BF16) for l in range(DEPTH)]
    wglu = [sb("wglu%d" % l, [128, 4, 512], BF16) for l in range(DEPTH)]
    bcg = sb("bcg", [128, 2, D])
    rwln = [sb("rwln%d" % l, [128, 2, 384]) for l in range(DEPTH)]
    bif = [sb("bif%d" % l, [64, 8]) for l in range(DEPTH)]
    EpB = sb("EpB", [128, 2, 16, 64])
    EnB = sb("EnB", [64, 16, 2, 128], BF16)
    bblkB = sb("bblkB", [128, 4, 1024], BF16)
    cpadB = sb("cpadB", [128, 16, 2, 64], BF16)
    S5K = ["EpB", "EnB", "bblkB", "cpadB"]
    epd = [dscr("epd%d" % l, [128, 2, 16, 64], F32) for l in range(DEPTH)]
    end_ = [dscr("end%d" % l, [64, 16, 2, 128], BF16) for l in range(DEPTH)]
    bblkq = [dscr("bblkq%d" % l, [128, 4, 1024], BF16) for l in range(DEPTH)]
    cpadq = [dscr("cpadq%d" % l, [128, 16, 2, 64], BF16) for l in range(DEPTH)]
    for l in range(DEPTH):
        LOAD(pp[l][:], pp_d[l], [], ["pp%d" % l])
        LOAD(pq[l][:], pq_d[l].rearrange("p (t j) -> p t j", j=5), [], ["pq%d" % l])
        for a_ in range(2):
            for r_ in range(2):
                LOAD(rwln[l][r_ * 64:(r_ + 1) * 64, a_, :], rwln_d[l, a_, r_].partition_broadcast(64), ["rwln%d" % l], ["rwln%d" % l])
        LOAD(bif[l][:], bif_d[l].partition_broadcast(64), [], ["bif%d" % l])
        S.op("pool", lambda e, l=l: e.dma_start(out=lora[l][0:64, :], in_=rw_w2[l]), [], ["lora%d" % l], dma=True)
        S.op("pool", lambda e, l=l: e.dma_start(out=lora[l][64:128, :], in_=rw_a2[l]), [], ["lora%da" % l], dma=True)
        S.op("pool", lambda e, l=l: e.dma_start(out=wglu[l][:], in_=w_glu[l].rearrange("(kc p) c -> p kc c", p=128)),
             [], ["wglu%d" % l], dma=True)
        S.op("pool", lambda e, l=l: e.dma_start(out=bblkq[l], in_=bblk_d[l]), [], ["bblkq%d" % l], dma=True)
        S.op("pool", lambda e, l=l: e.dma_start(out=cpadq[l], in_=cpad_d[l]), [], ["cpadq%d" % l], dma=True)
        o_, w_ = PP["ka"]
        DVE(lambda e, l=l, o_=o_: e.tensor_scalar(out=omka[l][:], in0=pp[l][:, o_:o_ + 6], scalar1=-1.0, scalar2=1.0,
                                                  op0=ALU.mult, op1=ALU.add), ["pp%d" % l], ["omka%d" % l])

    if stage == -2:
        S.emit(final_wait_ops=final_ops); es.close(); return nc

    def load_bc(i):
        LOAD(bcg[:].rearrange("p a b -> p (a b)"), bc_d[2 * i:2 * i + 2, :].rearrange("a b -> (a b)").partition_broadcast(128),
             [], ["bcg"])

    def load_s5(l):
        LOAD(EpB[:], epd[l], ["epd%d" % l], ["EpB"])
        LOAD(EnB[:], end_[l], ["end%d" % l], ["EnB"])
        LOAD(bblkB[:], bblkq[l], ["bblkq%d" % l], ["bblkB"])
        LOAD(cpadB[:], cpadq[l], ["cpadq%d" % l], ["cpadB"])

    def PPc(l, name, j=None):
        o_, w_ = PP[name]
        if j is None:
            return pp[l][:, o_:o_ + w_]
        return pp[l][:, o_ + j:o_ + j + 1]

    ar_reset()
    zr = ar("zr", [128, 16]); zi = ar("zi", [128, 16]); dtt = ar("dtt", [128, 16])
    t1 = ar("t1", [128, 16]); t2 = ar("t2", [128, 16]); t3 = ar("t3", [128, 16]); mg = ar("mg", [128, 16])
    lr = ar("lr", [128, 16])
    Epf = ar("Epf", [128, 2, 16, 64])
    Enf = [ar("enf%d" % c, [128, 16, 64]) for c in range(2)]
    ta = ar("ta", [128, 16, 32]); tb_ = ar("tb", [128, 16, 32])
    cfr = ar("cfr", [128, 16]); cfi = ar("cfi", [128, 16]); den = ar("den", [128, 16])
    enc = [ar("enc%d" % c, [128, 16, 64]) for c in range(2)]
    Ent = ar("Ent", [64, 16, 2, 128])
    KT = ["zr", "zi", "dtt", "t1", "t2", "t3", "mg", "lr", "Epf", "enf0", "enf1", "ta", "tb", "cfr", "cfi", "den", "enc0", "enc1"]

    def cexp32(sign, outr, outi):
        ACT(lambda e: e.activation(out=mg[:], in_=zr[:], func=AF.Exp, scale=sign / 32.0), KT, KT)
        ACT(lambda e: e.activation(out=t1[:], in_=zi[:], func=AF.Sin, scale=sign / 32.0), KT, KT)
        ACT(lambda e: e.activation(out=t2[:], in_=zi[:], func=AF.Sin, scale=sign / 32.0, bias=hpi[:, 0:1]), KT + ["hpi"], KT)
        DVE(lambda e: e.tensor_tensor(out=outr, in0=mg[:], in1=t2[:], op=ALU.mult), KT, KT)
        DVE(lambda e: e.tensor_tensor(out=outi, in0=mg[:], in1=t1[:], op=ALU.mult), KT, KT)
        for _ in range(5):
            DVE(lambda e: e.tensor_tensor(out=t1[:], in0=outr, in1=outr, op=ALU.mult), KT, KT)
            DVE(lambda e: e.tensor_tensor(out=t2[:], in0=outi, in1=outi, op=ALU.mult), KT, KT)
            DVE(lambda e: e.tensor_tensor(out=t3[:], in0=outr, in1=outi, op=ALU.mult), KT, KT)
            DVE(lambda e: e.tensor_tensor(out=outr, in0=t1[:], in1=t2[:], op=ALU.subtract), KT, KT)
            DVE(lambda e: e.tensor_scalar(out=outi, in0=t3[:], scalar1=2.0, scalar2=None, op0=ALU.mult), KT, KT)

    def powers(tabr, tabi):
        ln_ = 1
        while ln_ < 64:
            Lr = tabr[:, :, ln_ - 1:ln_].broadcast_to([128, 16, ln_])
            Li = tabi[:, :, ln_ - 1:ln_].broadcast_to([128, 16, ln_])
            a_ = ta[:, :, 0:ln_]
            b_ = tb_[:, :, 0:ln_]
            sr = tabr[:, :, 0:ln_]
            si = tabi[:, :, 0:ln_]
            dr = tabr[:, :, ln_:2 * ln_]
            di = tabi[:, :, ln_:2 * ln_]
            DVE(lambda e, a_=a_, sr=sr, Lr=Lr: e.tensor_tensor(out=a_, in0=sr, in1=Lr, op=ALU.mult), KT, KT)
            DVE(lambda e, b_=b_, si=si, Li=Li: e.tensor_tensor(out=b_, in0=si, in1=Li, op=ALU.mult), KT, KT)
            DVE(lambda e, a_=a_, b_=b_, dr=dr: e.tensor_tensor(out=dr, in0=a_, in1=b_, op=ALU.subtract), KT, KT)
            DVE(lambda e, a_=a_, sr=sr, Li=Li: e.tensor_tensor(out=a_, in0=sr, in1=Li, op=ALU.mult), KT, KT)
            DVE(lambda e, b_=b_, si=si, Lr=Lr: e.tensor_tensor(out=b_, in0=si, in1=Lr, op=ALU.mult), KT, KT)
            DVE(lambda e, a_=a_, b_=b_, di=di: e.tensor_tensor(out=di, in0=a_, in1=b_, op=ALU.add), KT, KT)
            ln_ *= 2

    hpi = sb("hpi", [128, 1])
    POOL(lambda e: e.memset(hpi[:], math.pi / 2), [], ["hpi"])
    for l in range(DEPTH):
        KP = ["pp%d" % l]
        ACT(lambda e, l=l: e.activation(out=dtt[:], in_=PPc(l, "ldt"), func=AF.Exp), KT + KP, KT)
        DVE(lambda e, l=l: e.tensor_tensor(out=zr[:], in0=PPc(l, "are"), in1=dtt[:], op=ALU.mult), KT + KP, KT)
        DVE(lambda e, l=l: e.tensor_tensor(out=zi[:], in0=PPc(l, "aim"), in1=dtt[:], op=ALU.mult), KT + KP, KT)

        if stage == -10:
            S.emit(final_wait_ops=final_ops); es.close(); return nc
        cexp32(1.0, Epf[:, 0, :, 0], Epf[:, 1, :, 0])

        if stage == -9:
            S.emit(final_wait_ops=final_ops); es.close(); return nc
        powers(Epf[:, 0], Epf[:, 1])

        if stage == -8:
            S.emit(final_wait_ops=final_ops); es.close(); return nc
        cexp32(-1.0, Enf[0][:, :, 0], Enf[1][:, :, 0])
        powers(Enf[0], Enf[1])

        if stage == -7:
            S.emit(final_wait_ops=final_ops); es.close(); return nc
        DVE(lambda e: e.tensor_scalar(out=lr[:], in0=Epf[:, 0, :, 0], scalar1=-1.0, scalar2=None, op0=ALU.add), KT, KT)
        DVE(lambda e, l=l: e.tensor_tensor(out=t1[:], in0=PPc(l, "are"), in1=PPc(l, "are"), op=ALU.mult), KT + KP, KT)
        DVE(lambda e, l=l: e.tensor_tensor(out=t2[:], in0=PPc(l, "aim"), in1=PPc(l, "aim"), op=ALU.mult), KT + KP, KT)
        DVE(lambda e: e.tensor_tensor(out=den[:], in0=t1[:], in1=t2[:], op=ALU.add), KT, KT)
        DVE(lambda e: e.reciprocal(out=den[:], in_=den[:]), KT, KT)
        DVE(lambda e, l=l: e.tensor_tensor(out=t1[:], in0=lr[:], in1=PPc(l, "are"), op=ALU.mult), KT + KP, KT)
        DVE(lambda e, l=l: e.tensor_tensor(out=t2[:], in0=Epf[:, 1, :, 0], in1=PPc(l, "aim"), op=ALU.mult), KT + KP, KT)
        DVE(lambda e: e.tensor_tensor(out=cfr[:], in0=t1[:], in1=t2[:], op=ALU.add), KT, KT)
        DVE(lambda e: e.tensor_tensor(out=cfr[:], in0=cfr[:], in1=den[:], op=ALU.mult), KT, KT)
        DVE(lambda e, l=l: e.tensor_tensor(out=t1[:], in0=Epf[:, 1, :, 0], in1=PPc(l, "are"), op=ALU.mult), KT + KP, KT)
        DVE(lambda e, l=l: e.tensor_tensor(out=t2[:], in0=lr[:], in1=PPc(l, "aim"), op=ALU.mult), KT + KP, KT)
        DVE(lambda e: e.tensor_tensor(out=cfi[:], in0=t1[:], in1=t2[:], op=ALU.subtract), KT, KT)
        DVE(lambda e: e.tensor_tensor(out=cfi[:], in0=cfi[:], in1=den[:], op=ALU.mult), KT, KT)
        CR = cfr[:].unsqueeze(2).broadcast_to([128, 16, 64])
        CI = cfi[:].unsqueeze(2).broadcast_to([128, 16, 64])
        DVE(lambda e, CR=CR: e.tensor_tensor(out=enc[0][:], in0=Enf[0][:], in1=CR, op=ALU.mult), KT, KT)
        DVE(lambda e, CI=CI: e.tensor_tensor(out=enc[1][:], in0=Enf[1][:], in1=CI, op=ALU.mult), KT, KT)
        DVE(lambda e: e.tensor_tensor(out=enc[0][:], in0=enc[0][:], in1=enc[1][:], op=ALU.subtract), KT, KT)
        DVE(lambda e, CI=CI: e.tensor_tensor(out=enc[1][:], in0=Enf[0][:], in1=CI, op=ALU.mult), KT, KT)
        DVE(lambda e, CR=CR: e.tensor_tensor(out=Enf[0][:], in0=Enf[1][:], in1=CR, op=ALU.mult), KT, KT)
        DVE(lambda e: e.tensor_tensor(out=enc[1][:], in0=enc[1][:], in1=Enf[0][:], op=ALU.add), KT, KT)

        if stage == -6:
            S.emit(final_wait_ops=final_ops); es.close(); return nc
        for c in range(2):
            for i in range(16):
                ps_, pk = fbank()
                TR(ps_[0:64, 0:128], enc[c][:, i, :], ident_f[:], KT + ["ident_f"], pk)
                ACT(lambda e, ps_=ps_, i=i, c=c: e.copy(out=Ent[:, i, c, :], in_=ps_[0:64, 0:128]), pk, ["Ent"])

        if stage == -5:
            S.emit(final_wait_ops=final_ops); es.close(); return nc
        S.op("pool", lambda e, l=l: e.dma_start(out=epd[l], in_=Epf), KT, ["epd%d" % l], dma=True)
        S.op("pool", lambda e, l=l: e.dma_start(out=end_[l], in_=Ent), ["Ent"], ["end%d" % l], dma=True)

    x_tok = sb("x_tok", [128, 1, D])
    xT = sb("xT", [128, 8, NT], BF16)
    wbuf = [sb("wbuf%d" % i, [128, 8, 512], BF16) for i in range(4)]
    wr = {"i": 0}

    def load_group(l, name):
        gi = GIDX[name]
        g = GROUPS[gi]
        bi = wr["i"] % 4
        wr["i"] += 1
        kc = g[2] // 128
        if not (os.environ.get("DBG_NOLOAD") and wr["i"] > 8):
            LOAD(wbuf[bi][:, 0:kc, 0:g[4]], wq[l][gi], ["wq%d_%d" % (l, gi)], ["wbuf%d" % bi])
        return wbuf[bi], "wbuf%d" % bi

    shiftst = [sb("shiftst%d" % l, [128, 19]) for l in range(DEPTH)]
    sshift = sb("sshift", [128, 19, 2])
    sshift_o = sb("sshift_o", [128, 19, 2])
    S0T = [sb("s0t%d" % l, [128, 6, 64]) for l in range(DEPTH)]
    S0Tb = sb("s0tb", [128, 6, 64], BF16)
    h0 = [sb("h0%d" % l, [128, 2, 16]) for l in range(DEPTH)]
    convst = [sb("convst%d" % l, [96, 16, 3]) for l in range(DEPTH)]
    sconv = sb("sconv", [96, 16, 2, 3])
    Cst = [sb("cst%d" % l, [96, 2, 4, 193]) for l in range(DEPTH)]
    Cstb = sb("cstb", [96, 2, 4, 193], BF16)
    mst = [sb("mst%d" % l, [4, 1]) for l in range(DEPTH)]
    stg = sb("stg", [64, 12, 64])

    yrwT = sb("yrwT", [128, 6, NT], BF16)
    ys5T = sb("ys5T", [128, 4, NT], BF16)
    ymlT = sb("ymlT", [128, 6, NT], BF16)
    gate = [sb("gate%d" % i, [128, NT], BF16) for i in range(2)]
    mrg = sb("mrg", [128, 8, NT]); mrgb = sb("mrgb", [128, 8, NT], BF16)
    lnt = sb("lnt", [128, D]); lnb = sb("lnb", [128, D], BF16); lnx = sb("lnx", [128, D]); ctmp2 = [sb("ctmp%d" % i, [128, NT]) for i in range(2)]
    lst = sb("lst", [128, 2, 6]); lmv = sb("lmv", [128, 2]); lrs = sb("lrs", [128, 1])
    gC = sb("gC", [128, 6, 2])

    ar_reset()
    U = [ar("U%d" % i, [128, NT + 1]) for i in range(2)]
    dtmp = [ar("dtmp%d" % i, [128, NT]) for i in range(2)]
    xs18 = ar("xs18", [128, NT]); txw = ar("txw", [128, NT], BF16)
    ldec_2 = [ar("ldec%d" % i_, [128, NT]) for i_ in range(2)]; Gc_2 = [ar("Gc%d" % i_, [128, NT]) for i_ in range(2)]; aa_2 = [ar("aa%d" % i_, [128, NT]) for i_ in range(2)]
    eneg_2 = [ar("eneg%d" % i_, [128, NT]) for i_ in range(2)]; eprev_2 = [ar("eprev%d" % i_, [128, NT]) for i_ in range(2)]; ehat_2 = [ar("ehat%d" % i_, [128, NT]) for i_ in range(2)]; epos_2 = [ar("epos%d" % i_, [128, NT]) for i_ in range(2)]
    kx_2 = [ar("kx%d" % i_, [128, NT]) for i_ in range(2)]; kkr_2 = [ar("kkr%d" % i_, [128, NT]) for i_ in range(2)]; kksq_2 = [ar("kksq%d" % i_, [128, NT], BF16) for i_ in range(2)]
    rn_2 = [ar("rn%d" % i_, [128, NT]) for i_ in range(2)]; kkn_2 = [ar("kkn%d" % i_, [128, NT]) for i_ in range(2)]; tk_2 = [ar("tk%d" % i_, [128, NT]) for i_ in range(2)]
    kmod_2 = [ar("kmod%d" % i_, [128, NT]) for i_ in range(2)]; bb_2 = [ar("bb%d" % i_, [128, NT]) for i_ in range(2)]; rx_2 = [ar("rx%d" % i_, [128, NT]) for i_ in range(2)]
    rkp = ar("rkp", [128, 6, NT], BF16)
    rt_ = ar("rt_", [128, 6, NT], BF16); kt_ = ar("kt_", [128, 6, NT], BF16); bt_ = ar("bt_", [128, 6, NT], BF16)
    at_ = ar("at_", [128, 6, NT], BF16); khat = ar("khat", [128, 6, NT], BF16); bhat = ar("bhat", [128, 6, NT], BF16)
    vT = ar("vT", [128, 6, NT], BF16); grw = ar("grw", [128, 6, NT], BF16)
    TB = []
    for pz in ("A", "B"):
        TB.append(dict(
            vtok=ar("vtok" + pz, [128, 6, 64], BF16), khtok=ar("khtok" + pz, [128, 6, 64], BF16), bhtok=ar("bhtok" + pz, [128, 6, 64], BF16),
            Nsb=[ar("Nsb%d%s" % (i, pz), [128, 6, 64], BF16) for i in range(2)],
            NTsb=[ar("NTsb%d%s" % (i, pz), [128, 6, 64], BF16) for i in range(2)],
            Msb=ar("Msb" + pz, [128, 6, 64], BF16), P1sb=ar("P1sb" + pz, [128, 6, 64], BF16), P2sb=ar("P2sb" + pz, [128, 6, 64], BF16), z=pz))
    Ysb = [ar("Ysb%d" % i, [128, 6, 64], BF16) for i in range(2)]
    yo = ar("yo", [128, 6, 64]); ycen = ar("ycen", [128, 6, 64]); ysq = ar("ysq", [128, 6, 64])
    ystat = ar("ystat", [128, 4, 6]); rkd = ar("rkd", [128, 6]); ytb = ar("ytb", [128, 6, 64], BF16)
    RW_END = arp["o"]
    ar_reset()
    uT = ar("uT", [128, 4, NT], BF16); u32 = ar("u32", [128, 4, NT]); gs5 = ar("gs5", [128, 4, NT], BF16)
    wtok = ar("wtok", [64, 16, 2, 128], BF16)
    s5a = ar("s5a", [64, 4, 128]); s5b = ar("s5b", [64, 4, 128])
    Gr = ar("Gr", [128, 16, 64]); Gi = ar("Gi", [128, 16, 64])
    hA = ar("hA", [128, 16, 64]); hB = ar("hB", [128, 16, 64]); hC = ar("hC", [128, 16, 64]); hD = ar("hD", [128, 16, 64])
    hre = ar("hre", [128, 16, 64], BF16); himn = ar("himn", [128, 16, 64], BF16)
    yv = ar("yv", [128, 4, 64]); gt = ar("gt", [128, 4, 64]); gsg = ar("gsg", [128, 4, 64])
    gl = ar("gl", [128, 4, 64]); glb = ar("glb", [128, 4, 64], BF16); sgl = ar("sgl", [128, 4, 64])
    ar_reset()
    qkraw = ar("qkraw", [96, 16, NT + 3]); cvacc = ar("cvacc", [96, 16, NT]); cvtmp = ar("cvtmp", [96, 16, NT])
    hbuf = ar("hbuf0", [96, 16, 6])
    qkT = ar("qkT", [96, 16, NT], BF16)
    vaug = ar("vaug", [64, NT // 64, 4, 193], BF16)
    iftok = ar("iftok", [64, NT // 64, 8])
    zsl_2 = [ar("zsl%d" % i_, [128, NT]) for i_ in range(2)]; ogT = ar("ogT", [128, 6, NT], BF16)
    lfi = ar("lfi", [64, 8]); bcs = ar("bcs", [64, 4]); zz = ar("zz", [64, 4]); ee = ar("ee", [64, 4]); clampt = ar("clampt", [64, 4])
    zmax = ar("zmax", [4, 1]); mu4 = ar("mu4", [4, 1]); f4 = ar("f4", [4, 1]); bend = ar("bend", [4, 1]); dg = ar("dg", [4, 8])
    bc8 = ar("bc8", [128, 8])
    PTs = ar("PTs", [64, 4, 64]); PTb = ar("PTb", [64, 4, 64], BF16)
    hh = ar("hh", [64, 4, 192]); hcen = ar("hcen", [64, 4, 192]); hsq = ar("hsq", [64, 4, 192]); hst = ar("hst", [64, 4, 4])
    hnb = ar("hnb", [64, 768], BF16); khm = ar("khm", [64, 4, 192], BF16)

    BR_T = {0: yrwT, 1: ys5T, 2: ymlT}
    BR_K = {0: "yrwT", 1: "ys5T", 2: "ymlT"}
    BR_KC = {0: 6, 1: 4, 2: 6}

    def ln_block(src, srckeys, rows, out_dram=None):
        for hf in range(2):
            DVE(lambda e, hf=hf: e.bn_stats(out=lst[0:rows, hf, :], in_=src[0:rows, hf * 512:(hf + 1) * 512]),
                srckeys, ["lst"])
        DVE(lambda e: e.bn_aggr(out=lmv[0:rows, :], in_=lst[0:rows].rearrange("p a b -> p (a b)")), ["lst"], ["lmv"])
        ACT(lambda e: e.activation(out=lrs[0:rows, :], in_=lmv[0:rows, 1:2], func=AF.Sqrt, bias=LN_EPS, scale=1.0),
            ["lmv"], ["lrs"])
        DVE(lambda e: e.reciprocal(out=lrs[0:rows, :], in_=lrs[0:rows, :]), ["lrs"], ["lrs"])
        DVE(lambda e: e.tensor_scalar(out=lnt[0:rows, :], in0=src[0:rows, :], scalar1=lmv[0:rows, 0:1],
                                      scalar2=lrs[0:rows, 0:1], op0=ALU.subtract, op1=ALU.mult),
            srckeys + ["lmv", "lrs"], ["lnt"])
        DVE(lambda e: e.tensor_tensor(out=lnt[0:rows, :], in0=lnt[0:rows, :], in1=bcg[0:rows, 0, :], op=ALU.mult),
            ["lnt", "bcg"], ["lnt"])
        POOL(lambda e: e.tensor_tensor(out=x_tok[0:rows, 0, :], in0=lnt[0:rows, :], in1=bcg[0:rows, 1, :],
                                       op=ALU.add), ["lnt", "bcg"], ["x_tok"])
        if out_dram is not None:
            STORE(out_dram, x_tok[0:rows, 0, :], ["x_tok"])
        ACT(lambda e: e.copy(out=lnb[0:rows, :], in_=x_tok[0:rows, 0, :]), ["x_tok"], ["lnb"])
        pt, pk = bbank()
        for kc in range(8):
            TR(pt[:, kc * 128:kc * 128 + rows], lnb[0:rows, kc * 128:(kc + 1) * 128], ident_b[0:rows, 0:rows],
               ["lnb", "ident_b"], pk)
        DVE(lambda e: e.tensor_copy(out=xT[:, :, 0:rows],
                                    in_=pt.rearrange("p (k t) -> p k t", t=128)[:, :, 0:rows]), pk, ["xT"])

    def proj(wb, wk, c0, M, N):
        ps_, pk = fbank()
        MM(ps_[0:M, 0:N], [(wb[:, kc, c0:c0 + M], xT[:, kc, 0:N]) for kc in range(8)], ([] if os.environ.get('DBG_SKIPW') else [wk]) + ["xT"], pk)
        return ps_[0:M, 0:N], pk

    s5cur = {"l": None}

    def ensure_s5(l):
        if s5cur["l"] != l:
            load_s5(l)
            s5cur["l"] = l

    def run_pass(kind, N, tiles, xsrc, ydst, first, last):
        og = "p" if kind != "sample" else "s"
        load_bc(0)
        LOAD(lnx[0:N, :], xsrc, [], ["lnx"])
        ln_block(lnx, ["lnx"], N)
        for l in range(DEPTH):
            if first:
                for t_, k_ in ((shiftst[l], "shiftst%d" % l), (S0T[l], "s0t%d" % l), (h0[l], "h0%d" % l),
                               (convst[l], "convst%d" % l), (Cst[l], "cst%d" % l), (mst[l], "mst%d" % l)):
                    POOL(lambda e, t_=t_: e.memset(t_[:], 0.0), [], [k_])
            layer(kind, N, tiles, l, og, last)
            phase_c(kind, N, l, ydst if l == DEPTH - 1 else None)

    def layer(kind, N, tiles, l, og, last):
        pl = "pp%d" % l
        samp = kind == "sample"
        nq = len(tiles)
        if samp:
            for q, (o, C, sq) in enumerate(tiles):
                LOAD(lnx[0:19, 0:128], st_shift[l, sq].rearrange("(t p) -> t p", p=128), [], ["lnx"])
                ps_, pk = fbank()
                TR(ps_[:, 0:19], lnx[0:19, 0:128], ident_f[0:19, 0:19], ["lnx", "ident_f"], pk)
                ACT(lambda e, ps_=ps_, q=q: e.copy(out=sshift[:, :, q], in_=ps_[:, 0:19]), pk + ["sshift"], ["sshift"])
                LOAD(lnx[0:48, 128:224], st_conv[l, sq].rearrange("j (t p) -> (j t) p", p=96), [], ["lnx"])
                ps_, pk = fbank()
                TR(ps_[0:96, 0:48], lnx[0:48, 128:224], ident_f[0:48, 0:48], ["lnx", "ident_f"], pk)
                ACT(lambda e, ps_=ps_, q=q: e.copy(out=sconv[:, :, q, :], in_=ps_[0:96, 0:48].rearrange("p (j t) -> p t j", j=3)),
                    pk + ["sconv"], ["sconv"])

        def shift_tile(ps_, pk, ct, out_ap, outkeys, ui):
            Ub = U[ui]
            uk = "U%d" % ui
            ACT(lambda e: e.copy(out=Ub[:, 1:N + 1], in_=ps_), pk, [uk])
            if not samp:
                ACT(lambda e: e.copy(out=Ub[:, 0:1], in_=shiftst[l][:, ct:ct + 1]), ["shiftst%d" % l, uk], [uk])
            else:
                ACT(lambda e: e.copy(out=Ub[:, 0:1], in_=sshift[:, ct, 0:1]), ["sshift", uk], [uk])
            DVE(lambda e: e.tensor_tensor(out=dtmp[ui][:, 0:N], in0=Ub[:, 0:N], in1=Ub[:, 1:N + 1], op=ALU.subtract),
                [uk], ["dtmp%d" % ui])
            DVE(lambda e: e.scalar_tensor_tensor(out=out_ap, in0=dtmp[ui][:, 0:N], scalar=PPc(l, "mu", ct),
                                                 in1=Ub[:, 1:N + 1], op0=ALU.mult, op1=ALU.add),
                ["dtmp%d" % ui, uk, pl], outkeys)
            if samp:
                DVE(lambda e: e.tensor_tensor(out=dtmp[ui][:, 0:1], in0=sshift[:, ct, 1:2], in1=Ub[:, 65:66], op=ALU.subtract),
                    [uk, "sshift", "dtmp%d" % ui], ["dtmp%d" % ui])
                DVE(lambda e: e.scalar_tensor_tensor(out=out_ap[:, 64:65], in0=dtmp[ui][:, 0:1], scalar=PPc(l, "mu", ct),
                                                     in1=Ub[:, 65:66], op0=ALU.mult, op1=ALU.add),
                    ["dtmp%d" % ui, uk, pl] + outkeys, outkeys)
                ACT(lambda e: e.copy(out=sshift_o[:, ct, 0:1], in_=Ub[:, 64:65]), [uk], ["sshift_o"])
                ACT(lambda e: e.copy(out=sshift_o[:, ct, 1:2], in_=Ub[:, 128:129]), [uk, "sshift_o"], ["sshift_o"])
            else:
                ACT(lambda e: e.copy(out=shiftst[l][:, ct:ct + 1], in_=Ub[:, N:N + 1]), [uk], ["shiftst%d" % l])

        ui = [0]

        def nui():
            ui[0] ^= 1
            return ui[0]

        S.label = 'rwA'
        wb, wk = load_group(l, "rwx")
        ps_, pk = proj(wb, wk, 0, 128, N)
        shift_tile(ps_, pk, 18, xs18[:, 0:N], ["xs18"], nui())
        ACT(lambda e: e.activation(out=txw[0:64, 0:N], in_=xs18[0:64, 0:N], func=AF.Tanh), ["xs18"], ["txw"])
        ACT(lambda e: e.copy(out=txw[64:128, 0:N], in_=xs18[64:128, 0:N]), ["xs18", "txw"], ["txw"])
        lk = ["lora%d" % l, "lora%da" % l]

        def jtile(j, wbk, wkk, wbr, wkr, wbv, wkv, c0):
            jp = j % 2
            ldec = ldec_2[jp]
            Gc = Gc_2[jp]
            aa = aa_2[jp]
            eneg = eneg_2[jp]
            eprev = eprev_2[jp]
            ehat = ehat_2[jp]
            epos = epos_2[jp]
            kx = kx_2[jp]
            kkr = kkr_2[jp]
            rn = rn_2[jp]
            kkn = kkn_2[jp]
            tk = tk_2[jp]
            kmod = kmod_2[jp]
            bb = bb_2[jp]
            rx = rx_2[jp]
            kksq = kksq_2[jp]
            ps_, pk = fbank()
            MM(ps_[:, 0:N], [(lora[l][0:64, j * 128:(j + 1) * 128], txw[0:64, 0:N])], lk + ["txw"], pk)
            ACT(lambda e, ps_=ps_: e.activation(out=ldec[:, 0:N], in_=ps_[:, 0:N], func=AF.Sigmoid, bias=PPc(l, "w0", j), scale=1.0),
                pk + [pl], ["ldec%d" % jp])
            POOL(lambda e: e.tensor_scalar(out=ldec[:, 0:N], in0=ldec[:, 0:N], scalar1=-EXPM05, scalar2=None, op0=ALU.mult),
                 ["ldec%d" % jp], ["ldec%d" % jp])
            ps2, pk2 = fbank()
            MM(ps2[:, 0:N], [(lora[l][64:128, j * 128:(j + 1) * 128], txw[64:128, 0:N])], lk + ["txw"], pk2)
            ACT(lambda e, ps2=ps2: e.activation(out=aa[:, 0:N], in_=ps2[:, 0:N], func=AF.Sigmoid, bias=PPc(l, "a0", j), scale=1.0),
                pk2 + [pl], ["aa%d" % jp])
            DVE(lambda e: e.tensor_tensor_scan(out=Gc[:, 0:N], data0=scanm[:, 0:N], data1=ldec[:, 0:N], initial=0.0,
                                               op0=ALU.mult, op1=ALU.add), ["ldec%d" % jp, "scanm"], ["Gc%d" % jp])
            for q, (o, C, sq) in enumerate(tiles):
                ACT(lambda e, q=q, o=o, C=C: e.activation(out=gC[:, j, q:q + 1], in_=Gc[:, o + C - 1:o + C], func=AF.Exp),
                    ["Gc%d" % jp, "gC"], ["gC"])
            ps_, pk = proj(wbk, wkk, c0, 128, N)
            shift_tile(ps_, pk, 6 + j, kx[:, 0:N], ["kx%d" % jp], nui())
            ACT(lambda e: e.activation(out=eneg[:, 0:N], in_=Gc[:, 0:N], func=AF.Exp, scale=-1.0), ["Gc%d" % jp], ["eneg%d" % jp])
            DVE(lambda e: e.tensor_tensor(out=eprev[:, 0:N], in0=Gc[:, 0:N], in1=ldec[:, 0:N], op=ALU.subtract),
                ["Gc%d" % jp, "ldec%d" % jp], ["eprev%d" % jp])
            ACT(lambda e: e.activation(out=eprev[:, 0:N], in_=eprev[:, 0:N], func=AF.Exp), ["eprev%d" % jp], ["eprev%d" % jp])
            for q, (o, C, sq) in enumerate(tiles):
                ACT(lambda e, o=o, C=C: e.activation(out=ehat[:, o:o + C], in_=Gc[:, o:o + C], func=AF.Exp, scale=-1.0),
                    ["Gc%d" % jp, "ehat%d" % jp], ["ehat%d" % jp])
                DVE(lambda e, o=o, C=C, q=q: e.tensor_scalar(out=ehat[:, o:o + C], in0=ehat[:, o:o + C],
                                                             scalar1=gC[:, j, q:q + 1], scalar2=None, op0=ALU.mult),
                    ["ehat%d" % jp, "gC"], ["ehat%d" % jp])
            DVE(lambda e: e.tensor_scalar(out=kkr[:, 0:N], in0=kx[:, 0:N], scalar1=PPc(l, "kk", j), scalar2=None,
                                          op0=ALU.mult), ["kx%d" % jp, pl], ["kkr%d" % jp])
            POOL(lambda e: e.tensor_tensor(out=kksq[:, 0:N], in0=kkr[:, 0:N], in1=kkr[:, 0:N], op=ALU.mult), ["kkr%d" % jp], ["kksq%d" % jp])
            ps2, pk2 = fbank()
            MM(ps2[:, 0:N], [(blk1[:], kksq[:, 0:N])], ["blk1", "kksq%d" % jp], pk2)
            ACT(lambda e, ps2=ps2: e.activation(out=rn[:, 0:N], in_=ps2[:, 0:N], func=AF.Sqrt, bias=1e-12, scale=1.0), pk2, ["rn%d" % jp])
            DVE(lambda e: e.reciprocal(out=rn[:, 0:N], in_=rn[:, 0:N]), ["rn%d" % jp], ["rn%d" % jp])
            DVE(lambda e: e.tensor_tensor(out=kkn[:, 0:N], in0=kkr[:, 0:N], in1=rn[:, 0:N], op=ALU.mult), ["kkr%d" % jp, "rn%d" % jp], ["kkn%d" % jp])
            DVE(lambda e: e.tensor_scalar(out=tk[:, 0:N], in0=aa[:, 0:N], scalar1=PPc(l, "ka", j),
                                          scalar2=omka[l][:, j:j + 1], op0=ALU.mult, op1=ALU.add),
                ["aa%d" % jp, pl, "omka%d" % l], ["tk%d" % jp])
            DVE(lambda e: e.tensor_tensor(out=kmod[:, 0:N], in0=kx[:, 0:N], in1=tk[:, 0:N], op=ALU.mult), ["kx%d" % jp, "tk%d" % jp], ["kmod%d" % jp])
            POOL(lambda e: e.tensor_tensor(out=bb[:, 0:N], in0=kkn[:, 0:N], in1=aa[:, 0:N], op=ALU.mult), ["kkn%d" % jp, "aa%d" % jp], ["bb%d" % jp])
            DVE(lambda e: e.tensor_tensor(out=kt_[:, j, 0:N], in0=kmod[:, 0:N], in1=eneg[:, 0:N], op=ALU.mult),
                ["kmod%d" % jp, "eneg%d" % jp, "kt__%d" % j], ["kt__%d" % j])
            POOL(lambda e: e.tensor_tensor(out=bt_[:, j, 0:N], in0=bb[:, 0:N], in1=eneg[:, 0:N], op=ALU.mult),
                 ["bb%d" % jp, "eneg%d" % jp, "bt__%d" % j], ["bt__%d" % j])
            DVE(lambda e: e.scalar_tensor_tensor(out=at_[:, j, 0:N], in0=kkn[:, 0:N], scalar=-1.0, in1=eprev[:, 0:N],
                                                 op0=ALU.mult, op1=ALU.mult), ["kkn%d" % jp, "eprev%d" % jp, "at__%d" % j], ["at__%d" % j])
            POOL(lambda e: e.tensor_tensor(out=khat[:, j, 0:N], in0=kmod[:, 0:N], in1=ehat[:, 0:N], op=ALU.mult),
                 ["kmod%d" % jp, "ehat%d" % jp, "khat_%d" % j], ["khat_%d" % j])
            DVE(lambda e: e.tensor_tensor(out=bhat[:, j, 0:N], in0=bb[:, 0:N], in1=ehat[:, 0:N], op=ALU.mult),
                ["bb%d" % jp, "ehat%d" % jp, "bhat_%d" % j], ["bhat_%d" % j])
            ps_, pk = proj(wbr, wkr, c0, 128, N)
            shift_tile(ps_, pk, j, rx[:, 0:N], ["rx%d" % jp], nui())
            ACT(lambda e: e.activation(out=epos[:, 0:N], in_=Gc[:, 0:N], func=AF.Exp), ["Gc%d" % jp], ["epos%d" % jp])
            DVE(lambda e: e.tensor_tensor(out=rt_[:, j, 0:N], in0=rx[:, 0:N], in1=epos[:, 0:N], op=ALU.mult),
                ["rx%d" % jp, "epos%d" % jp, "rt__%d" % j], ["rt__%d" % j])
            DVE(lambda e: e.scalar_tensor_tensor(out=rkp[:, j, 0:N], in0=rx[:, 0:N], scalar=PPc(l, "rk", j),
                                                 in1=kmod[:, 0:N], op0=ALU.mult, op1=ALU.mult),
                ["rx%d" % jp, pl, "kmod%d" % jp, "rkp_%d" % j], ["rkp_%d" % j])
            ps_, pk = proj(wbv, wkv, c0, 128, N)
            shift_tile(ps_, pk, 12 + j, vT[:, j, 0:N], ["vT_%d" % j], nui())

        for part, js in (("0", range(4)), ("1", range(4, 6))):
            wbk, wkk = load_group(l, "rwk" + part)
            wbr, wkr = load_group(l, "rwr" + part)
            wbv, wkv = load_group(l, "rwv" + part)
            for j in js:
                jtile(j, wbk, wkk, wbr, wkr, wbv, wkv, (j % 4) * 128)

        for gn, js in (("rwg0", range(4)), ("rwg1", range(4, 6))):
            wb, wk = load_group(l, gn)
            for j in js:
                ps_, pk = proj(wb, wk, (j % 4) * 128, 128, N)
                ACT(lambda e, ps_=ps_, j=j: e.activation(out=grw[:, j, 0:N], in_=ps_, func=AF.Silu), pk + ["grw_%d" % j], ["grw_%d" % j])

        for q, (o, C, sq) in enumerate(tiles):
            rwkv_tile(l, q, o, C, sq, kind, og, last, nq)
        if samp:
            for q, (o, C, sq) in enumerate(tiles):
                ps_, pk = fbank()
                TR(ps_[0:19, 0:128], sshift_o[:, :, q], ident_f[:], ["sshift_o", "ident_f"], pk)
                ACT(lambda e, ps_=ps_: e.copy(out=lnx[0:19, 0:128], in_=ps_[0:19, 0:128]), pk + ["lnx"], ["lnx"])
                STORE(o_shift["s"][l, sq].rearrange("(t p) -> t p", p=128), lnx[0:19, 0:128], ["lnx"])
        elif last:
            ps_, pk = fbank()
            TR(ps_[0:19, 0:128], shiftst[l][:, :], ident_f[:], ["shiftst%d" % l, "ident_f"], pk)
            ACT(lambda e, ps_=ps_: e.copy(out=lnx[0:19, 0:128], in_=ps_[0:19, 0:128]), pk + ["lnx"], ["lnx"])
            STORE(o_shift["p"][l, 0].rearrange("(t p) -> t p", p=128), lnx[0:19, 0:128], ["lnx"])

        S.label = 's5A'
        ensure_s5(l)
        wb, wk = load_group(l, "s5u")
        for j in range(4):
            ps_, pk = proj(wb, wk, j * 128, 128, N)
            ACT(lambda e, ps_=ps_, j=j: e.copy(out=u32[:, j, 0:N], in_=ps_), pk + ["u32_%d" % j], ["u32_%d" % j])
            DVE(lambda e, j=j: e.tensor_copy(out=uT[:, j, 0:N], in_=u32[:, j, 0:N]), ["u32_%d" % j, "uT_%d" % j], ["uT_%d" % j])
        wb, wk = load_group(l, "s5g")
        for j in range(4):
            ps_, pk = proj(wb, wk, j * 128, 128, N)
            ACT(lambda e, ps_=ps_, j=j: e.activation(out=gs5[:, j, 0:N], in_=ps_, func=AF.Silu), pk + ["gs5_%d" % j], ["gs5_%d" % j])
        for q, (o, C, sq) in enumerate(tiles):
            s5_tile(l, q, o, C, sq, kind, og, last, nq)
        ensure_s5(1 - l)

        S.label = 'mlA'
        S.label = 'mlA'
        for gi_ in range(4):
            wb, wk = load_group(l, "mlqk%d" % gi_)
            for jj in range(4):
                t = gi_ * 4 + jj
                ps_, pk = proj(wb, wk, jj * 96, 96, N)
                ACT(lambda e, ps_=ps_, t=t: e.copy(out=qkraw[:, t, 3:N + 3], in_=ps_), pk + ["qkraw_%d" % t], ["qkraw_%d" % t])
        qk = ["qkraw"]
        if not samp:
            ACT(lambda e: e.copy(out=qkraw[:, :, 0:3], in_=convst[l][:, :, :]), ["convst%d" % l] + qk, qk)
        else:
            ACT(lambda e: e.copy(out=qkraw[:, :, 0:3], in_=sconv[:, :, 0, :]), ["sconv"] + qk, qk)

        def wbc(jt, n_):
            return pq[l][:, :, jt:jt + 1].broadcast_to([96, 16, n_])

        pqk = "pq%d" % l
        DVE(lambda e: e.tensor_tensor(out=cvacc[:, :, 0:N], in0=qkraw[:, :, 0:N], in1=wbc(0, N), op=ALU.mult), qk + [pqk], ["cvacc"])
        POOL(lambda e: e.tensor_tensor(out=cvacc[:, :, 0:N], in0=cvacc[:, :, 0:N], in1=wbc(4, N), op=ALU.add), ["cvacc", pqk], ["cvacc"])
        for jt in range(1, 4):
            POOL(lambda e, jt=jt: e.tensor_tensor(out=cvtmp[:, :, 0:N], in0=qkraw[:, :, jt:N + jt], in1=wbc(jt, N), op=ALU.mult),
                 qk + [pqk, "cvtmp"], ["cvtmp"])
            DVE(lambda e: e.tensor_tensor(out=cvacc[:, :, 0:N], in0=cvacc[:, :, 0:N], in1=cvtmp[:, :, 0:N], op=ALU.add),
                ["cvacc", "cvtmp"], ["cvacc"])
        if samp:
            hk = "hbuf0"
            hb = hbuf
            POOL(lambda e: e.tensor_copy(out=hb[:, :, 0:3], in_=sconv[:, :, 1, :]), ["sconv", hk], [hk])
            POOL(lambda e: e.tensor_copy(out=hb[:, :, 3:6], in_=qkraw[:, :, 67:70]), qk + [hk], [hk])
            f3 = cvacc[:, :, 64:67]
            DVE(lambda e: e.tensor_tensor(out=f3, in0=hb[:, :, 0:3], in1=wbc(0, 3), op=ALU.mult), [hk, pqk, "cvacc"], ["cvacc"])
            DVE(lambda e: e.tensor_tensor(out=f3, in0=f3, in1=wbc(4, 3), op=ALU.add), [pqk, "cvacc"], ["cvacc"])
            for jt in range(1, 4):
                DVE(lambda e, jt=jt: e.tensor_tensor(out=cvtmp[:, :, 0:3], in0=hb[:, :, jt:jt + 3], in1=wbc(jt, 3), op=ALU.mult),
                    [hk, pqk, "cvtmp"], ["cvtmp"])
                DVE(lambda e: e.tensor_tensor(out=f3, in0=f3, in1=cvtmp[:, :, 0:3], op=ALU.add), ["cvacc", "cvtmp"], ["cvacc"])
        ACT(lambda e: e.activation(out=qkT[:, :, 0:N], in_=cvacc[:, :, 0:N], func=AF.Silu), ["cvacc", "qkT"], ["qkT"])
        POOL(lambda e: e.tensor_scalar(out=qkT[:, 8:16, 0:N], in0=qkT[:, 8:16, 0:N], scalar1=1.0 / math.sqrt(192.0), scalar2=None,
                                       op0=ALU.mult), ["qkT"], ["qkT"])
        if not samp:
            ACT(lambda e: e.copy(out=convst[l][:, :, :], in_=qkraw[:, :, N:N + 3]), qk + ["convst%d" % l], ["convst%d" % l])
        conv_out(l, N, tiles, kind, og, last)
        wb0, wk0 = load_group(l, "mlv0")
        wb1, wk1 = load_group(l, "mlv1")
        for q, (o, C, sq) in enumerate(tiles):
            ps_, pk = fpair()
            MM(ps_[0:C, 0, 0:512], [(xT[:, kc, o:o + C], wb0[:, kc, 0:512]) for kc in range(8)], [wk0, "xT"], [pk[0]])
            MM(ps_[0:C, 1, 0:264], [(xT[:, kc, o:o + C], wb1[:, kc, 0:264]) for kc in range(8)], [wk1, "xT"], [pk[1]])
            vk = ["vaug"]
            ACT(lambda e, ps_=ps_, q=q, C=C: e.copy(out=vaug[0:C, q, 0:2, 0:192],
                                                    in_=ps_[0:C, 0, 0:384].rearrange("p (h d) -> p h d", d=192)), [pk[0]] + vk, vk)
            ACT(lambda e, ps_=ps_, q=q, C=C: e.copy(out=vaug[0:C, q, 2, 0:128], in_=ps_[0:C, 0, 384:512]), [pk[0]] + vk, vk)
            DVE(lambda e, ps_=ps_, q=q, C=C: e.tensor_copy(out=vaug[0:C, q, 2, 128:192], in_=ps_[0:C, 1, 0:64]), [pk[1]] + vk, vk)
            DVE(lambda e, ps_=ps_, q=q, C=C: e.tensor_copy(out=vaug[0:C, q, 3, 0:192], in_=ps_[0:C, 1, 64:256]), [pk[1]] + vk, vk)
            POOL(lambda e, q=q, C=C: e.memset(vaug[0:C, q, :, 192:193], 1.0), vk, vk)
            DVE(lambda e, ps_=ps_, q=q, C=C: e.tensor_tensor(out=iftok[0:C, q, :], in0=ps_[0:C, 1, 256:264], in1=bif[l][0:C, :],
                                                             op=ALU.add), [pk[1], "bif%d" % l, "iftok"], ["iftok"])
        wbo0, wko0 = load_group(l, "mlo0")
        wbo1, wko1 = load_group(l, "mlo1")
        for j in range(6):
            wb, wk = (wbo0, wko0) if j < 4 else (wbo1, wko1)
            ps_, pk = proj(wb, wk, (j % 4) * 128, 128, N)
            ACT(lambda e, ps_=ps_, j=j: e.activation(out=ogT[:, j, 0:N], in_=ps_, func=AF.Sigmoid), pk + ["ogT_%d" % j], ["ogT_%d" % j])
        wbz0, wkz0 = load_group(l, "mlz0")
        wbz1, wkz1 = load_group(l, "mlz1")
        for j in range(6):
            wb, wk = (wbz0, wkz0) if j < 4 else (wbz1, wkz1)
            ps_, pk = proj(wb, wk, (j % 4) * 128, 128, N)
            zsl = zsl_2[j % 2]
            ACT(lambda e, ps_=ps_, zsl=zsl: e.activation(out=zsl[:, 0:N], in_=ps_, func=AF.Silu), pk, ["zsl%d" % (j % 2)])
            DVE(lambda e, j=j, zsl=zsl: e.scalar_tensor_tensor(out=ogT[:, j, 0:N], in0=ogT[:, j, 0:N], scalar=PPc(l, "mlg", j),
                                                               in1=zsl[:, 0:N], op0=ALU.mult, op1=ALU.mult),
                ["ogT_%d" % j, "zsl%d" % (j % 2), pl], ["ogT_%d" % j])
        for q, (o, C, sq) in enumerate(tiles):
            ml_tile(l, q, o, C, sq, kind, og, last, nq)

    def conv_out(l, N, tiles, kind, og, last):
        if kind == "sample":
            ends = [(o + C - 3, sq) for (o, C, sq) in tiles]
        elif last:
            ends = [(N - 3, 0)]
        else:
            return
        for (e0, sq) in ends:
            for hf in range(2):
                for i2 in range(2):
                    i = hf * 2 + i2
                    wbi, wki = load_group(l, "mlqk%d" % i)
                    ps_, pk = fbank()
                    MM(ps_[0:3, 0:384], [(xT[:, kc, e0:e0 + 3], wbi[:, kc, 0:384]) for kc in range(8)], [wki, "xT"], pk)
                    ACT(lambda e, i2=i2, ps_=ps_: e.copy(out=lnt[0:3, i2 * 384:(i2 + 1) * 384], in_=ps_[0:3, 0:384]), pk + ["lnt"], ["lnt"])
                STORE(o_conv[og][l, sq, :, hf * 768:(hf + 1) * 768], lnt[0:3, 0:768], ["lnt"])

    def rwkv_tile(l, q, o, C, sq, kind, og, last, nq):
        S.label = 'rwT'
        tb_ = TB[q % 2]
        pz = tb_['z']
        vtok, khtok, bhtok, Nsb, NTsb, Msb, P1sb, P2sb = (tb_[k_] for k_ in ('vtok', 'khtok', 'bhtok', 'Nsb', 'NTsb', 'Msb', 'P1sb', 'P2sb'))
        samp = kind == "sample"
        sk = "s0t%d" % l
        sl = slice(o, o + C)
        PARTS = [(0, 0), (1, 64)]
        if samp:
            LOAD(stg[:], st_wkv[l, sq].rearrange("h v k -> v h k"), [], ["stg"])
            for j in range(6):
                ps_, pk = fbank()
                TR(ps_[:, 0:64], stg[:, 2 * j:2 * j + 2, :].rearrange("p a b -> p (a b)"), ident_f[0:64, 0:64], ["stg", "ident_f"], pk)
                ACT(lambda e, ps_=ps_, j=j: e.copy(out=S0T[l][:, j, :], in_=ps_[:, 0:64]), pk + [sk], [sk])
        ACT(lambda e: e.copy(out=S0Tb[:], in_=S0T[l][:]), [sk], ["s0tb"])

        def both(fn_):
            if C == 64:
                fn_(slice(0, 128))
            else:
                for par, pb in PARTS:
                    fn_(slice(pb, pb + C))

        for src, srck, dst, dk in ((vT, "vT", vtok, "vtok" + pz), (khat, "khat", khtok, "khtok" + pz), (bhat, "bhat", bhtok, "bhtok" + pz)):
            def fnt(e, src=src):
                for j in range(6):
                    for par, pb in PARTS:
                        ins = e.transpose(out=PSB[pb:pb + C, par, j * 64:(j + 1) * 64], in_=src[pb:pb + 64, j, sl],
                                          identity=ident_b[pb:pb + 64, pb:pb + 64])
                return ins
            PE(fnt, [srck, "ident_b"], ["pb0", "pb1"])
            for par, pb in PARTS:
                ACT(lambda e, dst=dst, par=par, pb=pb: e.copy(out=dst[pb:pb + C, :, :],
                                                                in_=PSB[pb:pb + C, par, 0:384].rearrange("p (j d) -> p j d", d=64)),
                    ["pb%d" % par, dk], [dk])

        def score(lt, lk_, rt2, rk_, mask, mk, dst, dk):
            ps_, pk = fpair()
            def fn(e, ps_=ps_):
                for j in range(6):
                    for par, pb in PARTS:
                        ins = e.matmul(ps_[pb:pb + C, par, j * 64:j * 64 + C], lhsT=lt[pb:pb + 64, j, sl],
                                       rhs=rt2[pb:pb + 64, j, sl], start=True, stop=True)
                return ins
            PE(fn, [lk_, rk_], pk)
            for par, pb in PARTS:
                DVE(lambda e, ps_=ps_, par=par, pb=pb: e.tensor_tensor(
                    out=dst[pb:pb + C, :, 0:C], in0=ps_[pb:pb + C, par, 0:384].rearrange("p (j t) -> p j t", t=64)[:, :, 0:C],
                    in1=mask[pb:pb + C, :, 0:C], op=ALU.mult), [pk[par], mk, dk], [dk])

        def evac(ps_, pk, dst, dk, eng=ACT):
            for par, pb in PARTS:
                if par == 0:
                    ACT(lambda e, par=par, pb=pb: e.copy(out=dst[pb:pb + C, :, :],
                                                         in_=ps_[pb:pb + C, par, 0:384].rearrange("p (j d) -> p j d", d=64)),
                        [pk[par], dk], [dk])
                else:
                    DVE(lambda e, par=par, pb=pb: e.tensor_copy(out=dst[pb:pb + C, :, :],
                                                                in_=ps_[pb:pb + C, par, 0:384].rearrange("p (j d) -> p j d", d=64)),
                        [pk[par], dk], [dk])

        def evac_sq(ps_, pk, dst, dk):
            for par, pb in PARTS:
                DVE(lambda e, par=par, pb=pb: e.tensor_copy(
                    out=dst[pb:pb + C, :, 0:C], in_=ps_[pb:pb + C, par, 0:384].rearrange("p (j t) -> p j t", t=64)[:, :, 0:C]),
                    [pk[par], dk], [dk])

        score(bt_, "bt_", at_, "at_", m_strict, "m_strict", Nsb[0], "Nsb0" + pz)
        score(at_, "at_", bt_, "bt_", m_lower, "m_lower", NTsb[0], "NTsb0" + pz)
        score(kt_, "kt_", at_, "at_", m_strict, "m_strict", Msb, "Msb" + pz)

        ps_, pk = fpair()
        def fnx(e, ps_=ps_):
            for j in range(6):
                for par, pb in PARTS:
                    out = ps_[pb:pb + C, par, j * 64:(j + 1) * 64]
                    e.matmul(out, lhsT=at_[pb:pb + 64, j, sl], rhs=S0Tb[pb:pb + 64, j, :], start=True, stop=False)
                    ins = e.matmul(out, lhsT=Msb[pb:pb + C, j, 0:C], rhs=vtok[pb:pb + C, j, :], start=False, stop=True)
            return ins
        PE(fnx, ["at_", "s0tb", "Msb" + pz, "vtok" + pz], pk)
        evac(ps_, pk, Ysb[0], "Ysb0")
        lev = int(round(math.log2(C)))
        cur = 0
        for lv in range(lev):
            Pc, PTc, Yc = Nsb[cur], NTsb[cur], Ysb[cur]
            Pn, PTn, Yn = Nsb[1 - cur], NTsb[1 - cur], Ysb[1 - cur]
            ps_, pk = fpair()
            def fny(e, ps_=ps_, Pc=Pc, Yc=Yc):
                for j in range(6):
                    for par, pb in PARTS:
                        out = ps_[pb:pb + C, par, j * 64:(j + 1) * 64]
                        e.matmul(out, lhsT=ident_b[pb:pb + C, pb:pb + C], rhs=Yc[pb:pb + C, j, :], start=True, stop=False)
                        ins = e.matmul(out, lhsT=Pc[pb:pb + C, j, 0:C], rhs=Yc[pb:pb + C, j, :], start=False, stop=True)
                return ins
            PE(fny, ["Nsb%d" % cur + pz, "Ysb%d" % cur, "ident_b"], pk)
            evac(ps_, pk, Yn, "Ysb%d" % (1 - cur))
            if lv < lev - 1:
                ps2, pk2 = fpair()
                def fnp(e, ps2=ps2, Pc=Pc, PTc=PTc):
                    for j in range(6):
                        for par, pb in PARTS:
                            ins = e.matmul(ps2[pb:pb + C, par, j * 64:j * 64 + C], lhsT=PTc[pb:pb + C, j, 0:C],
                                           rhs=Pc[pb:pb + C, j, 0:C], start=True, stop=True)
                    return ins
                PE(fnp, ["Nsb%d" % cur + pz, "NTsb%d" % cur + pz], pk2)
                evac_sq(ps2, pk2, Pn, "Nsb%d" % (1 - cur) + pz)
                if lv < lev - 2:
                    ps3, pk3 = fpair()
                    def fnq(e, ps3=ps3, Pc=Pc, PTc=PTc):
                        for j in range(6):
                            for par, pb in PARTS:
                                ins = e.matmul(ps3[pb:pb + C, par, j * 64:j * 64 + C], lhsT=Pc[pb:pb + C, j, 0:C],
                                               rhs=PTc[pb:pb + C, j, 0:C], start=True, stop=True)
                        return ins
                    PE(fnq, ["Nsb%d" % cur + pz, "NTsb%d" % cur + pz], pk3)
                    evac_sq(ps3, pk3, PTn, "NTsb%d" % (1 - cur) + pz)
            cur = 1 - cur
        UT = Ysb[cur]
        uk = "Ysb%d" % cur
        score(kt_, "kt_", rt_, "rt_", m_incl, "m_incl", P1sb, "P1sb" + pz)
        score(bt_, "bt_", rt_, "rt_", m_incl, "m_incl", P2sb, "P2sb" + pz)
        ps_, pk = fpair()
        def fno(e, ps_=ps_, UT=UT):
            for j in range(6):
                for par, pb in PARTS:
                    out = ps_[pb:pb + C, par, j * 64:(j + 1) * 64]
                    e.matmul(out, lhsT=rt_[pb:pb + 64, j, sl], rhs=S0Tb[pb:pb + 64, j, :], start=True, stop=False)
                    e.matmul(out, lhsT=P1sb[pb:pb + C, j, 0:C], rhs=vtok[pb:pb + C, j, :], start=False, stop=False)
                    ins = e.matmul(out, lhsT=P2sb[pb:pb + C, j, 0:C], rhs=UT[pb:pb + C, j, :], start=False, stop=True)
            return ins
        PE(fno, ["rt_", "s0tb", "P1sb" + pz, "P2sb" + pz, "vtok" + pz, uk], pk)
        evac(ps_, pk, yo, "yo")
        psr, pkr = fpair()
        def fnr(e, psr=psr):
            for j in range(6):
                for par, pb in PARTS:
                    ins = e.matmul(psr[pb:pb + C, par, j:j + 1], lhsT=rkp[pb:pb + 64, j, sl], rhs=onesb[pb:pb + 64, 0:1],
                                   start=True, stop=True)
            return ins
        PE(fnr, ["rkp", "onesb"], pkr)
        for par, pb in PARTS:
            ACT(lambda e, par=par, pb=pb, psr=psr: e.copy(out=rkd[pb:pb + C, :], in_=psr[pb:pb + C, par, 0:6]), [pkr[par], "rkd"], ["rkd"])
        both(lambda P: DVE(lambda e: e.tensor_reduce(out=ystat[P, 0, :], in_=yo[P], axis=AX.X, op=ALU.add), ["yo", "ystat"], ["ystat"]))
        both(lambda P: DVE(lambda e: e.tensor_scalar(out=ystat[P, 0, :], in0=ystat[P, 0, :], scalar1=1.0 / 64, scalar2=None, op0=ALU.mult),
                           ["ystat"], ["ystat"]))
        both(lambda P: DVE(lambda e: e.tensor_tensor(out=ycen[P], in0=yo[P], in1=ystat[P, 0, :].unsqueeze(2).broadcast_to([P.stop - P.start, 6, 64]),
                                                     op=ALU.subtract), ["yo", "ystat", "ycen"], ["ycen"]))
        both(lambda P: POOL(lambda e: e.tensor_tensor(out=ysq[P], in0=ycen[P], in1=ycen[P], op=ALU.mult), ["ycen", "ysq"], ["ysq"]))
        both(lambda P: DVE(lambda e: e.tensor_reduce(out=ystat[P, 1, :], in_=ysq[P], axis=AX.X, op=ALU.add), ["ysq", "ystat"], ["ystat"]))
        both(lambda P: ACT(lambda e: e.activation(out=ystat[P, 2, :], in_=ystat[P, 1, :], func=AF.Sqrt, bias=RW_GN_EPS, scale=1.0 / 64),
                           ["ystat"], ["ystat"]))
        both(lambda P: DVE(lambda e: e.reciprocal(out=ystat[P, 2, :], in_=ystat[P, 2, :]), ["ystat"], ["ystat"]))
        both(lambda P: DVE(lambda e: e.tensor_tensor(out=ycen[P], in0=ycen[P], in1=ystat[P, 2, :].unsqueeze(2).broadcast_to([P.stop - P.start, 6, 64]),
                                                     op=ALU.mult), ["ycen", "ystat"], ["ycen"]))
        both(lambda P: DVE(lambda e: e.tensor_tensor(out=ycen[P].rearrange("p j d -> p (j d)"), in0=ycen[P].rearrange("p j d -> p (j d)"),
                                                     in1=rwln[l][P, 0, :], op=ALU.mult), ["ycen", "rwln%d" % l], ["ycen"]))
        both(lambda P: POOL(lambda e: e.tensor_tensor(out=ycen[P].rearrange("p j d -> p (j d)"), in0=ycen[P].rearrange("p j d -> p (j d)"),
                                                      in1=rwln[l][P, 1, :], op=ALU.add), ["ycen", "rwln%d" % l], ["ycen"]))
        both(lambda P: POOL(lambda e: e.tensor_tensor(out=ysq[P], in0=vtok[P], in1=rkd[P, :].unsqueeze(2).broadcast_to([P.stop - P.start, 6, 64]),
                                                      op=ALU.mult), ["vtok" + pz, "rkd", "ysq"], ["ysq"]))
        both(lambda P: DVE(lambda e: e.tensor_tensor(out=ytb[P], in0=ycen[P], in1=ysq[P], op=ALU.add), ["ycen", "ysq", "ytb"], ["ytb"]))
        def fnb(e):
            for j in range(6):
                for par, pb in PARTS:
                    ins = e.transpose(out=PSB[pb:pb + 64, par, j * 64:j * 64 + C], in_=ytb[pb:pb + C, j, :],
                                      identity=ident_b[pb:pb + C, pb:pb + C])
            return ins
        PE(fnb, ["ytb", "ident_b"], ["pb0", "pb1"])
        for par, pb in PARTS:
            DVE(lambda e, par=par, pb=pb: e.tensor_tensor(out=yrwT[pb:pb + 64, :, sl],
                                                          in0=PSB[pb:pb + 64, par, 0:384].rearrange("p (j t) -> p j t", t=64)[:, :, 0:C],
                                                          in1=grw[pb:pb + 64, :, sl], op=ALU.mult), ["pb%d" % par, "grw", "yrwT"], ["yrwT"])
        ps_, pk = fpair()
        def fns(e, ps_=ps_, UT=UT):
            for j in range(6):
                for par, pb in PARTS:
                    out = ps_[pb:pb + 64, par, j * 64:(j + 1) * 64]
                    e.matmul(out, lhsT=khtok[pb:pb + C, j, :], rhs=vtok[pb:pb + C, j, :], start=True, stop=False)
                    ins = e.matmul(out, lhsT=bhtok[pb:pb + C, j, :], rhs=UT[pb:pb + C, j, :], start=False, stop=True)
            return ins
        PE(fns, ["khtok" + pz, "bhtok" + pz, "vtok" + pz, uk], pk)
        for j in range(6):
            for par, pb in PARTS:
                DVE(lambda e, j=j, ps_=ps_, par=par, pb=pb: e.scalar_tensor_tensor(
                    out=S0T[l][pb:pb + 64, j, :], in0=S0T[l][pb:pb + 64, j, :], scalar=gC[pb:pb + 64, j, q:q + 1],
                    in1=ps_[pb:pb + 64, par, j * 64:(j + 1) * 64], op0=ALU.mult, op1=ALU.add), [pk[par], sk, "gC"], [sk])
        if samp or (last and q == nq - 1):
            b_ = sq if samp else 0
            for j in range(6):
                ps2, pk2 = fbank()
                TR(ps2[0:64, 0:128], S0T[l][:, j, :], ident_f[:], [sk, "ident_f"], pk2)
                ACT(lambda e, ps2=ps2, j=j: e.copy(out=stg[:, 2 * j:2 * j + 2, :].rearrange("p a b -> p (a b)"), in_=ps2[0:64, 0:128]),
                    pk2 + ["stg"], ["stg"])
            STORE(o_wkv[og][l, b_].rearrange("h v k -> v h k"), stg[:], ["stg"])

    def s5_tile(l, q, o, C, sq, kind, og, last, nq):
        S.label = 's5T'
        samp = kind == "sample"
        hk = "h0%d" % l
        sl = slice(o, o + C)
        if samp:
            for c, srcd in ((0, st_s5re), (1, st_s5im)):
                LOAD(lnx[0:16, c * 128:(c + 1) * 128], srcd[l, sq].rearrange("(i g) p -> i (g p)", g=2), [], ["lnx"])
            for c in range(2):
                ps_, pk = fbank()
                TR(ps_[:, 0:16], lnx[0:16, c * 128:(c + 1) * 128], ident_f[0:16, 0:16], ["lnx", "ident_f"], pk)
                ACT(lambda e, ps_=ps_, c=c: e.copy(out=h0[l][:, c, :], in_=ps_[:, 0:16]), pk + [hk], [hk])
        for ut in range(4):
            ps_, pk = fpair()
            MM(ps_[0:C, 0, :], [(uT[:, ut, sl], bblkB[:, ut, 0:512])], ["uT", "bblkB"], [pk[0]])
            MM(ps_[0:C, 1, :], [(uT[:, ut, sl], bblkB[:, ut, 512:1024])], ["uT", "bblkB"], [pk[1]])
            pvv = ps_[0:C].rearrange("p b (i c q) -> p (b i) c q", i=2, c=2)
            bur, bui = pvv[:, :, 0, :], pvv[:, :, 1, :]
            enr, eni = EnB[0:C, ut * 4:(ut + 1) * 4, 0, :], EnB[0:C, ut * 4:(ut + 1) * 4, 1, :]
            wr_, wi_ = wtok[0:C, ut * 4:(ut + 1) * 4, 0, :], wtok[0:C, ut * 4:(ut + 1) * 4, 1, :]
            ek = ["EnB"]
            DVE(lambda e, bur=bur, enr=enr: e.tensor_tensor(out=s5a[0:C], in0=bur, in1=enr, op=ALU.mult), pk + ek, ["s5a"])
            DVE(lambda e, bui=bui, eni=eni: e.tensor_tensor(out=s5b[0:C], in0=bui, in1=eni, op=ALU.mult), pk + ek, ["s5b"])
            POOL(lambda e, wr_=wr_: e.tensor_tensor(out=wr_, in0=s5a[0:C], in1=s5b[0:C], op=ALU.subtract), ["s5a", "s5b", "wtok"], ["wtok"])
            DVE(lambda e, bur=bur, eni=eni: e.tensor_tensor(out=s5a[0:C], in0=bur, in1=eni, op=ALU.mult), pk + ek + ["s5a"], ["s5a"])
            DVE(lambda e, bui=bui, enr=enr: e.tensor_tensor(out=s5b[0:C], in0=bui, in1=enr, op=ALU.mult), pk + ek + ["s5b"], ["s5b"])
            POOL(lambda e, wi_=wi_: e.tensor_tensor(out=wi_, in0=s5a[0:C], in1=s5b[0:C], op=ALU.add), ["s5a", "s5b", "wtok"], ["wtok"])
        for c, Gd, gk in ((0, Gr, "Gr"), (1, Gi, "Gi")):
            ps_, pk = fpair()
            def fnc(e, ps_=ps_, c=c):
                for i in range(16):
                    ins = e.matmul(ps_[:, i // 8, (i % 8) * 64:(i % 8) * 64 + C], lhsT=wtok[0:C, i, c, :], rhs=tri_b[0:C, 0:C],
                                   start=True, stop=True)
                return ins
            PE(fnc, ["wtok", "tri_b"], pk)
            DVE(lambda e, ps_=ps_, Gd=Gd, c=c: e.tensor_tensor(
                out=Gd[:, :, 0:C].rearrange("p (b i) t -> p b i t", b=2),
                in0=ps_[:, :, :].rearrange("p b (i t) -> p b i t", t=64)[:, :, :, 0:C],
                in1=h0[l][:, c, :].rearrange("p (b i) -> p b i", b=2).unsqueeze(3).broadcast_to([128, 2, 8, C]), op=ALU.add),
                pk + [hk], [gk])
        er, ei = EpB[:, 0, :, 0:C], EpB[:, 1, :, 0:C]
        ek = ["EpB"]
        DVE(lambda e: e.tensor_tensor(out=hA[:, :, 0:C], in0=Gr[:, :, 0:C], in1=er, op=ALU.mult), ["Gr"] + ek, ["hA"])
        DVE(lambda e: e.tensor_tensor(out=hB[:, :, 0:C], in0=Gi[:, :, 0:C], in1=ei, op=ALU.mult), ["Gi"] + ek, ["hB"])
        DVE(lambda e: e.tensor_tensor(out=hre[:, :, 0:C], in0=hA[:, :, 0:C], in1=hB[:, :, 0:C], op=ALU.subtract), ["hA", "hB"], ["hre"])
        POOL(lambda e: e.tensor_tensor(out=hC[:, :, 0:C], in0=Gi[:, :, 0:C], in1=er, op=ALU.mult), ["Gi"] + ek, ["hC"])
        POOL(lambda e: e.tensor_tensor(out=hD[:, :, 0:C], in0=Gr[:, :, 0:C], in1=ei, op=ALU.mult), ["Gr"] + ek, ["hD"])
        DVE(lambda e: e.scalar_tensor_tensor(out=himn[:, :, 0:C], in0=hC[:, :, 0:C], scalar=-1.0, in1=hD[:, :, 0:C],
                                             op0=ALU.mult, op1=ALU.subtract), ["hC", "hD"], ["himn"])
        DVE(lambda e: e.tensor_tensor(out=h0[l][:, 0, :], in0=hA[:, :, C - 1], in1=hB[:, :, C - 1], op=ALU.subtract),
            ["hA", "hB", hk, "Gr", "Gi"], [hk])
        DVE(lambda e: e.tensor_tensor(out=h0[l][:, 1, :], in0=hC[:, :, C - 1], in1=hD[:, :, C - 1], op=ALU.add), ["hC", "hD", hk], [hk])
        if samp or (last and q == nq - 1):
            b_ = sq if samp else 0
            for c, dd in ((0, o_s5re), (1, o_s5im)):
                ps3, pk3 = fbank()
                TR(ps3[0:16, 0:128], h0[l][:, c, :], ident_f[:], [hk, "ident_f"], pk3)
                ACT(lambda e, ps3=ps3, c=c: e.copy(out=lnx[0:16, c * 128:(c + 1) * 128], in_=ps3[0:16, 0:128]), pk3 + ["lnx"], ["lnx"])
                STORE(dd[og][l, b_].rearrange("(i g) p -> i (g p)", g=2), lnx[0:16, c * 128:(c + 1) * 128], ["lnx"])
        ps_, pk = fbank()
        def fny(e, ps_=ps_):
            for ut in range(4):
                for hf in range(2):
                    out = ps_[hf * 64:(hf + 1) * 64, ut * 64:ut * 64 + C]
                    n_ = 0
                    for ii in range(2):
                        i = ut * 4 + hf * 2 + ii
                        for c, hsrc in ((0, hre), (1, himn)):
                            ins = e.matmul(out, lhsT=cpadB[:, i, c, :], rhs=hsrc[:, i, 0:C], start=(n_ == 0), stop=(n_ == 3))
                            n_ += 1
            return ins
        PE(fny, ["cpadB", "hre", "himn"], pk)
        for ut in range(4):
            DVE(lambda e, ut=ut, ps_=ps_: e.scalar_tensor_tensor(out=yv[:, ut, 0:C], in0=u32[:, ut, sl], scalar=PPc(l, "s5d", ut),
                                                                 in1=ps_[:, ut * 64:ut * 64 + C], op0=ALU.mult, op1=ALU.add),
                pk + ["u32", "pp%d" % l, "yv"], ["yv"])
        POOL(lambda e: e.tensor_tensor(out=gt[:, :, 0:C], in0=yv[:, :, 0:C], in1=yv[:, :, 0:C], op=ALU.mult), ["yv"], ["gt"])
        DVE(lambda e: e.tensor_scalar(out=gt[:, :, 0:C], in0=gt[:, :, 0:C], scalar1=0.044715, scalar2=1.0, op0=ALU.mult, op1=ALU.add),
            ["gt"], ["gt"])
        DVE(lambda e: e.tensor_tensor(out=gt[:, :, 0:C], in0=gt[:, :, 0:C], in1=yv[:, :, 0:C], op=ALU.mult), ["gt", "yv"], ["gt"])
        ACT(lambda e: e.activation(out=gsg[:, :, 0:C], in_=gt[:, :, 0:C], func=AF.Sigmoid, scale=1.5957691216057308), ["gt"], ["gsg"])
        DVE(lambda e: e.tensor_tensor(out=gl[:, :, 0:C], in0=yv[:, :, 0:C], in1=gsg[:, :, 0:C], op=ALU.mult), ["yv", "gsg"], ["gl"])
        ACT(lambda e: e.copy(out=glb[:, :, 0:C], in_=gl[:, :, 0:C]), ["gl"], ["glb"])
        ps2, pk2 = fbank()
        def fng(e):
            for ct in range(4):
                for kc in range(4):
                    ins = e.matmul(ps2[:, ct * 64:ct * 64 + C], lhsT=wglu[l][:, kc, ct * 128:(ct + 1) * 128], rhs=glb[:, kc, 0:C],
                                   start=(kc == 0), stop=(kc == 3))
            return ins
        PE(fng, ["wglu%d" % l, "glb"], pk2)
        for ct in range(4):
            ACT(lambda e, ct=ct: e.activation(out=sgl[:, ct, 0:C], in_=ps2[:, ct * 64:ct * 64 + C], func=AF.Sigmoid,
                                              bias=PPc(l, "bglu", ct), scale=1.0), pk2 + ["pp%d" % l, "sgl"], ["sgl"])
        POOL(lambda e: e.tensor_tensor(out=gl[:, :, 0:C], in0=gl[:, :, 0:C], in1=sgl[:, :, 0:C], op=ALU.mult), ["gl", "sgl"], ["gl"])
        DVE(lambda e: e.tensor_tensor(out=ys5T[:, :, sl], in0=gl[:, :, 0:C], in1=gs5[:, :, sl], op=ALU.mult),
            ["gl", "gs5", "ys5T"], ["ys5T"])

    def ml_tile(l, q, o, C, sq, kind, og, last, nq):
        S.label = 'mlT'
        samp = kind == "sample"
        ck, mk_ = "cst%d" % l, "mst%d" % l
        sl = slice(o, o + C)
        if samp:
            for kt in range(2):
                LOAD(Cst[l][:, kt, :, 0:192], st_c[l, sq, :, kt * 96:(kt + 1) * 96, :].rearrange("h p v -> p h v"), [ck], [ck])
                LOAD(Cst[l][:, kt, :, 192:193], st_n[l, sq, :, kt * 96:(kt + 1) * 96].rearrange("h (p o) -> p h o", o=1), [ck], [ck], slow=True)
            LOAD(mst[l][:], st_m[l, sq].rearrange("(h o) -> h o", o=1), [], [mk_], slow=True)
        ik = "iftok"
        ACT(lambda e: e.activation(out=lfi[0:C, 4:8], in_=iftok[0:C, q, 4:8], func=AF.Sigmoid), [ik], ["lfi"])
        ACT(lambda e: e.activation(out=lfi[0:C, 4:8], in_=lfi[0:C, 4:8], func=AF.Ln), ["lfi"], ["lfi"])
        ps_, pk = fbank()
        MM(ps_[0:C, 0:4], [(tri_f[0:C, 0:C], lfi[0:C, 4:8])], ["tri_f", "lfi"], pk)
        ps7, pk7 = fbank()
        MM(ps7[0:4, 0:1], [(lfi[0:C, 4:8], ones_f[0:C, 0:1])], ["ones_f", "lfi"], pk7)
        ACT(lambda e: e.copy(out=bcs[0:C, :], in_=ps_[0:C, 0:4]), pk, ["bcs"])
        ACT(lambda e: e.copy(out=bend[:], in_=ps7[0:4, 0:1]), pk7, ["bend"])
        DVE(lambda e: e.tensor_tensor(out=zz[0:C, :], in0=iftok[0:C, q, 0:4], in1=bcs[0:C, :], op=ALU.subtract), [ik, "bcs"], ["zz"])
        ps2, pk2 = fbank()
        TR(ps2[0:4, 0:C], zz[0:C, :], ident_f[0:C, 0:C], ["zz", "ident_f"], pk2)
        DVE(lambda e: e.tensor_reduce(out=zmax[:], in_=ps2[0:4, 0:C], axis=AX.X, op=ALU.max), pk2, ["zmax"])
        DVE(lambda e: e.tensor_tensor(out=mu4[:], in0=zmax[:], in1=mst[l][:], op=ALU.max), ["zmax", mk_], ["mu4"])
        DVE(lambda e: e.tensor_tensor(out=f4[:], in0=mst[l][:], in1=mu4[:], op=ALU.subtract), [mk_, "mu4"], ["f4"])
        ACT(lambda e: e.activation(out=f4[:], in_=f4[:], func=AF.Exp), ["f4"], ["f4"])
        DVE(lambda e: e.tensor_tensor(out=mst[l][:], in0=bend[:], in1=mu4[:], op=ALU.add), ["bend", "mu4", "f4", mk_], [mk_])
        DVE(lambda e: e.tensor_scalar(out=dg[:, 0:4], in0=ident_f[0:4, 0:4], scalar1=mu4[:, 0:1], scalar2=None, op0=ALU.mult),
            ["ident_f", "mu4"], ["dg"])
        DVE(lambda e: e.tensor_scalar(out=dg[:, 4:8], in0=ident_f[0:4, 0:4], scalar1=f4[:, 0:1], scalar2=None, op0=ALU.mult),
            ["ident_f", "f4", "dg"], ["dg"])
        ps3, pk3 = fbank()
        MM(ps3[:, 0:8], [(ones_f[0:4, :], dg[:, :])], ["ones_f", "dg"], pk3)
        ACT(lambda e: e.copy(out=bc8[:], in_=ps3[:, 0:8]), pk3, ["bc8"])
        DVE(lambda e: e.tensor_tensor(out=ee[0:C, :], in0=zz[0:C, :], in1=bc8[0:C, 0:4], op=ALU.subtract), ["zz", "bc8"], ["ee"])
        ACT(lambda e: e.activation(out=ee[0:C, :], in_=ee[0:C, :], func=AF.Exp), ["ee"], ["ee"])
        DVE(lambda e: e.tensor_tensor(out=clampt[0:C, :], in0=bcs[0:C, :], in1=bc8[0:C, 0:4], op=ALU.add), ["bcs", "bc8"], ["clampt"])
        ACT(lambda e: e.activation(out=clampt[0:C, :], in_=clampt[0:C, :], func=AF.Exp, scale=-1.0), ["clampt"], ["clampt"])
        for hd in range(4):
            DVE(lambda e, hd=hd: e.tensor_scalar(out=Cst[l][:, :, hd, :], in0=Cst[l][:, :, hd, :], scalar1=bc8[0:96, 4 + hd:5 + hd],
                                                 scalar2=None, op0=ALU.mult), [ck, "bc8"], [ck])
        ACT(lambda e: e.copy(out=Cstb[:], in_=Cst[l][:]), [ck], ["cstb"])
        ps4, pk4 = fbank()
        def fnsc(e):
            for hd in range(4):
                for kt in range(2):
                    ins = e.matmul(ps4[0:C, hd * 64:hd * 64 + C], lhsT=qkT[:, 8 + 2 * hd + kt, sl], rhs=qkT[:, 2 * hd + kt, sl],
                                   start=(kt == 0), stop=(kt == 1))
            return ins
        PE(fnsc, ["qkT"], pk4)
        p4 = ps4[0:C, 0:256].rearrange("p (h t) -> p h t", t=64)[:, :, 0:C]
        DVE(lambda e: e.tensor_tensor(out=PTs[0:C, :, 0:C], in0=p4, in1=m_incl[0:C, 0:4, 0:C], op=ALU.mult), pk4 + ["m_incl"], ["PTs"])
        DVE(lambda e: e.tensor_tensor(out=PTb[0:C, :, 0:C], in0=PTs[0:C, :, 0:C], in1=ee[0:C, :].unsqueeze(2).broadcast_to([C, 4, C]),
                                      op=ALU.mult), ["PTs", "ee"], ["PTb"])
        ps5, pk5 = fpair()
        def fnnd(e):
            for hd in range(4):
                out = ps5[0:C, hd // 2, (hd % 2) * 193:(hd % 2) * 193 + 193]
                e.matmul(out, lhsT=PTb[0:C, hd, 0:C], rhs=vaug[0:C, q, hd, :], start=True, stop=False)
                e.matmul(out, lhsT=qkT[:, 2 * hd, sl], rhs=Cstb[:, 0, hd, :], start=False, stop=False)
                ins = e.matmul(out, lhsT=qkT[:, 2 * hd + 1, sl], rhs=Cstb[:, 1, hd, :], start=False, stop=True)
            return ins
        PE(fnnd, ["PTb", "vaug", "cstb", "qkT"], pk5)
        nd = ps5[0:C, :, 0:386].rearrange("p b (h d) -> p b h d", d=193)
        hv4 = hst[0:C, 0, :].rearrange("p (b h) -> p b h", b=2)
        ACT(lambda e: e.activation(out=hv4.unsqueeze(3), in_=nd[:, :, :, 192:193], func=AF.Abs), pk5, ["hst"])
        DVE(lambda e: e.tensor_tensor(out=hst[0:C, 0, :], in0=hst[0:C, 0, :], in1=clampt[0:C, :], op=ALU.max),
            ["hst", "clampt"], ["hst"])
        DVE(lambda e: e.reciprocal(out=hst[0:C, 0, :], in_=hst[0:C, 0, :]), ["hst"], ["hst"])
        DVE(lambda e: e.tensor_tensor(out=hh[0:C].rearrange("p (b h) d -> p b h d", b=2), in0=nd[:, :, :, 0:192],
                                      in1=hv4.unsqueeze(3).broadcast_to([C, 2, 2, 192]), op=ALU.mult), pk5 + ["hst"], ["hh"])
        DVE(lambda e: e.tensor_reduce(out=hst[0:C, 1, :], in_=hh[0:C], axis=AX.X, op=ALU.add), ["hh", "hst"], ["hst"])
        DVE(lambda e: e.tensor_scalar(out=hst[0:C, 1, :], in0=hst[0:C, 1, :], scalar1=1.0 / 192, scalar2=None, op0=ALU.mult), ["hst"], ["hst"])
        DVE(lambda e: e.tensor_tensor(out=hcen[0:C], in0=hh[0:C], in1=hst[0:C, 1, :].unsqueeze(2).broadcast_to([C, 4, 192]),
                                      op=ALU.subtract), ["hh", "hst"], ["hcen"])
        POOL(lambda e: e.tensor_tensor(out=hsq[0:C], in0=hcen[0:C], in1=hcen[0:C], op=ALU.mult), ["hcen"], ["hsq"])
        DVE(lambda e: e.tensor_reduce(out=hst[0:C, 2, :], in_=hsq[0:C], axis=AX.X, op=ALU.add), ["hsq", "hst"], ["hst"])
        ACT(lambda e: e.activation(out=hst[0:C, 3, :], in_=hst[0:C, 2, :], func=AF.Sqrt, bias=LN_EPS, scale=1.0 / 192), ["hst"], ["hst"])
        DVE(lambda e: e.reciprocal(out=hst[0:C, 3, :], in_=hst[0:C, 3, :]), ["hst"], ["hst"])
        DVE(lambda e: e.tensor_tensor(out=hnb[0:C, :].rearrange("p (h d) -> p h d", d=192), in0=hcen[0:C],
                                      in1=hst[0:C, 3, :].unsqueeze(2).broadcast_to([C, 4, 192]), op=ALU.mult), ["hcen", "hst"], ["hnb"])
        pt, pk = bbank()
        for j in range(6):
            TR(pt[:, j * 64:j * 64 + C], hnb[0:C, j * 128:(j + 1) * 128], ident_b[0:C, 0:C], ["hnb", "ident_b"], pk)
        DVE(lambda e, pt=pt: e.tensor_tensor(out=ymlT[:, :, sl], in0=pt[:, 0:384].rearrange("p (j t) -> p j t", t=64)[:, :, 0:C],
                                             in1=ogT[:, :, sl], op=ALU.mult), pk + ["ogT", "ymlT"], ["ymlT"])
        pt2, pk2_ = bbank()
        for t in range(8):
            TR(pt2[0:C, t * 96:(t + 1) * 96], qkT[:, 8 + t, sl], ident_b[0:96, 0:96], ["qkT", "ident_b"], pk2_)
        DVE(lambda e, pt2=pt2: e.tensor_tensor(out=khm[0:C], in0=pt2[0:C, 0:768].rearrange("p (h d) -> p h d", d=192),
                                               in1=ee[0:C, :].unsqueeze(2).broadcast_to([C, 4, 192]), op=ALU.mult), pk2_ + ["ee"], ["khm"])
        for kt in range(2):
            ps6, pk6 = fpair()
            def fncu(e, ps6=ps6, kt=kt):
                for hd in range(4):
                    ins = e.matmul(ps6[0:96, hd // 2, (hd % 2) * 193:(hd % 2) * 193 + 193], lhsT=khm[0:C, hd, kt * 96:(kt + 1) * 96],
                                   rhs=vaug[0:C, q, hd, :], start=True, stop=True)
                return ins
            PE(fncu, ["khm", "vaug"], pk6)
            DVE(lambda e, ps6=ps6, kt=kt: e.tensor_tensor(out=Cst[l][:, kt, :, :].rearrange("p (b h) d -> p b h d", b=2),
                                                          in0=Cst[l][:, kt, :, :].rearrange("p (b h) d -> p b h d", b=2),
                                                          in1=ps6[0:96, :, 0:386].rearrange("p b (h d) -> p b h d", d=193), op=ALU.add),
                pk6 + [ck], [ck])
        if samp or (last and q == nq - 1):
            b_ = sq if samp else 0
            for kt in range(2):
                STORE(o_c[og][l, b_, :, kt * 96:(kt + 1) * 96, :].rearrange("h p v -> p h v"), Cst[l][:, kt, :, 0:192], [ck])
                STORE(o_n[og][l, b_, :, kt * 96:(kt + 1) * 96].rearrange("h (p o) -> p h o", o=1), Cst[l][:, kt, :, 192:193], [ck], slow=True)
            STORE(o_m[og][l, b_].rearrange("(h o) -> h o", o=1), mst[l][:], [mk_], slow=True)

    def phase_c(kind, N, l, ydst):
        S.label = 'C'
        pl = "pp%d" % l
        for jg in range(2):
            for b in range(3):
                wbm, wkm = load_group(l, "mg%d%d" % (b, jg))
                wbb, wkb = load_group(l, "br%d%d" % (b, jg))
                for jj in range(4):
                    j = jg * 4 + jj
                    ps_, pk = proj(wbm, wkm, jj * 128, 128, N)
                    gi_ = (b + jj) % 2
                    ACT(lambda e, ps_=ps_, gi_=gi_, b=b, j=j: e.activation(out=gate[gi_][:, 0:N], in_=ps_, func=AF.Sigmoid,
                                                                           bias=PPc(l, "bmrg", b * 8 + j), scale=1.0),
                        pk + [pl], ["gate%d" % gi_])
                    ps2, pk2 = fbank()
                    nk = BR_KC[b]
                    MM(ps2[:, 0:N], [(wbb[:, kc, jj * 128:(jj + 1) * 128], BR_T[b][:, kc, 0:N]) for kc in range(nk)],
                       [wkb, BR_K[b]], pk2)
                    if b == 0:
                        DVE(lambda e, ps2=ps2, gi_=gi_, j=j: e.tensor_tensor(out=mrg[:, j, 0:N], in0=ps2[:, 0:N], in1=gate[gi_][:, 0:N],
                                                                             op=ALU.mult), pk2 + ["gate%d" % gi_, "mrg%d" % j], ["mrg%d" % j])
                    else:
                        ctmp = ctmp2[jj % 2]
                        DVE(lambda e, ps2=ps2, gi_=gi_, ctmp=ctmp: e.tensor_tensor(out=ctmp[:, 0:N], in0=ps2[:, 0:N], in1=gate[gi_][:, 0:N],
                                                                        op=ALU.mult), pk2 + ["gate%d" % gi_], ["ctmp%d" % (jj % 2)])
                        if b == 1:
                            POOL(lambda e, j=j, ctmp=ctmp: e.tensor_tensor(out=mrg[:, j, 0:N], in0=mrg[:, j, 0:N], in1=ctmp[:, 0:N], op=ALU.add),
                                 ["mrg%d" % j, "ctmp%d" % (jj % 2)], ["mrg%d" % j])
                        else:
                            POOL(lambda e, j=j, ctmp=ctmp: e.tensor_tensor(out=mrgb[:, j, 0:N], in0=mrg[:, j, 0:N], in1=ctmp[:, 0:N], op=ALU.add),
                                 ["mrg%d" % j, "ctmp%d" % (jj % 2), "mrgb%d" % j], ["mrgb%d" % j])
        load_bc(1 + l)
        wo0, wok0 = load_group(l, "wo0")
        wo1, wok1 = load_group(l, "wo1")
        rows = N
        ps_, pk = fpair()
        MM(ps_[0:rows, 0, :], [(mrgb[:, kc, 0:rows], wo0[:, kc, 0:512]) for kc in range(8)], [wok0] + ["mrgb%d" % j_ for j_ in range(8)], [pk[0]])
        MM(ps_[0:rows, 1, :], [(mrgb[:, kc, 0:rows], wo1[:, kc, 0:512]) for kc in range(8)], [wok1] + ["mrgb%d" % j_ for j_ in range(8)], [pk[1]])
        DVE(lambda e, ps_=ps_: e.scalar_tensor_tensor(
            out=lnx[0:rows, :].rearrange("p (b n) -> p b n", b=2), in0=x_tok[0:rows, 0, :].rearrange("p (b n) -> p b n", b=2),
            scalar=DN_ALPHA, in1=ps_[0:rows, :, :], op0=ALU.mult, op1=ALU.add), pk + ["x_tok", "lnx"], ["lnx"])
        ln_block(lnx, ["lnx"], rows, out_dram=ydst)

    pass
    if stage >= 1000:
        S.limit = stage - 1000
        stage = 3
    try:
        if stage >= 1:
            run_pass("meta", 16, [(0, 16, 0)], meta, None, True, False)
        npp = SEQ // NT
        tl2 = [(0, 64, 0), (64, 64, 1)]
        for p in range(npp):
            if stage >= 2 and (stage >= 99 or p < stage - 1):
                run_pass("prompt", NT, tl2, xp[p * NT:(p + 1) * NT, :], yp[p * NT:(p + 1) * NT, :], False, p == npp - 1)
        for sp_ in range(2):
            if stage >= 99:
                run_pass("sample", NT, [(0, 64, 2 * sp_), (64, 64, 2 * sp_ + 1)], xs[sp_ * NT:(sp_ + 1) * NT, :],
                         ys[sp_ * NT:(sp_ + 1) * NT, :], False, False)


    except StopBuild:
        pass
    pass
    S.emit(final_wait_ops=final_ops)
    es.close()
    return nc


_NC = None


def _get_nc():
    global _NC
    if _NC is None:
        _NC = build()
    return _NC


def _host_inputs(inp, c):
    f = lambda a: np.ascontiguousarray(a, dtype=np.float32)
    p = c % 4
    sl = slice(4 * c, 4 * c + 4)
    m = {}
    m["xp"] = f(inp["x_prompt"][p])
    m["xs"] = f(inp["x_sample"][sl].reshape(NSAMP * 64, D))
    m["meta"] = f(inp["meta"])
    m["st_shift"] = f(inp["state_rwkv_shift"][:, sl])
    m["st_wkv"] = f(inp["state_rwkv_wkv"][:, sl])
    m["st_s5re"] = f(inp["state_s5_re"][:, sl])
    m["st_s5im"] = f(inp["state_s5_im"][:, sl])
    m["st_conv"] = f(inp["state_mlstm_conv"][:, sl])
    m["st_c"] = f(inp["state_mlstm_c"][:, sl])
    m["st_n"] = f(inp["state_mlstm_n"][:, sl])
    m["st_m"] = f(inp["state_mlstm_m"][:, sl])
    for k in ("w_in", "w_br_rw", "w_br_s5", "w_br_ml", "w_out", "s5_w_glu", "rw_w2", "rw_a2"):
        m[k] = f(inp[k])
    return m


def _shared_inputs(inp):
    f32 = np.float32
    pp = np.zeros((DEPTH, 128, NPP), f32)
    pq = np.zeros((DEPTH, 96, 80), f32)

    def cols(v, n):
        return np.asarray(v, f32).reshape(n, 128).T

    for l in range(DEPTH):
        def put(name, arr):
            o, w = PP[name]
            pp[l, :, o:o + w] = arr
        put("mu", cols(inp["rw_mu"][l], 19))
        put("w0", cols(inp["rw_w0"][l], 6))
        put("a0", cols(inp["rw_a0"][l], 6))
        put("kk", cols(inp["rw_kk"][l], 6))
        put("ka", cols(inp["rw_ka"][l], 6))
        put("rk", cols(np.asarray(inp["rw_rk"][l]).reshape(768), 6))
        put("s5d", cols(inp["s5_d"][l], 4))
        put("bglu", cols(inp["s5_b_glu"][l], 4))
        put("mlg", cols(inp["ml_ln_g"][l], 6))
        put("bmrg", cols(inp["b_merge"][l], 24))
        are = np.asarray(inp["s5_a_re"][l], f32).reshape(16, 2, 64).transpose(1, 2, 0).reshape(128, 16)
        aim = np.asarray(inp["s5_a_im"][l], f32).reshape(16, 2, 64).transpose(1, 2, 0).reshape(128, 16)
        ldt = np.repeat(np.asarray(inp["s5_log_dt"][l], f32).reshape(16, 2, 1), 64, axis=2).transpose(1, 2, 0).reshape(128, 16)
        put("are", are)
        put("aim", aim)
        put("ldt", ldt)
        cw = np.asarray(inp["ml_conv_w"][l], f32).reshape(4, 16, 96)
        cb = np.asarray(inp["ml_conv_b"][l], f32).reshape(16, 96)
        pqv = np.zeros((96, 16, 5), f32)
        pqv[:, :, 0:4] = cw.transpose(2, 1, 0)
        pqv[:, :, 4] = cb.T
        pq[l] = pqv.reshape(96, 80)
    bc = np.stack([inp["in_ln_g"], inp["in_ln_b"], inp["ln_g"][0], inp["ln_b"][0], inp["ln_g"][1], inp["ln_b"][1]]).astype(f32)
    rwln = np.stack([np.asarray(inp["rw_ln_g"], f32), np.asarray(inp["rw_ln_b"], f32)], axis=1)
    rwln = np.ascontiguousarray(rwln.reshape(DEPTH, 2, 6, 2, 64).transpose(0, 1, 3, 2, 4).reshape(DEPTH, 2, 2, 384))
    bif = np.asarray(inp["ml_b_if"], f32)
    bblk = np.zeros((DEPTH, 128, 4, 1024), f32)
    cpad = np.zeros((DEPTH, 128, 16, 2, 64), f32)
    for l in range(DEPTH):
        for c, (bk, ck) in enumerate((("s5_b_re", "s5_c_re"), ("s5_b_im", "s5_c_im"))):
            B = np.asarray(inp[bk][l], f32)
            Cm = np.asarray(inp[ck][l], f32)
            for g in range(32):
                i = g // 2
                ut = g // 8
                col0 = (i % 4) * 256 + c * 128 + (g % 2) * 64
                bblk[l, (g % 8) * 16:(g % 8) * 16 + 16, ut, col0:col0 + 64] = B[g].T
                oc = (i % 2) * 32 + (g % 2) * 16
                cpad[l, (g % 2) * 64:(g % 2) * 64 + 64, i, c, oc:oc + 16] = Cm[g].T
    return dict(pp=pp, pq=pq, bc=bc, rwln=rwln, bif=bif, bblk=bblk, cpad=cpad)


def kernel(**inp):
    inp = {k: np.asarray(v) for k, v in inp.items()}
    nc = _get_nc()
    shared = _shared_inputs(inp)
    in_maps = []
    for c in range(8):
        m = _host_inputs(inp, c)
        m.update(shared)
        in_maps.append(m)
    res = run_bass_kernel_spmd(nc, in_maps, core_ids=list(range(8)))
    R = res.results
    y_prompt = np.stack([R[c]["yp"] for c in range(4)], 0)
    y_sample = np.concatenate([R[c]["ys"].reshape(NSAMP, 64, D) for c in range(8)], 0)
    outs = [y_prompt, y_sample]
    for nm in ("shift", "wkv", "s5re", "s5im", "conv", "c", "n", "m"):
        outs.append(np.concatenate([R[c]["p_" + nm] for c in range(4)], 1))
    for nm in ("shift", "wkv", "s5re", "s5im", "conv", "c", "n", "m"):
        outs.append(np.concatenate([R[c]["s_" + nm] for c in range(8)], 1))
    return tuple(np.ascontiguousarray(o, dtype=np.float32) for o in outs)
```

```python
import contextlib
import math
import numpy as np
import concourse.bass as bass
import concourse.mybir as mybir
from concourse.bass_utils import run_bass_kernel_spmd

F32 = mybir.dt.float32
BF16 = mybir.dt.bfloat16
ALU = mybir.AluOpType
AF = mybir.ActivationFunctionType
AX = mybir.AxisListType

ENGS = ("pe", "act", "dve", "pool", "sp")
D = 1024
DEPTH = 2
NT = 128
SEQ = 4096
NMETA = 16
NSAMP = 4
RWW = 768
RWS = 2432
S5W = 512
MLW = 768
INC = 11144
DN_ALPHA = (2 * DEPTH) ** 0.25
LN_EPS = 1e-5
RW_GN_EPS = 64e-5
C_RW = 0
C_RWG = 2432
C_S5U = 3200
C_S5G = 3712
C_MLQK = 4224
C_MLV = 5760
C_MLIF = 6528
C_MLO = 6536
C_MLZ = 7304
C_MRG = 8072
EXPM05 = math.exp(-0.5)


class StopBuild(Exception):
    pass


class _FirstHook:
    def __init__(self, eng, wait):
        self._e = eng
        self._w = wait

    def _wrap(self, f):
        def g(*a, **k):
            ins = f(*a, **k)
            if self._w is not None:
                ins._wait_ge(*self._w)
                self._w = None
            return ins
        return g

    def __getattr__(self, name):
        v = getattr(self._e, name)
        if name in ("matmul", "transpose"):
            return self._wrap(v)
        return v


class Sched:
    limit = None
    resched = True
    def __init__(self, nc, n_dma_sems=12):
        self.nc = nc
        self.ops = []
        self.last_w = {}
        self.readers = {}
        self.n_dma_sems = n_dma_sems
        self.arena = {}

    def xl(self, keys):
        out = []
        for k in keys:
            r = self.arena.get(k)
            if r is None:
                r = self.arena.get(k.rstrip('0123456789'))
            if r is None:
                assert not ('_' in k and k.rsplit('_', 1)[1].isdigit() and k.rsplit('_', 1)[0] in self.arena), k
                out.append(k)
            else:
                out.extend("ar%d" % u for u in range(r[0] // 256, (r[1] + 255) // 256))
        return out

    def op(self, eng, fn, reads=(), writes=(), dma=False, single=False):
        if self.limit is not None and len(self.ops) >= self.limit:
            raise StopBuild()
        reads = self.xl(reads)
        writes = self.xl(writes)
        i = len(self.ops)
        deps = set()
        for k in reads:
            w = self.last_w.get(k)
            if w is not None:
                deps.add(w)
        for k in writes:
            w = self.last_w.get(k)
            if w is not None:
                deps.add(w)
            for r in self.readers.get(k, ()):
                deps.add(r)
        deps.discard(i)
        odeps = set()
        if eng == "pe" and not dma:
            odeps = {d for d in deps if (self.ops[d]["eng"] == "pe" and not self.ops[d]["dma"])}
            deps = deps - odeps
        self.ops.append(dict(eng=eng, fn=fn, deps=deps, odeps=odeps, dma=dma, used=False, single=single, label=getattr(self, 'label', ''),
                             dur=getattr(self, 'dur', None), tbl=getattr(self, 'tbl', None)))
        for d in deps:
            self.ops[d]["used"] = True
        for k in writes:
            self.last_w[k] = i
            self.readers[k] = []
        for k in reads:
            if k not in writes:
                self.readers.setdefault(k, []).append(i)
        return i

    def reschedule(self, final_wait_ops):
        ops = self.ops
        n = len(ops)
        DUR = {"pe": 0.35, "act": 0.3, "dve": 0.3, "pool": 0.45, "sp": 0.2}
        succ = [[] for _ in range(n)]
        indeg = [0] * n
        for i, o in enumerate(ops):
            for d in (o["deps"] | o["odeps"]):
                succ[d].append(i)
                indeg[i] += 1
        lastq = {}
        for i, o in enumerate(ops):
            if o["dma"]:
                q = o["eng"]
                if q in lastq:
                    succ[lastq[q]].append(i)
                    indeg[i] += 1
                lastq[q] = i
        fin = [0.0] * n
        ready_t = [0.0] * n
        cur = {e: 0.0 for e in ENGS}
        ready = {e: [] for e in ENGS}
        for i in range(n):
            if indeg[i] == 0:
                ready[ops[i]["eng"]].append(i)
        order = []
        acttbl = [None]
        done = 0
        while done < n:
            best = None
            for e in ENGS:
                lst = ready[e]
                if not lst:
                    continue
                bi_, bs_ = None, None
                for i in lst[:24]:
                    st = max(ready_t[i], cur[e])
                    if e == "act" and ops[i]["tbl"] is not None and ops[i]["tbl"] != acttbl[0]:
                        st += 1.3
                    key = (st, i)
                    if bs_ is None or key < bs_:
                        bi_, bs_ = i, key
                if best is None or bs_ < best[0]:
                    best = (bs_, bi_, e)
            (st, _), i, e = best
            ready[e].remove(i)
            o = ops[i]
            if o["dma"]:
                cur[e] = st + 0.1
                fin[i] = st + 2.5
            else:
                if e == "act" and o["tbl"] is not None:
                    acttbl[0] = o["tbl"]
                fin[i] = st + (o["dur"] or DUR[e])
                cur[e] = fin[i]
            order.append(i)
            done += 1
            for j in succ[i]:
                ready_t[j] = max(ready_t[j], fin[i] + 0.15)
                indeg[j] -= 1
                if indeg[j] == 0:
                    ready[ops[j]["eng"]].append(j)
        assert len(order) == n
        pos = {old: new for new, old in enumerate(order)}
        newops = []
        for old_i in order:
            o = ops[old_i]
            o["deps"] = {pos[d] for d in o["deps"]}
            o["odeps"] = {pos[d] for d in o["odeps"]}
            newops.append(o)
        self.ops = newops
        return [pos[i] for i in final_wait_ops]

    def emit(self, final_wait_ops=()):
        nc = self.nc
        if self.resched:
            final_wait_ops = self.reschedule(list(final_wait_ops))
        ops = self.ops
        for i in final_wait_ops:
            ops[i]["used"] = True
        cnt = {e: 0 for e in ENGS}
        dma_cnt = {}
        dma_rr = {e: 0 for e in ENGS}
        for o in ops:
            if o["dma"]:
                q = o["eng"]
                s = (q, dma_rr[q] % self.n_dma_sems)
                dma_rr[q] += 1
                dma_cnt[s] = dma_cnt.get(s, 0) + 1
                o["sig"] = ("dma", s, 16 * dma_cnt[s])
            elif o["used"]:
                cnt[o["eng"]] += 1
                o["sig"] = ("eng", o["eng"], cnt[o["eng"]])
            else:
                o["sig"] = None
        with contextlib.ExitStack() as st:
            esem = {e: st.enter_context(nc.semaphore("s_" + e)) for e in ENGS}
            dsem = {}
            for s in dma_cnt:
                dsem[s] = st.enter_context(nc.semaphore("d_%s_%d" % s))
            block = st.enter_context(nc.Block())

            def semof(sig):
                return esem[sig[1]] if sig[0] == "eng" else dsem[sig[1]]

            def run(e, engobj):
                seen = {}
                for o in ops:
                    if o["eng"] != e:
                        continue
                    need = {}
                    for d in o["deps"]:
                        sg = ops[d]["sig"]
                        key = (sg[0], sg[1])
                        need[key] = max(need.get(key, 0), sg[2])
                    if o["dma"]:
                        sg = o["sig"]
                        key = (sg[0], sg[1])
                        if sg[2] > 16:
                            need[key] = max(need.get(key, 0), sg[2] - 16)
                    waits = []
                    for key, v in need.items():
                        if seen.get(key, 0) < v:
                            waits.append((esem[key[1]] if key[0] == "eng" else dsem[key[1]], v))
                            seen[key] = v
                    emb = []
                    if not o["dma"] and (o["single"] or e == "pe"):
                        emb = waits[-1:]
                        waits = waits[:-1]
                    for sm, v in waits:
                        engobj.wait_ge(sm, v)
                    if emb and not o["single"]:
                        ins = o["fn"](_FirstHook(engobj, emb[0]))
                    else:
                        ins = o["fn"](engobj)
                        for sm, v in emb:
                            ins._wait_ge(sm, v)
                    sg = o["sig"]
                    if sg is not None:
                        ins.then_inc(semof(sg), 16 if sg[0] == "dma" else 1)
                if e == "sp":
                    for i in final_wait_ops:
                        sg = ops[i]["sig"]
                        engobj.wait_ge(semof(sg), sg[2])

            @block.tensor
            def _(eng):
                run("pe", eng)

            @block.scalar
            def _(eng):
                run("act", eng)

            @block.vector
            def _(eng):
                run("dve", eng)

            @block.gpsimd
            def _(eng):
                run("pool", eng)

            @block.sync
            def _(eng):
                run("sp", eng)


def w_groups():
    g = []
    g.append(("rwx", "w_in", 1024, 2304, 128))
    g.append(("rwk0", "w_in", 1024, 768, 512))
    g.append(("rwk1", "w_in", 1024, 1280, 256))
    g.append(("rwr0", "w_in", 1024, 0, 512))
    g.append(("rwr1", "w_in", 1024, 512, 256))
    g.append(("rwv0", "w_in", 1024, 1536, 512))
    g.append(("rwv1", "w_in", 1024, 2048, 256))
    g.append(("rwg0", "w_in", 1024, C_RWG, 512))
    g.append(("rwg1", "w_in", 1024, C_RWG + 512, 256))
    g.append(("s5u", "w_in", 1024, C_S5U, 512))
    g.append(("s5g", "w_in", 1024, C_S5G, 512))
    for i in range(4):
        g.append(("mlqk%d" % i, "w_in", 1024, C_MLQK + 384 * i, 384))
    g.append(("mlv0", "w_in", 1024, C_MLV, 512))
    g.append(("mlv1", "w_in", 1024, C_MLV + 512, 264))
    g.append(("mlo0", "w_in", 1024, C_MLO, 512))
    g.append(("mlo1", "w_in", 1024, C_MLO + 512, 256))
    g.append(("mlz0", "w_in", 1024, C_MLZ, 512))
    g.append(("mlz1", "w_in", 1024, C_MLZ + 512, 256))
    for jg in range(2):
        for b, (nm, k) in enumerate((("w_br_rw", 768), ("w_br_s5", 512), ("w_br_ml", 768))):
            g.append(("mg%d%d" % (b, jg), "w_in", 1024, C_MRG + b * 1024 + jg * 512, 512))
            g.append(("br%d%d" % (b, jg), nm, k, jg * 512, 512))
    for jg in range(2):
        g.append(("wo%d" % jg, "w_out", 1024, jg * 512, 512))
    return g


GROUPS = w_groups()
GIDX = {g[0]: i for i, g in enumerate(GROUPS)}

PP = {}
_o = 0
for _n, _w in (("mu", 19), ("w0", 6), ("a0", 6), ("kk", 6), ("ka", 6), ("rk", 6), ("s5d", 4), ("bglu", 4),
               ("mlg", 6), ("bmrg", 24), ("are", 16), ("aim", 16), ("ldt", 16)):
    PP[_n] = (_o, _w)
    _o += _w
NPP = _o


def build(stage=99):
    nc = bass.Bass("TRN2", target_bir_lowering=False)
    es = contextlib.ExitStack()
    S = Sched(nc)

    def din(name, shape):
        return nc.dram_tensor(name, list(shape), F32, kind="ExternalInput").ap()

    def dout(name, shape):
        return nc.dram_tensor(name, list(shape), F32, kind="ExternalOutput").ap()

    def dscr(name, shape, dt):
        return nc.dram_tensor(name, list(shape), dt, kind="Internal").ap()

    xp = din("xp", [SEQ, D])
    xs = din("xs", [NSAMP * 64, D])
    meta = din("meta", [NMETA, D])
    st_shift = din("st_shift", [DEPTH, NSAMP, RWS])
    st_wkv = din("st_wkv", [DEPTH, NSAMP, 12, 64, 64])
    st_s5re = din("st_s5re", [DEPTH, NSAMP, 32, 64])
    st_s5im = din("st_s5im", [DEPTH, NSAMP, 32, 64])
    st_conv = din("st_conv", [DEPTH, NSAMP, 3, 1536])
    st_c = din("st_c", [DEPTH, NSAMP, 4, 192, 192])
    st_n = din("st_n", [DEPTH, NSAMP, 4, 192])
    st_m = din("st_m", [DEPTH, NSAMP, 4])
    w_in = din("w_in", [DEPTH, D, INC])
    wsrc = dict(w_in=w_in, w_br_rw=din("w_br_rw", [DEPTH, 768, D]), w_br_s5=din("w_br_s5", [DEPTH, 512, D]),
                w_br_ml=din("w_br_ml", [DEPTH, 768, D]), w_out=din("w_out", [DEPTH, D, D]))
    w_glu = din("s5_w_glu", [DEPTH, 512, 512])
    rw_w2 = din("rw_w2", [DEPTH, 64, 768])
    rw_a2 = din("rw_a2", [DEPTH, 64, 768])
    pp_d = din("pp", [DEPTH, 128, NPP])
    pq_d = din("pq", [DEPTH, 96, 16 * 5])
    bc_d = din("bc", [6, D])
    rwln_d = din("rwln", [DEPTH, 2, 2, 384])
    bif_d = din("bif", [DEPTH, 8])
    bblk_d = din("bblk", [DEPTH, 128, 4, 1024])
    cpad_d = din("cpad", [DEPTH, 128, 16, 2, 64])

    yp = dout("yp", [SEQ, D])
    ys = dout("ys", [NSAMP * 64, D])
    o_shift = {"p": dout("p_shift", [DEPTH, 1, RWS]), "s": dout("s_shift", [DEPTH, NSAMP, RWS])}
    o_wkv = {"p": dout("p_wkv", [DEPTH, 1, 12, 64, 64]), "s": dout("s_wkv", [DEPTH, NSAMP, 12, 64, 64])}
    o_s5re = {"p": dout("p_s5re", [DEPTH, 1, 32, 64]), "s": dout("s_s5re", [DEPTH, NSAMP, 32, 64])}
    o_s5im = {"p": dout("p_s5im", [DEPTH, 1, 32, 64]), "s": dout("s_s5im", [DEPTH, NSAMP, 32, 64])}
    o_conv = {"p": dout("p_conv", [DEPTH, 1, 3, 1536]), "s": dout("s_conv", [DEPTH, NSAMP, 3, 1536])}
    o_c = {"p": dout("p_c", [DEPTH, 1, 4, 192, 192]), "s": dout("s_c", [DEPTH, NSAMP, 4, 192, 192])}
    o_n = {"p": dout("p_n", [DEPTH, 1, 4, 192]), "s": dout("s_n", [DEPTH, NSAMP, 4, 192])}
    o_m = {"p": dout("p_m", [DEPTH, 1, 4]), "s": dout("s_m", [DEPTH, NSAMP, 4])}

    wq = [[dscr("wq%d_%d" % (l, i), [128, g[2] // 128, g[4]], BF16) for i, g in enumerate(GROUPS)]
          for l in range(DEPTH)]

    def sb(name, shape, dt=F32):
        return es.enter_context(nc.sbuf_tensor(name, list(shape), dt))

    def psum(name, shape, dt):
        return es.enter_context(nc.psum_tensor(name, list(shape), dt))

    final_ops = []

    def ACT(fn, r, w):
        tbl = None
        names = fn.__code__.co_names
        for nm_, t_ in (("Sigmoid", "sig"), ("Exp", "exp"), ("Ln", "ln"), ("Sqrt", "sqrt"), ("Silu", "silu"), ("Sin", "silu")):
            if nm_ in names:
                tbl = t_
        S.tbl = tbl
        i_ = S.op("act", fn, r, w, single=True)
        S.tbl = None
        return i_

    def DVE(fn, r, w):
        return S.op("dve", fn, r, w, single=True)

    def POOL(fn, r, w):
        return S.op("pool", fn, r, w, single=True)

    def PE(fn, r, w):
        return S.op("pe", fn, r, w)

    def LOAD(out, in_, r, w, slow=False):
        return S.op("sp", lambda e: e.dma_start(out=out, in_=in_, allow_slow_non_contiguous=slow), r, w, dma=True)

    def STORE(out, in_, r, slow=False):
        i = S.op("pool", lambda e: e.dma_start(out=out, in_=in_, allow_slow_non_contiguous=slow), r, (), dma=True)
        final_ops.append(i)
        return i

    def MM(out, pairs, r, w):
        S.dur = 0.1 + 0.07 * len(pairs)
        def fn(e):
            n = len(pairs)
            for i, (l_, r_) in enumerate(pairs):
                ins = e.matmul(out, lhsT=l_, rhs=r_, start=(i == 0), stop=(i == n - 1))
            return ins
        i_ = PE(fn, r, w)
        S.dur = None
        return i_

    def TR(out, in_, ident, r, w):
        return S.op("pe", lambda e: e.transpose(out=out, in_=in_, identity=ident), r, w, single=True)

    PSF = [psum("psf%d" % i, [128, 2, 512], F32) for i in range(3)]
    PSB = psum("psb", [128, 2, 1024], BF16)
    rr = {"b": 0, "p": 0, "t": 0}

    def fbank():
        i = rr["b"] % 6
        rr["b"] += 1
        return PSF[i // 2][:, i % 2, :], ["pf%d" % i]

    def fpair():
        i = rr["p"] % 3
        rr["p"] += 1
        return PSF[i], ["pf%d" % (2 * i), "pf%d" % (2 * i + 1)]

    def bbank():
        i = rr["t"] % 2
        rr["t"] += 1
        return PSB[:, i, :], ["pb%d" % i]

    ident_b = sb("ident_b", [128, 128], BF16)
    ident_f = sb("ident_f", [128, 128], F32)
    m_strict = sb("m_strict", [128, 6, 64], F32)
    m_incl = sb("m_incl", [128, 6, 64], F32)
    m_lower = sb("m_lower", [128, 6, 64], F32)
    tri_b = sb("tri_b", [64, 64], BF16)
    tri2 = sb("tri2", [128, 64], BF16)
    tri_f = sb("tri_f", [64, 64], F32)
    blk1 = sb("blk1", [128, 128], BF16)
    bsel = sb("bsel", [128, 2], BF16)
    ones_f = sb("ones_f", [128, 128], F32)
    onesb = sb("onesb", [128, 1], BF16)
    scanm = sb("scanm", [128, NT], F32)

    def mk_sel(t, pattern, base, cm, op, key):
        POOL(lambda e: e.memset(t, 1.0), [], [key])
        POOL(lambda e: e.affine_select(out=t, in_=t, pattern=pattern, compare_op=op, fill=0.0, base=base,
                                       channel_multiplier=cm), [key], [key])

    mk_sel(ident_f[:], [[-1, 128]], 0, 1, ALU.is_equal, "ident_f")
    POOL(lambda e: e.tensor_copy(out=ident_b[:], in_=ident_f[:]), ["ident_f"], ["ident_b"])
    for hf_ in range(2):
        ps_ = slice(hf_ * 64, hf_ * 64 + 64)
        mk_sel(m_strict[ps_], [[0, 6], [1, 64]], -1, -1, ALU.is_ge, "m_strict")
        mk_sel(m_incl[ps_], [[0, 6], [1, 64]], 0, -1, ALU.is_ge, "m_incl")
        mk_sel(m_lower[ps_], [[0, 6], [-1, 64]], -1, 1, ALU.is_ge, "m_lower")
    mk_sel(tri_f[:], [[1, 64]], 0, -1, ALU.is_ge, "tri_f")
    POOL(lambda e: e.tensor_copy(out=tri_b[:], in_=tri_f[:]), ["tri_f"], ["tri_b"])
    POOL(lambda e: e.tensor_copy(out=tri2[0:64, :], in_=tri_f[:]), ["tri_f"], ["tri2"])
    POOL(lambda e: e.tensor_copy(out=tri2[64:128, :], in_=m_incl[64:128, 0, :]), ["m_incl", "tri2"], ["tri2"])
    POOL(lambda e: e.memset(ones_f[:], 1.0), [], ["ones_f"])
    POOL(lambda e: e.memset(onesb[:], 1.0), [], ["onesb"])
    POOL(lambda e: e.memset(blk1[:], 0.0), [], ["blk1"])
    POOL(lambda e: e.memset(blk1[0:64, 0:64], 1.0), ["blk1"], ["blk1"])
    POOL(lambda e: e.memset(blk1[64:128, 64:128], 1.0), ["blk1"], ["blk1"])
    POOL(lambda e: e.memset(bsel[:], 0.0), [], ["bsel"])
    POOL(lambda e: e.memset(bsel[0:64, 0:1], 1.0), ["bsel"], ["bsel"])
    POOL(lambda e: e.memset(bsel[64:128, 1:2], 1.0), ["bsel"], ["bsel"])
    POOL(lambda e: e.memset(scanm[:], 1.0), [], ["scanm"])
    POOL(lambda e: e.memset(scanm[:].rearrange("p (c t) -> p c t", t=64)[:, :, 0:1], 0.0), ["scanm"], ["scanm"])

    if stage == -4:
        S.emit(final_wait_ops=final_ops); es.close(); return nc
    for l in range(DEPTH):
        for i, (nm, src, K, c0, wd) in enumerate(GROUPS):
            srcap = wsrc[src][l, :, c0:c0 + wd].rearrange("(kc p) c -> p kc c", p=128)
            S.op("pool", lambda e, o_=wq[l][i], s_=srcap: e.dma_start(out=o_, in_=s_), [], ["wq%d_%d" % (l, i)], dma=True)

    if stage == -3:
        S.emit(final_wait_ops=final_ops); es.close(); return nc
    ARW = 15360
    arena_t = sb("arena", [128, ARW], F32)
    arp = {"o": 0}

    def ar_reset(o=0):
        arp["o"] = o

    def ar(name, shape, dt=F32):
        P = shape[0]
        n = int(np.prod(shape[1:]))
        words = n if dt == F32 else (n + 1) // 2
        words = (words + 15) // 16 * 16
        o = arp["o"]
        assert o + words <= ARW, (name, o, words)
        arp["o"] = o + words
        v = arena_t[0:P, o:o + words]
        if dt != F32:
            v = v.bitcast(BF16)
        v = v[:, 0:n]
        if len(shape) == 3:
            v = v.rearrange("p (a b) -> p a b", b=shape[2])
        elif len(shape) == 4:
            v = v.rearrange("p (a b c) -> p a b c", b=shape[2], c=shape[3])
        S.arena[name] = (o * 4, (o + words) * 4)
        if len(shape) >= 3:
            esz = 4 if dt == F32 else 2
            sub = int(np.prod(shape[2:])) * esz
            for a_ in range(shape[1]):
                S.arena["%s_%d" % (name, a_)] = (o * 4 + a_ * sub, o * 4 + (a_ + 1) * sub)
        return v

    pp = [sb("pp%d" % l, [128, NPP]) for l in range(DEPTH)]
    pq = [sb("pq%d" % l, [96, 16, 5]) for l in range(DEPTH)]
    omka = [sb("omka%d" % l, [128, 6]) for l in range(DEPTH)]
    lora = [sb("lora%d" % l, [128, 768], BF16) for l in range(DEPTH)]
    wglu = [sb("wglu%d" % l, [128, 4, 512], BF16) for l in range(DEPTH)]
    bcg = sb("bcg", [128, 2, D])
    rwln = [sb("rwln%d" % l, [128, 2, 384]) for l in range(DEPTH)]
    bif = [sb("bif%d" % l, [64, 8]) for l in range(DEPTH)]
    EpB = sb("EpB", [128, 2, 16, 64])
    EnB = sb("EnB", [128, 16, 2, 128], BF16)
    bblkB = sb("bblkB", [128, 4, 1024], BF16)
    cpadB = sb("cpadB", [128, 16, 2, 64], BF16)
    S5K = ["EpB", "EnB", "bblkB", "cpadB"]
    epd = [dscr("epd%d" % l, [128, 2, 16, 64], F32) for l in range(DEPTH)]
    end_ = [dscr("end%d" % l, [64, 16, 2, 128], BF16) for l in range(DEPTH)]
    bblkq = [dscr("bblkq%d" % l, [128, 4, 1024], BF16) for l in range(DEPTH)]
    cpadq = [dscr("cpadq%d" % l, [128, 16, 2, 64], BF16) for l in range(DEPTH)]
    for l in range(DEPTH):
        LOAD(pp[l][:], pp_d[l], [], ["pp%d" % l])
        LOAD(pq[l][:], pq_d[l].rearrange("p (t j) -> p t j", j=5), [], ["pq%d" % l])
        for a_ in range(2):
            for r_ in range(2):
                LOAD(rwln[l][r_ * 64:(r_ + 1) * 64, a_, :], rwln_d[l, a_, r_].partition_broadcast(64), ["rwln%d" % l], ["rwln%d" % l])
        LOAD(bif[l][:], bif_d[l].partition_broadcast(64), [], ["bif%d" % l])
        S.op("pool", lambda e, l=l: e.dma_start(out=lora[l][0:64, :], in_=rw_w2[l]), [], ["lora%d" % l], dma=True)
        S.op("pool", lambda e, l=l: e.dma_start(out=lora[l][64:128, :], in_=rw_a2[l]), [], ["lora%da" % l], dma=True)
        S.op("pool", lambda e, l=l: e.dma_start(out=wglu[l][:], in_=w_glu[l].rearrange("(kc p) c -> p kc c", p=128)),
             [], ["wglu%d" % l], dma=True)
        S.op("pool", lambda e, l=l: e.dma_start(out=bblkq[l], in_=bblk_d[l]), [], ["bblkq%d" % l], dma=True)
        S.op("pool", lambda e, l=l: e.dma_start(out=cpadq[l], in_=cpad_d[l]), [], ["cpadq%d" % l], dma=True)
        o_, w_ = PP["ka"]
        DVE(lambda e, l=l, o_=o_: e.tensor_scalar(out=omka[l][:], in0=pp[l][:, o_:o_ + 6], scalar1=-1.0, scalar2=1.0,
                                                  op0=ALU.mult, op1=ALU.add), ["pp%d" % l], ["omka%d" % l])

    if stage == -2:
        S.emit(final_wait_ops=final_ops); es.close(); return nc

    def load_bc(i):
        LOAD(bcg[:].rearrange("p a b -> p (a b)"), bc_d[2 * i:2 * i + 2, :].rearrange("a b -> (a b)").partition_broadcast(128),
             [], ["bcg"])

    def load_s5(l):
        LOAD(EpB[:], epd[l], ["epd%d" % l], ["EpB"])
        LOAD(EnB[0:64], end_[l], ["end%d" % l, "EnB"], ["EnB"])
        LOAD(EnB[64:128], end_[l], ["end%d" % l, "EnB"], ["EnB"])
        LOAD(bblkB[:], bblkq[l], ["bblkq%d" % l], ["bblkB"])
        LOAD(cpadB[:], cpadq[l], ["cpadq%d" % l], ["cpadB"])

    def PPc(l, name, j=None):
        o_, w_ = PP[name]
        if j is None:
            return pp[l][:, o_:o_ + w_]
        return pp[l][:, o_ + j:o_ + j + 1]

    ar_reset()
    zr = ar("zr", [128, 16]); zi = ar("zi", [128, 16]); dtt = ar("dtt", [128, 16])
    t1 = ar("t1", [128, 16]); t2 = ar("t2", [128, 16]); t3 = ar("t3", [128, 16]); mg = ar("mg", [128, 16])
    lr = ar("lr", [128, 16])
    Epf = ar("Epf", [128, 2, 16, 64])
    Enf = [ar("enf%d" % c, [128, 16, 64]) for c in range(2)]
    ta = ar("ta", [128, 16, 32]); tb_ = ar("tb", [128, 16, 32])
    cfr = ar("cfr", [128, 16]); cfi = ar("cfi", [128, 16]); den = ar("den", [128, 16])
    enc = [ar("enc%d" % c, [128, 16, 64]) for c in range(2)]
    Ent = ar("Ent", [64, 16, 2, 128])
    KT = ["zr", "zi", "dtt", "t1", "t2", "t3", "mg", "lr", "Epf", "enf0", "enf1", "ta", "tb", "cfr", "cfi", "den", "enc0", "enc1"]

    def cexp32(sign, outr, outi):
        ACT(lambda e: e.activation(out=mg[:], in_=zr[:], func=AF.Exp, scale=sign / 32.0), KT, KT)
        ACT(lambda e: e.activation(out=t1[:], in_=zi[:], func=AF.Sin, scale=sign / 32.0), KT, KT)
        ACT(lambda e: e.activation(out=t2[:], in_=zi[:], func=AF.Sin, scale=sign / 32.0, bias=hpi[:, 0:1]), KT + ["hpi"], KT)
        DVE(lambda e: e.tensor_tensor(out=outr, in0=mg[:], in1=t2[:], op=ALU.mult), KT, KT)
        DVE(lambda e: e.tensor_tensor(out=outi, in0=mg[:], in1=t1[:], op=ALU.mult), KT, KT)
        for _ in range(5):
            DVE(lambda e: e.tensor_tensor(out=t1[:], in0=outr, in1=outr, op=ALU.mult), KT, KT)
            DVE(lambda e: e.tensor_tensor(out=t2[:], in0=outi, in1=outi, op=ALU.mult), KT, KT)
            DVE(lambda e: e.tensor_tensor(out=t3[:], in0=outr, in1=outi, op=ALU.mult), KT, KT)
            DVE(lambda e: e.tensor_tensor(out=outr, in0=t1[:], in1=t2[:], op=ALU.subtract), KT, KT)
            DVE(lambda e: e.tensor_scalar(out=outi, in0=t3[:], scalar1=2.0, scalar2=None, op0=ALU.mult), KT, KT)

    def powers(tabr, tabi):
        ln_ = 1
        while ln_ < 64:
            Lr = tabr[:, :, ln_ - 1:ln_].broadcast_to([128, 16, ln_])
            Li = tabi[:, :, ln_ - 1:ln_].broadcast_to([128, 16, ln_])
            a_ = ta[:, :, 0:ln_]
            b_ = tb_[:, :, 0:ln_]
            sr = tabr[:, :, 0:ln_]
            si = tabi[:, :, 0:ln_]
            dr = tabr[:, :, ln_:2 * ln_]
            di = tabi[:, :, ln_:2 * ln_]
            DVE(lambda e, a_=a_, sr=sr, Lr=Lr: e.tensor_tensor(out=a_, in0=sr, in1=Lr, op=ALU.mult), KT, KT)
            DVE(lambda e, b_=b_, si=si, Li=Li: e.tensor_tensor(out=b_, in0=si, in1=Li, op=ALU.mult), KT, KT)
            DVE(lambda e, a_=a_, b_=b_, dr=dr: e.tensor_tensor(out=dr, in0=a_, in1=b_, op=ALU.subtract), KT, KT)
            DVE(lambda e, a_=a_, sr=sr, Li=Li: e.tensor_tensor(out=a_, in0=sr, in1=Li, op=ALU.mult), KT, KT)
            DVE(lambda e, b_=b_, si=si, Lr=Lr: e.tensor_tensor(out=b_, in0=si, in1=Lr, op=ALU.mult), KT, KT)
            DVE(lambda e, a_=a_, b_=b_, di=di: e.tensor_tensor(out=di, in0=a_, in1=b_, op=ALU.add), KT, KT)
            ln_ *= 2

    hpi = sb("hpi", [128, 1])
    POOL(lambda e: e.memset(hpi[:], math.pi / 2), [], ["hpi"])
    for l in range(DEPTH):
        KP = ["pp%d" % l]
        ACT(lambda e, l=l: e.activation(out=dtt[:], in_=PPc(l, "ldt"), func=AF.Exp), KT + KP, KT)
        DVE(lambda e, l=l: e.tensor_tensor(out=zr[:], in0=PPc(l, "are"), in1=dtt[:], op=ALU.mult), KT + KP, KT)
        DVE(lambda e, l=l: e.tensor_tensor(out=zi[:], in0=PPc(l, "aim"), in1=dtt[:], op=ALU.mult), KT + KP, KT)

        if stage == -10:
            S.emit(final_wait_ops=final_ops); es.close(); return nc
        cexp32(1.0, Epf[:, 0, :, 0], Epf[:, 1, :, 0])

        if stage == -9:
            S.emit(final_wait_ops=final_ops); es.close(); return nc
        powers(Epf[:, 0], Epf[:, 1])

        if stage == -8:
            S.emit(final_wait_ops=final_ops); es.close(); return nc
        cexp32(-1.0, Enf[0][:, :, 0], Enf[1][:, :, 0])
        powers(Enf[0], Enf[1])

        if stage == -7:
            S.emit(final_wait_ops=final_ops); es.close(); return nc
        DVE(lambda e: e.tensor_scalar(out=lr[:], in0=Epf[:, 0, :, 0], scalar1=-1.0, scalar2=None, op0=ALU.add), KT, KT)
        DVE(lambda e, l=l: e.tensor_tensor(out=t1[:], in0=PPc(l, "are"), in1=PPc(l, "are"), op=ALU.mult), KT + KP, KT)
        DVE(lambda e, l=l: e.tensor_tensor(out=t2[:], in0=PPc(l, "aim"), in1=PPc(l, "aim"), op=ALU.mult), KT + KP, KT)
        DVE(lambda e: e.tensor_tensor(out=den[:], in0=t1[:], in1=t2[:], op=ALU.add), KT, KT)
        DVE(lambda e: e.reciprocal(out=den[:], in_=den[:]), KT, KT)
        DVE(lambda e, l=l: e.tensor_tensor(out=t1[:], in0=lr[:], in1=PPc(l, "are"), op=ALU.mult), KT + KP, KT)
        DVE(lambda e, l=l: e.tensor_tensor(out=t2[:], in0=Epf[:, 1, :, 0], in1=PPc(l, "aim"), op=ALU.mult), KT + KP, KT)
        DVE(lambda e: e.tensor_tensor(out=cfr[:], in0=t1[:], in1=t2[:], op=ALU.add), KT, KT)
        DVE(lambda e: e.tensor_tensor(out=cfr[:], in0=cfr[:], in1=den[:], op=ALU.mult), KT, KT)
        DVE(lambda e, l=l: e.tensor_tensor(out=t1[:], in0=Epf[:, 1, :, 0], in1=PPc(l, "are"), op=ALU.mult), KT + KP, KT)
        DVE(lambda e, l=l: e.tensor_tensor(out=t2[:], in0=lr[:], in1=PPc(l, "aim"), op=ALU.mult), KT + KP, KT)
        DVE(lambda e: e.tensor_tensor(out=cfi[:], in0=t1[:], in1=t2[:], op=ALU.subtract), KT, KT)
        DVE(lambda e: e.tensor_tensor(out=cfi[:], in0=cfi[:], in1=den[:], op=ALU.mult), KT, KT)
        CR = cfr[:].unsqueeze(2).broadcast_to([128, 16, 64])
        CI = cfi[:].unsqueeze(2).broadcast_to([128, 16, 64])
        DVE(lambda e, CR=CR: e.tensor_tensor(out=enc[0][:], in0=Enf[0][:], in1=CR, op=ALU.mult), KT, KT)
        DVE(lambda e, CI=CI: e.tensor_tensor(out=enc[1][:], in0=Enf[1][:], in1=CI, op=ALU.mult), KT, KT)
        DVE(lambda e: e.tensor_tensor(out=enc[0][:], in0=enc[0][:], in1=enc[1][:], op=ALU.subtract), KT, KT)
        DVE(lambda e, CI=CI: e.tensor_tensor(out=enc[1][:], in0=Enf[0][:], in1=CI, op=ALU.mult), KT, KT)
        DVE(lambda e, CR=CR: e.tensor_tensor(out=Enf[0][:], in0=Enf[1][:], in1=CR, op=ALU.mult), KT, KT)
        DVE(lambda e: e.tensor_tensor(out=enc[1][:], in0=enc[1][:], in1=Enf[0][:], op=ALU.add), KT, KT)

        if stage == -6:
            S.emit(final_wait_ops=final_ops); es.close(); return nc
        for c in range(2):
            for i in range(16):
                ps_, pk = fbank()
                TR(ps_[0:64, 0:128], enc[c][:, i, :], ident_f[:], KT + ["ident_f"], pk)
                ACT(lambda e, ps_=ps_, i=i, c=c: e.copy(out=Ent[:, i, c, :], in_=ps_[0:64, 0:128]), pk, ["Ent"])

        if stage == -5:
            S.emit(final_wait_ops=final_ops); es.close(); return nc
        S.op("pool", lambda e, l=l: e.dma_start(out=epd[l], in_=Epf), KT, ["epd%d" % l], dma=True)
        S.op("pool", lambda e, l=l: e.dma_start(out=end_[l], in_=Ent), ["Ent"], ["end%d" % l], dma=True)

    x_tok = sb("x_tok", [128, 1, D])
    xT = sb("xT", [128, 8, NT], BF16)
    wbuf = [sb("wbuf%d" % i, [128, 8, 512], BF16) for i in range(4)]
    wr = {"i": 0}

    def load_group(l, name):
        gi = GIDX[name]
        g = GROUPS[gi]
        bi = wr["i"] % 4
        wr["i"] += 1
        kc = g[2] // 128
        LOAD(wbuf[bi][:, 0:kc, 0:g[4]], wq[l][gi], ["wq%d_%d" % (l, gi)], ["wbuf%d" % bi])
        return wbuf[bi], "wbuf%d" % bi

    shiftst = [sb("shiftst%d" % l, [128, 19]) for l in range(DEPTH)]
    sshift = sb("sshift", [128, 19, 2])
    sshift_o = sb("sshift_o", [128, 19, 2])
    S0T = [sb("s0t%d" % l, [128, 6, 64]) for l in range(DEPTH)]
    S0Tb = sb("s0tb", [128, 6, 64], BF16)
    h0 = [sb("h0%d" % l, [128, 2, 16]) for l in range(DEPTH)]
    convst = [sb("convst%d" % l, [96, 16, 3]) for l in range(DEPTH)]
    sconv = sb("sconv", [96, 16, 2, 3])
    Cst = [sb("cst%d" % l, [96, 2, 4, 193]) for l in range(DEPTH)]
    Cstb = sb("cstb", [96, 2, 4, 193], BF16)
    mst = [sb("mst%d" % l, [4, 1]) for l in range(DEPTH)]
    stg = sb("stg", [64, 12, 64])

    yrwT = sb("yrwT", [128, 6, NT], BF16)
    ys5T = sb("ys5T", [128, 4, NT], BF16)
    ymlT = sb("ymlT", [128, 6, NT], BF16)
    gate = [sb("gate%d" % i, [128, NT], BF16) for i in range(2)]
    mrg = sb("mrg", [128, 8, NT]); mrgb = sb("mrgb", [128, 8, NT], BF16)
    lnt = sb("lnt", [128, D]); lnb = sb("lnb", [128, D], BF16); lnx = sb("lnx", [128, D]); ctmp2 = [sb("ctmp%d" % i, [128, NT]) for i in range(2)]
    lst = sb("lst", [128, 2, 6]); lmv = sb("lmv", [128, 2]); lrs = sb("lrs", [128, 1])
    gC = sb("gC", [128, 6, 2])

    ar_reset()
    U = [ar("U%d" % i, [128, NT + 1]) for i in range(2)]
    dtmp = [ar("dtmp%d" % i, [128, NT]) for i in range(2)]
    xs18 = ar("xs18", [128, NT]); txw = ar("txw", [128, NT], BF16)
    ldec_2 = [ar("ldec%d" % i_, [128, NT]) for i_ in range(2)]; Gc_2 = [ar("Gc%d" % i_, [128, NT]) for i_ in range(2)]; aa_2 = [ar("aa%d" % i_, [128, NT]) for i_ in range(2)]
    eneg_2 = [ar("eneg%d" % i_, [128, NT]) for i_ in range(2)]; eprev_2 = [ar("eprev%d" % i_, [128, NT]) for i_ in range(2)]; ehat_2 = [ar("ehat%d" % i_, [128, NT]) for i_ in range(2)]; epos_2 = [ar("epos%d" % i_, [128, NT]) for i_ in range(2)]
    kx_2 = [ar("kx%d" % i_, [128, NT]) for i_ in range(2)]; kkr_2 = [ar("kkr%d" % i_, [128, NT]) for i_ in range(2)]; kksq_2 = [ar("kksq%d" % i_, [128, NT], BF16) for i_ in range(2)]
    rn_2 = [ar("rn%d" % i_, [128, NT]) for i_ in range(2)]; kkn_2 = [ar("kkn%d" % i_, [128, NT]) for i_ in range(2)]; tk_2 = [ar("tk%d" % i_, [128, NT]) for i_ in range(2)]
    kmod_2 = [ar("kmod%d" % i_, [128, NT]) for i_ in range(2)]; bb_2 = [ar("bb%d" % i_, [128, NT]) for i_ in range(2)]; rx_2 = [ar("rx%d" % i_, [128, NT]) for i_ in range(2)]
    rkp = ar("rkp", [128, 6, NT], BF16)
    rt_ = ar("rt_", [128, 6, NT], BF16); kt_ = ar("kt_", [128, 6, NT], BF16); bt_ = ar("bt_", [128, 6, NT], BF16)
    at_ = ar("at_", [128, 6, NT], BF16); khat = ar("khat", [128, 6, NT], BF16); bhat = ar("bhat", [128, 6, NT], BF16)
    vT = ar("vT", [128, 6, NT], BF16); grw = ar("grw", [128, 6, NT], BF16)
    TB = []
    for pz in ("A", "B"):
        TB.append(dict(
            vtok=ar("vtok" + pz, [128, 6, 64], BF16), khtok=ar("khtok" + pz, [128, 6, 64], BF16), bhtok=ar("bhtok" + pz, [128, 6, 64], BF16),
            Nsb=[ar("Nsb%d%s" % (i, pz), [128, 6, 64], BF16) for i in range(2)],
            NTsb=[ar("NTsb%d%s" % (i, pz), [128, 6, 64], BF16) for i in range(2)],
            Msb=ar("Msb" + pz, [128, 6, 64], BF16), P1sb=ar("P1sb" + pz, [128, 6, 64], BF16), P2sb=ar("P2sb" + pz, [128, 6, 64], BF16), z=pz))
    Ysb = [ar("Ysb%d" % i, [128, 6, 64], BF16) for i in range(2)]
    yo = ar("yo", [128, 6, 64]); ycen = ar("ycen", [128, 6, 64]); ysq = ar("ysq", [128, 6, 64])
    ystat = ar("ystat", [128, 4, 6]); rkd = ar("rkd", [128, 6]); ytb = ar("ytb", [128, 6, 64], BF16)
    RW_END = arp["o"]
    ar_reset()
    uT = ar("uT", [128, 4, NT], BF16); u32 = ar("u32", [128, 4, NT]); gs5 = ar("gs5", [128, 4, NT], BF16)
    wtok = ar("wtok", [128, 16, 2, 128], BF16)
    s5a = ar("s5a", [128, 4, 128]); s5b = ar("s5b", [128, 4, 128])
    Gr = ar("Gr", [128, 16, 64]); Gi = ar("Gi", [128, 16, 64])
    hA = ar("hA", [128, 16, 64]); hB = ar("hB", [128, 16, 64]); hC = ar("hC", [128, 16, 64]); hD = ar("hD", [128, 16, 64])
    hre = ar("hre", [128, 16, 64], BF16); himn = ar("himn", [128, 16, 64], BF16)
    yv = ar("yv", [128, 4, 64]); gt = ar("gt", [128, 4, 64]); gsg = ar("gsg", [128, 4, 64])
    gl = ar("gl", [128, 4, 64]); glb = ar("glb", [128, 4, 64], BF16); sgl = ar("sgl", [128, 4, 64])
    ar_reset()
    qkraw = ar("qkraw", [96, 16, NT + 3]); cvacc = ar("cvacc", [96, 16, NT]); cvtmp = ar("cvtmp", [96, 16, NT])
    hbuf = ar("hbuf0", [96, 16, 6])
    qkT = ar("qkT", [96, 16, NT], BF16)
    vaug = ar("vaug", [64, NT // 64, 4, 193], BF16)
    iftok = ar("iftok", [64, NT // 64, 8])
    zsl_2 = [ar("zsl%d" % i_, [128, NT]) for i_ in range(2)]; ogT = ar("ogT", [128, 6, NT], BF16)
    lfi = ar("lfi", [64, 8]); bcs = ar("bcs", [64, 4]); zz = ar("zz", [64, 4]); ee = ar("ee", [64, 4]); clampt = ar("clampt", [64, 4])
    zmax = ar("zmax", [4, 1]); mu4 = ar("mu4", [4, 1]); f4 = ar("f4", [4, 1]); bend = ar("bend", [4, 1]); dg = ar("dg", [4, 8])
    bc8 = ar("bc8", [128, 8])
    PTs = ar("PTs", [64, 4, 64]); PTb = ar("PTb", [64, 4, 64], BF16)
    hh = ar("hh", [64, 4, 192]); hcen = ar("hcen", [64, 4, 192]); hsq = ar("hsq", [64, 4, 192]); hst = ar("hst", [64, 4, 4])
    hnb = ar("hnb", [64, 768], BF16); khm = ar("khm", [64, 4, 192], BF16)

    BR_T = {0: yrwT, 1: ys5T, 2: ymlT}
    BR_K = {0: "yrwT", 1: "ys5T", 2: "ymlT"}
    BR_KC = {0: 6, 1: 4, 2: 6}

    def ln_block(src, srckeys, rows, out_dram=None):
        for hf in range(2):
            DVE(lambda e, hf=hf: e.bn_stats(out=lst[0:rows, hf, :], in_=src[0:rows, hf * 512:(hf + 1) * 512]),
                srckeys, ["lst"])
        DVE(lambda e: e.bn_aggr(out=lmv[0:rows, :], in_=lst[0:rows].rearrange("p a b -> p (a b)")), ["lst"], ["lmv"])
        ACT(lambda e: e.activation(out=lrs[0:rows, :], in_=lmv[0:rows, 1:2], func=AF.Sqrt, bias=LN_EPS, scale=1.0),
            ["lmv"], ["lrs"])
        DVE(lambda e: e.reciprocal(out=lrs[0:rows, :], in_=lrs[0:rows, :]), ["lrs"], ["lrs"])
        DVE(lambda e: e.tensor_scalar(out=lnt[0:rows, :], in0=src[0:rows, :], scalar1=lmv[0:rows, 0:1],
                                      scalar2=lrs[0:rows, 0:1], op0=ALU.subtract, op1=ALU.mult),
            srckeys + ["lmv", "lrs"], ["lnt"])
        DVE(lambda e: e.tensor_tensor(out=lnt[0:rows, :], in0=lnt[0:rows, :], in1=bcg[0:rows, 0, :], op=ALU.mult),
            ["lnt", "bcg"], ["lnt"])
        POOL(lambda e: e.tensor_tensor(out=x_tok[0:rows, 0, :], in0=lnt[0:rows, :], in1=bcg[0:rows, 1, :],
                                       op=ALU.add), ["lnt", "bcg"], ["x_tok"])
        if out_dram is not None:
            STORE(out_dram, x_tok[0:rows, 0, :], ["x_tok"])
        ACT(lambda e: e.copy(out=lnb[0:rows, :], in_=x_tok[0:rows, 0, :]), ["x_tok"], ["lnb"])
        pt, pk = bbank()
        for kc in range(8):
            TR(pt[:, kc * 128:kc * 128 + rows], lnb[0:rows, kc * 128:(kc + 1) * 128], ident_b[0:rows, 0:rows],
               ["lnb", "ident_b"], pk)
        DVE(lambda e: e.tensor_copy(out=xT[:, :, 0:rows],
                                    in_=pt.rearrange("p (k t) -> p k t", t=128)[:, :, 0:rows]), pk, ["xT"])

    def proj(wb, wk, c0, M, N):
        ps_, pk = fbank()
        MM(ps_[0:M, 0:N], [(wb[:, kc, c0:c0 + M], xT[:, kc, 0:N]) for kc in range(8)], [wk, "xT"], pk)
        return ps_[0:M, 0:N], pk

    s5cur = {"l": None}

    def ensure_s5(l):
        if s5cur["l"] != l:
            load_s5(l)
            s5cur["l"] = l

    def run_pass(kind, N, tiles, xsrc, ydst, first, last):
        og = "p" if kind != "sample" else "s"
        load_bc(0)
        LOAD(lnx[0:N, :], xsrc, [], ["lnx"])
        ln_block(lnx, ["lnx"], N)
        for l in range(DEPTH):
            if first:
                for t_, k_ in ((shiftst[l], "shiftst%d" % l), (S0T[l], "s0t%d" % l), (h0[l], "h0%d" % l),
                               (convst[l], "convst%d" % l), (Cst[l], "cst%d" % l), (mst[l], "mst%d" % l)):
                    POOL(lambda e, t_=t_: e.memset(t_[:], 0.0), [], [k_])
            layer(kind, N, tiles, l, og, last)
            phase_c(kind, N, l, ydst if l == DEPTH - 1 else None)

    def layer(kind, N, tiles, l, og, last):
        pl = "pp%d" % l
        samp = kind == "sample"
        nq = len(tiles)
        if samp:
            for q, (o, C, sq) in enumerate(tiles):
                LOAD(lnx[0:19, 0:128], st_shift[l, sq].rearrange("(t p) -> t p", p=128), [], ["lnx"])
                ps_, pk = fbank()
                TR(ps_[:, 0:19], lnx[0:19, 0:128], ident_f[0:19, 0:19], ["lnx", "ident_f"], pk)
                ACT(lambda e, ps_=ps_, q=q: e.copy(out=sshift[:, :, q], in_=ps_[:, 0:19]), pk + ["sshift"], ["sshift"])
                LOAD(lnx[0:48, 128:224], st_conv[l, sq].rearrange("j (t p) -> (j t) p", p=96), [], ["lnx"])
                ps_, pk = fbank()
                TR(ps_[0:96, 0:48], lnx[0:48, 128:224], ident_f[0:48, 0:48], ["lnx", "ident_f"], pk)
                ACT(lambda e, ps_=ps_, q=q: e.copy(out=sconv[:, :, q, :], in_=ps_[0:96, 0:48].rearrange("p (j t) -> p t j", j=3)),
                    pk + ["sconv"], ["sconv"])

        def shift_tile(ps_, pk, ct, out_ap, outkeys, ui):
            Ub = U[ui]
            uk = "U%d" % ui
            ACT(lambda e: e.copy(out=Ub[:, 1:N + 1], in_=ps_), pk, [uk])
            if not samp:
                ACT(lambda e: e.copy(out=Ub[:, 0:1], in_=shiftst[l][:, ct:ct + 1]), ["shiftst%d" % l, uk], [uk])
            else:
                ACT(lambda e: e.copy(out=Ub[:, 0:1], in_=sshift[:, ct, 0:1]), ["sshift", uk], [uk])
            DVE(lambda e: e.tensor_tensor(out=dtmp[ui][:, 0:N], in0=Ub[:, 0:N], in1=Ub[:, 1:N + 1], op=ALU.subtract),
                [uk], ["dtmp%d" % ui])
            DVE(lambda e: e.scalar_tensor_tensor(out=out_ap, in0=dtmp[ui][:, 0:N], scalar=PPc(l, "mu", ct),
                                                 in1=Ub[:, 1:N + 1], op0=ALU.mult, op1=ALU.add),
                ["dtmp%d" % ui, uk, pl], outkeys)
            if samp:
                DVE(lambda e: e.tensor_tensor(out=dtmp[ui][:, 0:1], in0=sshift[:, ct, 1:2], in1=Ub[:, 65:66], op=ALU.subtract),
                    [uk, "sshift", "dtmp%d" % ui], ["dtmp%d" % ui])
                DVE(lambda e: e.scalar_tensor_tensor(out=out_ap[:, 64:65], in0=dtmp[ui][:, 0:1], scalar=PPc(l, "mu", ct),
                                                     in1=Ub[:, 65:66], op0=ALU.mult, op1=ALU.add),
                    ["dtmp%d" % ui, uk, pl] + outkeys, outkeys)
                ACT(lambda e: e.copy(out=sshift_o[:, ct, 0:1], in_=Ub[:, 64:65]), [uk], ["sshift_o"])
                ACT(lambda e: e.copy(out=sshift_o[:, ct, 1:2], in_=Ub[:, 128:129]), [uk, "sshift_o"], ["sshift_o"])
            else:
                ACT(lambda e: e.copy(out=shiftst[l][:, ct:ct + 1], in_=Ub[:, N:N + 1]), [uk], ["shiftst%d" % l])

        ui = [0]

        def nui():
            ui[0] ^= 1
            return ui[0]

        S.label = 'rwA'
        wb, wk = load_group(l, "rwx")
        ps_, pk = proj(wb, wk, 0, 128, N)
        shift_tile(ps_, pk, 18, xs18[:, 0:N], ["xs18"], nui())
        ACT(lambda e: e.activation(out=txw[0:64, 0:N], in_=xs18[0:64, 0:N], func=AF.Tanh), ["xs18"], ["txw"])
        ACT(lambda e: e.copy(out=txw[64:128, 0:N], in_=xs18[64:128, 0:N]), ["xs18", "txw"], ["txw"])
        lk = ["lora%d" % l, "lora%da" % l]

        def jtile(j, wbk, wkk, wbr, wkr, wbv, wkv, c0):
            jp = j % 2
            ldec = ldec_2[jp]
            Gc = Gc_2[jp]
            aa = aa_2[jp]
            eneg = eneg_2[jp]
            eprev = eprev_2[jp]
            ehat = ehat_2[jp]
            epos = epos_2[jp]
            kx = kx_2[jp]
            kkr = kkr_2[jp]
            rn = rn_2[jp]
            kkn = kkn_2[jp]
            tk = tk_2[jp]
            kmod = kmod_2[jp]
            bb = bb_2[jp]
            rx = rx_2[jp]
            kksq = kksq_2[jp]
            ps_, pk = fbank()
            MM(ps_[:, 0:N], [(lora[l][0:64, j * 128:(j + 1) * 128], txw[0:64, 0:N])], lk + ["txw"], pk)
            ACT(lambda e, ps_=ps_: e.activation(out=ldec[:, 0:N], in_=ps_[:, 0:N], func=AF.Sigmoid, bias=PPc(l, "w0", j), scale=1.0),
                pk + [pl], ["ldec%d" % jp])
            POOL(lambda e: e.tensor_scalar(out=ldec[:, 0:N], in0=ldec[:, 0:N], scalar1=-EXPM05, scalar2=None, op0=ALU.mult),
                 ["ldec%d" % jp], ["ldec%d" % jp])
            ps2, pk2 = fbank()
            MM(ps2[:, 0:N], [(lora[l][64:128, j * 128:(j + 1) * 128], txw[64:128, 0:N])], lk + ["txw"], pk2)
            ACT(lambda e, ps2=ps2: e.activation(out=aa[:, 0:N], in_=ps2[:, 0:N], func=AF.Sigmoid, bias=PPc(l, "a0", j), scale=1.0),
                pk2 + [pl], ["aa%d" % jp])
            DVE(lambda e: e.tensor_tensor_scan(out=Gc[:, 0:N], data0=scanm[:, 0:N], data1=ldec[:, 0:N], initial=0.0,
                                               op0=ALU.mult, op1=ALU.add), ["ldec%d" % jp, "scanm"], ["Gc%d" % jp])
            for q, (o, C, sq) in enumerate(tiles):
                ACT(lambda e, q=q, o=o, C=C: e.activation(out=gC[:, j, q:q + 1], in_=Gc[:, o + C - 1:o + C], func=AF.Exp),
                    ["Gc%d" % jp, "gC"], ["gC"])
            ps_, pk = proj(wbk, wkk, c0, 128, N)
            shift_tile(ps_, pk, 6 + j, kx[:, 0:N], ["kx%d" % jp], nui())
            ACT(lambda e: e.activation(out=eneg[:, 0:N], in_=Gc[:, 0:N], func=AF.Exp, scale=-1.0), ["Gc%d" % jp], ["eneg%d" % jp])
            DVE(lambda e: e.tensor_tensor(out=eprev[:, 0:N], in0=Gc[:, 0:N], in1=ldec[:, 0:N], op=ALU.subtract),
                ["Gc%d" % jp, "ldec%d" % jp], ["eprev%d" % jp])
            ACT(lambda e: e.activation(out=eprev[:, 0:N], in_=eprev[:, 0:N], func=AF.Exp), ["eprev%d" % jp], ["eprev%d" % jp])
            for q, (o, C, sq) in enumerate(tiles):
                ACT(lambda e, o=o, C=C: e.activation(out=ehat[:, o:o + C], in_=Gc[:, o:o + C], func=AF.Exp, scale=-1.0),
                    ["Gc%d" % jp, "ehat%d" % jp], ["ehat%d" % jp])
                DVE(lambda e, o=o, C=C, q=q: e.tensor_scalar(out=ehat[:, o:o + C], in0=ehat[:, o:o + C],
                                                             scalar1=gC[:, j, q:q + 1], scalar2=None, op0=ALU.mult),
                    ["ehat%d" % jp, "gC"], ["ehat%d" % jp])
            DVE(lambda e: e.tensor_scalar(out=kkr[:, 0:N], in0=kx[:, 0:N], scalar1=PPc(l, "kk", j), scalar2=None,
                                          op0=ALU.mult), ["kx%d" % jp, pl], ["kkr%d" % jp])
            POOL(lambda e: e.tensor_tensor(out=kksq[:, 0:N], in0=kkr[:, 0:N], in1=kkr[:, 0:N], op=ALU.mult), ["kkr%d" % jp], ["kksq%d" % jp])
            ps2, pk2 = fbank()
            MM(ps2[:, 0:N], [(blk1[:], kksq[:, 0:N])], ["blk1", "kksq%d" % jp], pk2)
            ACT(lambda e, ps2=ps2: e.activation(out=rn[:, 0:N], in_=ps2[:, 0:N], func=AF.Sqrt, bias=1e-12, scale=1.0), pk2, ["rn%d" % jp])
            DVE(lambda e: e.reciprocal(out=rn[:, 0:N], in_=rn[:, 0:N]), ["rn%d" % jp], ["rn%d" % jp])
            DVE(lambda e: e.tensor_tensor(out=kkn[:, 0:N], in0=kkr[:, 0:N], in1=rn[:, 0:N], op=ALU.mult), ["kkr%d" % jp, "rn%d" % jp], ["kkn%d" % jp])
            DVE(lambda e: e.tensor_scalar(out=tk[:, 0:N], in0=aa[:, 0:N], scalar1=PPc(l, "ka", j),
                                          scalar2=omka[l][:, j:j + 1], op0=ALU.mult, op1=ALU.add),
                ["aa%d" % jp, pl, "omka%d" % l], ["tk%d" % jp])
            DVE(lambda e: e.tensor_tensor(out=kmod[:, 0:N], in0=kx[:, 0:N], in1=tk[:, 0:N], op=ALU.mult), ["kx%d" % jp, "tk%d" % jp], ["kmod%d" % jp])
            POOL(lambda e: e.tensor_tensor(out=bb[:, 0:N], in0=kkn[:, 0:N], in1=aa[:, 0:N], op=ALU.mult), ["kkn%d" % jp, "aa%d" % jp], ["bb%d" % jp])
            DVE(lambda e: e.tensor_tensor(out=kt_[:, j, 0:N], in0=kmod[:, 0:N], in1=eneg[:, 0:N], op=ALU.mult),
                ["kmod%d" % jp, "eneg%d" % jp, "kt__%d" % j], ["kt__%d" % j])
            POOL(lambda e: e.tensor_tensor(out=bt_[:, j, 0:N], in0=bb[:, 0:N], in1=eneg[:, 0:N], op=ALU.mult),
                 ["bb%d" % jp, "eneg%d" % jp, "bt__%d" % j], ["bt__%d" % j])
            DVE(lambda e: e.scalar_tensor_tensor(out=at_[:, j, 0:N], in0=kkn[:, 0:N], scalar=-1.0, in1=eprev[:, 0:N],
                                                 op0=ALU.mult, op1=ALU.mult), ["kkn%d" % jp, "eprev%d" % jp, "at__%d" % j], ["at__%d" % j])
            POOL(lambda e: e.tensor_tensor(out=khat[:, j, 0:N], in0=kmod[:, 0:N], in1=ehat[:, 0:N], op=ALU.mult),
                 ["kmod%d" % jp, "ehat%d" % jp, "khat_%d" % j], ["khat_%d" % j])
            DVE(lambda e: e.tensor_tensor(out=bhat[:, j, 0:N], in0=bb[:, 0:N], in1=ehat[:, 0:N], op=ALU.mult),
                ["bb%d" % jp, "ehat%d" % jp, "bhat_%d" % j], ["bhat_%d" % j])
            ps_, pk = proj(wbr, wkr, c0, 128, N)
            shift_tile(ps_, pk, j, rx[:, 0:N], ["rx%d" % jp], nui())
            ACT(lambda e: e.activation(out=epos[:, 0:N], in_=Gc[:, 0:N], func=AF.Exp), ["Gc%d" % jp], ["epos%d" % jp])
            DVE(lambda e: e.tensor_tensor(out=rt_[:, j, 0:N], in0=rx[:, 0:N], in1=epos[:, 0:N], op=ALU.mult),
                ["rx%d" % jp, "epos%d" % jp, "rt__%d" % j], ["rt__%d" % j])
            DVE(lambda e: e.scalar_tensor_tensor(out=rkp[:, j, 0:N], in0=rx[:, 0:N], scalar=PPc(l, "rk", j),
                                                 in1=kmod[:, 0:N], op0=ALU.mult, op1=ALU.mult),
                ["rx%d" % jp, pl, "kmod%d" % jp, "rkp_%d" % j], ["rkp_%d" % j])
            ps_, pk = proj(wbv, wkv, c0, 128, N)
            shift_tile(ps_, pk, 12 + j, vT[:, j, 0:N], ["vT_%d" % j], nui())

        for part, js in (("0", range(4)), ("1", range(4, 6))):
            wbk, wkk = load_group(l, "rwk" + part)
            wbr, wkr = load_group(l, "rwr" + part)
            wbv, wkv = load_group(l, "rwv" + part)
            for j in js:
                jtile(j, wbk, wkk, wbr, wkr, wbv, wkv, (j % 4) * 128)

        for gn, js in (("rwg0", range(4)), ("rwg1", range(4, 6))):
            wb, wk = load_group(l, gn)
            for j in js:
                ps_, pk = proj(wb, wk, (j % 4) * 128, 128, N)
                ACT(lambda e, ps_=ps_, j=j: e.activation(out=grw[:, j, 0:N], in_=ps_, func=AF.Silu), pk + ["grw_%d" % j], ["grw_%d" % j])

        for q, (o, C, sq) in enumerate(tiles):
            rwkv_tile(l, q, o, C, sq, kind, og, last, nq)
        if samp:
            for q, (o, C, sq) in enumerate(tiles):
                ps_, pk = fbank()
                TR(ps_[0:19, 0:128], sshift_o[:, :, q], ident_f[:], ["sshift_o", "ident_f"], pk)
                ACT(lambda e, ps_=ps_: e.copy(out=lnx[0:19, 0:128], in_=ps_[0:19, 0:128]), pk + ["lnx"], ["lnx"])
                STORE(o_shift["s"][l, sq].rearrange("(t p) -> t p", p=128), lnx[0:19, 0:128], ["lnx"])
        elif last:
            ps_, pk = fbank()
            TR(ps_[0:19, 0:128], shiftst[l][:, :], ident_f[:], ["shiftst%d" % l, "ident_f"], pk)
            ACT(lambda e, ps_=ps_: e.copy(out=lnx[0:19, 0:128], in_=ps_[0:19, 0:128]), pk + ["lnx"], ["lnx"])
            STORE(o_shift["p"][l, 0].rearrange("(t p) -> t p", p=128), lnx[0:19, 0:128], ["lnx"])

        S.label = 's5A'
        ensure_s5(l)
        wb, wk = load_group(l, "s5u")
        for j in range(4):
            ps_, pk = proj(wb, wk, j * 128, 128, N)
            ACT(lambda e, ps_=ps_, j=j: e.copy(out=u32[:, j, 0:N], in_=ps_), pk + ["u32_%d" % j], ["u32_%d" % j])
            DVE(lambda e, j=j: e.tensor_copy(out=uT[:, j, 0:N], in_=u32[:, j, 0:N]), ["u32_%d" % j, "uT_%d" % j], ["uT_%d" % j])
        wb, wk = load_group(l, "s5g")
        for j in range(4):
            ps_, pk = proj(wb, wk, j * 128, 128, N)
            ACT(lambda e, ps_=ps_, j=j: e.activation(out=gs5[:, j, 0:N], in_=ps_, func=AF.Silu), pk + ["gs5_%d" % j], ["gs5_%d" % j])
        s5_en(l, N)
        for q, (o, C, sq) in enumerate(tiles):
            s5_tile(l, q, o, C, sq, kind, og, last, nq)
        ensure_s5(1 - l)

        S.label = 'mlA'
        S.label = 'mlA'
        for gi_ in range(4):
            wb, wk = load_group(l, "mlqk%d" % gi_)
            for jj in range(4):
                t = gi_ * 4 + jj
                ps_, pk = proj(wb, wk, jj * 96, 96, N)
                ACT(lambda e, ps_=ps_, t=t: e.copy(out=qkraw[:, t, 3:N + 3], in_=ps_), pk + ["qkraw_%d" % t], ["qkraw_%d" % t])
        qk = ["qkraw"]
        if not samp:
            ACT(lambda e: e.copy(out=qkraw[:, :, 0:3], in_=convst[l][:, :, :]), ["convst%d" % l] + qk, qk)
        else:
            ACT(lambda e: e.copy(out=qkraw[:, :, 0:3], in_=sconv[:, :, 0, :]), ["sconv"] + qk, qk)

        def wbc(jt, n_):
            return pq[l][:, :, jt:jt + 1].broadcast_to([96, 16, n_])

        pqk = "pq%d" % l
        DVE(lambda e: e.tensor_tensor(out=cvacc[:, :, 0:N], in0=qkraw[:, :, 0:N], in1=wbc(0, N), op=ALU.mult), qk + [pqk], ["cvacc"])
        POOL(lambda e: e.tensor_tensor(out=cvacc[:, :, 0:N], in0=cvacc[:, :, 0:N], in1=wbc(4, N), op=ALU.add), ["cvacc", pqk], ["cvacc"])
        for jt in range(1, 4):
            POOL(lambda e, jt=jt: e.tensor_tensor(out=cvtmp[:, :, 0:N], in0=qkraw[:, :, jt:N + jt], in1=wbc(jt, N), op=ALU.mult),
                 qk + [pqk, "cvtmp"], ["cvtmp"])
            DVE(lambda e: e.tensor_tensor(out=cvacc[:, :, 0:N], in0=cvacc[:, :, 0:N], in1=cvtmp[:, :, 0:N], op=ALU.add),
                ["cvacc", "cvtmp"], ["cvacc"])
        if samp:
            hk = "hbuf0"
            hb = hbuf
            POOL(lambda e: e.tensor_copy(out=hb[:, :, 0:3], in_=sconv[:, :, 1, :]), ["sconv", hk], [hk])
            POOL(lambda e: e.tensor_copy(out=hb[:, :, 3:6], in_=qkraw[:, :, 67:70]), qk + [hk], [hk])
            f3 = cvacc[:, :, 64:67]
            DVE(lambda e: e.tensor_tensor(out=f3, in0=hb[:, :, 0:3], in1=wbc(0, 3), op=ALU.mult), [hk, pqk, "cvacc"], ["cvacc"])
            DVE(lambda e: e.tensor_tensor(out=f3, in0=f3, in1=wbc(4, 3), op=ALU.add), [pqk, "cvacc"], ["cvacc"])
            for jt in range(1, 4):
                DVE(lambda e, jt=jt: e.tensor_tensor(out=cvtmp[:, :, 0:3], in0=hb[:, :, jt:jt + 3], in1=wbc(jt, 3), op=ALU.mult),
                    [hk, pqk, "cvtmp"], ["cvtmp"])
                DVE(lambda e: e.tensor_tensor(out=f3, in0=f3, in1=cvtmp[:, :, 0:3], op=ALU.add), ["cvacc", "cvtmp"], ["cvacc"])
        ACT(lambda e: e.activation(out=qkT[:, :, 0:N], in_=cvacc[:, :, 0:N], func=AF.Silu), ["cvacc", "qkT"], ["qkT"])
        POOL(lambda e: e.tensor_scalar(out=qkT[:, 8:16, 0:N], in0=qkT[:, 8:16, 0:N], scalar1=1.0 / math.sqrt(192.0), scalar2=None,
                                       op0=ALU.mult), ["qkT"], ["qkT"])
        if not samp:
            ACT(lambda e: e.copy(out=convst[l][:, :, :], in_=qkraw[:, :, N:N + 3]), qk + ["convst%d" % l], ["convst%d" % l])
        conv_out(l, N, tiles, kind, og, last)
        wb0, wk0 = load_group(l, "mlv0")
        wb1, wk1 = load_group(l, "mlv1")
        for q, (o, C, sq) in enumerate(tiles):
            ps_, pk = fpair()
            MM(ps_[0:C, 0, 0:512], [(xT[:, kc, o:o + C], wb0[:, kc, 0:512]) for kc in range(8)], [wk0, "xT"], [pk[0]])
            MM(ps_[0:C, 1, 0:264], [(xT[:, kc, o:o + C], wb1[:, kc, 0:264]) for kc in range(8)], [wk1, "xT"], [pk[1]])
            vk = ["vaug"]
            ACT(lambda e, ps_=ps_, q=q, C=C: e.copy(out=vaug[0:C, q, 0:2, 0:192],
                                                    in_=ps_[0:C, 0, 0:384].rearrange("p (h d) -> p h d", d=192)), [pk[0]] + vk, vk)
            ACT(lambda e, ps_=ps_, q=q, C=C: e.copy(out=vaug[0:C, q, 2, 0:128], in_=ps_[0:C, 0, 384:512]), [pk[0]] + vk, vk)
            DVE(lambda e, ps_=ps_, q=q, C=C: e.tensor_copy(out=vaug[0:C, q, 2, 128:192], in_=ps_[0:C, 1, 0:64]), [pk[1]] + vk, vk)
            DVE(lambda e, ps_=ps_, q=q, C=C: e.tensor_copy(out=vaug[0:C, q, 3, 0:192], in_=ps_[0:C, 1, 64:256]), [pk[1]] + vk, vk)
            POOL(lambda e, q=q, C=C: e.memset(vaug[0:C, q, :, 192:193], 1.0), vk, vk)
            DVE(lambda e, ps_=ps_, q=q, C=C: e.tensor_tensor(out=iftok[0:C, q, :], in0=ps_[0:C, 1, 256:264], in1=bif[l][0:C, :],
                                                             op=ALU.add), [pk[1], "bif%d" % l, "iftok"], ["iftok"])
        wbo0, wko0 = load_group(l, "mlo0")
        wbo1, wko1 = load_group(l, "mlo1")
        for j in range(6):
            wb, wk = (wbo0, wko0) if j < 4 else (wbo1, wko1)
            ps_, pk = proj(wb, wk, (j % 4) * 128, 128, N)
            ACT(lambda e, ps_=ps_, j=j: e.activation(out=ogT[:, j, 0:N], in_=ps_, func=AF.Sigmoid), pk + ["ogT_%d" % j], ["ogT_%d" % j])
        wbz0, wkz0 = load_group(l, "mlz0")
        wbz1, wkz1 = load_group(l, "mlz1")
        for j in range(6):
            wb, wk = (wbz0, wkz0) if j < 4 else (wbz1, wkz1)
            ps_, pk = proj(wb, wk, (j % 4) * 128, 128, N)
            zsl = zsl_2[j % 2]
            ACT(lambda e, ps_=ps_, zsl=zsl: e.activation(out=zsl[:, 0:N], in_=ps_, func=AF.Silu), pk, ["zsl%d" % (j % 2)])
            DVE(lambda e, j=j, zsl=zsl: e.scalar_tensor_tensor(out=ogT[:, j, 0:N], in0=ogT[:, j, 0:N], scalar=PPc(l, "mlg", j),
                                                               in1=zsl[:, 0:N], op0=ALU.mult, op1=ALU.mult),
                ["ogT_%d" % j, "zsl%d" % (j % 2), pl], ["ogT_%d" % j])
        for q, (o, C, sq) in enumerate(tiles):
            ml_tile(l, q, o, C, sq, kind, og, last, nq)

    def conv_out(l, N, tiles, kind, og, last):
        if kind == "sample":
            ends = [(o + C - 3, sq) for (o, C, sq) in tiles]
        elif last:
            ends = [(N - 3, 0)]
        else:
            return
        for (e0, sq) in ends:
            for hf in range(2):
                for i2 in range(2):
                    i = hf * 2 + i2
                    wbi, wki = load_group(l, "mlqk%d" % i)
                    ps_, pk = fbank()
                    MM(ps_[0:3, 0:384], [(xT[:, kc, e0:e0 + 3], wbi[:, kc, 0:384]) for kc in range(8)], [wki, "xT"], pk)
                    ACT(lambda e, i2=i2, ps_=ps_: e.copy(out=lnt[0:3, i2 * 384:(i2 + 1) * 384], in_=ps_[0:3, 0:384]), pk + ["lnt"], ["lnt"])
                STORE(o_conv[og][l, sq, :, hf * 768:(hf + 1) * 768], lnt[0:3, 0:768], ["lnt"])

    def rwkv_tile(l, q, o, C, sq, kind, og, last, nq):
        S.label = 'rwT'
        tb_ = TB[q % 2]
        pz = tb_['z']
        vtok, khtok, bhtok, Nsb, NTsb, Msb, P1sb, P2sb = (tb_[k_] for k_ in ('vtok', 'khtok', 'bhtok', 'Nsb', 'NTsb', 'Msb', 'P1sb', 'P2sb'))
        samp = kind == "sample"
        sk = "s0t%d" % l
        sl = slice(o, o + C)
        PARTS = [(0, 0), (1, 64)]
        if samp:
            LOAD(stg[:], st_wkv[l, sq].rearrange("h v k -> v h k"), [], ["stg"])
            for j in range(6):
                ps_, pk = fbank()
                TR(ps_[:, 0:64], stg[:, 2 * j:2 * j + 2, :].rearrange("p a b -> p (a b)"), ident_f[0:64, 0:64], ["stg", "ident_f"], pk)
                ACT(lambda e, ps_=ps_, j=j: e.copy(out=S0T[l][:, j, :], in_=ps_[:, 0:64]), pk + [sk], [sk])
        ACT(lambda e: e.copy(out=S0Tb[:], in_=S0T[l][:]), [sk], ["s0tb"])

        def both(fn_):
            if C == 64:
                fn_(slice(0, 128))
            else:
                for par, pb in PARTS:
                    fn_(slice(pb, pb + C))

        for src, srck, dst, dk in ((vT, "vT", vtok, "vtok" + pz), (khat, "khat", khtok, "khtok" + pz), (bhat, "bhat", bhtok, "bhtok" + pz)):
            def fnt(e, src=src):
                for j in range(6):
                    for par, pb in PARTS:
                        ins = e.transpose(out=PSB[pb:pb + C, par, j * 64:(j + 1) * 64], in_=src[pb:pb + 64, j, sl],
                                          identity=ident_b[pb:pb + 64, pb:pb + 64])
                return ins
            PE(fnt, [srck, "ident_b"], ["pb0", "pb1"])
            for par, pb in PARTS:
                ACT(lambda e, dst=dst, par=par, pb=pb: e.copy(out=dst[pb:pb + C, :, :],
                                                                in_=PSB[pb:pb + C, par, 0:384].rearrange("p (j d) -> p j d", d=64)),
                    ["pb%d" % par, dk], [dk])

        def score(lt, lk_, rt2, rk_, mask, mk, dst, dk):
            ps_, pk = fpair()
            def fn(e, ps_=ps_):
                for j in range(6):
                    for par, pb in PARTS:
                        ins = e.matmul(ps_[pb:pb + C, par, j * 64:j * 64 + C], lhsT=lt[pb:pb + 64, j, sl],
                                       rhs=rt2[pb:pb + 64, j, sl], start=True, stop=True)
                return ins
            PE(fn, [lk_, rk_], pk)
            for par, pb in PARTS:
                DVE(lambda e, ps_=ps_, par=par, pb=pb: e.tensor_tensor(
                    out=dst[pb:pb + C, :, 0:C], in0=ps_[pb:pb + C, par, 0:384].rearrange("p (j t) -> p j t", t=64)[:, :, 0:C],
                    in1=mask[pb:pb + C, :, 0:C], op=ALU.mult), [pk[par], mk, dk], [dk])

        def evac(ps_, pk, dst, dk, eng=ACT):
            for par, pb in PARTS:
                if par == 0:
                    ACT(lambda e, par=par, pb=pb: e.copy(out=dst[pb:pb + C, :, :],
                                                         in_=ps_[pb:pb + C, par, 0:384].rearrange("p (j d) -> p j d", d=64)),
                        [pk[par], dk], [dk])
                else:
                    DVE(lambda e, par=par, pb=pb: e.tensor_copy(out=dst[pb:pb + C, :, :],
                                                                in_=ps_[pb:pb + C, par, 0:384].rearrange("p (j d) -> p j d", d=64)),
                        [pk[par], dk], [dk])

        def evac_sq(ps_, pk, dst, dk):
            for par, pb in PARTS:
                DVE(lambda e, par=par, pb=pb: e.tensor_copy(
                    out=dst[pb:pb + C, :, 0:C], in_=ps_[pb:pb + C, par, 0:384].rearrange("p (j t) -> p j t", t=64)[:, :, 0:C]),
                    [pk[par], dk], [dk])

        score(bt_, "bt_", at_, "at_", m_strict, "m_strict", Nsb[0], "Nsb0" + pz)
        score(at_, "at_", bt_, "bt_", m_lower, "m_lower", NTsb[0], "NTsb0" + pz)
        score(kt_, "kt_", at_, "at_", m_strict, "m_strict", Msb, "Msb" + pz)

        ps_, pk = fpair()
        def fnx(e, ps_=ps_):
            for j in range(6):
                for par, pb in PARTS:
                    out = ps_[pb:pb + C, par, j * 64:(j + 1) * 64]
                    e.matmul(out, lhsT=at_[pb:pb + 64, j, sl], rhs=S0Tb[pb:pb + 64, j, :], start=True, stop=False)
                    ins = e.matmul(out, lhsT=Msb[pb:pb + C, j, 0:C], rhs=vtok[pb:pb + C, j, :], start=False, stop=True)
            return ins
        PE(fnx, ["at_", "s0tb", "Msb" + pz, "vtok" + pz], pk)
        evac(ps_, pk, Ysb[0], "Ysb0")
        lev = int(round(math.log2(C)))
        cur = 0
        for lv in range(lev):
            Pc, PTc, Yc = Nsb[cur], NTsb[cur], Ysb[cur]
            Pn, PTn, Yn = Nsb[1 - cur], NTsb[1 - cur], Ysb[1 - cur]
            ps_, pk = fpair()
            def fny(e, ps_=ps_, Pc=Pc, Yc=Yc):
                for j in range(6):
                    for par, pb in PARTS:
                        out = ps_[pb:pb + C, par, j * 64:(j + 1) * 64]
                        e.matmul(out, lhsT=ident_b[pb:pb + C, pb:pb + C], rhs=Yc[pb:pb + C, j, :], start=True, stop=False)
                        ins = e.matmul(out, lhsT=Pc[pb:pb + C, j, 0:C], rhs=Yc[pb:pb + C, j, :], start=False, stop=True)
                return ins
            PE(fny, ["Nsb%d" % cur + pz, "Ysb%d" % cur, "ident_b"], pk)
            evac(ps_, pk, Yn, "Ysb%d" % (1 - cur))
            if lv < lev - 1:
                ps2, pk2 = fpair()
                def fnp(e, ps2=ps2, Pc=Pc, PTc=PTc):
                    for j in range(6):
                        for par, pb in PARTS:
                            ins = e.matmul(ps2[pb:pb + C, par, j * 64:j * 64 + C], lhsT=PTc[pb:pb + C, j, 0:C],
                                           rhs=Pc[pb:pb + C, j, 0:C], start=True, stop=True)
                    return ins
                PE(fnp, ["Nsb%d" % cur + pz, "NTsb%d" % cur + pz], pk2)
                evac_sq(ps2, pk2, Pn, "Nsb%d" % (1 - cur) + pz)
                if lv < lev - 2:
                    ps3, pk3 = fpair()
                    def fnq(e, ps3=ps3, Pc=Pc, PTc=PTc):
                        for j in range(6):
                            for par, pb in PARTS:
                                ins = e.matmul(ps3[pb:pb + C, par, j * 64:j * 64 + C], lhsT=Pc[pb:pb + C, j, 0:C],
                                               rhs=PTc[pb:pb + C, j, 0:C], start=True, stop=True)
                        return ins
                    PE(fnq, ["Nsb%d" % cur + pz, "NTsb%d" % cur + pz], pk3)
                    evac_sq(ps3, pk3, PTn, "NTsb%d" % (1 - cur) + pz)
            cur = 1 - cur
        UT = Ysb[cur]
        uk = "Ysb%d" % cur
        score(kt_, "kt_", rt_, "rt_", m_incl, "m_incl", P1sb, "P1sb" + pz)
        score(bt_, "bt_", rt_, "rt_", m_incl, "m_incl", P2sb, "P2sb" + pz)
        ps_, pk = fpair()
        def fno(e, ps_=ps_, UT=UT):
            for j in range(6):
                for par, pb in PARTS:
                    out = ps_[pb:pb + C, par, j * 64:(j + 1) * 64]
                    e.matmul(out, lhsT=rt_[pb:pb + 64, j, sl], rhs=S0Tb[pb:pb + 64, j, :], start=True, stop=False)
                    e.matmul(out, lhsT=P1sb[pb:pb + C, j, 0:C], rhs=vtok[pb:pb + C, j, :], start=False, stop=False)
                    ins = e.matmul(out, lhsT=P2sb[pb:pb + C, j, 0:C], rhs=UT[pb:pb + C, j, :], start=False, stop=True)
            return ins
        PE(fno, ["rt_", "s0tb", "P1sb" + pz, "P2sb" + pz, "vtok" + pz, uk], pk)
        evac(ps_, pk, yo, "yo")
        psr, pkr = fpair()
        def fnr(e, psr=psr):
            for j in range(6):
                for par, pb in PARTS:
                    ins = e.matmul(psr[pb:pb + C, par, j:j + 1], lhsT=rkp[pb:pb + 64, j, sl], rhs=onesb[pb:pb + 64, 0:1],
                                   start=True, stop=True)
            return ins
        PE(fnr, ["rkp", "onesb"], pkr)
        for par, pb in PARTS:
            ACT(lambda e, par=par, pb=pb, psr=psr: e.copy(out=rkd[pb:pb + C, :], in_=psr[pb:pb + C, par, 0:6]), [pkr[par], "rkd"], ["rkd"])
        both(lambda P: DVE(lambda e: e.tensor_reduce(out=ystat[P, 0, :], in_=yo[P], axis=AX.X, op=ALU.add), ["yo", "ystat"], ["ystat"]))
        both(lambda P: DVE(lambda e: e.tensor_scalar(out=ystat[P, 0, :], in0=ystat[P, 0, :], scalar1=1.0 / 64, scalar2=None, op0=ALU.mult),
                           ["ystat"], ["ystat"]))
        both(lambda P: DVE(lambda e: e.tensor_tensor(out=ycen[P], in0=yo[P], in1=ystat[P, 0, :].unsqueeze(2).broadcast_to([P.stop - P.start, 6, 64]),
                                                     op=ALU.subtract), ["yo", "ystat", "ycen"], ["ycen"]))
        both(lambda P: POOL(lambda e: e.tensor_tensor(out=ysq[P], in0=ycen[P], in1=ycen[P], op=ALU.mult), ["ycen", "ysq"], ["ysq"]))
        both(lambda P: DVE(lambda e: e.tensor_reduce(out=ystat[P, 1, :], in_=ysq[P], axis=AX.X, op=ALU.add), ["ysq", "ystat"], ["ystat"]))
        both(lambda P: ACT(lambda e: e.activation(out=ystat[P, 2, :], in_=ystat[P, 1, :], func=AF.Sqrt, bias=RW_GN_EPS, scale=1.0 / 64),
                           ["ystat"], ["ystat"]))
        both(lambda P: DVE(lambda e: e.reciprocal(out=ystat[P, 2, :], in_=ystat[P, 2, :]), ["ystat"], ["ystat"]))
        both(lambda P: DVE(lambda e: e.tensor_tensor(out=ycen[P], in0=ycen[P], in1=ystat[P, 2, :].unsqueeze(2).broadcast_to([P.stop - P.start, 6, 64]),
                                                     op=ALU.mult), ["ycen", "ystat"], ["ycen"]))
        both(lambda P: DVE(lambda e: e.tensor_tensor(out=ycen[P].rearrange("p j d -> p (j d)"), in0=ycen[P].rearrange("p j d -> p (j d)"),
                                                     in1=rwln[l][P, 0, :], op=ALU.mult), ["ycen", "rwln%d" % l], ["ycen"]))
        both(lambda P: POOL(lambda e: e.tensor_tensor(out=ycen[P].rearrange("p j d -> p (j d)"), in0=ycen[P].rearrange("p j d -> p (j d)"),
                                                      in1=rwln[l][P, 1, :], op=ALU.add), ["ycen", "rwln%d" % l], ["ycen"]))
        both(lambda P: POOL(lambda e: e.tensor_tensor(out=ysq[P], in0=vtok[P], in1=rkd[P, :].unsqueeze(2).broadcast_to([P.stop - P.start, 6, 64]),
                                                      op=ALU.mult), ["vtok" + pz, "rkd", "ysq"], ["ysq"]))
        both(lambda P: DVE(lambda e: e.tensor_tensor(out=ytb[P], in0=ycen[P], in1=ysq[P], op=ALU.add), ["ycen", "ysq", "ytb"], ["ytb"]))
        def fnb(e):
            for j in range(6):
                for par, pb in PARTS:
                    ins = e.transpose(out=PSB[pb:pb + 64, par, j * 64:j * 64 + C], in_=ytb[pb:pb + C, j, :],
                                      identity=ident_b[pb:pb + C, pb:pb + C])
            return ins
        PE(fnb, ["ytb", "ident_b"], ["pb0", "pb1"])
        for par, pb in PARTS:
            DVE(lambda e, par=par, pb=pb: e.tensor_tensor(out=yrwT[pb:pb + 64, :, sl],
                                                          in0=PSB[pb:pb + 64, par, 0:384].rearrange("p (j t) -> p j t", t=64)[:, :, 0:C],
                                                          in1=grw[pb:pb + 64, :, sl], op=ALU.mult), ["pb%d" % par, "grw", "yrwT"], ["yrwT"])
        ps_, pk = fpair()
        def fns(e, ps_=ps_, UT=UT):
            for j in range(6):
                for par, pb in PARTS:
                    out = ps_[pb:pb + 64, par, j * 64:(j + 1) * 64]
                    e.matmul(out, lhsT=khtok[pb:pb + C, j, :], rhs=vtok[pb:pb + C, j, :], start=True, stop=False)
                    ins = e.matmul(out, lhsT=bhtok[pb:pb + C, j, :], rhs=UT[pb:pb + C, j, :], start=False, stop=True)
            return ins
        PE(fns, ["khtok" + pz, "bhtok" + pz, "vtok" + pz, uk], pk)
        for j in range(6):
            for par, pb in PARTS:
                DVE(lambda e, j=j, ps_=ps_, par=par, pb=pb: e.scalar_tensor_tensor(
                    out=S0T[l][pb:pb + 64, j, :], in0=S0T[l][pb:pb + 64, j, :], scalar=gC[pb:pb + 64, j, q:q + 1],
                    in1=ps_[pb:pb + 64, par, j * 64:(j + 1) * 64], op0=ALU.mult, op1=ALU.add), [pk[par], sk, "gC"], [sk])
        if samp or (last and q == nq - 1):
            b_ = sq if samp else 0
            for j in range(6):
                ps2, pk2 = fbank()
                TR(ps2[0:64, 0:128], S0T[l][:, j, :], ident_f[:], [sk, "ident_f"], pk2)
                ACT(lambda e, ps2=ps2, j=j: e.copy(out=stg[:, 2 * j:2 * j + 2, :].rearrange("p a b -> p (a b)"), in_=ps2[0:64, 0:128]),
                    pk2 + ["stg"], ["stg"])
            STORE(o_wkv[og][l, b_].rearrange("h v k -> v h k"), stg[:], ["stg"])

    def s5_en(l, N):
        S.label = 's5T'
        C = N
        sl = slice(0, N)
        for ut in range(4):
            ps_, pk = fpair()
            MM(ps_[0:C, 0, :], [(uT[:, ut, sl], bblkB[:, ut, 0:512])], ["uT", "bblkB"], [pk[0]])
            MM(ps_[0:C, 1, :], [(uT[:, ut, sl], bblkB[:, ut, 512:1024])], ["uT", "bblkB"], [pk[1]])
            pvv = ps_[0:C].rearrange("p b (i c q) -> p (b i) c q", i=2, c=2)
            bur, bui = pvv[:, :, 0, :], pvv[:, :, 1, :]
            enr, eni = EnB[0:C, ut * 4:(ut + 1) * 4, 0, :], EnB[0:C, ut * 4:(ut + 1) * 4, 1, :]
            wr_, wi_ = wtok[0:C, ut * 4:(ut + 1) * 4, 0, :], wtok[0:C, ut * 4:(ut + 1) * 4, 1, :]
            ek = ["EnB"]
            DVE(lambda e, bur=bur, enr=enr: e.tensor_tensor(out=s5a[0:C], in0=bur, in1=enr, op=ALU.mult), pk + ek, ["s5a"])
            DVE(lambda e, bui=bui, eni=eni: e.tensor_tensor(out=s5b[0:C], in0=bui, in1=eni, op=ALU.mult), pk + ek, ["s5b"])
            POOL(lambda e, wr_=wr_: e.tensor_tensor(out=wr_, in0=s5a[0:C], in1=s5b[0:C], op=ALU.subtract), ["s5a", "s5b", "wtok"], ["wtok"])
            DVE(lambda e, bur=bur, eni=eni: e.tensor_tensor(out=s5a[0:C], in0=bur, in1=eni, op=ALU.mult), pk + ek + ["s5a"], ["s5a"])
            DVE(lambda e, bui=bui, enr=enr: e.tensor_tensor(out=s5b[0:C], in0=bui, in1=enr, op=ALU.mult), pk + ek + ["s5b"], ["s5b"])
            POOL(lambda e, wi_=wi_: e.tensor_tensor(out=wi_, in0=s5a[0:C], in1=s5b[0:C], op=ALU.add), ["s5a", "s5b", "wtok"], ["wtok"])

    def s5_tile(l, q, o, C, sq, kind, og, last, nq):
        S.label = 's5T'
        samp = kind == "sample"
        hk = "h0%d" % l
        sl = slice(o, o + C)
        if samp:
            for c, srcd in ((0, st_s5re), (1, st_s5im)):
                LOAD(lnx[0:16, c * 128:(c + 1) * 128], srcd[l, sq].rearrange("(i g) p -> i (g p)", g=2), [], ["lnx"])
            for c in range(2):
                ps_, pk = fbank()
                TR(ps_[:, 0:16], lnx[0:16, c * 128:(c + 1) * 128], ident_f[0:16, 0:16], ["lnx", "ident_f"], pk)
                ACT(lambda e, ps_=ps_, c=c: e.copy(out=h0[l][:, c, :], in_=ps_[:, 0:16]), pk + [hk], [hk])
        for c, Gd, gk in ((0, Gr, "Gr"), (1, Gi, "Gi")):
            ps_, pk = fpair()
            def fnc(e, ps_=ps_, c=c):
                for i in range(16):
                    ins = e.matmul(ps_[:, i // 8, (i % 8) * 64:(i % 8) * 64 + C], lhsT=wtok[o:o + C, i, c, :], rhs=tri2[o:o + C, 0:C],
                                   start=True, stop=True)
                return ins
            PE(fnc, ["wtok", "tri2"], pk)
            DVE(lambda e, ps_=ps_, Gd=Gd, c=c: e.tensor_tensor(
                out=Gd[:, :, 0:C].rearrange("p (b i) t -> p b i t", b=2),
                in0=ps_[:, :, :].rearrange("p b (i t) -> p b i t", t=64)[:, :, :, 0:C],
                in1=h0[l][:, c, :].rearrange("p (b i) -> p b i", b=2).unsqueeze(3).broadcast_to([128, 2, 8, C]), op=ALU.add),
                pk + [hk], [gk])
        er, ei = EpB[:, 0, :, 0:C], EpB[:, 1, :, 0:C]
        ek = ["EpB"]
        DVE(lambda e: e.tensor_tensor(out=hA[:, :, 0:C], in0=Gr[:, :, 0:C], in1=er, op=ALU.mult), ["Gr"] + ek, ["hA"])
        DVE(lambda e: e.tensor_tensor(out=hB[:, :, 0:C], in0=Gi[:, :, 0:C], in1=ei, op=ALU.mult), ["Gi"] + ek, ["hB"])
        DVE(lambda e: e.tensor_tensor(out=hre[:, :, 0:C], in0=hA[:, :, 0:C], in1=hB[:, :, 0:C], op=ALU.subtract), ["hA", "hB"], ["hre"])
        POOL(lambda e: e.tensor_tensor(out=hC[:, :, 0:C], in0=Gi[:, :, 0:C], in1=er, op=ALU.mult), ["Gi"] + ek, ["hC"])
        POOL(lambda e: e.tensor_tensor(out=hD[:, :, 0:C], in0=Gr[:, :, 0:C], in1=ei, op=ALU.mult), ["Gr"] + ek, ["hD"])
        DVE(lambda e: e.scalar_tensor_tensor(out=himn[:, :, 0:C], in0=hC[:, :, 0:C], scalar=-1.0, in1=hD[:, :, 0:C],
                                             op0=ALU.mult, op1=ALU.subtract), ["hC", "hD"], ["himn"])
        DVE(lambda e: e.tensor_tensor(out=h0[l][:, 0, :], in0=hA[:, :, C - 1], in1=hB[:, :, C - 1], op=ALU.subtract),
            ["hA", "hB", hk, "Gr", "Gi"], [hk])
        DVE(lambda e: e.tensor_tensor(out=h0[l][:, 1, :], in0=hC[:, :, C - 1], in1=hD[:, :, C - 1], op=ALU.add), ["hC", "hD", hk], [hk])
        if samp or (last and q == nq - 1):
            b_ = sq if samp else 0
            for c, dd in ((0, o_s5re), (1, o_s5im)):
                ps3, pk3 = fbank()
                TR(ps3[0:16, 0:128], h0[l][:, c, :], ident_f[:], [hk, "ident_f"], pk3)
                ACT(lambda e, ps3=ps3, c=c: e.copy(out=lnx[0:16, c * 128:(c + 1) * 128], in_=ps3[0:16, 0:128]), pk3 + ["lnx"], ["lnx"])
                STORE(dd[og][l, b_].rearrange("(i g) p -> i (g p)", g=2), lnx[0:16, c * 128:(c + 1) * 128], ["lnx"])
        ps_, pk = fbank()
        def fny(e, ps_=ps_):
            for ut in range(4):
                for hf in range(2):
                    out = ps_[hf * 64:(hf + 1) * 64, ut * 64:ut * 64 + C]
                    n_ = 0
                    for ii in range(2):
                        i = ut * 4 + hf * 2 + ii
                        for c, hsrc in ((0, hre), (1, himn)):
                            ins = e.matmul(out, lhsT=cpadB[:, i, c, :], rhs=hsrc[:, i, 0:C], start=(n_ == 0), stop=(n_ == 3))
                            n_ += 1
            return ins
        PE(fny, ["cpadB", "hre", "himn"], pk)
        for ut in range(4):
            DVE(lambda e, ut=ut, ps_=ps_: e.scalar_tensor_tensor(out=yv[:, ut, 0:C], in0=u32[:, ut, sl], scalar=PPc(l, "s5d", ut),
                                                                 in1=ps_[:, ut * 64:ut * 64 + C], op0=ALU.mult, op1=ALU.add),
                pk + ["u32", "pp%d" % l, "yv"], ["yv"])
        POOL(lambda e: e.tensor_tensor(out=gt[:, :, 0:C], in0=yv[:, :, 0:C], in1=yv[:, :, 0:C], op=ALU.mult), ["yv"], ["gt"])
        DVE(lambda e: e.tensor_scalar(out=gt[:, :, 0:C], in0=gt[:, :, 0:C], scalar1=0.044715, scalar2=1.0, op0=ALU.mult, op1=ALU.add),
            ["gt"], ["gt"])
        DVE(lambda e: e.tensor_tensor(out=gt[:, :, 0:C], in0=gt[:, :, 0:C], in1=yv[:, :, 0:C], op=ALU.mult), ["gt", "yv"], ["gt"])
        ACT(lambda e: e.activation(out=gsg[:, :, 0:C], in_=gt[:, :, 0:C], func=AF.Sigmoid, scale=1.5957691216057308), ["gt"], ["gsg"])
        DVE(lambda e: e.tensor_tensor(out=gl[:, :, 0:C], in0=yv[:, :, 0:C], in1=gsg[:, :, 0:C], op=ALU.mult), ["yv", "gsg"], ["gl"])
        ACT(lambda e: e.copy(out=glb[:, :, 0:C], in_=gl[:, :, 0:C]), ["gl"], ["glb"])
        ps2, pk2 = fbank()
        def fng(e):
            for ct in range(4):
                for kc in range(4):
                    ins = e.matmul(ps2[:, ct * 64:ct * 64 + C], lhsT=wglu[l][:, kc, ct * 128:(ct + 1) * 128], rhs=glb[:, kc, 0:C],
                                   start=(kc == 0), stop=(kc == 3))
            return ins
        PE(fng, ["wglu%d" % l, "glb"], pk2)
        for ct in range(4):
            ACT(lambda e, ct=ct: e.activation(out=sgl[:, ct, 0:C], in_=ps2[:, ct * 64:ct * 64 + C], func=AF.Sigmoid,
                                              bias=PPc(l, "bglu", ct), scale=1.0), pk2 + ["pp%d" % l, "sgl"], ["sgl"])
        POOL(lambda e: e.tensor_tensor(out=gl[:, :, 0:C], in0=gl[:, :, 0:C], in1=sgl[:, :, 0:C], op=ALU.mult), ["gl", "sgl"], ["gl"])
        DVE(lambda e: e.tensor_tensor(out=ys5T[:, :, sl], in0=gl[:, :, 0:C], in1=gs5[:, :, sl], op=ALU.mult),
            ["gl", "gs5", "ys5T"], ["ys5T"])

    def ml_tile(l, q, o, C, sq, kind, og, last, nq):
        S.label = 'mlT'
        samp = kind == "sample"
        ck, mk_ = "cst%d" % l, "mst%d" % l
        sl = slice(o, o + C)
        if samp:
            for kt in range(2):
                LOAD(Cst[l][:, kt, :, 0:192], st_c[l, sq, :, kt * 96:(kt + 1) * 96, :].rearrange("h p v -> p h v"), [ck], [ck])
                LOAD(Cst[l][:, kt, :, 192:193], st_n[l, sq, :, kt * 96:(kt + 1) * 96].rearrange("h (p o) -> p h o", o=1), [ck], [ck], slow=True)
            LOAD(mst[l][:], st_m[l, sq].rearrange("(h o) -> h o", o=1), [], [mk_], slow=True)
        ik = "iftok"
        ACT(lambda e: e.activation(out=lfi[0:C, 4:8], in_=iftok[0:C, q, 4:8], func=AF.Sigmoid), [ik], ["lfi"])
        ACT(lambda e: e.activation(out=lfi[0:C, 4:8], in_=lfi[0:C, 4:8], func=AF.Ln), ["lfi"], ["lfi"])
        ps_, pk = fbank()
        MM(ps_[0:C, 0:4], [(tri_f[0:C, 0:C], lfi[0:C, 4:8])], ["tri_f", "lfi"], pk)
        ps7, pk7 = fbank()
        MM(ps7[0:4, 0:1], [(lfi[0:C, 4:8], ones_f[0:C, 0:1])], ["ones_f", "lfi"], pk7)
        ACT(lambda e: e.copy(out=bcs[0:C, :], in_=ps_[0:C, 0:4]), pk, ["bcs"])
        ACT(lambda e: e.copy(out=bend[:], in_=ps7[0:4, 0:1]), pk7, ["bend"])
        DVE(lambda e: e.tensor_tensor(out=zz[0:C, :], in0=iftok[0:C, q, 0:4], in1=bcs[0:C, :], op=ALU.subtract), [ik, "bcs"], ["zz"])
        ps2, pk2 = fbank()
        TR(ps2[0:4, 0:C], zz[0:C, :], ident_f[0:C, 0:C], ["zz", "ident_f"], pk2)
        DVE(lambda e: e.tensor_reduce(out=zmax[:], in_=ps2[0:4, 0:C], axis=AX.X, op=ALU.max), pk2, ["zmax"])
        DVE(lambda e: e.tensor_tensor(out=mu4[:], in0=zmax[:], in1=mst[l][:], op=ALU.max), ["zmax", mk_], ["mu4"])
        DVE(lambda e: e.tensor_tensor(out=f4[:], in0=mst[l][:], in1=mu4[:], op=ALU.subtract), [mk_, "mu4"], ["f4"])
        ACT(lambda e: e.activation(out=f4[:], in_=f4[:], func=AF.Exp), ["f4"], ["f4"])
        DVE(lambda e: e.tensor_tensor(out=mst[l][:], in0=bend[:], in1=mu4[:], op=ALU.add), ["bend", "mu4", "f4", mk_], [mk_])
        DVE(lambda e: e.tensor_scalar(out=dg[:, 0:4], in0=ident_f[0:4, 0:4], scalar1=mu4[:, 0:1], scalar2=None, op0=ALU.mult),
            ["ident_f", "mu4"], ["dg"])
        DVE(lambda e: e.tensor_scalar(out=dg[:, 4:8], in0=ident_f[0:4, 0:4], scalar1=f4[:, 0:1], scalar2=None, op0=ALU.mult),
            ["ident_f", "f4", "dg"], ["dg"])
        ps3, pk3 = fbank()
        MM(ps3[:, 0:8], [(ones_f[0:4, :], dg[:, :])], ["ones_f", "dg"], pk3)
        ACT(lambda e: e.copy(out=bc8[:], in_=ps3[:, 0:8]), pk3, ["bc8"])
        DVE(lambda e: e.tensor_tensor(out=ee[0:C, :], in0=zz[0:C, :], in1=bc8[0:C, 0:4], op=ALU.subtract), ["zz", "bc8"], ["ee"])
        ACT(lambda e: e.activation(out=ee[0:C, :], in_=ee[0:C, :], func=AF.Exp), ["ee"], ["ee"])
        DVE(lambda e: e.tensor_tensor(out=clampt[0:C, :], in0=bcs[0:C, :], in1=bc8[0:C, 0:4], op=ALU.add), ["bcs", "bc8"], ["clampt"])
        ACT(lambda e: e.activation(out=clampt[0:C, :], in_=clampt[0:C, :], func=AF.Exp, scale=-1.0), ["clampt"], ["clampt"])
        for hd in range(4):
            DVE(lambda e, hd=hd: e.tensor_scalar(out=Cst[l][:, :, hd, :], in0=Cst[l][:, :, hd, :], scalar1=bc8[0:96, 4 + hd:5 + hd],
                                                 scalar2=None, op0=ALU.mult), [ck, "bc8"], [ck])
        ACT(lambda e: e.copy(out=Cstb[:], in_=Cst[l][:]), [ck], ["cstb"])
        ps4, pk4 = fbank()
        def fnsc(e):
            for hd in range(4):
                for kt in range(2):
                    ins = e.matmul(ps4[0:C, hd * 64:hd * 64 + C], lhsT=qkT[:, 8 + 2 * hd + kt, sl], rhs=qkT[:, 2 * hd + kt, sl],
                                   start=(kt == 0), stop=(kt == 1))
            return ins
        PE(fnsc, ["qkT"], pk4)
        p4 = ps4[0:C, 0:256].rearrange("p (h t) -> p h t", t=64)[:, :, 0:C]
        DVE(lambda e: e.tensor_tensor(out=PTs[0:C, :, 0:C], in0=p4, in1=m_incl[0:C, 0:4, 0:C], op=ALU.mult), pk4 + ["m_incl"], ["PTs"])
        DVE(lambda e: e.tensor_tensor(out=PTb[0:C, :, 0:C], in0=PTs[0:C, :, 0:C], in1=ee[0:C, :].unsqueeze(2).broadcast_to([C, 4, C]),
                                      op=ALU.mult), ["PTs", "ee"], ["PTb"])
        ps5, pk5 = fpair()
        def fnnd(e):
            for hd in range(4):
                out = ps5[0:C, hd // 2, (hd % 2) * 193:(hd % 2) * 193 + 193]
                e.matmul(out, lhsT=PTb[0:C, hd, 0:C], rhs=vaug[0:C, q, hd, :], start=True, stop=False)
                e.matmul(out, lhsT=qkT[:, 2 * hd, sl], rhs=Cstb[:, 0, hd, :], start=False, stop=False)
                ins = e.matmul(out, lhsT=qkT[:, 2 * hd + 1, sl], rhs=Cstb[:, 1, hd, :], start=False, stop=True)
            return ins
        PE(fnnd, ["PTb", "vaug", "cstb", "qkT"], pk5)
        nd = ps5[0:C, :, 0:386].rearrange("p b (h d) -> p b h d", d=193)
        hv4 = hst[0:C, 0, :].rearrange("p (b h) -> p b h", b=2)
        ACT(lambda e: e.activation(out=hv4.unsqueeze(3), in_=nd[:, :, :, 192:193], func=AF.Abs), pk5, ["hst"])
        DVE(lambda e: e.tensor_tensor(out=hst[0:C, 0, :], in0=hst[0:C, 0, :], in1=clampt[0:C, :], op=ALU.max),
            ["hst", "clampt"], ["hst"])
        DVE(lambda e: e.reciprocal(out=hst[0:C, 0, :], in_=hst[0:C, 0, :]), ["hst"], ["hst"])
        DVE(lambda e: e.tensor_tensor(out=hh[0:C].rearrange("p (b h) d -> p b h d", b=2), in0=nd[:, :, :, 0:192],
                                      in1=hv4.unsqueeze(3).broadcast_to([C, 2, 2, 192]), op=ALU.mult), pk5 + ["hst"], ["hh"])
        DVE(lambda e: e.tensor_reduce(out=hst[0:C, 1, :], in_=hh[0:C], axis=AX.X, op=ALU.add), ["hh", "hst"], ["hst"])
        DVE(lambda e: e.tensor_scalar(out=hst[0:C, 1, :], in0=hst[0:C, 1, :], scalar1=1.0 / 192, scalar2=None, op0=ALU.mult), ["hst"], ["hst"])
        DVE(lambda e: e.tensor_tensor(out=hcen[0:C], in0=hh[0:C], in1=hst[0:C, 1, :].unsqueeze(2).broadcast_to([C, 4, 192]),
                                      op=ALU.subtract), ["hh", "hst"], ["hcen"])
        POOL(lambda e: e.tensor_tensor(out=hsq[0:C], in0=hcen[0:C], in1=hcen[0:C], op=ALU.mult), ["hcen"], ["hsq"])
        DVE(lambda e: e.tensor_reduce(out=hst[0:C, 2, :], in_=hsq[0:C], axis=AX.X, op=ALU.add), ["hsq", "hst"], ["hst"])
        ACT(lambda e: e.activation(out=hst[0:C, 3, :], in_=hst[0:C, 2, :], func=AF.Sqrt, bias=LN_EPS, scale=1.0 / 192), ["hst"], ["hst"])
        DVE(lambda e: e.reciprocal(out=hst[0:C, 3, :], in_=hst[0:C, 3, :]), ["hst"], ["hst"])
        DVE(lambda e: e.tensor_tensor(out=hnb[0:C, :].rearrange("p (h d) -> p h d", d=192), in0=hcen[0:C],
                                      in1=hst[0:C, 3, :].unsqueeze(2).broadcast_to([C, 4, 192]), op=ALU.mult), ["hcen", "hst"], ["hnb"])
        pt, pk = bbank()
        for j in range(6):
            TR(pt[:, j * 64:j * 64 + C], hnb[0:C, j * 128:(j + 1) * 128], ident_b[0:C, 0:C], ["hnb", "ident_b"], pk)
        DVE(lambda e, pt=pt: e.tensor_tensor(out=ymlT[:, :, sl], in0=pt[:, 0:384].rearrange("p (j t) -> p j t", t=64)[:, :, 0:C],
                                             in1=ogT[:, :, sl], op=ALU.mult), pk + ["ogT", "ymlT"], ["ymlT"])
        pt2, pk2_ = bbank()
        for t in range(8):
            TR(pt2[0:C, t * 96:(t + 1) * 96], qkT[:, 8 + t, sl], ident_b[0:96, 0:96], ["qkT", "ident_b"], pk2_)
        DVE(lambda e, pt2=pt2: e.tensor_tensor(out=khm[0:C], in0=pt2[0:C, 0:768].rearrange("p (h d) -> p h d", d=192),
                                               in1=ee[0:C, :].unsqueeze(2).broadcast_to([C, 4, 192]), op=ALU.mult), pk2_ + ["ee"], ["khm"])
        for kt in range(2):
            ps6, pk6 = fpair()
            def fncu(e, ps6=ps6, kt=kt):
                for hd in range(4):
                    ins = e.matmul(ps6[0:96, hd // 2, (hd % 2) * 193:(hd % 2) * 193 + 193], lhsT=khm[0:C, hd, kt * 96:(kt + 1) * 96],
                                   rhs=vaug[0:C, q, hd, :], start=True, stop=True)
                return ins
            PE(fncu, ["khm", "vaug"], pk6)
            DVE(lambda e, ps6=ps6, kt=kt: e.tensor_tensor(out=Cst[l][:, kt, :, :].rearrange("p (b h) d -> p b h d", b=2),
                                                          in0=Cst[l][:, kt, :, :].rearrange("p (b h) d -> p b h d", b=2),
                                                          in1=ps6[0:96, :, 0:386].rearrange("p b (h d) -> p b h d", d=193), op=ALU.add),
                pk6 + [ck], [ck])
        if samp or (last and q == nq - 1):
            b_ = sq if samp else 0
            for kt in range(2):
                STORE(o_c[og][l, b_, :, kt * 96:(kt + 1) * 96, :].rearrange("h p v -> p h v"), Cst[l][:, kt, :, 0:192], [ck])
                STORE(o_n[og][l, b_, :, kt * 96:(kt + 1) * 96].rearrange("h (p o) -> p h o", o=1), Cst[l][:, kt, :, 192:193], [ck], slow=True)
            STORE(o_m[og][l, b_].rearrange("(h o) -> h o", o=1), mst[l][:], [mk_], slow=True)

    def phase_c(kind, N, l, ydst):
        S.label = 'C'
        pl = "pp%d" % l
        for jg in range(2):
            for b in range(3):
                wbm, wkm = load_group(l, "mg%d%d" % (b, jg))
                wbb, wkb = load_group(l, "br%d%d" % (b, jg))
                for jj in range(4):
                    j = jg * 4 + jj
                    ps_, pk = proj(wbm, wkm, jj * 128, 128, N)
                    gi_ = (b + jj) % 2
                    ACT(lambda e, ps_=ps_, gi_=gi_, b=b, j=j: e.activation(out=gate[gi_][:, 0:N], in_=ps_, func=AF.Sigmoid,
                                                                           bias=PPc(l, "bmrg", b * 8 + j), scale=1.0),
                        pk + [pl], ["gate%d" % gi_])
                    ps2, pk2 = fbank()
                    nk = BR_KC[b]
                    MM(ps2[:, 0:N], [(wbb[:, kc, jj * 128:(jj + 1) * 128], BR_T[b][:, kc, 0:N]) for kc in range(nk)],
                       [wkb, BR_K[b]], pk2)
                    if b == 0:
                        DVE(lambda e, ps2=ps2, gi_=gi_, j=j: e.tensor_tensor(out=mrg[:, j, 0:N], in0=ps2[:, 0:N], in1=gate[gi_][:, 0:N],
                                                                             op=ALU.mult), pk2 + ["gate%d" % gi_, "mrg%d" % j], ["mrg%d" % j])
                    else:
                        ctmp = ctmp2[jj % 2]
                        DVE(lambda e, ps2=ps2, gi_=gi_, ctmp=ctmp: e.tensor_tensor(out=ctmp[:, 0:N], in0=ps2[:, 0:N], in1=gate[gi_][:, 0:N],
                                                                        op=ALU.mult), pk2 + ["gate%d" % gi_], ["ctmp%d" % (jj % 2)])
                        if b == 1:
                            POOL(lambda e, j=j, ctmp=ctmp: e.tensor_tensor(out=mrg[:, j, 0:N], in0=mrg[:, j, 0:N], in1=ctmp[:, 0:N], op=ALU.add),
                                 ["mrg%d" % j, "ctmp%d" % (jj % 2)], ["mrg%d" % j])
                        else:
                            POOL(lambda e, j=j, ctmp=ctmp: e.tensor_tensor(out=mrgb[:, j, 0:N], in0=mrg[:, j, 0:N], in1=ctmp[:, 0:N], op=ALU.add),
                                 ["mrg%d" % j, "ctmp%d" % (jj % 2), "mrgb%d" % j], ["mrgb%d" % j])
        load_bc(1 + l)
        wo0, wok0 = load_group(l, "wo0")
        wo1, wok1 = load_group(l, "wo1")
        rows = N
        ps_, pk = fpair()
        MM(ps_[0:rows, 0, :], [(mrgb[:, kc, 0:rows], wo0[:, kc, 0:512]) for kc in range(8)], [wok0] + ["mrgb%d" % j_ for j_ in range(8)], [pk[0]])
        MM(ps_[0:rows, 1, :], [(mrgb[:, kc, 0:rows], wo1[:, kc, 0:512]) for kc in range(8)], [wok1] + ["mrgb%d" % j_ for j_ in range(8)], [pk[1]])
        DVE(lambda e, ps_=ps_: e.scalar_tensor_tensor(
            out=lnx[0:rows, :].rearrange("p (b n) -> p b n", b=2), in0=x_tok[0:rows, 0, :].rearrange("p (b n) -> p b n", b=2),
            scalar=DN_ALPHA, in1=ps_[0:rows, :, :], op0=ALU.mult, op1=ALU.add), pk + ["x_tok", "lnx"], ["lnx"])
        ln_block(lnx, ["lnx"], rows, out_dram=ydst)

    pass
    if stage >= 1000:
        S.limit = stage - 1000
        stage = 3
    try:
        if stage >= 1:
            run_pass("meta", 16, [(0, 16, 0)], meta, None, True, False)
        npp = SEQ // NT
        tl2 = [(0, 64, 0), (64, 64, 1)]
        for p in range(npp):
            if stage >= 2 and (stage >= 99 or p < stage - 1):
                run_pass("prompt", NT, tl2, xp[p * NT:(p + 1) * NT, :], yp[p * NT:(p + 1) * NT, :], False, p == npp - 1)
        for sp_ in range(2):
            if stage >= 99:
                run_pass("sample", NT, [(0, 64, 2 * sp_), (64, 64, 2 * sp_ + 1)], xs[sp_ * NT:(sp_ + 1) * NT, :],
                         ys[sp_ * NT:(sp_ + 1) * NT, :], False, False)


    except StopBuild:
        pass
    pass
    S.emit(final_wait_ops=final_ops)
    es.close()
    return nc


_NC = None


def _get_nc():
    global _NC
    if _NC is None:
        _NC = build()
    return _NC


def _host_inputs(inp, c):
    f = lambda a: np.ascontiguousarray(a, dtype=np.float32)
    p = c % 4
    sl = slice(4 * c, 4 * c + 4)
    m = {}
    m["xp"] = f(inp["x_prompt"][p])
    m["xs"] = f(inp["x_sample"][sl].reshape(NSAMP * 64, D))
    m["meta"] = f(inp["meta"])
    m["st_shift"] = f(inp["state_rwkv_shift"][:, sl])
    m["st_wkv"] = f(inp["state_rwkv_wkv"][:, sl])
    m["st_s5re"] = f(inp["state_s5_re"][:, sl])
    m["st_s5im"] = f(inp["state_s5_im"][:, sl])
    m["st_conv"] = f(inp["state_mlstm_conv"][:, sl])
    m["st_c"] = f(inp["state_mlstm_c"][:, sl])
    m["st_n"] = f(inp["state_mlstm_n"][:, sl])
    m["st_m"] = f(inp["state_mlstm_m"][:, sl])
    for k in ("w_in", "w_br_rw", "w_br_s5", "w_br_ml", "w_out", "s5_w_glu", "rw_w2", "rw_a2"):
        m[k] = f(inp[k])
    return m


def _shared_inputs(inp):
    f32 = np.float32
    pp = np.zeros((DEPTH, 128, NPP), f32)
    pq = np.zeros((DEPTH, 96, 80), f32)

    def cols(v, n):
        return np.asarray(v, f32).reshape(n, 128).T

    for l in range(DEPTH):
        def put(name, arr):
            o, w = PP[name]
            pp[l, :, o:o + w] = arr
        put("mu", cols(inp["rw_mu"][l], 19))
        put("w0", cols(inp["rw_w0"][l], 6))
        put("a0", cols(inp["rw_a0"][l], 6))
        put("kk", cols(inp["rw_kk"][l], 6))
        put("ka", cols(inp["rw_ka"][l], 6))
        put("rk", cols(np.asarray(inp["rw_rk"][l]).reshape(768), 6))
        put("s5d", cols(inp["s5_d"][l], 4))
        put("bglu", cols(inp["s5_b_glu"][l], 4))
        put("mlg", cols(inp["ml_ln_g"][l], 6))
        put("bmrg", cols(inp["b_merge"][l], 24))
        are = np.asarray(inp["s5_a_re"][l], f32).reshape(16, 2, 64).transpose(1, 2, 0).reshape(128, 16)
        aim = np.asarray(inp["s5_a_im"][l], f32).reshape(16, 2, 64).transpose(1, 2, 0).reshape(128, 16)
        ldt = np.repeat(np.asarray(inp["s5_log_dt"][l], f32).reshape(16, 2, 1), 64, axis=2).transpose(1, 2, 0).reshape(128, 16)
        put("are", are)
        put("aim", aim)
        put("ldt", ldt)
        cw = np.asarray(inp["ml_conv_w"][l], f32).reshape(4, 16, 96)
        cb = np.asarray(inp["ml_conv_b"][l], f32).reshape(16, 96)
        pqv = np.zeros((96, 16, 5), f32)
        pqv[:, :, 0:4] = cw.transpose(2, 1, 0)
        pqv[:, :, 4] = cb.T
        pq[l] = pqv.reshape(96, 80)
    bc = np.stack([inp["in_ln_g"], inp["in_ln_b"], inp["ln_g"][0], inp["ln_b"][0], inp["ln_g"][1], inp["ln_b"][1]]).astype(f32)
    rwln = np.stack([np.asarray(inp["rw_ln_g"], f32), np.asarray(inp["rw_ln_b"], f32)], axis=1)
    rwln = np.ascontiguousarray(rwln.reshape(DEPTH, 2, 6, 2, 64).transpose(0, 1, 3, 2, 4).reshape(DEPTH, 2, 2, 384))
    bif = np.asarray(inp["ml_b_if"], f32)
    bblk = np.zeros((DEPTH, 128, 4, 1024), f32)
    cpad = np.zeros((DEPTH, 128, 16, 2, 64), f32)
    for l in range(DEPTH):
        for c, (bk, ck) in enumerate((("s5_b_re", "s5_c_re"), ("s5_b_im", "s5_c_im"))):
            B = np.asarray(inp[bk][l], f32)
            Cm = np.asarray(inp[ck][l], f32)
            for g in range(32):
                i = g // 2
                ut = g // 8
                col0 = (i % 4) * 256 + c * 128 + (g % 2) * 64
                bblk[l, (g % 8) * 16:(g % 8) * 16 + 16, ut, col0:col0 + 64] = B[g].T
                oc = (i % 2) * 32 + (g % 2) * 16
                cpad[l, (g % 2) * 64:(g % 2) * 64 + 64, i, c, oc:oc + 16] = Cm[g].T
    return dict(pp=pp, pq=pq, bc=bc, rwln=rwln, bif=bif, bblk=bblk, cpad=cpad)


def kernel(**inp):
    inp = {k: np.asarray(v) for k, v in inp.items()}
    nc = _get_nc()
    shared = _shared_inputs(inp)
    in_maps = []
    for c in range(8):
        m = _host_inputs(inp, c)
        m.update(shared)
        in_maps.append(m)
    res = run_bass_kernel_spmd(nc, in_maps, core_ids=list(range(8)))
    R = res.results
    y_prompt = np.stack([R[c]["yp"] for c in range(4)], 0)
    y_sample = np.concatenate([R[c]["ys"].reshape(NSAMP, 64, D) for c in range(8)], 0)
    outs = [y_prompt, y_sample]
    for nm in ("shift", "wkv", "s5re", "s5im", "conv", "c", "n", "m"):
        outs.append(np.concatenate([R[c]["p_" + nm] for c in range(4)], 1))
    for nm in ("shift", "wkv", "s5re", "s5im", "conv", "c", "n", "m"):
        outs.append(np.concatenate([R[c]["s_" + nm] for c in range(8)], 1))
    return tuple(np.ascontiguousarray(o, dtype=np.float32) for o in outs)
```

```python
import contextlib
import math
import numpy as np
import concourse.bass as bass
import concourse.mybir as mybir
from concourse.bass_utils import run_bass_kernel_spmd

F32 = mybir.dt.float32
BF16 = mybir.dt.bfloat16
ALU = mybir.AluOpType
AF = mybir.ActivationFunctionType
AX = mybir.AxisListType

ENGS = ("pe", "act", "dve", "pool", "sp")
D = 1024
DEPTH = 2
NT = 128
SEQ = 4096
NMETA = 16
NSAMP = 4
RWW = 768
RWS = 2432
S5W = 512
MLW = 768
INC = 11144
DN_ALPHA = (2 * DEPTH) ** 0.25
LN_EPS = 1e-5
RW_GN_EPS = 64e-5
C_RW = 0
C_RWG = 2432
C_S5U = 3200
C_S5G = 3712
C_MLQK = 4224
C_MLV = 5760
C_MLIF = 6528
C_MLO = 6536
C_MLZ = 7304
C_MRG = 8072
EXPM05 = math.exp(-0.5)


class StopBuild(Exception):
    pass


class _FirstHook:
    def __init__(self, eng, wait):
        self._e = eng
        self._w = wait

    def _wrap(self, f):
        def g(*a, **k):
            ins = f(*a, **k)
            if self._w is not None:
                ins._wait_ge(*self._w)
                self._w = None
            return ins
        return g

    def __getattr__(self, name):
        v = getattr(self._e, name)
        if name in ("matmul", "transpose"):
            return self._wrap(v)
        return v


class Sched:
    limit = None
    resched = True
    def __init__(self, nc, n_dma_sems=12):
        self.nc = nc
        self.ops = []
        self.last_w = {}
        self.readers = {}
        self.n_dma_sems = n_dma_sems
        self.arena = {}

    def xl(self, keys):
        out = []
        for k in keys:
            r = self.arena.get(k)
            if r is None:
                r = self.arena.get(k.rstrip('0123456789'))
            if r is None:
                assert not ('_' in k and k.rsplit('_', 1)[1].isdigit() and k.rsplit('_', 1)[0] in self.arena), k
                out.append(k)
            else:
                out.extend("ar%d" % u for u in range(r[0] // 256, (r[1] + 255) // 256))
        return out

    def op(self, eng, fn, reads=(), writes=(), dma=False, single=False):
        if self.limit is not None and len(self.ops) >= self.limit:
            raise StopBuild()
        reads = self.xl(reads)
        writes = self.xl(writes)
        i = len(self.ops)
        deps = set()
        for k in reads:
            w = self.last_w.get(k)
            if w is not None:
                deps.add(w)
        for k in writes:
            w = self.last_w.get(k)
            if w is not None:
                deps.add(w)
            for r in self.readers.get(k, ()):
                deps.add(r)
        deps.discard(i)
        odeps = set()
        if eng == "pe" and not dma:
            odeps = {d for d in deps if (self.ops[d]["eng"] == "pe" and not self.ops[d]["dma"])}
            deps = deps - odeps
        self.ops.append(dict(eng=eng, fn=fn, deps=deps, odeps=odeps, dma=dma, used=False, single=single, label=getattr(self, 'label', ''),
                             dur=getattr(self, 'dur', None), tbl=getattr(self, 'tbl', None)))
        for d in deps:
            self.ops[d]["used"] = True
        for k in writes:
            self.last_w[k] = i
            self.readers[k] = []
        for k in reads:
            if k not in writes:
                self.readers.setdefault(k, []).append(i)
        return i

    def reschedule(self, final_wait_ops):
        ops = self.ops
        n = len(ops)
        DUR = {"pe": 0.35, "act": 0.3, "dve": 0.38, "pool": 0.65, "sp": 0.2}
        succ = [[] for _ in range(n)]
        indeg = [0] * n
        for i, o in enumerate(ops):
            for d in (o["deps"] | o["odeps"]):
                succ[d].append(i)
                indeg[i] += 1
        lastq = {}
        for i, o in enumerate(ops):
            if o["dma"]:
                q = o["eng"]
                if q in lastq:
                    succ[lastq[q]].append(i)
                    indeg[i] += 1
                lastq[q] = i
        fin = [0.0] * n
        ready_t = [0.0] * n
        cur = {e: 0.0 for e in ENGS}
        ready = {e: [] for e in ENGS}
        for i in range(n):
            if indeg[i] == 0:
                ready[ops[i]["eng"]].append(i)
        order = []
        acttbl = [None]
        done = 0
        while done < n:
            best = None
            for e in ENGS:
                lst = ready[e]
                if not lst:
                    continue
                bi_, bs_ = None, None
                for i in lst[:24]:
                    st = max(ready_t[i], cur[e])
                    if e == "act" and ops[i]["tbl"] is not None and ops[i]["tbl"] != acttbl[0]:
                        st += 1.3
                    key = (st, i)
                    if bs_ is None or key < bs_:
                        bi_, bs_ = i, key
                if best is None or bs_ < best[0]:
                    best = (bs_, bi_, e)
            (st, _), i, e = best
            ready[e].remove(i)
            o = ops[i]
            if o["dma"]:
                cur[e] = st + 0.1
                fin[i] = st + 2.5
            else:
                if e == "act" and o["tbl"] is not None:
                    acttbl[0] = o["tbl"]
                fin[i] = st + (o["dur"] or DUR[e])
                cur[e] = fin[i]
            order.append(i)
            done += 1
            for j in succ[i]:
                ready_t[j] = max(ready_t[j], fin[i] + 0.15)
                indeg[j] -= 1
                if indeg[j] == 0:
                    ready[ops[j]["eng"]].append(j)
        assert len(order) == n
        pos = {old: new for new, old in enumerate(order)}
        newops = []
        for old_i in order:
            o = ops[old_i]
            o["deps"] = {pos[d] for d in o["deps"]}
            o["odeps"] = {pos[d] for d in o["odeps"]}
            newops.append(o)
        self.ops = newops
        return [pos[i] for i in final_wait_ops]

    def emit(self, final_wait_ops=()):
        nc = self.nc
        if self.resched:
            final_wait_ops = self.reschedule(list(final_wait_ops))
        ops = self.ops
        for i in final_wait_ops:
            ops[i]["used"] = True
        cnt = {e: 0 for e in ENGS}
        dma_cnt = {}
        dma_rr = {e: 0 for e in ENGS}
        for o in ops:
            if o["dma"]:
                q = o["eng"]
                s = (q, dma_rr[q] % self.n_dma_sems)
                dma_rr[q] += 1
                dma_cnt[s] = dma_cnt.get(s, 0) + 1
                o["sig"] = ("dma", s, 16 * dma_cnt[s])
            elif o["used"]:
                cnt[o["eng"]] += 1
                o["sig"] = ("eng", o["eng"], cnt[o["eng"]])
            else:
                o["sig"] = None
        with contextlib.ExitStack() as st:
            esem = {e: st.enter_context(nc.semaphore("s_" + e)) for e in ENGS}
            dsem = {}
            for s in dma_cnt:
                dsem[s] = st.enter_context(nc.semaphore("d_%s_%d" % s))
            block = st.enter_context(nc.Block())

            def semof(sig):
                return esem[sig[1]] if sig[0] == "eng" else dsem[sig[1]]

            def run(e, engobj):
                seen = {}
                for o in ops:
                    if o["eng"] != e:
                        continue
                    need = {}
                    for d in o["deps"]:
                        sg = ops[d]["sig"]
                        key = (sg[0], sg[1])
                        need[key] = max(need.get(key, 0), sg[2])
                    if o["dma"]:
                        sg = o["sig"]
                        key = (sg[0], sg[1])
                        if sg[2] > 16:
                            need[key] = max(need.get(key, 0), sg[2] - 16)
                    waits = []
                    for key, v in need.items():
                        if seen.get(key, 0) < v:
                            waits.append((esem[key[1]] if key[0] == "eng" else dsem[key[1]], v))
                            seen[key] = v
                    emb = []
                    if not o["dma"] and (o["single"] or e == "pe"):
                        emb = waits[-1:]
                        waits = waits[:-1]
                    for sm, v in waits:
                        engobj.wait_ge(sm, v)
                    if emb and not o["single"]:
                        ins = o["fn"](_FirstHook(engobj, emb[0]))
                    else:
                        ins = o["fn"](engobj)
                        for sm, v in emb:
                            ins._wait_ge(sm, v)
                    sg = o["sig"]
                    if sg is not None:
                        ins.then_inc(semof(sg), 16 if sg[0] == "dma" else 1)
                if e == "sp":
                    for i in final_wait_ops:
                        sg = ops[i]["sig"]
                        engobj.wait_ge(semof(sg), sg[2])

            @block.tensor
            def _(eng):
                run("pe", eng)

            @block.scalar
            def _(eng):
                run("act", eng)

            @block.vector
            def _(eng):
                run("dve", eng)

            @block.gpsimd
            def _(eng):
                run("pool", eng)

            @block.sync
            def _(eng):
                run("sp", eng)


def w_groups():
    g = []
    g.append(("rwx", "w_in", 1024, 2304, 128))
    g.append(("rwk0", "w_in", 1024, 768, 512))
    g.append(("rwk1", "w_in", 1024, 1280, 256))
    g.append(("rwr0", "w_in", 1024, 0, 512))
    g.append(("rwr1", "w_in", 1024, 512, 256))
    g.append(("rwv0", "w_in", 1024, 1536, 512))
    g.append(("rwv1", "w_in", 1024, 2048, 256))
    g.append(("rwg0", "w_in", 1024, C_RWG, 512))
    g.append(("rwg1", "w_in", 1024, C_RWG + 512, 256))
    g.append(("s5u", "w_in", 1024, C_S5U, 512))
    g.append(("s5g", "w_in", 1024, C_S5G, 512))
    for i in range(4):
        g.append(("mlqk%d" % i, "w_in", 1024, C_MLQK + 384 * i, 384))
    g.append(("mlv0", "w_in", 1024, C_MLV, 512))
    g.append(("mlv1", "w_in", 1024, C_MLV + 512, 264))
    g.append(("mlo0", "w_in", 1024, C_MLO, 512))
    g.append(("mlo1", "w_in", 1024, C_MLO + 512, 256))
    g.append(("mlz0", "w_in", 1024, C_MLZ, 512))
    g.append(("mlz1", "w_in", 1024, C_MLZ + 512, 256))
    for jg in range(2):
        for b, (nm, k) in enumerate((("w_br_rw", 768), ("w_br_s5", 512), ("w_br_ml", 768))):
            g.append(("mg%d%d" % (b, jg), "w_in", 1024, C_MRG + b * 1024 + jg * 512, 512))
            g.append(("br%d%d" % (b, jg), nm, k, jg * 512, 512))
    for jg in range(2):
        g.append(("wo%d" % jg, "w_out", 1024, jg * 512, 512))
    return g


GROUPS = w_groups()
GIDX = {g[0]: i for i, g in enumerate(GROUPS)}

PP = {}
_o = 0
for _n, _w in (("mu", 19), ("w0", 6), ("a0", 6), ("kk", 6), ("ka", 6), ("rk", 6), ("s5d", 4), ("bglu", 4),
               ("mlg", 6), ("bmrg", 24), ("are", 16), ("aim", 16), ("ldt", 16)):
    PP[_n] = (_o, _w)
    _o += _w
NPP = _o


def build(stage=99):
    nc = bass.Bass("TRN2", target_bir_lowering=False)
    es = contextlib.ExitStack()
    S = Sched(nc)

    def din(name, shape):
        return nc.dram_tensor(name, list(shape), F32, kind="ExternalInput").ap()

    def dout(name, shape):
        return nc.dram_tensor(name, list(shape), F32, kind="ExternalOutput").ap()

    def dscr(name, shape, dt):
        return nc.dram_tensor(name, list(shape), dt, kind="Internal").ap()

    xp = din("xp", [SEQ, D])
    xs = din("xs", [NSAMP * 64, D])
    meta = din("meta", [NMETA, D])
    st_shift = din("st_shift", [DEPTH, NSAMP, RWS])
    st_wkv = din("st_wkv", [DEPTH, NSAMP, 12, 64, 64])
    st_s5re = din("st_s5re", [DEPTH, NSAMP, 32, 64])
    st_s5im = din("st_s5im", [DEPTH, NSAMP, 32, 64])
    st_conv = din("st_conv", [DEPTH, NSAMP, 3, 1536])
    st_c = din("st_c", [DEPTH, NSAMP, 4, 192, 192])
    st_n = din("st_n", [DEPTH, NSAMP, 4, 192])
    st_m = din("st_m", [DEPTH, NSAMP, 4])
    w_in = din("w_in", [DEPTH, D, INC])
    wsrc = dict(w_in=w_in, w_br_rw=din("w_br_rw", [DEPTH, 768, D]), w_br_s5=din("w_br_s5", [DEPTH, 512, D]),
                w_br_ml=din("w_br_ml", [DEPTH, 768, D]), w_out=din("w_out", [DEPTH, D, D]))
    w_glu = din("s5_w_glu", [DEPTH, 512, 512])
    rw_w2 = din("rw_w2", [DEPTH, 64, 768])
    rw_a2 = din("rw_a2", [DEPTH, 64, 768])
    pp_d = din("pp", [DEPTH, 128, NPP])
    pq_d = din("pq", [DEPTH, 96, 16 * 5])
    bc_d = din("bc", [6, D])
    rwln_d = din("rwln", [DEPTH, 2, 2, 384])
    bif_d = din("bif", [DEPTH, 8])
    bblk_d = din("bblk", [DEPTH, 128, 4, 1024])
    cpad_d = din("cpad", [DEPTH, 128, 16, 2, 64])

    yp = dout("yp", [SEQ, D])
    ys = dout("ys", [NSAMP * 64, D])
    o_shift = {"p": dout("p_shift", [DEPTH, 1, RWS]), "s": dout("s_shift", [DEPTH, NSAMP, RWS])}
    o_wkv = {"p": dout("p_wkv", [DEPTH, 1, 12, 64, 64]), "s": dout("s_wkv", [DEPTH, NSAMP, 12, 64, 64])}
    o_s5re = {"p": dout("p_s5re", [DEPTH, 1, 32, 64]), "s": dout("s_s5re", [DEPTH, NSAMP, 32, 64])}
    o_s5im = {"p": dout("p_s5im", [DEPTH, 1, 32, 64]), "s": dout("s_s5im", [DEPTH, NSAMP, 32, 64])}
    o_conv = {"p": dout("p_conv", [DEPTH, 1, 3, 1536]), "s": dout("s_conv", [DEPTH, NSAMP, 3, 1536])}
    o_c = {"p": dout("p_c", [DEPTH, 1, 4, 192, 192]), "s": dout("s_c", [DEPTH, NSAMP, 4, 192, 192])}
    o_n = {"p": dout("p_n", [DEPTH, 1, 4, 192]), "s": dout("s_n", [DEPTH, NSAMP, 4, 192])}
    o_m = {"p": dout("p_m", [DEPTH, 1, 4]), "s": dout("s_m", [DEPTH, NSAMP, 4])}

    wq = [[dscr("wq%d_%d" % (l, i), [128, g[2] // 128, g[4]], BF16) for i, g in enumerate(GROUPS)]
          for l in range(DEPTH)]

    def sb(name, shape, dt=F32):
        return es.enter_context(nc.sbuf_tensor(name, list(shape), dt))

    def psum(name, shape, dt):
        return es.enter_context(nc.psum_tensor(name, list(shape), dt))

    final_ops = []

    def ACT(fn, r, w):
        tbl = None
        names = fn.__code__.co_names
        for nm_, t_ in (("Sigmoid", "sig"), ("Exp", "exp"), ("Ln", "ln"), ("Sqrt", "sqrt"), ("Silu", "silu"), ("Sin", "silu")):
            if nm_ in names:
                tbl = t_
        S.tbl = tbl
        i_ = S.op("act", fn, r, w, single=True)
        S.tbl = None
        return i_

    def DVE(fn, r, w):
        return S.op("dve", fn, r, w, single=True)

    def POOL(fn, r, w):
        return S.op("pool", fn, r, w, single=True)

    class _Cnt:
        def __init__(self):
            self.n = 0

        def matmul(self, *a, **k):
            self.n += 1
            return self

        transpose = matmul

    def PE(fn, r, w):
        own = S.dur is None
        if own:
            c_ = _Cnt()
            fn(c_)
            S.dur = 0.1 + 0.075 * c_.n
        i_ = S.op("pe", fn, r, w)
        if own:
            S.dur = None
        return i_

    def LOAD(out, in_, r, w, slow=False):
        return S.op("sp", lambda e: e.dma_start(out=out, in_=in_, allow_slow_non_contiguous=slow), r, w, dma=True)

    def STORE(out, in_, r, slow=False):
        i = S.op("pool", lambda e: e.dma_start(out=out, in_=in_, allow_slow_non_contiguous=slow), r, (), dma=True)
        final_ops.append(i)
        return i

    def MM(out, pairs, r, w):
        S.dur = 0.1 + 0.07 * len(pairs)
        def fn(e):
            n = len(pairs)
            for i, (l_, r_) in enumerate(pairs):
                ins = e.matmul(out, lhsT=l_, rhs=r_, start=(i == 0), stop=(i == n - 1))
            return ins
        i_ = PE(fn, r, w)
        S.dur = None
        return i_

    def TR(out, in_, ident, r, w):
        return S.op("pe", lambda e: e.transpose(out=out, in_=in_, identity=ident), r, w, single=True)

    PSF = [psum("psf%d" % i, [128, 2, 512], F32) for i in range(3)]
    PSB = psum("psb", [128, 2, 1024], BF16)
    rr = {"b": 0, "p": 0, "t": 0}

    def fbank():
        i = rr["b"] % 6
        rr["b"] += 1
        return PSF[i // 2][:, i % 2, :], ["pf%d" % i]

    def fpair():
        i = rr["p"] % 3
        rr["p"] += 1
        return PSF[i], ["pf%d" % (2 * i), "pf%d" % (2 * i + 1)]

    def bbank():
        i = rr["t"] % 2
        rr["t"] += 1
        return PSB[:, i, :], ["pb%d" % i]

    ident_b = sb("ident_b", [128, 128], BF16)
    ident_f = sb("ident_f", [128, 128], F32)
    m_strict = sb("m_strict", [128, 6, 64], F32)
    m_incl = sb("m_incl", [128, 6, 64], F32)
    m_lower = sb("m_lower", [128, 6, 64], F32)
    tri_b = sb("tri_b", [64, 64], BF16)
    tri2 = sb("tri2", [128, 64], BF16)
    tri_f = sb("tri_f", [64, 64], F32)
    blk1 = sb("blk1", [128, 128], BF16)
    bsel = sb("bsel", [128, 2], BF16)
    ones_f = sb("ones_f", [128, 128], F32)
    onesb = sb("onesb", [128, 1], BF16)
    scanm = sb("scanm", [128, NT], F32)

    def mk_sel(t, pattern, base, cm, op, key):
        POOL(lambda e: e.memset(t, 1.0), [], [key])
        POOL(lambda e: e.affine_select(out=t, in_=t, pattern=pattern, compare_op=op, fill=0.0, base=base,
                                       channel_multiplier=cm), [key], [key])

    mk_sel(ident_f[:], [[-1, 128]], 0, 1, ALU.is_equal, "ident_f")
    POOL(lambda e: e.tensor_copy(out=ident_b[:], in_=ident_f[:]), ["ident_f"], ["ident_b"])
    for hf_ in range(2):
        ps_ = slice(hf_ * 64, hf_ * 64 + 64)
        mk_sel(m_strict[ps_], [[0, 6], [1, 64]], -1, -1, ALU.is_ge, "m_strict")
        mk_sel(m_incl[ps_], [[0, 6], [1, 64]], 0, -1, ALU.is_ge, "m_incl")
        mk_sel(m_lower[ps_], [[0, 6], [-1, 64]], -1, 1, ALU.is_ge, "m_lower")
    mk_sel(tri_f[:], [[1, 64]], 0, -1, ALU.is_ge, "tri_f")
    POOL(lambda e: e.tensor_copy(out=tri_b[:], in_=tri_f[:]), ["tri_f"], ["tri_b"])
    POOL(lambda e: e.tensor_copy(out=tri2[0:64, :], in_=tri_f[:]), ["tri_f"], ["tri2"])
    POOL(lambda e: e.tensor_copy(out=tri2[64:128, :], in_=m_incl[64:128, 0, :]), ["m_incl", "tri2"], ["tri2"])
    POOL(lambda e: e.memset(ones_f[:], 1.0), [], ["ones_f"])
    POOL(lambda e: e.memset(onesb[:], 1.0), [], ["onesb"])
    POOL(lambda e: e.memset(blk1[:], 0.0), [], ["blk1"])
    POOL(lambda e: e.memset(blk1[0:64, 0:64], 1.0), ["blk1"], ["blk1"])
    POOL(lambda e: e.memset(blk1[64:128, 64:128], 1.0), ["blk1"], ["blk1"])
    POOL(lambda e: e.memset(bsel[:], 0.0), [], ["bsel"])
    POOL(lambda e: e.memset(bsel[0:64, 0:1], 1.0), ["bsel"], ["bsel"])
    POOL(lambda e: e.memset(bsel[64:128, 1:2], 1.0), ["bsel"], ["bsel"])
    POOL(lambda e: e.memset(scanm[:], 1.0), [], ["scanm"])
    POOL(lambda e: e.memset(scanm[:].rearrange("p (c t) -> p c t", t=64)[:, :, 0:1], 0.0), ["scanm"], ["scanm"])

    if stage == -4:
        S.emit(final_wait_ops=final_ops); es.close(); return nc
    for l in range(DEPTH):
        for i, (nm, src, K, c0, wd) in enumerate(GROUPS):
            srcap = wsrc[src][l, :, c0:c0 + wd].rearrange("(kc p) c -> p kc c", p=128)
            S.op("pool", lambda e, o_=wq[l][i], s_=srcap: e.dma_start(out=o_, in_=s_), [], ["wq%d_%d" % (l, i)], dma=True)

    if stage == -3:
        S.emit(final_wait_ops=final_ops); es.close(); return nc
    ARW = 15360
    arena_t = sb("arena", [128, ARW], F32)
    arp = {"o": 0}

    def ar_reset(o=0):
        arp["o"] = o

    def ar(name, shape, dt=F32):
        P = shape[0]
        n = int(np.prod(shape[1:]))
        words = n if dt == F32 else (n + 1) // 2
        words = (words + 15) // 16 * 16
        o = arp["o"]
        assert o + words <= ARW, (name, o, words)
        arp["o"] = o + words
        v = arena_t[0:P, o:o + words]
        if dt != F32:
            v = v.bitcast(BF16)
        v = v[:, 0:n]
        if len(shape) == 3:
            v = v.rearrange("p (a b) -> p a b", b=shape[2])
        elif len(shape) == 4:
            v = v.rearrange("p (a b c) -> p a b c", b=shape[2], c=shape[3])
        S.arena[name] = (o * 4, (o + words) * 4)
        if len(shape) >= 3:
            esz = 4 if dt == F32 else 2
            sub = int(np.prod(shape[2:])) * esz
            for a_ in range(shape[1]):
                S.arena["%s_%d" % (name, a_)] = (o * 4 + a_ * sub, o * 4 + (a_ + 1) * sub)
        return v

    pp = [sb("pp%d" % l, [128, NPP]) for l in range(DEPTH)]
    pq = [sb("pq%d" % l, [96, 16, 5]) for l in range(DEPTH)]
    omka = [sb("omka%d" % l, [128, 6]) for l in range(DEPTH)]
    lora = [sb("lora%d" % l, [128, 768], BF16) for l in range(DEPTH)]
    wglu = [sb("wglu%d" % l, [128, 4, 512], BF16) for l in range(DEPTH)]
    bcg = sb("bcg", [128, 2, D])
    rwln = [sb("rwln%d" % l, [128, 2, 384]) for l in range(DEPTH)]
    bif = [sb("bif%d" % l, [64, 8]) for l in range(DEPTH)]
    EpB = sb("EpB", [128, 2, 16, 64])
    EnB = sb("EnB", [128, 16, 2, 128], BF16)
    bblkB = sb("bblkB", [128, 4, 1024], BF16)
    cpadB = sb("cpadB", [128, 16, 2, 64], BF16)
    S5K = ["EpB", "EnB", "bblkB", "cpadB"]
    epd = [dscr("epd%d" % l, [128, 2, 16, 64], F32) for l in range(DEPTH)]
    end_ = [dscr("end%d" % l, [64, 16, 2, 128], BF16) for l in range(DEPTH)]
    bblkq = [dscr("bblkq%d" % l, [128, 4, 1024], BF16) for l in range(DEPTH)]
    cpadq = [dscr("cpadq%d" % l, [128, 16, 2, 64], BF16) for l in range(DEPTH)]
    for l in range(DEPTH):
        LOAD(pp[l][:], pp_d[l], [], ["pp%d" % l])
        LOAD(pq[l][:], pq_d[l].rearrange("p (t j) -> p t j", j=5), [], ["pq%d" % l])
        for a_ in range(2):
            for r_ in range(2):
                LOAD(rwln[l][r_ * 64:(r_ + 1) * 64, a_, :], rwln_d[l, a_, r_].partition_broadcast(64), ["rwln%d" % l], ["rwln%d" % l])
        LOAD(bif[l][:], bif_d[l].partition_broadcast(64), [], ["bif%d" % l])
        S.op("pool", lambda e, l=l: e.dma_start(out=lora[l][0:64, :], in_=rw_w2[l]), [], ["lora%d" % l], dma=True)
        S.op("pool", lambda e, l=l: e.dma_start(out=lora[l][64:128, :], in_=rw_a2[l]), [], ["lora%da" % l], dma=True)
        S.op("pool", lambda e, l=l: e.dma_start(out=wglu[l][:], in_=w_glu[l].rearrange("(kc p) c -> p kc c", p=128)),
             [], ["wglu%d" % l], dma=True)
        S.op("pool", lambda e, l=l: e.dma_start(out=bblkq[l], in_=bblk_d[l]), [], ["bblkq%d" % l], dma=True)
        S.op("pool", lambda e, l=l: e.dma_start(out=cpadq[l], in_=cpad_d[l]), [], ["cpadq%d" % l], dma=True)
        o_, w_ = PP["ka"]
        DVE(lambda e, l=l, o_=o_: e.tensor_scalar(out=omka[l][:], in0=pp[l][:, o_:o_ + 6], scalar1=-1.0, scalar2=1.0,
                                                  op0=ALU.mult, op1=ALU.add), ["pp%d" % l], ["omka%d" % l])

    if stage == -2:
        S.emit(final_wait_ops=final_ops); es.close(); return nc

    def load_bc(i):
        LOAD(bcg[:].rearrange("p a b -> p (a b)"), bc_d[2 * i:2 * i + 2, :].rearrange("a b -> (a b)").partition_broadcast(128),
             [], ["bcg"])

    def load_s5(l):
        LOAD(EpB[:], epd[l], ["epd%d" % l], ["EpB"])
        LOAD(EnB[0:64], end_[l], ["end%d" % l, "EnB"], ["EnB"])
        LOAD(EnB[64:128], end_[l], ["end%d" % l, "EnB"], ["EnB"])
        LOAD(bblkB[:], bblkq[l], ["bblkq%d" % l], ["bblkB"])
        LOAD(cpadB[:], cpadq[l], ["cpadq%d" % l], ["cpadB"])

    def PPc(l, name, j=None):
        o_, w_ = PP[name]
        if j is None:
            return pp[l][:, o_:o_ + w_]
        return pp[l][:, o_ + j:o_ + j + 1]

    ar_reset()
    zr = ar("zr", [128, 16]); zi = ar("zi", [128, 16]); dtt = ar("dtt", [128, 16])
    t1 = ar("t1", [128, 16]); t2 = ar("t2", [128, 16]); t3 = ar("t3", [128, 16]); mg = ar("mg", [128, 16])
    lr = ar("lr", [128, 16])
    Epf = ar("Epf", [128, 2, 16, 64])
    Enf = [ar("enf%d" % c, [128, 16, 64]) for c in range(2)]
    ta = ar("ta", [128, 16, 32]); tb_ = ar("tb", [128, 16, 32])
    cfr = ar("cfr", [128, 16]); cfi = ar("cfi", [128, 16]); den = ar("den", [128, 16])
    enc = [ar("enc%d" % c, [128, 16, 64]) for c in range(2)]
    Ent = ar("Ent", [64, 16, 2, 128])
    KT = ["zr", "zi", "dtt", "t1", "t2", "t3", "mg", "lr", "Epf", "enf0", "enf1", "ta", "tb", "cfr", "cfi", "den", "enc0", "enc1"]

    def cexp32(sign, outr, outi):
        ACT(lambda e: e.activation(out=mg[:], in_=zr[:], func=AF.Exp, scale=sign / 32.0), KT, KT)
        ACT(lambda e: e.activation(out=t1[:], in_=zi[:], func=AF.Sin, scale=sign / 32.0), KT, KT)
        ACT(lambda e: e.activation(out=t2[:], in_=zi[:], func=AF.Sin, scale=sign / 32.0, bias=hpi[:, 0:1]), KT + ["hpi"], KT)
        DVE(lambda e: e.tensor_tensor(out=outr, in0=mg[:], in1=t2[:], op=ALU.mult), KT, KT)
        DVE(lambda e: e.tensor_tensor(out=outi, in0=mg[:], in1=t1[:], op=ALU.mult), KT, KT)
        for _ in range(5):
            DVE(lambda e: e.tensor_tensor(out=t1[:], in0=outr, in1=outr, op=ALU.mult), KT, KT)
            DVE(lambda e: e.tensor_tensor(out=t2[:], in0=outi, in1=outi, op=ALU.mult), KT, KT)
            DVE(lambda e: e.tensor_tensor(out=t3[:], in0=outr, in1=outi, op=ALU.mult), KT, KT)
            DVE(lambda e: e.tensor_tensor(out=outr, in0=t1[:], in1=t2[:], op=ALU.subtract), KT, KT)
            DVE(lambda e: e.tensor_scalar(out=outi, in0=t3[:], scalar1=2.0, scalar2=None, op0=ALU.mult), KT, KT)

    def powers(tabr, tabi):
        ln_ = 1
        while ln_ < 64:
            Lr = tabr[:, :, ln_ - 1:ln_].broadcast_to([128, 16, ln_])
            Li = tabi[:, :, ln_ - 1:ln_].broadcast_to([128, 16, ln_])
            a_ = ta[:, :, 0:ln_]
            b_ = tb_[:, :, 0:ln_]
            sr = tabr[:, :, 0:ln_]
            si = tabi[:, :, 0:ln_]
            dr = tabr[:, :, ln_:2 * ln_]
            di = tabi[:, :, ln_:2 * ln_]
            DVE(lambda e, a_=a_, sr=sr, Lr=Lr: e.tensor_tensor(out=a_, in0=sr, in1=Lr, op=ALU.mult), KT, KT)
            DVE(lambda e, b_=b_, si=si, Li=Li: e.tensor_tensor(out=b_, in0=si, in1=Li, op=ALU.mult), KT, KT)
            DVE(lambda e, a_=a_, b_=b_, dr=dr: e.tensor_tensor(out=dr, in0=a_, in1=b_, op=ALU.subtract), KT, KT)
            DVE(lambda e, a_=a_, sr=sr, Li=Li: e.tensor_tensor(out=a_, in0=sr, in1=Li, op=ALU.mult), KT, KT)
            DVE(lambda e, b_=b_, si=si, Lr=Lr: e.tensor_tensor(out=b_, in0=si, in1=Lr, op=ALU.mult), KT, KT)
            DVE(lambda e, a_=a_, b_=b_, di=di: e.tensor_tensor(out=di, in0=a_, in1=b_, op=ALU.add), KT, KT)
            ln_ *= 2

    hpi = sb("hpi", [128, 1])
    POOL(lambda e: e.memset(hpi[:], math.pi / 2), [], ["hpi"])
    for l in range(DEPTH):
        KP = ["pp%d" % l]
        ACT(lambda e, l=l: e.activation(out=dtt[:], in_=PPc(l, "ldt"), func=AF.Exp), KT + KP, KT)
        DVE(lambda e, l=l: e.tensor_tensor(out=zr[:], in0=PPc(l, "are"), in1=dtt[:], op=ALU.mult), KT + KP, KT)
        DVE(lambda e, l=l: e.tensor_tensor(out=zi[:], in0=PPc(l, "aim"), in1=dtt[:], op=ALU.mult), KT + KP, KT)

        if stage == -10:
            S.emit(final_wait_ops=final_ops); es.close(); return nc
        cexp32(1.0, Epf[:, 0, :, 0], Epf[:, 1, :, 0])

        if stage == -9:
            S.emit(final_wait_ops=final_ops); es.close(); return nc
        powers(Epf[:, 0], Epf[:, 1])

        if stage == -8:
            S.emit(final_wait_ops=final_ops); es.close(); return nc
        cexp32(-1.0, Enf[0][:, :, 0], Enf[1][:, :, 0])
        powers(Enf[0], Enf[1])

        if stage == -7:
            S.emit(final_wait_ops=final_ops); es.close(); return nc
        DVE(lambda e: e.tensor_scalar(out=lr[:], in0=Epf[:, 0, :, 0], scalar1=-1.0, scalar2=None, op0=ALU.add), KT, KT)
        DVE(lambda e, l=l: e.tensor_tensor(out=t1[:], in0=PPc(l, "are"), in1=PPc(l, "are"), op=ALU.mult), KT + KP, KT)
        DVE(lambda e, l=l: e.tensor_tensor(out=t2[:], in0=PPc(l, "aim"), in1=PPc(l, "aim"), op=ALU.mult), KT + KP, KT)
        DVE(lambda e: e.tensor_tensor(out=den[:], in0=t1[:], in1=t2[:], op=ALU.add), KT, KT)
        DVE(lambda e: e.reciprocal(out=den[:], in_=den[:]), KT, KT)
        DVE(lambda e, l=l: e.tensor_tensor(out=t1[:], in0=lr[:], in1=PPc(l, "are"), op=ALU.mult), KT + KP, KT)
        DVE(lambda e, l=l: e.tensor_tensor(out=t2[:], in0=Epf[:, 1, :, 0], in1=PPc(l, "aim"), op=ALU.mult), KT + KP, KT)
        DVE(lambda e: e.tensor_tensor(out=cfr[:], in0=t1[:], in1=t2[:], op=ALU.add), KT, KT)
        DVE(lambda e: e.tensor_tensor(out=cfr[:], in0=cfr[:], in1=den[:], op=ALU.mult), KT, KT)
        DVE(lambda e, l=l: e.tensor_tensor(out=t1[:], in0=Epf[:, 1, :, 0], in1=PPc(l, "are"), op=ALU.mult), KT + KP, KT)
        DVE(lambda e, l=l: e.tensor_tensor(out=t2[:], in0=lr[:], in1=PPc(l, "aim"), op=ALU.mult), KT + KP, KT)
        DVE(lambda e: e.tensor_tensor(out=cfi[:], in0=t1[:], in1=t2[:], op=ALU.subtract), KT, KT)
        DVE(lambda e: e.tensor_tensor(out=cfi[:], in0=cfi[:], in1=den[:], op=ALU.mult), KT, KT)
        CR = cfr[:].unsqueeze(2).broadcast_to([128, 16, 64])
        CI = cfi[:].unsqueeze(2).broadcast_to([128, 16, 64])
        DVE(lambda e, CR=CR: e.tensor_tensor(out=enc[0][:], in0=Enf[0][:], in1=CR, op=ALU.mult), KT, KT)
        DVE(lambda e, CI=CI: e.tensor_tensor(out=enc[1][:], in0=Enf[1][:], in1=CI, op=ALU.mult), KT, KT)
        DVE(lambda e: e.tensor_tensor(out=enc[0][:], in0=enc[0][:], in1=enc[1][:], op=ALU.subtract), KT, KT)
        DVE(lambda e, CI=CI: e.tensor_tensor(out=enc[1][:], in0=Enf[0][:], in1=CI, op=ALU.mult), KT, KT)
        DVE(lambda e, CR=CR: e.tensor_tensor(out=Enf[0][:], in0=Enf[1][:], in1=CR, op=ALU.mult), KT, KT)
        DVE(lambda e: e.tensor_tensor(out=enc[1][:], in0=enc[1][:], in1=Enf[0][:], op=ALU.add), KT, KT)

        if stage == -6:
            S.emit(final_wait_ops=final_ops); es.close(); return nc
        for c in range(2):
            for i in range(16):
                ps_, pk = fbank()
                TR(ps_[0:64, 0:128], enc[c][:, i, :], ident_f[:], KT + ["ident_f"], pk)
                ACT(lambda e, ps_=ps_, i=i, c=c: e.copy(out=Ent[:, i, c, :], in_=ps_[0:64, 0:128]), pk, ["Ent"])

        if stage == -5:
            S.emit(final_wait_ops=final_ops); es.close(); return nc
        S.op("pool", lambda e, l=l: e.dma_start(out=epd[l], in_=Epf), KT, ["epd%d" % l], dma=True)
        S.op("pool", lambda e, l=l: e.dma_start(out=end_[l], in_=Ent), ["Ent"], ["end%d" % l], dma=True)

    x_tok = sb("x_tok", [128, 1, D])
    xT = sb("xT", [128, 8, NT], BF16)
    wbuf = [sb("wbuf%d" % i, [128, 8, 512], BF16) for i in range(4)]
    wr = {"i": 0}

    def load_group(l, name):
        gi = GIDX[name]
        g = GROUPS[gi]
        bi = wr["i"] % 4
        wr["i"] += 1
        kc = g[2] // 128
        LOAD(wbuf[bi][:, 0:kc, 0:g[4]], wq[l][gi], ["wq%d_%d" % (l, gi)], ["wbuf%d" % bi])
        return wbuf[bi], "wbuf%d" % bi

    shiftst = [sb("shiftst%d" % l, [128, 19]) for l in range(DEPTH)]
    sshift = sb("sshift", [128, 19, 2])
    sshift_o = sb("sshift_o", [128, 19, 2])
    S0T = [sb("s0t%d" % l, [128, 6, 64]) for l in range(DEPTH)]
    S0Tb = sb("s0tb", [128, 6, 64], BF16)
    h0 = [sb("h0%d" % l, [128, 2, 16]) for l in range(DEPTH)]
    convst = [sb("convst%d" % l, [96, 16, 3]) for l in range(DEPTH)]
    sconv = sb("sconv", [96, 16, 2, 3])
    Cst = [sb("cst%d" % l, [96, 2, 4, 193]) for l in range(DEPTH)]
    Cstb = sb("cstb", [96, 2, 4, 193], BF16)
    mst = [sb("mst%d" % l, [4, 1]) for l in range(DEPTH)]
    stg = sb("stg", [64, 12, 64])

    yrwT = sb("yrwT", [128, 6, NT], BF16)
    ys5T = sb("ys5T", [128, 4, NT], BF16)
    ymlT = sb("ymlT", [128, 6, NT], BF16)
    gate = [sb("gate%d" % i, [128, NT], BF16) for i in range(2)]
    mrg = sb("mrg", [128, 8, NT]); mrgb = sb("mrgb", [128, 8, NT], BF16)
    lnt = sb("lnt", [128, D]); lnb = sb("lnb", [128, D], BF16); lnx = sb("lnx", [128, D]); ctmp2 = [sb("ctmp%d" % i, [128, NT]) for i in range(2)]
    lst = sb("lst", [128, 2, 6]); lmv = sb("lmv", [128, 2]); lrs = sb("lrs", [128, 1])
    gC = sb("gC", [128, 6, 2])

    ar_reset()
    U = [ar("U%d" % i, [128, NT + 1]) for i in range(2)]
    dtmp = [ar("dtmp%d" % i, [128, NT]) for i in range(2)]
    xs18 = ar("xs18", [128, NT]); txw = ar("txw", [128, NT], BF16)
    ldec_2 = [ar("ldec%d" % i_, [128, NT]) for i_ in range(2)]; Gc_2 = [ar("Gc%d" % i_, [128, NT]) for i_ in range(2)]; aa_2 = [ar("aa%d" % i_, [128, NT]) for i_ in range(2)]
    eneg_2 = [ar("eneg%d" % i_, [128, NT]) for i_ in range(2)]; eprev_2 = [ar("eprev%d" % i_, [128, NT]) for i_ in range(2)]; ehat_2 = [ar("ehat%d" % i_, [128, NT]) for i_ in range(2)]; epos_2 = [ar("epos%d" % i_, [128, NT]) for i_ in range(2)]
    kx_2 = [ar("kx%d" % i_, [128, NT]) for i_ in range(2)]; kkr_2 = [ar("kkr%d" % i_, [128, NT]) for i_ in range(2)]; kksq_2 = [ar("kksq%d" % i_, [128, NT], BF16) for i_ in range(2)]
    rn_2 = [ar("rn%d" % i_, [128, NT]) for i_ in range(2)]; kkn_2 = [ar("kkn%d" % i_, [128, NT]) for i_ in range(2)]; tk_2 = [ar("tk%d" % i_, [128, NT]) for i_ in range(2)]
    kmod_2 = [ar("kmod%d" % i_, [128, NT]) for i_ in range(2)]; bb_2 = [ar("bb%d" % i_, [128, NT]) for i_ in range(2)]; rx_2 = [ar("rx%d" % i_, [128, NT]) for i_ in range(2)]
    rkp = ar("rkp", [128, 6, NT], BF16)
    rt_ = ar("rt_", [128, 6, NT], BF16); kt_ = ar("kt_", [128, 6, NT], BF16); bt_ = ar("bt_", [128, 6, NT], BF16)
    at_ = ar("at_", [128, 6, NT], BF16); khat = ar("khat", [128, 6, NT], BF16); bhat = ar("bhat", [128, 6, NT], BF16)
    vT = ar("vT", [128, 6, NT], BF16); grw = ar("grw", [128, 6, NT], BF16)
    TB = []
    for pz in ("A", "B"):
        TB.append(dict(
            vtok=ar("vtok" + pz, [128, 6, 64], BF16), khtok=ar("khtok" + pz, [128, 6, 64], BF16), bhtok=ar("bhtok" + pz, [128, 6, 64], BF16),
            Nsb=[ar("Nsb%d%s" % (i, pz), [128, 6, 64], BF16) for i in range(2)],
            NTsb=[ar("NTsb%d%s" % (i, pz), [128, 6, 64], BF16) for i in range(2)],
            Msb=ar("Msb" + pz, [128, 6, 64], BF16), P1sb=ar("P1sb" + pz, [128, 6, 64], BF16), P2sb=ar("P2sb" + pz, [128, 6, 64], BF16), z=pz))
    Ysb = [ar("Ysb%d" % i, [128, 6, 64], BF16) for i in range(2)]
    yo = ar("yo", [128, 6, 64]); ycen = ar("ycen", [128, 6, 64]); ysq = ar("ysq", [128, 6, 64])
    ystat = ar("ystat", [128, 4, 6]); rkd = ar("rkd", [128, 6]); ytb = ar("ytb", [128, 6, 64], BF16)
    RW_END = arp["o"]
    ar_reset()
    uT = ar("uT", [128, 4, NT], BF16); u32 = ar("u32", [128, 4, NT]); gs5 = ar("gs5", [128, 4, NT], BF16)
    wtok = ar("wtok", [128, 16, 2, 128], BF16)
    s5a = ar("s5a", [128, 4, 128]); s5b = ar("s5b", [128, 4, 128])
    Gr = ar("Gr", [128, 16, 64]); Gi = ar("Gi", [128, 16, 64])
    hA = ar("hA", [128, 16, 64]); hB = ar("hB", [128, 16, 64]); hC = ar("hC", [128, 16, 64]); hD = ar("hD", [128, 16, 64])
    hre = ar("hre", [128, 16, NT], BF16); himn = ar("himn", [128, 16, NT], BF16)
    yv = ar("yv", [128, 4, NT]); gt = ar("gt", [128, 4, NT]); gsg = ar("gsg", [128, 4, NT])
    gl = ar("gl", [128, 4, NT]); glb = ar("glb", [128, 4, NT], BF16); sgl = ar("sgl", [128, 4, NT])
    ar_reset()
    qkraw = ar("qkraw", [96, 16, NT + 3]); cvacc = ar("cvacc", [96, 16, NT]); cvtmp = ar("cvtmp", [96, 16, NT])
    hbuf = ar("hbuf0", [96, 16, 6])
    qkT = ar("qkT", [96, 16, NT], BF16)
    vaug = ar("vaug", [64, NT // 64, 4, 193], BF16)
    iftok = ar("iftok", [64, NT // 64, 8])
    zsl_2 = [ar("zsl%d" % i_, [128, NT]) for i_ in range(2)]; ogT = ar("ogT", [128, 6, NT], BF16)
    lfi = ar("lfi", [64, 8]); bcs = ar("bcs", [64, 4]); zz = ar("zz", [64, 4]); ee = ar("ee", [64, 4]); clampt = ar("clampt", [64, 4])
    zmax = ar("zmax", [4, 1]); mu4 = ar("mu4", [4, 1]); f4 = ar("f4", [4, 1]); bend = ar("bend", [4, 1]); dg = ar("dg", [4, 8])
    bc8 = ar("bc8", [128, 8])
    PTs = ar("PTs", [64, 4, 64]); PTb = ar("PTb", [64, 4, 64], BF16)
    hh = ar("hh", [64, 4, 192]); hcen = ar("hcen", [64, 4, 192]); hsq = ar("hsq", [64, 4, 192]); hst = ar("hst", [64, 4, 4])
    hnb = ar("hnb", [64, 768], BF16); khm = ar("khm", [64, 4, 192], BF16)

    BR_T = {0: yrwT, 1: ys5T, 2: ymlT}
    BR_K = {0: "yrwT", 1: "ys5T", 2: "ymlT"}
    BR_KC = {0: 6, 1: 4, 2: 6}

    def ln_block(src, srckeys, rows, out_dram=None):
        for hf in range(2):
            DVE(lambda e, hf=hf: e.bn_stats(out=lst[0:rows, hf, :], in_=src[0:rows, hf * 512:(hf + 1) * 512]),
                srckeys, ["lst"])
        DVE(lambda e: e.bn_aggr(out=lmv[0:rows, :], in_=lst[0:rows].rearrange("p a b -> p (a b)")), ["lst"], ["lmv"])
        ACT(lambda e: e.activation(out=lrs[0:rows, :], in_=lmv[0:rows, 1:2], func=AF.Sqrt, bias=LN_EPS, scale=1.0),
            ["lmv"], ["lrs"])
        DVE(lambda e: e.reciprocal(out=lrs[0:rows, :], in_=lrs[0:rows, :]), ["lrs"], ["lrs"])
        DVE(lambda e: e.tensor_scalar(out=lnt[0:rows, :], in0=src[0:rows, :], scalar1=lmv[0:rows, 0:1],
                                      scalar2=lrs[0:rows, 0:1], op0=ALU.subtract, op1=ALU.mult),
            srckeys + ["lmv", "lrs"], ["lnt"])
        DVE(lambda e: e.tensor_tensor(out=lnt[0:rows, :], in0=lnt[0:rows, :], in1=bcg[0:rows, 0, :], op=ALU.mult),
            ["lnt", "bcg"], ["lnt"])
        POOL(lambda e: e.tensor_tensor(out=x_tok[0:rows, 0, :], in0=lnt[0:rows, :], in1=bcg[0:rows, 1, :],
                                       op=ALU.add), ["lnt", "bcg"], ["x_tok"])
        if out_dram is not None:
            STORE(out_dram, x_tok[0:rows, 0, :], ["x_tok"])
        ACT(lambda e: e.copy(out=lnb[0:rows, :], in_=x_tok[0:rows, 0, :]), ["x_tok"], ["lnb"])
        pt, pk = bbank()
        for kc in range(8):
            TR(pt[:, kc * 128:kc * 128 + rows], lnb[0:rows, kc * 128:(kc + 1) * 128], ident_b[0:rows, 0:rows],
               ["lnb", "ident_b"], pk)
        DVE(lambda e: e.tensor_copy(out=xT[:, :, 0:rows],
                                    in_=pt.rearrange("p (k t) -> p k t", t=128)[:, :, 0:rows]), pk, ["xT"])

    def proj(wb, wk, c0, M, N):
        ps_, pk = fbank()
        MM(ps_[0:M, 0:N], [(wb[:, kc, c0:c0 + M], xT[:, kc, 0:N]) for kc in range(8)], [wk, "xT"], pk)
        return ps_[0:M, 0:N], pk

    s5cur = {"l": None}

    def ensure_s5(l):
        if s5cur["l"] != l:
            load_s5(l)
            s5cur["l"] = l

    def run_pass(kind, N, tiles, xsrc, ydst, first, last):
        og = "p" if kind != "sample" else "s"
        load_bc(0)
        LOAD(lnx[0:N, :], xsrc, [], ["lnx"])
        ln_block(lnx, ["lnx"], N)
        for l in range(DEPTH):
            if first:
                for t_, k_ in ((shiftst[l], "shiftst%d" % l), (S0T[l], "s0t%d" % l), (h0[l], "h0%d" % l),
                               (convst[l], "convst%d" % l), (Cst[l], "cst%d" % l), (mst[l], "mst%d" % l)):
                    POOL(lambda e, t_=t_: e.memset(t_[:], 0.0), [], [k_])
            layer(kind, N, tiles, l, og, last)
            phase_c(kind, N, l, ydst if l == DEPTH - 1 else None)

    def layer(kind, N, tiles, l, og, last):
        pl = "pp%d" % l
        samp = kind == "sample"
        nq = len(tiles)
        if samp:
            for q, (o, C, sq) in enumerate(tiles):
                LOAD(lnx[0:19, 0:128], st_shift[l, sq].rearrange("(t p) -> t p", p=128), [], ["lnx"])
                ps_, pk = fbank()
                TR(ps_[:, 0:19], lnx[0:19, 0:128], ident_f[0:19, 0:19], ["lnx", "ident_f"], pk)
                ACT(lambda e, ps_=ps_, q=q: e.copy(out=sshift[:, :, q], in_=ps_[:, 0:19]), pk + ["sshift"], ["sshift"])
                LOAD(lnx[0:48, 128:224], st_conv[l, sq].rearrange("j (t p) -> (j t) p", p=96), [], ["lnx"])
                ps_, pk = fbank()
                TR(ps_[0:96, 0:48], lnx[0:48, 128:224], ident_f[0:48, 0:48], ["lnx", "ident_f"], pk)
                ACT(lambda e, ps_=ps_, q=q: e.copy(out=sconv[:, :, q, :], in_=ps_[0:96, 0:48].rearrange("p (j t) -> p t j", j=3)),
                    pk + ["sconv"], ["sconv"])

        def shift_tile(ps_, pk, ct, out_ap, outkeys, ui):
            Ub = U[ui]
            uk = "U%d" % ui
            ACT(lambda e: e.copy(out=Ub[:, 1:N + 1], in_=ps_), pk, [uk])
            if not samp:
                ACT(lambda e: e.copy(out=Ub[:, 0:1], in_=shiftst[l][:, ct:ct + 1]), ["shiftst%d" % l, uk], [uk])
            else:
                ACT(lambda e: e.copy(out=Ub[:, 0:1], in_=sshift[:, ct, 0:1]), ["sshift", uk], [uk])
            DVE(lambda e: e.tensor_tensor(out=dtmp[ui][:, 0:N], in0=Ub[:, 0:N], in1=Ub[:, 1:N + 1], op=ALU.subtract),
                [uk], ["dtmp%d" % ui])
            DVE(lambda e: e.scalar_tensor_tensor(out=out_ap, in0=dtmp[ui][:, 0:N], scalar=PPc(l, "mu", ct),
                                                 in1=Ub[:, 1:N + 1], op0=ALU.mult, op1=ALU.add),
                ["dtmp%d" % ui, uk, pl], outkeys)
            if samp:
                DVE(lambda e: e.tensor_tensor(out=dtmp[ui][:, 0:1], in0=sshift[:, ct, 1:2], in1=Ub[:, 65:66], op=ALU.subtract),
                    [uk, "sshift", "dtmp%d" % ui], ["dtmp%d" % ui])
                DVE(lambda e: e.scalar_tensor_tensor(out=out_ap[:, 64:65], in0=dtmp[ui][:, 0:1], scalar=PPc(l, "mu", ct),
                                                     in1=Ub[:, 65:66], op0=ALU.mult, op1=ALU.add),
                    ["dtmp%d" % ui, uk, pl] + outkeys, outkeys)
                ACT(lambda e: e.copy(out=sshift_o[:, ct, 0:1], in_=Ub[:, 64:65]), [uk], ["sshift_o"])
                ACT(lambda e: e.copy(out=sshift_o[:, ct, 1:2], in_=Ub[:, 128:129]), [uk, "sshift_o"], ["sshift_o"])
            else:
                ACT(lambda e: e.copy(out=shiftst[l][:, ct:ct + 1], in_=Ub[:, N:N + 1]), [uk], ["shiftst%d" % l])

        ui = [0]

        def nui():
            ui[0] ^= 1
            return ui[0]

        S.label = 'rwA'
        wb, wk = load_group(l, "rwx")
        ps_, pk = proj(wb, wk, 0, 128, N)
        shift_tile(ps_, pk, 18, xs18[:, 0:N], ["xs18"], nui())
        ACT(lambda e: e.activation(out=txw[0:64, 0:N], in_=xs18[0:64, 0:N], func=AF.Tanh), ["xs18"], ["txw"])
        ACT(lambda e: e.copy(out=txw[64:128, 0:N], in_=xs18[64:128, 0:N]), ["xs18", "txw"], ["txw"])
        lk = ["lora%d" % l, "lora%da" % l]

        def jtile(j, wbk, wkk, wbr, wkr, wbv, wkv, c0):
            jp = j % 2
            ldec = ldec_2[jp]
            Gc = Gc_2[jp]
            aa = aa_2[jp]
            eneg = eneg_2[jp]
            eprev = eprev_2[jp]
            ehat = ehat_2[jp]
            epos = epos_2[jp]
            kx = kx_2[jp]
            kkr = kkr_2[jp]
            rn = rn_2[jp]
            kkn = kkn_2[jp]
            tk = tk_2[jp]
            kmod = kmod_2[jp]
            bb = bb_2[jp]
            rx = rx_2[jp]
            kksq = kksq_2[jp]
            ps_, pk = fbank()
            MM(ps_[:, 0:N], [(lora[l][0:64, j * 128:(j + 1) * 128], txw[0:64, 0:N])], lk + ["txw"], pk)
            ACT(lambda e, ps_=ps_: e.activation(out=ldec[:, 0:N], in_=ps_[:, 0:N], func=AF.Sigmoid, bias=PPc(l, "w0", j), scale=1.0),
                pk + [pl], ["ldec%d" % jp])
            POOL(lambda e: e.tensor_scalar(out=ldec[:, 0:N], in0=ldec[:, 0:N], scalar1=-EXPM05, scalar2=None, op0=ALU.mult),
                 ["ldec%d" % jp], ["ldec%d" % jp])
            ps2, pk2 = fbank()
            MM(ps2[:, 0:N], [(lora[l][64:128, j * 128:(j + 1) * 128], txw[64:128, 0:N])], lk + ["txw"], pk2)
            ACT(lambda e, ps2=ps2: e.activation(out=aa[:, 0:N], in_=ps2[:, 0:N], func=AF.Sigmoid, bias=PPc(l, "a0", j), scale=1.0),
                pk2 + [pl], ["aa%d" % jp])
            DVE(lambda e: e.tensor_tensor_scan(out=Gc[:, 0:N], data0=scanm[:, 0:N], data1=ldec[:, 0:N], initial=0.0,
                                               op0=ALU.mult, op1=ALU.add), ["ldec%d" % jp, "scanm"], ["Gc%d" % jp])
            for q, (o, C, sq) in enumerate(tiles):
                ACT(lambda e, q=q, o=o, C=C: e.activation(out=gC[:, j, q:q + 1], in_=Gc[:, o + C - 1:o + C], func=AF.Exp),
                    ["Gc%d" % jp, "gC"], ["gC"])
            ps_, pk = proj(wbk, wkk, c0, 128, N)
            shift_tile(ps_, pk, 6 + j, kx[:, 0:N], ["kx%d" % jp], nui())
            ACT(lambda e: e.activation(out=eneg[:, 0:N], in_=Gc[:, 0:N], func=AF.Exp, scale=-1.0), ["Gc%d" % jp], ["eneg%d" % jp])
            DVE(lambda e: e.tensor_tensor(out=eprev[:, 0:N], in0=Gc[:, 0:N], in1=ldec[:, 0:N], op=ALU.subtract),
                ["Gc%d" % jp, "ldec%d" % jp], ["eprev%d" % jp])
            ACT(lambda e: e.activation(out=eprev[:, 0:N], in_=eprev[:, 0:N], func=AF.Exp), ["eprev%d" % jp], ["eprev%d" % jp])
            for q, (o, C, sq) in enumerate(tiles):
                ACT(lambda e, o=o, C=C: e.activation(out=ehat[:, o:o + C], in_=Gc[:, o:o + C], func=AF.Exp, scale=-1.0),
                    ["Gc%d" % jp, "ehat%d" % jp], ["ehat%d" % jp])
                DVE(lambda e, o=o, C=C, q=q: e.tensor_scalar(out=ehat[:, o:o + C], in0=ehat[:, o:o + C],
                                                             scalar1=gC[:, j, q:q + 1], scalar2=None, op0=ALU.mult),
                    ["ehat%d" % jp, "gC"], ["ehat%d" % jp])
            DVE(lambda e: e.tensor_scalar(out=kkr[:, 0:N], in0=kx[:, 0:N], scalar1=PPc(l, "kk", j), scalar2=None,
                                          op0=ALU.mult), ["kx%d" % jp, pl], ["kkr%d" % jp])
            POOL(lambda e: e.tensor_tensor(out=kksq[:, 0:N], in0=kkr[:, 0:N], in1=kkr[:, 0:N], op=ALU.mult), ["kkr%d" % jp], ["kksq%d" % jp])
            ps2, pk2 = fbank()
            MM(ps2[:, 0:N], [(blk1[:], kksq[:, 0:N])], ["blk1", "kksq%d" % jp], pk2)
            ACT(lambda e, ps2=ps2: e.activation(out=rn[:, 0:N], in_=ps2[:, 0:N], func=AF.Sqrt, bias=1e-12, scale=1.0), pk2, ["rn%d" % jp])
            DVE(lambda e: e.reciprocal(out=rn[:, 0:N], in_=rn[:, 0:N]), ["rn%d" % jp], ["rn%d" % jp])
            DVE(lambda e: e.tensor_tensor(out=kkn[:, 0:N], in0=kkr[:, 0:N], in1=rn[:, 0:N], op=ALU.mult), ["kkr%d" % jp, "rn%d" % jp], ["kkn%d" % jp])
            DVE(lambda e: e.tensor_scalar(out=tk[:, 0:N], in0=aa[:, 0:N], scalar1=PPc(l, "ka", j),
                                          scalar2=omka[l][:, j:j + 1], op0=ALU.mult, op1=ALU.add),
                ["aa%d" % jp, pl, "omka%d" % l], ["tk%d" % jp])
            DVE(lambda e: e.tensor_tensor(out=kmod[:, 0:N], in0=kx[:, 0:N], in1=tk[:, 0:N], op=ALU.mult), ["kx%d" % jp, "tk%d" % jp], ["kmod%d" % jp])
            POOL(lambda e: e.tensor_tensor(out=bb[:, 0:N], in0=kkn[:, 0:N], in1=aa[:, 0:N], op=ALU.mult), ["kkn%d" % jp, "aa%d" % jp], ["bb%d" % jp])
            DVE(lambda e: e.tensor_tensor(out=kt_[:, j, 0:N], in0=kmod[:, 0:N], in1=eneg[:, 0:N], op=ALU.mult),
                ["kmod%d" % jp, "eneg%d" % jp, "kt__%d" % j], ["kt__%d" % j])
            POOL(lambda e: e.tensor_tensor(out=bt_[:, j, 0:N], in0=bb[:, 0:N], in1=eneg[:, 0:N], op=ALU.mult),
                 ["bb%d" % jp, "eneg%d" % jp, "bt__%d" % j], ["bt__%d" % j])
            DVE(lambda e: e.scalar_tensor_tensor(out=at_[:, j, 0:N], in0=kkn[:, 0:N], scalar=-1.0, in1=eprev[:, 0:N],
                                                 op0=ALU.mult, op1=ALU.mult), ["kkn%d" % jp, "eprev%d" % jp, "at__%d" % j], ["at__%d" % j])
            POOL(lambda e: e.tensor_tensor(out=khat[:, j, 0:N], in0=kmod[:, 0:N], in1=ehat[:, 0:N], op=ALU.mult),
                 ["kmod%d" % jp, "ehat%d" % jp, "khat_%d" % j], ["khat_%d" % j])
            DVE(lambda e: e.tensor_tensor(out=bhat[:, j, 0:N], in0=bb[:, 0:N], in1=ehat[:, 0:N], op=ALU.mult),
                ["bb%d" % jp, "ehat%d" % jp, "bhat_%d" % j], ["bhat_%d" % j])
            ps_, pk = proj(wbr, wkr, c0, 128, N)
            shift_tile(ps_, pk, j, rx[:, 0:N], ["rx%d" % jp], nui())
            ACT(lambda e: e.activation(out=epos[:, 0:N], in_=Gc[:, 0:N], func=AF.Exp), ["Gc%d" % jp], ["epos%d" % jp])
            DVE(lambda e: e.tensor_tensor(out=rt_[:, j, 0:N], in0=rx[:, 0:N], in1=epos[:, 0:N], op=ALU.mult),
                ["rx%d" % jp, "epos%d" % jp, "rt__%d" % j], ["rt__%d" % j])
            DVE(lambda e: e.scalar_tensor_tensor(out=rkp[:, j, 0:N], in0=rx[:, 0:N], scalar=PPc(l, "rk", j),
                                                 in1=kmod[:, 0:N], op0=ALU.mult, op1=ALU.mult),
                ["rx%d" % jp, pl, "kmod%d" % jp, "rkp_%d" % j], ["rkp_%d" % j])
            ps_, pk = proj(wbv, wkv, c0, 128, N)
            shift_tile(ps_, pk, 12 + j, vT[:, j, 0:N], ["vT_%d" % j], nui())

        for part, js in (("0", range(4)), ("1", range(4, 6))):
            wbk, wkk = load_group(l, "rwk" + part)
            wbr, wkr = load_group(l, "rwr" + part)
            wbv, wkv = load_group(l, "rwv" + part)
            for j in js:
                jtile(j, wbk, wkk, wbr, wkr, wbv, wkv, (j % 4) * 128)

        for gn, js in (("rwg0", range(4)), ("rwg1", range(4, 6))):
            wb, wk = load_group(l, gn)
            for j in js:
                ps_, pk = proj(wb, wk, (j % 4) * 128, 128, N)
                ACT(lambda e, ps_=ps_, j=j: e.activation(out=grw[:, j, 0:N], in_=ps_, func=AF.Silu), pk + ["grw_%d" % j], ["grw_%d" % j])

        for q, (o, C, sq) in enumerate(tiles):
            rwkv_tile(l, q, o, C, sq, kind, og, last, nq)
        if samp:
            for q, (o, C, sq) in enumerate(tiles):
                ps_, pk = fbank()
                TR(ps_[0:19, 0:128], sshift_o[:, :, q], ident_f[:], ["sshift_o", "ident_f"], pk)
                ACT(lambda e, ps_=ps_: e.copy(out=lnx[0:19, 0:128], in_=ps_[0:19, 0:128]), pk + ["lnx"], ["lnx"])
                STORE(o_shift["s"][l, sq].rearrange("(t p) -> t p", p=128), lnx[0:19, 0:128], ["lnx"])
        elif last:
            ps_, pk = fbank()
            TR(ps_[0:19, 0:128], shiftst[l][:, :], ident_f[:], ["shiftst%d" % l, "ident_f"], pk)
            ACT(lambda e, ps_=ps_: e.copy(out=lnx[0:19, 0:128], in_=ps_[0:19, 0:128]), pk + ["lnx"], ["lnx"])
            STORE(o_shift["p"][l, 0].rearrange("(t p) -> t p", p=128), lnx[0:19, 0:128], ["lnx"])

        S.label = 's5A'
        ensure_s5(l)
        wb, wk = load_group(l, "s5u")
        for j in range(4):
            ps_, pk = proj(wb, wk, j * 128, 128, N)
            ACT(lambda e, ps_=ps_, j=j: e.copy(out=u32[:, j, 0:N], in_=ps_), pk + ["u32_%d" % j], ["u32_%d" % j])
            DVE(lambda e, j=j: e.tensor_copy(out=uT[:, j, 0:N], in_=u32[:, j, 0:N]), ["u32_%d" % j, "uT_%d" % j], ["uT_%d" % j])
        wb, wk = load_group(l, "s5g")
        for j in range(4):
            ps_, pk = proj(wb, wk, j * 128, 128, N)
            ACT(lambda e, ps_=ps_, j=j: e.activation(out=gs5[:, j, 0:N], in_=ps_, func=AF.Silu), pk + ["gs5_%d" % j], ["gs5_%d" % j])
        s5_en(l, N)
        for q, (o, C, sq) in enumerate(tiles):
            s5_tile(l, q, o, C, sq, kind, og, last, nq)
        s5_out(l, N)
        ensure_s5(1 - l)

        S.label = 'mlA'
        S.label = 'mlA'
        for gi_ in range(4):
            wb, wk = load_group(l, "mlqk%d" % gi_)
            for jj in range(4):
                t = gi_ * 4 + jj
                ps_, pk = proj(wb, wk, jj * 96, 96, N)
                ACT(lambda e, ps_=ps_, t=t: e.copy(out=qkraw[:, t, 3:N + 3], in_=ps_), pk + ["qkraw_%d" % t], ["qkraw_%d" % t])
        qk = ["qkraw"]
        if not samp:
            ACT(lambda e: e.copy(out=qkraw[:, :, 0:3], in_=convst[l][:, :, :]), ["convst%d" % l] + qk, qk)
        else:
            ACT(lambda e: e.copy(out=qkraw[:, :, 0:3], in_=sconv[:, :, 0, :]), ["sconv"] + qk, qk)

        def wbc(jt, n_):
            return pq[l][:, :, jt:jt + 1].broadcast_to([96, 16, n_])

        pqk = "pq%d" % l
        DVE(lambda e: e.tensor_tensor(out=cvacc[:, :, 0:N], in0=qkraw[:, :, 0:N], in1=wbc(0, N), op=ALU.mult), qk + [pqk], ["cvacc"])
        POOL(lambda e: e.tensor_tensor(out=cvacc[:, :, 0:N], in0=cvacc[:, :, 0:N], in1=wbc(4, N), op=ALU.add), ["cvacc", pqk], ["cvacc"])
        for jt in range(1, 4):
            POOL(lambda e, jt=jt: e.tensor_tensor(out=cvtmp[:, :, 0:N], in0=qkraw[:, :, jt:N + jt], in1=wbc(jt, N), op=ALU.mult),
                 qk + [pqk, "cvtmp"], ["cvtmp"])
            DVE(lambda e: e.tensor_tensor(out=cvacc[:, :, 0:N], in0=cvacc[:, :, 0:N], in1=cvtmp[:, :, 0:N], op=ALU.add),
                ["cvacc", "cvtmp"], ["cvacc"])
        if samp:
            hk = "hbuf0"
            hb = hbuf
            POOL(lambda e: e.tensor_copy(out=hb[:, :, 0:3], in_=sconv[:, :, 1, :]), ["sconv", hk], [hk])
            POOL(lambda e: e.tensor_copy(out=hb[:, :, 3:6], in_=qkraw[:, :, 67:70]), qk + [hk], [hk])
            f3 = cvacc[:, :, 64:67]
            DVE(lambda e: e.tensor_tensor(out=f3, in0=hb[:, :, 0:3], in1=wbc(0, 3), op=ALU.mult), [hk, pqk, "cvacc"], ["cvacc"])
            DVE(lambda e: e.tensor_tensor(out=f3, in0=f3, in1=wbc(4, 3), op=ALU.add), [pqk, "cvacc"], ["cvacc"])
            for jt in range(1, 4):
                DVE(lambda e, jt=jt: e.tensor_tensor(out=cvtmp[:, :, 0:3], in0=hb[:, :, jt:jt + 3], in1=wbc(jt, 3), op=ALU.mult),
                    [hk, pqk, "cvtmp"], ["cvtmp"])
                DVE(lambda e: e.tensor_tensor(out=f3, in0=f3, in1=cvtmp[:, :, 0:3], op=ALU.add), ["cvacc", "cvtmp"], ["cvacc"])
        ACT(lambda e: e.activation(out=qkT[:, :, 0:N], in_=cvacc[:, :, 0:N], func=AF.Silu), ["cvacc", "qkT"], ["qkT"])
        POOL(lambda e: e.tensor_scalar(out=qkT[:, 8:16, 0:N], in0=qkT[:, 8:16, 0:N], scalar1=1.0 / math.sqrt(192.0), scalar2=None,
                                       op0=ALU.mult), ["qkT"], ["qkT"])
        if not samp:
            ACT(lambda e: e.copy(out=convst[l][:, :, :], in_=qkraw[:, :, N:N + 3]), qk + ["convst%d" % l], ["convst%d" % l])
        conv_out(l, N, tiles, kind, og, last)
        wb0, wk0 = load_group(l, "mlv0")
        wb1, wk1 = load_group(l, "mlv1")
        for q, (o, C, sq) in enumerate(tiles):
            ps_, pk = fpair()
            MM(ps_[0:C, 0, 0:512], [(xT[:, kc, o:o + C], wb0[:, kc, 0:512]) for kc in range(8)], [wk0, "xT"], [pk[0]])
            MM(ps_[0:C, 1, 0:264], [(xT[:, kc, o:o + C], wb1[:, kc, 0:264]) for kc in range(8)], [wk1, "xT"], [pk[1]])
            vk = ["vaug"]
            ACT(lambda e, ps_=ps_, q=q, C=C: e.copy(out=vaug[0:C, q, 0:2, 0:192],
                                                    in_=ps_[0:C, 0, 0:384].rearrange("p (h d) -> p h d", d=192)), [pk[0]] + vk, vk)
            ACT(lambda e, ps_=ps_, q=q, C=C: e.copy(out=vaug[0:C, q, 2, 0:128], in_=ps_[0:C, 0, 384:512]), [pk[0]] + vk, vk)
            DVE(lambda e, ps_=ps_, q=q, C=C: e.tensor_copy(out=vaug[0:C, q, 2, 128:192], in_=ps_[0:C, 1, 0:64]), [pk[1]] + vk, vk)
            DVE(lambda e, ps_=ps_, q=q, C=C: e.tensor_copy(out=vaug[0:C, q, 3, 0:192], in_=ps_[0:C, 1, 64:256]), [pk[1]] + vk, vk)
            POOL(lambda e, q=q, C=C: e.memset(vaug[0:C, q, :, 192:193], 1.0), vk, vk)
            DVE(lambda e, ps_=ps_, q=q, C=C: e.tensor_tensor(out=iftok[0:C, q, :], in0=ps_[0:C, 1, 256:264], in1=bif[l][0:C, :],
                                                             op=ALU.add), [pk[1], "bif%d" % l, "iftok"], ["iftok"])
        wbo0, wko0 = load_group(l, "mlo0")
        wbo1, wko1 = load_group(l, "mlo1")
        for j in range(6):
            wb, wk = (wbo0, wko0) if j < 4 else (wbo1, wko1)
            ps_, pk = proj(wb, wk, (j % 4) * 128, 128, N)
            ACT(lambda e, ps_=ps_, j=j: e.activation(out=ogT[:, j, 0:N], in_=ps_, func=AF.Sigmoid), pk + ["ogT_%d" % j], ["ogT_%d" % j])
        wbz0, wkz0 = load_group(l, "mlz0")
        wbz1, wkz1 = load_group(l, "mlz1")
        for j in range(6):
            wb, wk = (wbz0, wkz0) if j < 4 else (wbz1, wkz1)
            ps_, pk = proj(wb, wk, (j % 4) * 128, 128, N)
            zsl = zsl_2[j % 2]
            ACT(lambda e, ps_=ps_, zsl=zsl: e.activation(out=zsl[:, 0:N], in_=ps_, func=AF.Silu), pk, ["zsl%d" % (j % 2)])
            DVE(lambda e, j=j, zsl=zsl: e.scalar_tensor_tensor(out=ogT[:, j, 0:N], in0=ogT[:, j, 0:N], scalar=PPc(l, "mlg", j),
                                                               in1=zsl[:, 0:N], op0=ALU.mult, op1=ALU.mult),
                ["ogT_%d" % j, "zsl%d" % (j % 2), pl], ["ogT_%d" % j])
        for q, (o, C, sq) in enumerate(tiles):
            ml_tile(l, q, o, C, sq, kind, og, last, nq)

    def conv_out(l, N, tiles, kind, og, last):
        if kind == "sample":
            ends = [(o + C - 3, sq) for (o, C, sq) in tiles]
        elif last:
            ends = [(N - 3, 0)]
        else:
            return
        for (e0, sq) in ends:
            for hf in range(2):
                for i2 in range(2):
                    i = hf * 2 + i2
                    wbi, wki = load_group(l, "mlqk%d" % i)
                    ps_, pk = fbank()
                    MM(ps_[0:3, 0:384], [(xT[:, kc, e0:e0 + 3], wbi[:, kc, 0:384]) for kc in range(8)], [wki, "xT"], pk)
                    ACT(lambda e, i2=i2, ps_=ps_: e.copy(out=lnt[0:3, i2 * 384:(i2 + 1) * 384], in_=ps_[0:3, 0:384]), pk + ["lnt"], ["lnt"])
                STORE(o_conv[og][l, sq, :, hf * 768:(hf + 1) * 768], lnt[0:3, 0:768], ["lnt"])

    def rwkv_tile(l, q, o, C, sq, kind, og, last, nq):
        S.label = 'rwT'
        tb_ = TB[q % 2]
        pz = tb_['z']
        vtok, khtok, bhtok, Nsb, NTsb, Msb, P1sb, P2sb = (tb_[k_] for k_ in ('vtok', 'khtok', 'bhtok', 'Nsb', 'NTsb', 'Msb', 'P1sb', 'P2sb'))
        samp = kind == "sample"
        sk = "s0t%d" % l
        sl = slice(o, o + C)
        PARTS = [(0, 0), (1, 64)]
        if samp:
            LOAD(stg[:], st_wkv[l, sq].rearrange("h v k -> v h k"), [], ["stg"])
            for j in range(6):
                ps_, pk = fbank()
                TR(ps_[:, 0:64], stg[:, 2 * j:2 * j + 2, :].rearrange("p a b -> p (a b)"), ident_f[0:64, 0:64], ["stg", "ident_f"], pk)
                ACT(lambda e, ps_=ps_, j=j: e.copy(out=S0T[l][:, j, :], in_=ps_[:, 0:64]), pk + [sk], [sk])
        ACT(lambda e: e.copy(out=S0Tb[:], in_=S0T[l][:]), [sk], ["s0tb"])

        def both(fn_):
            if C == 64:
                fn_(slice(0, 128))
            else:
                for par, pb in PARTS:
                    fn_(slice(pb, pb + C))

        for src, srck, dst, dk in ((vT, "vT", vtok, "vtok" + pz), (khat, "khat", khtok, "khtok" + pz), (bhat, "bhat", bhtok, "bhtok" + pz)):
            def fnt(e, src=src):
                for j in range(6):
                    for par, pb in PARTS:
                        ins = e.transpose(out=PSB[pb:pb + C, par, j * 64:(j + 1) * 64], in_=src[pb:pb + 64, j, sl],
                                          identity=ident_b[pb:pb + 64, pb:pb + 64])
                return ins
            PE(fnt, [srck, "ident_b"], ["pb0", "pb1"])
            for par, pb in PARTS:
                ACT(lambda e, dst=dst, par=par, pb=pb: e.copy(out=dst[pb:pb + C, :, :],
                                                                in_=PSB[pb:pb + C, par, 0:384].rearrange("p (j d) -> p j d", d=64)),
                    ["pb%d" % par, dk], [dk])

        def score(lt, lk_, rt2, rk_, mask, mk, dst, dk):
            ps_, pk = fpair()
            def fn(e, ps_=ps_):
                for j in range(6):
                    for par, pb in PARTS:
                        ins = e.matmul(ps_[pb:pb + C, par, j * 64:j * 64 + C], lhsT=lt[pb:pb + 64, j, sl],
                                       rhs=rt2[pb:pb + 64, j, sl], start=True, stop=True)
                return ins
            PE(fn, [lk_, rk_], pk)
            for par, pb in PARTS:
                DVE(lambda e, ps_=ps_, par=par, pb=pb: e.tensor_tensor(
                    out=dst[pb:pb + C, :, 0:C], in0=ps_[pb:pb + C, par, 0:384].rearrange("p (j t) -> p j t", t=64)[:, :, 0:C],
                    in1=mask[pb:pb + C, :, 0:C], op=ALU.mult), [pk[par], mk, dk], [dk])

        def evac(ps_, pk, dst, dk, eng=ACT):
            for par, pb in PARTS:
                if par == 0:
                    ACT(lambda e, par=par, pb=pb: e.copy(out=dst[pb:pb + C, :, :],
                                                         in_=ps_[pb:pb + C, par, 0:384].rearrange("p (j d) -> p j d", d=64)),
                        [pk[par], dk], [dk])
                else:
                    DVE(lambda e, par=par, pb=pb: e.tensor_copy(out=dst[pb:pb + C, :, :],
                                                                in_=ps_[pb:pb + C, par, 0:384].rearrange("p (j d) -> p j d", d=64)),
                        [pk[par], dk], [dk])

        def evac_sq(ps_, pk, dst, dk):
            for par, pb in PARTS:
                DVE(lambda e, par=par, pb=pb: e.tensor_copy(
                    out=dst[pb:pb + C, :, 0:C], in_=ps_[pb:pb + C, par, 0:384].rearrange("p (j t) -> p j t", t=64)[:, :, 0:C]),
                    [pk[par], dk], [dk])

        score(bt_, "bt_", at_, "at_", m_strict, "m_strict", Nsb[0], "Nsb0" + pz)
        score(at_, "at_", bt_, "bt_", m_lower, "m_lower", NTsb[0], "NTsb0" + pz)
        score(kt_, "kt_", at_, "at_", m_strict, "m_strict", Msb, "Msb" + pz)

        ps_, pk = fpair()
        def fnx(e, ps_=ps_):
            for j in range(6):
                for par, pb in PARTS:
                    out = ps_[pb:pb + C, par, j * 64:(j + 1) * 64]
                    e.matmul(out, lhsT=at_[pb:pb + 64, j, sl], rhs=S0Tb[pb:pb + 64, j, :], start=True, stop=False)
                    ins = e.matmul(out, lhsT=Msb[pb:pb + C, j, 0:C], rhs=vtok[pb:pb + C, j, :], start=False, stop=True)
            return ins
        PE(fnx, ["at_", "s0tb", "Msb" + pz, "vtok" + pz], pk)
        evac(ps_, pk, Ysb[0], "Ysb0")
        lev = int(round(math.log2(C)))
        cur = 0
        for lv in range(lev):
            Pc, PTc, Yc = Nsb[cur], NTsb[cur], Ysb[cur]
            Pn, PTn, Yn = Nsb[1 - cur], NTsb[1 - cur], Ysb[1 - cur]
            ps_, pk = fpair()
            def fny(e, ps_=ps_, Pc=Pc, Yc=Yc):
                for j in range(6):
                    for par, pb in PARTS:
                        out = ps_[pb:pb + C, par, j * 64:(j + 1) * 64]
                        e.matmul(out, lhsT=ident_b[pb:pb + C, pb:pb + C], rhs=Yc[pb:pb + C, j, :], start=True, stop=False)
                        ins = e.matmul(out, lhsT=Pc[pb:pb + C, j, 0:C], rhs=Yc[pb:pb + C, j, :], start=False, stop=True)
                return ins
            PE(fny, ["Nsb%d" % cur + pz, "Ysb%d" % cur, "ident_b"], pk)
            evac(ps_, pk, Yn, "Ysb%d" % (1 - cur))
            if lv < lev - 1:
                ps2, pk2 = fpair()
                def fnp(e, ps2=ps2, Pc=Pc, PTc=PTc):
                    for j in range(6):
                        for par, pb in PARTS:
                            ins = e.matmul(ps2[pb:pb + C, par, j * 64:j * 64 + C], lhsT=PTc[pb:pb + C, j, 0:C],
                                           rhs=Pc[pb:pb + C, j, 0:C], start=True, stop=True)
                    return ins
                PE(fnp, ["Nsb%d" % cur + pz, "NTsb%d" % cur + pz], pk2)
                evac_sq(ps2, pk2, Pn, "Nsb%d" % (1 - cur) + pz)
                if lv < lev - 2:
                    ps3, pk3 = fpair()
                    def fnq(e, ps3=ps3, Pc=Pc, PTc=PTc):
                        for j in range(6):
                            for par, pb in PARTS:
                                ins = e.matmul(ps3[pb:pb + C, par, j * 64:j * 64 + C], lhsT=Pc[pb:pb + C, j, 0:C],
                                               rhs=PTc[pb:pb + C, j, 0:C], start=True, stop=True)
                        return ins
                    PE(fnq, ["Nsb%d" % cur + pz, "NTsb%d" % cur + pz], pk3)
                    evac_sq(ps3, pk3, PTn, "NTsb%d" % (1 - cur) + pz)
            cur = 1 - cur
        UT = Ysb[cur]
        uk = "Ysb%d" % cur
        score(kt_, "kt_", rt_, "rt_", m_incl, "m_incl", P1sb, "P1sb" + pz)
        score(bt_, "bt_", rt_, "rt_", m_incl, "m_incl", P2sb, "P2sb" + pz)
        ps_, pk = fpair()
        def fno(e, ps_=ps_, UT=UT):
            for j in range(6):
                for par, pb in PARTS:
                    out = ps_[pb:pb + C, par, j * 64:(j + 1) * 64]
                    e.matmul(out, lhsT=rt_[pb:pb + 64, j, sl], rhs=S0Tb[pb:pb + 64, j, :], start=True, stop=False)
                    e.matmul(out, lhsT=P1sb[pb:pb + C, j, 0:C], rhs=vtok[pb:pb + C, j, :], start=False, stop=False)
                    ins = e.matmul(out, lhsT=P2sb[pb:pb + C, j, 0:C], rhs=UT[pb:pb + C, j, :], start=False, stop=True)
            return ins
        PE(fno, ["rt_", "s0tb", "P1sb" + pz, "P2sb" + pz, "vtok" + pz, uk], pk)
        evac(ps_, pk, yo, "yo")
        psr, pkr = fpair()
        def fnr(e, psr=psr):
            for j in range(6):
                for par, pb in PARTS:
                    ins = e.matmul(psr[pb:pb + C, par, j:j + 1], lhsT=rkp[pb:pb + 64, j, sl], rhs=onesb[pb:pb + 64, 0:1],
                                   start=True, stop=True)
            return ins
        PE(fnr, ["rkp", "onesb"], pkr)
        for par, pb in PARTS:
            ACT(lambda e, par=par, pb=pb, psr=psr: e.copy(out=rkd[pb:pb + C, :], in_=psr[pb:pb + C, par, 0:6]), [pkr[par], "rkd"], ["rkd"])
        both(lambda P: DVE(lambda e: e.tensor_reduce(out=ystat[P, 0, :], in_=yo[P], axis=AX.X, op=ALU.add), ["yo", "ystat"], ["ystat"]))
        both(lambda P: DVE(lambda e: e.tensor_scalar(out=ystat[P, 0, :], in0=ystat[P, 0, :], scalar1=1.0 / 64, scalar2=None, op0=ALU.mult),
                           ["ystat"], ["ystat"]))
        both(lambda P: DVE(lambda e: e.tensor_tensor(out=ycen[P], in0=yo[P], in1=ystat[P, 0, :].unsqueeze(2).broadcast_to([P.stop - P.start, 6, 64]),
                                                     op=ALU.subtract), ["yo", "ystat", "ycen"], ["ycen"]))
        both(lambda P: POOL(lambda e: e.tensor_tensor(out=ysq[P], in0=ycen[P], in1=ycen[P], op=ALU.mult), ["ycen", "ysq"], ["ysq"]))
        both(lambda P: DVE(lambda e: e.tensor_reduce(out=ystat[P, 1, :], in_=ysq[P], axis=AX.X, op=ALU.add), ["ysq", "ystat"], ["ystat"]))
        both(lambda P: ACT(lambda e: e.activation(out=ystat[P, 2, :], in_=ystat[P, 1, :], func=AF.Sqrt, bias=RW_GN_EPS, scale=1.0 / 64),
                           ["ystat"], ["ystat"]))
        both(lambda P: DVE(lambda e: e.reciprocal(out=ystat[P, 2, :], in_=ystat[P, 2, :]), ["ystat"], ["ystat"]))
        both(lambda P: DVE(lambda e: e.tensor_tensor(out=ycen[P], in0=ycen[P], in1=ystat[P, 2, :].unsqueeze(2).broadcast_to([P.stop - P.start, 6, 64]),
                                                     op=ALU.mult), ["ycen", "ystat"], ["ycen"]))
        both(lambda P: DVE(lambda e: e.tensor_tensor(out=ycen[P].rearrange("p j d -> p (j d)"), in0=ycen[P].rearrange("p j d -> p (j d)"),
                                                     in1=rwln[l][P, 0, :], op=ALU.mult), ["ycen", "rwln%d" % l], ["ycen"]))
        both(lambda P: POOL(lambda e: e.tensor_tensor(out=ycen[P].rearrange("p j d -> p (j d)"), in0=ycen[P].rearrange("p j d -> p (j d)"),
                                                      in1=rwln[l][P, 1, :], op=ALU.add), ["ycen", "rwln%d" % l], ["ycen"]))
        both(lambda P: POOL(lambda e: e.tensor_tensor(out=ysq[P], in0=vtok[P], in1=rkd[P, :].unsqueeze(2).broadcast_to([P.stop - P.start, 6, 64]),
                                                      op=ALU.mult), ["vtok" + pz, "rkd", "ysq"], ["ysq"]))
        both(lambda P: DVE(lambda e: e.tensor_tensor(out=ytb[P], in0=ycen[P], in1=ysq[P], op=ALU.add), ["ycen", "ysq", "ytb"], ["ytb"]))
        def fnb(e):
            for j in range(6):
                for par, pb in PARTS:
                    ins = e.transpose(out=PSB[pb:pb + 64, par, j * 64:j * 64 + C], in_=ytb[pb:pb + C, j, :],
                                      identity=ident_b[pb:pb + C, pb:pb + C])
            return ins
        PE(fnb, ["ytb", "ident_b"], ["pb0", "pb1"])
        for par, pb in PARTS:
            DVE(lambda e, par=par, pb=pb: e.tensor_tensor(out=yrwT[pb:pb + 64, :, sl],
                                                          in0=PSB[pb:pb + 64, par, 0:384].rearrange("p (j t) -> p j t", t=64)[:, :, 0:C],
                                                          in1=grw[pb:pb + 64, :, sl], op=ALU.mult), ["pb%d" % par, "grw", "yrwT"], ["yrwT"])
        ps_, pk = fpair()
        def fns(e, ps_=ps_, UT=UT):
            for j in range(6):
                for par, pb in PARTS:
                    out = ps_[pb:pb + 64, par, j * 64:(j + 1) * 64]
                    e.matmul(out, lhsT=khtok[pb:pb + C, j, :], rhs=vtok[pb:pb + C, j, :], start=True, stop=False)
                    ins = e.matmul(out, lhsT=bhtok[pb:pb + C, j, :], rhs=UT[pb:pb + C, j, :], start=False, stop=True)
            return ins
        PE(fns, ["khtok" + pz, "bhtok" + pz, "vtok" + pz, uk], pk)
        for j in range(6):
            for par, pb in PARTS:
                DVE(lambda e, j=j, ps_=ps_, par=par, pb=pb: e.scalar_tensor_tensor(
                    out=S0T[l][pb:pb + 64, j, :], in0=S0T[l][pb:pb + 64, j, :], scalar=gC[pb:pb + 64, j, q:q + 1],
                    in1=ps_[pb:pb + 64, par, j * 64:(j + 1) * 64], op0=ALU.mult, op1=ALU.add), [pk[par], sk, "gC"], [sk])
        if samp or (last and q == nq - 1):
            b_ = sq if samp else 0
            for j in range(6):
                ps2, pk2 = fbank()
                TR(ps2[0:64, 0:128], S0T[l][:, j, :], ident_f[:], [sk, "ident_f"], pk2)
                ACT(lambda e, ps2=ps2, j=j: e.copy(out=stg[:, 2 * j:2 * j + 2, :].rearrange("p a b -> p (a b)"), in_=ps2[0:64, 0:128]),
                    pk2 + ["stg"], ["stg"])
            STORE(o_wkv[og][l, b_].rearrange("h v k -> v h k"), stg[:], ["stg"])

    def s5_en(l, N):
        S.label = 's5T'
        C = N
        sl = slice(0, N)
        for ut in range(4):
            ps_, pk = fpair()
            MM(ps_[0:C, 0, :], [(uT[:, ut, sl], bblkB[:, ut, 0:512])], ["uT", "bblkB"], [pk[0]])
            MM(ps_[0:C, 1, :], [(uT[:, ut, sl], bblkB[:, ut, 512:1024])], ["uT", "bblkB"], [pk[1]])
            pvv = ps_[0:C].rearrange("p b (i c q) -> p (b i) c q", i=2, c=2)
            bur, bui = pvv[:, :, 0, :], pvv[:, :, 1, :]
            enr, eni = EnB[0:C, ut * 4:(ut + 1) * 4, 0, :], EnB[0:C, ut * 4:(ut + 1) * 4, 1, :]
            wr_, wi_ = wtok[0:C, ut * 4:(ut + 1) * 4, 0, :], wtok[0:C, ut * 4:(ut + 1) * 4, 1, :]
            ek = ["EnB"]
            DVE(lambda e, bur=bur, enr=enr: e.tensor_tensor(out=s5a[0:C], in0=bur, in1=enr, op=ALU.mult), pk + ek, ["s5a"])
            DVE(lambda e, bui=bui, eni=eni: e.tensor_tensor(out=s5b[0:C], in0=bui, in1=eni, op=ALU.mult), pk + ek, ["s5b"])
            POOL(lambda e, wr_=wr_: e.tensor_tensor(out=wr_, in0=s5a[0:C], in1=s5b[0:C], op=ALU.subtract), ["s5a", "s5b", "wtok"], ["wtok"])
            DVE(lambda e, bur=bur, eni=eni: e.tensor_tensor(out=s5a[0:C], in0=bur, in1=eni, op=ALU.mult), pk + ek + ["s5a"], ["s5a"])
            DVE(lambda e, bui=bui, enr=enr: e.tensor_tensor(out=s5b[0:C], in0=bui, in1=enr, op=ALU.mult), pk + ek + ["s5b"], ["s5b"])
            POOL(lambda e, wi_=wi_: e.tensor_tensor(out=wi_, in0=s5a[0:C], in1=s5b[0:C], op=ALU.add), ["s5a", "s5b", "wtok"], ["wtok"])

    def s5_tile(l, q, o, C, sq, kind, og, last, nq):
        S.label = 's5T'
        samp = kind == "sample"
        hk = "h0%d" % l
        sl = slice(o, o + C)
        if samp:
            for c, srcd in ((0, st_s5re), (1, st_s5im)):
                LOAD(lnx[0:16, c * 128:(c + 1) * 128], srcd[l, sq].rearrange("(i g) p -> i (g p)", g=2), [], ["lnx"])
            for c in range(2):
                ps_, pk = fbank()
                TR(ps_[:, 0:16], lnx[0:16, c * 128:(c + 1) * 128], ident_f[0:16, 0:16], ["lnx", "ident_f"], pk)
                ACT(lambda e, ps_=ps_, c=c: e.copy(out=h0[l][:, c, :], in_=ps_[:, 0:16]), pk + [hk], [hk])
        for c, Gd, gk in ((0, Gr, "Gr"), (1, Gi, "Gi")):
            ps_, pk = fpair()
            def fnc(e, ps_=ps_, c=c):
                for i in range(16):
                    ins = e.matmul(ps_[:, i // 8, (i % 8) * 64:(i % 8) * 64 + C], lhsT=wtok[o:o + C, i, c, :], rhs=tri2[o:o + C, 0:C],
                                   start=True, stop=True)
                return ins
            PE(fnc, ["wtok", "tri2"], pk)
            DVE(lambda e, ps_=ps_, Gd=Gd, c=c: e.tensor_tensor(
                out=Gd[:, :, 0:C].rearrange("p (b i) t -> p b i t", b=2),
                in0=ps_[:, :, :].rearrange("p b (i t) -> p b i t", t=64)[:, :, :, 0:C],
                in1=h0[l][:, c, :].rearrange("p (b i) -> p b i", b=2).unsqueeze(3).broadcast_to([128, 2, 8, C]), op=ALU.add),
                pk + [hk], [gk])
        er, ei = EpB[:, 0, :, 0:C], EpB[:, 1, :, 0:C]
        ek = ["EpB"]
        DVE(lambda e: e.tensor_tensor(out=hA[:, :, 0:C], in0=Gr[:, :, 0:C], in1=er, op=ALU.mult), ["Gr"] + ek, ["hA"])
        DVE(lambda e: e.tensor_tensor(out=hB[:, :, 0:C], in0=Gi[:, :, 0:C], in1=ei, op=ALU.mult), ["Gi"] + ek, ["hB"])
        DVE(lambda e: e.tensor_tensor(out=hre[:, :, sl], in0=hA[:, :, 0:C], in1=hB[:, :, 0:C], op=ALU.subtract), ["hA", "hB", "hre"], ["hre"])
        POOL(lambda e: e.tensor_tensor(out=hC[:, :, 0:C], in0=Gi[:, :, 0:C], in1=er, op=ALU.mult), ["Gi"] + ek, ["hC"])
        POOL(lambda e: e.tensor_tensor(out=hD[:, :, 0:C], in0=Gr[:, :, 0:C], in1=ei, op=ALU.mult), ["Gr"] + ek, ["hD"])
        DVE(lambda e: e.scalar_tensor_tensor(out=himn[:, :, sl], in0=hC[:, :, 0:C], scalar=-1.0, in1=hD[:, :, 0:C],
                                             op0=ALU.mult, op1=ALU.subtract), ["hC", "hD", "himn"], ["himn"])
        DVE(lambda e: e.tensor_tensor(out=h0[l][:, 0, :], in0=hA[:, :, C - 1], in1=hB[:, :, C - 1], op=ALU.subtract),
            ["hA", "hB", hk, "Gr", "Gi"], [hk])
        DVE(lambda e: e.tensor_tensor(out=h0[l][:, 1, :], in0=hC[:, :, C - 1], in1=hD[:, :, C - 1], op=ALU.add), ["hC", "hD", hk], [hk])
        if samp or (last and q == nq - 1):
            b_ = sq if samp else 0
            for c, dd in ((0, o_s5re), (1, o_s5im)):
                ps3, pk3 = fbank()
                TR(ps3[0:16, 0:128], h0[l][:, c, :], ident_f[:], [hk, "ident_f"], pk3)
                ACT(lambda e, ps3=ps3, c=c: e.copy(out=lnx[0:16, c * 128:(c + 1) * 128], in_=ps3[0:16, 0:128]), pk3 + ["lnx"], ["lnx"])
                STORE(dd[og][l, b_].rearrange("(i g) p -> i (g p)", g=2), lnx[0:16, c * 128:(c + 1) * 128], ["lnx"])

    def s5_out(l, N):
        S.label = 's5T'
        C = N
        sl = slice(0, N)
        ps_, pk = fbank()
        def fny(e, ps_=ps_):
            for ut in range(4):
                for hf in range(2):
                    out = ps_[hf * 64:(hf + 1) * 64, ut * 128:ut * 128 + C]
                    n_ = 0
                    for ii in range(2):
                        i = ut * 4 + hf * 2 + ii
                        for c, hsrc in ((0, hre), (1, himn)):
                            ins = e.matmul(out, lhsT=cpadB[:, i, c, :], rhs=hsrc[:, i, 0:C], start=(n_ == 0), stop=(n_ == 3))
                            n_ += 1
            return ins
        PE(fny, ["cpadB", "hre", "himn"], pk)
        for ut in range(4):
            DVE(lambda e, ut=ut, ps_=ps_: e.scalar_tensor_tensor(out=yv[:, ut, 0:C], in0=u32[:, ut, sl], scalar=PPc(l, "s5d", ut),
                                                                 in1=ps_[:, ut * 128:ut * 128 + C], op0=ALU.mult, op1=ALU.add),
                pk + ["u32", "pp%d" % l, "yv"], ["yv"])
        POOL(lambda e: e.tensor_tensor(out=gt[:, :, 0:C], in0=yv[:, :, 0:C], in1=yv[:, :, 0:C], op=ALU.mult), ["yv"], ["gt"])
        DVE(lambda e: e.tensor_scalar(out=gt[:, :, 0:C], in0=gt[:, :, 0:C], scalar1=0.044715, scalar2=1.0, op0=ALU.mult, op1=ALU.add),
            ["gt"], ["gt"])
        DVE(lambda e: e.tensor_tensor(out=gt[:, :, 0:C], in0=gt[:, :, 0:C], in1=yv[:, :, 0:C], op=ALU.mult), ["gt", "yv"], ["gt"])
        ACT(lambda e: e.activation(out=gsg[:, :, 0:C], in_=gt[:, :, 0:C], func=AF.Sigmoid, scale=1.5957691216057308), ["gt"], ["gsg"])
        DVE(lambda e: e.tensor_tensor(out=gl[:, :, 0:C], in0=yv[:, :, 0:C], in1=gsg[:, :, 0:C], op=ALU.mult), ["yv", "gsg"], ["gl"])
        ACT(lambda e: e.copy(out=glb[:, :, 0:C], in_=gl[:, :, 0:C]), ["gl"], ["glb"])
        ps2, pk2 = fbank()
        def fng(e):
            for ct in range(4):
                for kc in range(4):
                    ins = e.matmul(ps2[:, ct * 128:ct * 128 + C], lhsT=wglu[l][:, kc, ct * 128:(ct + 1) * 128], rhs=glb[:, kc, 0:C],
                                   start=(kc == 0), stop=(kc == 3))
            return ins
        PE(fng, ["wglu%d" % l, "glb"], pk2)
        for ct in range(4):
            ACT(lambda e, ct=ct: e.activation(out=sgl[:, ct, 0:C], in_=ps2[:, ct * 128:ct * 128 + C], func=AF.Sigmoid,
                                              bias=PPc(l, "bglu", ct), scale=1.0), pk2 + ["pp%d" % l, "sgl"], ["sgl"])
        POOL(lambda e: e.tensor_tensor(out=gl[:, :, 0:C], in0=gl[:, :, 0:C], in1=sgl[:, :, 0:C], op=ALU.mult), ["gl", "sgl"], ["gl"])
        DVE(lambda e: e.tensor_tensor(out=ys5T[:, :, sl], in0=gl[:, :, 0:C], in1=gs5[:, :, sl], op=ALU.mult),
            ["gl", "gs5", "ys5T"], ["ys5T"])

    def ml_tile(l, q, o, C, sq, kind, og, last, nq):
        S.label = 'mlT'
        samp = kind == "sample"
        ck, mk_ = "cst%d" % l, "mst%d" % l
        sl = slice(o, o + C)
        if samp:
            for kt in range(2):
                LOAD(Cst[l][:, kt, :, 0:192], st_c[l, sq, :, kt * 96:(kt + 1) * 96, :].rearrange("h p v -> p h v"), [ck], [ck])
                LOAD(Cst[l][:, kt, :, 192:193], st_n[l, sq, :, kt * 96:(kt + 1) * 96].rearrange("h (p o) -> p h o", o=1), [ck], [ck], slow=True)
            LOAD(mst[l][:], st_m[l, sq].rearrange("(h o) -> h o", o=1), [], [mk_], slow=True)
        ik = "iftok"
        ACT(lambda e: e.activation(out=lfi[0:C, 4:8], in_=iftok[0:C, q, 4:8], func=AF.Sigmoid), [ik], ["lfi"])
        ACT(lambda e: e.activation(out=lfi[0:C, 4:8], in_=lfi[0:C, 4:8], func=AF.Ln), ["lfi"], ["lfi"])
        ps_, pk = fbank()
        MM(ps_[0:C, 0:4], [(tri_f[0:C, 0:C], lfi[0:C, 4:8])], ["tri_f", "lfi"], pk)
        ps7, pk7 = fbank()
        MM(ps7[0:4, 0:1], [(lfi[0:C, 4:8], ones_f[0:C, 0:1])], ["ones_f", "lfi"], pk7)
        ACT(lambda e: e.copy(out=bcs[0:C, :], in_=ps_[0:C, 0:4]), pk, ["bcs"])
        ACT(lambda e: e.copy(out=bend[:], in_=ps7[0:4, 0:1]), pk7, ["bend"])
        DVE(lambda e: e.tensor_tensor(out=zz[0:C, :], in0=iftok[0:C, q, 0:4], in1=bcs[0:C, :], op=ALU.subtract), [ik, "bcs"], ["zz"])
        ps2, pk2 = fbank()
        TR(ps2[0:4, 0:C], zz[0:C, :], ident_f[0:C, 0:C], ["zz", "ident_f"], pk2)
        DVE(lambda e: e.tensor_reduce(out=zmax[:], in_=ps2[0:4, 0:C], axis=AX.X, op=ALU.max), pk2, ["zmax"])
        DVE(lambda e: e.tensor_tensor(out=mu4[:], in0=zmax[:], in1=mst[l][:], op=ALU.max), ["zmax", mk_], ["mu4"])
        DVE(lambda e: e.tensor_tensor(out=f4[:], in0=mst[l][:], in1=mu4[:], op=ALU.subtract), [mk_, "mu4"], ["f4"])
        ACT(lambda e: e.activation(out=f4[:], in_=f4[:], func=AF.Exp), ["f4"], ["f4"])
        DVE(lambda e: e.tensor_tensor(out=mst[l][:], in0=bend[:], in1=mu4[:], op=ALU.add), ["bend", "mu4", "f4", mk_], [mk_])
        DVE(lambda e: e.tensor_scalar(out=dg[:, 0:4], in0=ident_f[0:4, 0:4], scalar1=mu4[:, 0:1], scalar2=None, op0=ALU.mult),
            ["ident_f", "mu4"], ["dg"])
        DVE(lambda e: e.tensor_scalar(out=dg[:, 4:8], in0=ident_f[0:4, 0:4], scalar1=f4[:, 0:1], scalar2=None, op0=ALU.mult),
            ["ident_f", "f4", "dg"], ["dg"])
        ps3, pk3 = fbank()
        MM(ps3[:, 0:8], [(ones_f[0:4, :], dg[:, :])], ["ones_f", "dg"], pk3)
        ACT(lambda e: e.copy(out=bc8[:], in_=ps3[:, 0:8]), pk3, ["bc8"])
        DVE(lambda e: e.tensor_tensor(out=ee[0:C, :], in0=zz[0:C, :], in1=bc8[0:C, 0:4], op=ALU.subtract), ["zz", "bc8"], ["ee"])
        ACT(lambda e: e.activation(out=ee[0:C, :], in_=ee[0:C, :], func=AF.Exp), ["ee"], ["ee"])
        DVE(lambda e: e.tensor_tensor(out=clampt[0:C, :], in0=bcs[0:C, :], in1=bc8[0:C, 0:4], op=ALU.add), ["bcs", "bc8"], ["clampt"])
        ACT(lambda e: e.activation(out=clampt[0:C, :], in_=clampt[0:C, :], func=AF.Exp, scale=-1.0), ["clampt"], ["clampt"])
        for hd in range(4):
            DVE(lambda e, hd=hd: e.tensor_scalar(out=Cst[l][:, :, hd, :], in0=Cst[l][:, :, hd, :], scalar1=bc8[0:96, 4 + hd:5 + hd],
                                                 scalar2=None, op0=ALU.mult), [ck, "bc8"], [ck])
        ACT(lambda e: e.copy(out=Cstb[:], in_=Cst[l][:]), [ck], ["cstb"])
        ps4, pk4 = fbank()
        def fnsc(e):
            for hd in range(4):
                for kt in range(2):
                    ins = e.matmul(ps4[0:C, hd * 64:hd * 64 + C], lhsT=qkT[:, 8 + 2 * hd + kt, sl], rhs=qkT[:, 2 * hd + kt, sl],
                                   start=(kt == 0), stop=(kt == 1))
            return ins
        PE(fnsc, ["qkT"], pk4)
        p4 = ps4[0:C, 0:256].rearrange("p (h t) -> p h t", t=64)[:, :, 0:C]
        DVE(lambda e: e.tensor_tensor(out=PTs[0:C, :, 0:C], in0=p4, in1=m_incl[0:C, 0:4, 0:C], op=ALU.mult), pk4 + ["m_incl"], ["PTs"])
        DVE(lambda e: e.tensor_tensor(out=PTb[0:C, :, 0:C], in0=PTs[0:C, :, 0:C], in1=ee[0:C, :].unsqueeze(2).broadcast_to([C, 4, C]),
                                      op=ALU.mult), ["PTs", "ee"], ["PTb"])
        ps5, pk5 = fpair()
        def fnnd(e):
            for hd in range(4):
                out = ps5[0:C, hd // 2, (hd % 2) * 193:(hd % 2) * 193 + 193]
                e.matmul(out, lhsT=PTb[0:C, hd, 0:C], rhs=vaug[0:C, q, hd, :], start=True, stop=False)
                e.matmul(out, lhsT=qkT[:, 2 * hd, sl], rhs=Cstb[:, 0, hd, :], start=False, stop=False)
                ins = e.matmul(out, lhsT=qkT[:, 2 * hd + 1, sl], rhs=Cstb[:, 1, hd, :], start=False, stop=True)
            return ins
        PE(fnnd, ["PTb", "vaug", "cstb", "qkT"], pk5)
        nd = ps5[0:C, :, 0:386].rearrange("p b (h d) -> p b h d", d=193)
        hv4 = hst[0:C, 0, :].rearrange("p (b h) -> p b h", b=2)
        ACT(lambda e: e.activation(out=hv4.unsqueeze(3), in_=nd[:, :, :, 192:193], func=AF.Abs), pk5, ["hst"])
        DVE(lambda e: e.tensor_tensor(out=hst[0:C, 0, :], in0=hst[0:C, 0, :], in1=clampt[0:C, :], op=ALU.max),
            ["hst", "clampt"], ["hst"])
        DVE(lambda e: e.reciprocal(out=hst[0:C, 0, :], in_=hst[0:C, 0, :]), ["hst"], ["hst"])
        DVE(lambda e: e.tensor_tensor(out=hh[0:C].rearrange("p (b h) d -> p b h d", b=2), in0=nd[:, :, :, 0:192],
                                      in1=hv4.unsqueeze(3).broadcast_to([C, 2, 2, 192]), op=ALU.mult), pk5 + ["hst"], ["hh"])
        DVE(lambda e: e.tensor_reduce(out=hst[0:C, 1, :], in_=hh[0:C], axis=AX.X, op=ALU.add), ["hh", "hst"], ["hst"])
        DVE(lambda e: e.tensor_scalar(out=hst[0:C, 1, :], in0=hst[0:C, 1, :], scalar1=1.0 / 192, scalar2=None, op0=ALU.mult), ["hst"], ["hst"])
        DVE(lambda e: e.tensor_tensor(out=hcen[0:C], in0=hh[0:C], in1=hst[0:C, 1, :].unsqueeze(2).broadcast_to([C, 4, 192]),
                                      op=ALU.subtract), ["hh", "hst"], ["hcen"])
        POOL(lambda e: e.tensor_tensor(out=hsq[0:C], in0=hcen[0:C], in1=hcen[0:C], op=ALU.mult), ["hcen"], ["hsq"])
        DVE(lambda e: e.tensor_reduce(out=hst[0:C, 2, :], in_=hsq[0:C], axis=AX.X, op=ALU.add), ["hsq", "hst"], ["hst"])
        ACT(lambda e: e.activation(out=hst[0:C, 3, :], in_=hst[0:C, 2, :], func=AF.Sqrt, bias=LN_EPS, scale=1.0 / 192), ["hst"], ["hst"])
        DVE(lambda e: e.reciprocal(out=hst[0:C, 3, :], in_=hst[0:C, 3, :]), ["hst"], ["hst"])
        DVE(lambda e: e.tensor_tensor(out=hnb[0:C, :].rearrange("p (h d) -> p h d", d=192), in0=hcen[0:C],
                                      in1=hst[0:C, 3, :].unsqueeze(2).broadcast_to([C, 4, 192]), op=ALU.mult), ["hcen", "hst"], ["hnb"])
        pt, pk = bbank()
        for j in range(6):
            TR(pt[:, j * 64:j * 64 + C], hnb[0:C, j * 128:(j + 1) * 128], ident_b[0:C, 0:C], ["hnb", "ident_b"], pk)
        DVE(lambda e, pt=pt: e.tensor_tensor(out=ymlT[:, :, sl], in0=pt[:, 0:384].rearrange("p (j t) -> p j t", t=64)[:, :, 0:C],
                                             in1=ogT[:, :, sl], op=ALU.mult), pk + ["ogT", "ymlT"], ["ymlT"])
        pt2, pk2_ = bbank()
        for t in range(8):
            TR(pt2[0:C, t * 96:(t + 1) * 96], qkT[:, 8 + t, sl], ident_b[0:96, 0:96], ["qkT", "ident_b"], pk2_)
        DVE(lambda e, pt2=pt2: e.tensor_tensor(out=khm[0:C], in0=pt2[0:C, 0:768].rearrange("p (h d) -> p h d", d=192),
                                               in1=ee[0:C, :].unsqueeze(2).broadcast_to([C, 4, 192]), op=ALU.mult), pk2_ + ["ee"], ["khm"])
        for kt in range(2):
            ps6, pk6 = fpair()
            def fncu(e, ps6=ps6, kt=kt):
                for hd in range(4):
                    ins = e.matmul(ps6[0:96, hd // 2, (hd % 2) * 193:(hd % 2) * 193 + 193], lhsT=khm[0:C, hd, kt * 96:(kt + 1) * 96],
                                   rhs=vaug[0:C, q, hd, :], start=True, stop=True)
                return ins
            PE(fncu, ["khm", "vaug"], pk6)
            DVE(lambda e, ps6=ps6, kt=kt: e.tensor_tensor(out=Cst[l][:, kt, :, :].rearrange("p (b h) d -> p b h d", b=2),
                                                          in0=Cst[l][:, kt, :, :].rearrange("p (b h) d -> p b h d", b=2),
                                                          in1=ps6[0:96, :, 0:386].rearrange("p b (h d) -> p b h d", d=193), op=ALU.add),
                pk6 + [ck], [ck])
        if samp or (last and q == nq - 1):
            b_ = sq if samp else 0
            for kt in range(2):
                STORE(o_c[og][l, b_, :, kt * 96:(kt + 1) * 96, :].rearrange("h p v -> p h v"), Cst[l][:, kt, :, 0:192], [ck])
                STORE(o_n[og][l, b_, :, kt * 96:(kt + 1) * 96].rearrange("h (p o) -> p h o", o=1), Cst[l][:, kt, :, 192:193], [ck], slow=True)
            STORE(o_m[og][l, b_].rearrange("(h o) -> h o", o=1), mst[l][:], [mk_], slow=True)

    def phase_c(kind, N, l, ydst):
        S.label = 'C'
        pl = "pp%d" % l
        for jg in range(2):
            for b in range(3):
                wbm, wkm = load_group(l, "mg%d%d" % (b, jg))
                wbb, wkb = load_group(l, "br%d%d" % (b, jg))
                for jj in range(4):
                    j = jg * 4 + jj
                    ps_, pk = proj(wbm, wkm, jj * 128, 128, N)
                    gi_ = (b + jj) % 2
                    ACT(lambda e, ps_=ps_, gi_=gi_, b=b, j=j: e.activation(out=gate[gi_][:, 0:N], in_=ps_, func=AF.Sigmoid,
                                                                           bias=PPc(l, "bmrg", b * 8 + j), scale=1.0),
                        pk + [pl], ["gate%d" % gi_])
                    ps2, pk2 = fbank()
                    nk = BR_KC[b]
                    MM(ps2[:, 0:N], [(wbb[:, kc, jj * 128:(jj + 1) * 128], BR_T[b][:, kc, 0:N]) for kc in range(nk)],
                       [wkb, BR_K[b]], pk2)
                    if b == 0:
                        DVE(lambda e, ps2=ps2, gi_=gi_, j=j: e.tensor_tensor(out=mrg[:, j, 0:N], in0=ps2[:, 0:N], in1=gate[gi_][:, 0:N],
                                                                             op=ALU.mult), pk2 + ["gate%d" % gi_, "mrg%d" % j], ["mrg%d" % j])
                    else:
                        ctmp = ctmp2[jj % 2]
                        DVE(lambda e, ps2=ps2, gi_=gi_, ctmp=ctmp: e.tensor_tensor(out=ctmp[:, 0:N], in0=ps2[:, 0:N], in1=gate[gi_][:, 0:N],
                                                                        op=ALU.mult), pk2 + ["gate%d" % gi_], ["ctmp%d" % (jj % 2)])
                        if b == 1:
                            POOL(lambda e, j=j, ctmp=ctmp: e.tensor_tensor(out=mrg[:, j, 0:N], in0=mrg[:, j, 0:N], in1=ctmp[:, 0:N], op=ALU.add),
                                 ["mrg%d" % j, "ctmp%d" % (jj % 2)], ["mrg%d" % j])
                        else:
                            POOL(lambda e, j=j, ctmp=ctmp: e.tensor_tensor(out=mrgb[:, j, 0:N], in0=mrg[:, j, 0:N], in1=ctmp[:, 0:N], op=ALU.add),
                                 ["mrg%d" % j, "ctmp%d" % (jj % 2), "mrgb%d" % j], ["mrgb%d" % j])
        load_bc(1 + l)
        wo0, wok0 = load_group(l, "wo0")
        wo1, wok1 = load_group(l, "wo1")
        rows = N
        ps_, pk = fpair()
        MM(ps_[0:rows, 0, :], [(mrgb[:, kc, 0:rows], wo0[:, kc, 0:512]) for kc in range(8)], [wok0] + ["mrgb%d" % j_ for j_ in range(8)], [pk[0]])
        MM(ps_[0:rows, 1, :], [(mrgb[:, kc, 0:rows], wo1[:, kc, 0:512]) for kc in range(8)], [wok1] + ["mrgb%d" % j_ for j_ in range(8)], [pk[1]])
        DVE(lambda e, ps_=ps_: e.scalar_tensor_tensor(
            out=lnx[0:rows, :].rearrange("p (b n) -> p b n", b=2), in0=x_tok[0:rows, 0, :].rearrange("p (b n) -> p b n", b=2),
            scalar=DN_ALPHA, in1=ps_[0:rows, :, :], op0=ALU.mult, op1=ALU.add), pk + ["x_tok", "lnx"], ["lnx"])
        ln_block(lnx, ["lnx"], rows, out_dram=ydst)

    pass
    if stage >= 1000:
        S.limit = stage - 1000
        stage = 3
    try:
        if stage >= 1:
            run_pass("meta", 16, [(0, 16, 0)], meta, None, True, False)
        npp = SEQ // NT
        tl2 = [(0, 64, 0), (64, 64, 1)]
        for p in range(npp):
            if stage >= 2 and (stage >= 99 or p < stage - 1):
                run_pass("prompt", NT, tl2, xp[p * NT:(p + 1) * NT, :], yp[p * NT:(p + 1) * NT, :], False, p == npp - 1)
        for sp_ in range(2):
            if stage >= 99:
                run_pass("sample", NT, [(0, 64, 2 * sp_), (64, 64, 2 * sp_ + 1)], xs[sp_ * NT:(sp_ + 1) * NT, :],
                         ys[sp_ * NT:(sp_ + 1) * NT, :], False, False)


    except StopBuild:
        pass
    pass
    S.emit(final_wait_ops=final_ops)
    es.close()
    return nc


_NC = None


def _get_nc():
    global _NC
    if _NC is None:
        _NC = build()
    return _NC


def _host_inputs(inp, c):
    f = lambda a: np.ascontiguousarray(a, dtype=np.float32)
    p = c % 4
    sl = slice(4 * c, 4 * c + 4)
    m = {}
    m["xp"] = f(inp["x_prompt"][p])
    m["xs"] = f(inp["x_sample"][sl].reshape(NSAMP * 64, D))
    m["meta"] = f(inp["meta"])
    m["st_shift"] = f(inp["state_rwkv_shift"][:, sl])
    m["st_wkv"] = f(inp["state_rwkv_wkv"][:, sl])
    m["st_s5re"] = f(inp["state_s5_re"][:, sl])
    m["st_s5im"] = f(inp["state_s5_im"][:, sl])
    m["st_conv"] = f(inp["state_mlstm_conv"][:, sl])
    m["st_c"] = f(inp["state_mlstm_c"][:, sl])
    m["st_n"] = f(inp["state_mlstm_n"][:, sl])
    m["st_m"] = f(inp["state_mlstm_m"][:, sl])
    for k in ("w_in", "w_br_rw", "w_br_s5", "w_br_ml", "w_out", "s5_w_glu", "rw_w2", "rw_a2"):
        m[k] = f(inp[k])
    return m


def _shared_inputs(inp):
    f32 = np.float32
    pp = np.zeros((DEPTH, 128, NPP), f32)
    pq = np.zeros((DEPTH, 96, 80), f32)

    def cols(v, n):
        return np.asarray(v, f32).reshape(n, 128).T

    for l in range(DEPTH):
        def put(name, arr):
            o, w = PP[name]
            pp[l, :, o:o + w] = arr
        put("mu", cols(inp["rw_mu"][l], 19))
        put("w0", cols(inp["rw_w0"][l], 6))
        put("a0", cols(inp["rw_a0"][l], 6))
        put("kk", cols(inp["rw_kk"][l], 6))
        put("ka", cols(inp["rw_ka"][l], 6))
        put("rk", cols(np.asarray(inp["rw_rk"][l]).reshape(768), 6))
        put("s5d", cols(inp["s5_d"][l], 4))
        put("bglu", cols(inp["s5_b_glu"][l], 4))
        put("mlg", cols(inp["ml_ln_g"][l], 6))
        put("bmrg", cols(inp["b_merge"][l], 24))
        are = np.asarray(inp["s5_a_re"][l], f32).reshape(16, 2, 64).transpose(1, 2, 0).reshape(128, 16)
        aim = np.asarray(inp["s5_a_im"][l], f32).reshape(16, 2, 64).transpose(1, 2, 0).reshape(128, 16)
        ldt = np.repeat(np.asarray(inp["s5_log_dt"][l], f32).reshape(16, 2, 1), 64, axis=2).transpose(1, 2, 0).reshape(128, 16)
        put("are", are)
        put("aim", aim)
        put("ldt", ldt)
        cw = np.asarray(inp["ml_conv_w"][l], f32).reshape(4, 16, 96)
        cb = np.asarray(inp["ml_conv_b"][l], f32).reshape(16, 96)
        pqv = np.zeros((96, 16, 5), f32)
        pqv[:, :, 0:4] = cw.transpose(2, 1, 0)
        pqv[:, :, 4] = cb.T
        pq[l] = pqv.reshape(96, 80)
    bc = np.stack([inp["in_ln_g"], inp["in_ln_b"], inp["ln_g"][0], inp["ln_b"][0], inp["ln_g"][1], inp["ln_b"][1]]).astype(f32)
    rwln = np.stack([np.asarray(inp["rw_ln_g"], f32), np.asarray(inp["rw_ln_b"], f32)], axis=1)
    rwln = np.ascontiguousarray(rwln.reshape(DEPTH, 2, 6, 2, 64).transpose(0, 1, 3, 2, 4).reshape(DEPTH, 2, 2, 384))
    bif = np.asarray(inp["ml_b_if"], f32)
    bblk = np.zeros((DEPTH, 128, 4, 1024), f32)
    cpad = np.zeros((DEPTH, 128, 16, 2, 64), f32)
    for l in range(DEPTH):
        for c, (bk, ck) in enumerate((("s5_b_re", "s5_c_re"), ("s5_b_im", "s5_c_im"))):
            B = np.asarray(inp[bk][l], f32)
            Cm = np.asarray(inp[ck][l], f32)
            for g in range(32):
                i = g // 2
                ut = g // 8
                col0 = (i % 4) * 256 + c * 128 + (g % 2) * 64
                bblk[l, (g % 8) * 16:(g % 8) * 16 + 16, ut, col0:col0 + 64] = B[g].T
                oc = (i % 2) * 32 + (g % 2) * 16
                cpad[l, (g % 2) * 64:(g % 2) * 64 + 64, i, c, oc:oc + 16] = Cm[g].T
    return dict(pp=pp, pq=pq, bc=bc, rwln=rwln, bif=bif, bblk=bblk, cpad=cpad)


def kernel(**inp):
    inp = {k: np.asarray(v) for k, v in inp.items()}
    nc = _get_nc()
    shared = _shared_inputs(inp)
    in_maps = []
    for c in range(8):
        m = _host_inputs(inp, c)
        m.update(shared)
        in_maps.append(m)
    res = run_bass_kernel_spmd(nc, in_maps, core_ids=list(range(8)))
    R = res.results
    y_prompt = np.stack([R[c]["yp"] for c in range(4)], 0)
    y_sample = np.concatenate([R[c]["ys"].reshape(NSAMP, 64, D) for c in range(8)], 0)
    outs = [y_prompt, y_sample]
    for nm in ("shift", "wkv", "s5re", "s5im", "conv", "c", "n", "m"):
        outs.append(np.concatenate([R[c]["p_" + nm] for c in range(4)], 1))
    for nm in ("shift", "wkv", "s5re", "s5im", "conv", "c", "n", "m"):
        outs.append(np.concatenate([R[c]["s_" + nm] for c in range(8)], 1))
    return tuple(np.ascontiguousarray(o, dtype=np.float32) for o in outs)
```

```python
import contextlib
import math
import numpy as np
import concourse.bass as bass
import concourse.mybir as mybir
from concourse.bass_utils import run_bass_kernel_spmd

F32 = mybir.dt.float32
BF16 = mybir.dt.bfloat16
ALU = mybir.AluOpType
AF = mybir.ActivationFunctionType
AX = mybir.AxisListType

ENGS = ("pe", "act", "dve", "pool", "sp")
D = 1024
DEPTH = 2
NT = 128
SEQ = 4096
NMETA = 16
NSAMP = 4
RWW = 768
RWS = 2432
S5W = 512
MLW = 768
INC = 11144
DN_ALPHA = (2 * DEPTH) ** 0.25
LN_EPS = 1e-5
RW_GN_EPS = 64e-5
C_RW = 0
C_RWG = 2432
C_S5U = 3200
C_S5G = 3712
C_MLQK = 4224
C_MLV = 5760
C_MLIF = 6528
C_MLO = 6536
C_MLZ = 7304
C_MRG = 8072
EXPM05 = math.exp(-0.5)


class StopBuild(Exception):
    pass


class _FirstHook:
    def __init__(self, eng, wait):
        self._e = eng
        self._w = wait

    def _wrap(self, f):
        def g(*a, **k):
            ins = f(*a, **k)
            if self._w is not None:
                ins._wait_ge(*self._w)
                self._w = None
            return ins
        return g

    def __getattr__(self, name):
        v = getattr(self._e, name)
        if name in ("matmul", "transpose"):
            return self._wrap(v)
        return v


class Sched:
    limit = None
    resched = True
    def __init__(self, nc, n_dma_sems=12):
        self.nc = nc
        self.ops = []
        self.last_w = {}
        self.readers = {}
        self.n_dma_sems = n_dma_sems
        self.arena = {}

    def xl(self, keys):
        out = []
        for k in keys:
            r = self.arena.get(k)
            if r is None:
                r = self.arena.get(k.rstrip('0123456789'))
            if r is None:
                assert not ('_' in k and k.rsplit('_', 1)[1].isdigit() and k.rsplit('_', 1)[0] in self.arena), k
                out.append(k)
            else:
                out.extend("ar%d" % u for u in range(r[0] // 256, (r[1] + 255) // 256))
        return out

    def op(self, eng, fn, reads=(), writes=(), dma=False, single=False):
        if self.limit is not None and len(self.ops) >= self.limit:
            raise StopBuild()
        reads = self.xl(reads)
        writes = self.xl(writes)
        i = len(self.ops)
        deps = set()
        for k in reads:
            w = self.last_w.get(k)
            if w is not None:
                deps.add(w)
        for k in writes:
            w = self.last_w.get(k)
            if w is not None:
                deps.add(w)
            for r in self.readers.get(k, ()):
                deps.add(r)
        deps.discard(i)
        odeps = set()
        if eng == "pe" and not dma:
            odeps = {d for d in deps if (self.ops[d]["eng"] == "pe" and not self.ops[d]["dma"])}
            deps = deps - odeps
        self.ops.append(dict(eng=eng, fn=fn, deps=deps, odeps=odeps, dma=dma, used=False, single=single, label=getattr(self, 'label', ''),
                             dur=getattr(self, 'dur', None), tbl=getattr(self, 'tbl', None)))
        for d in deps:
            self.ops[d]["used"] = True
        for k in writes:
            self.last_w[k] = i
            self.readers[k] = []
        for k in reads:
            if k not in writes:
                self.readers.setdefault(k, []).append(i)
        return i

    def reschedule(self, final_wait_ops):
        ops = self.ops
        n = len(ops)
        DUR = {"pe": 0.35, "act": 0.3, "dve": 0.3, "pool": 0.45, "sp": 0.2}
        succ = [[] for _ in range(n)]
        indeg = [0] * n
        for i, o in enumerate(ops):
            for d in (o["deps"] | o["odeps"]):
                succ[d].append(i)
                indeg[i] += 1
        lastq = {}
        for i, o in enumerate(ops):
            if o["dma"]:
                q = o["eng"]
                if q in lastq:
                    succ[lastq[q]].append(i)
                    indeg[i] += 1
                lastq[q] = i
        fin = [0.0] * n
        ready_t = [0.0] * n
        cur = {e: 0.0 for e in ENGS}
        ready = {e: [] for e in ENGS}
        for i in range(n):
            if indeg[i] == 0:
                ready[ops[i]["eng"]].append(i)
        order = []
        acttbl = [None]
        done = 0
        while done < n:
            best = None
            for e in ENGS:
                lst = ready[e]
                if not lst:
                    continue
                bi_, bs_ = None, None
                for i in lst[:24]:
                    st = max(ready_t[i], cur[e])
                    if e == "act" and ops[i]["tbl"] is not None and ops[i]["tbl"] != acttbl[0]:
                        st += 1.3
                    key = (st, i)
                    if bs_ is None or key < bs_:
                        bi_, bs_ = i, key
                if best is None or bs_ < best[0]:
                    best = (bs_, bi_, e)
            (st, _), i, e = best
            ready[e].remove(i)
            o = ops[i]
            if o["dma"]:
                cur[e] = st + 0.1
                fin[i] = st + 2.5
            else:
                if e == "act" and o["tbl"] is not None:
                    acttbl[0] = o["tbl"]
                fin[i] = st + (o["dur"] or DUR[e])
                cur[e] = fin[i]
            order.append(i)
            done += 1
            for j in succ[i]:
                ready_t[j] = max(ready_t[j], fin[i] + 0.15)
                indeg[j] -= 1
                if indeg[j] == 0:
                    ready[ops[j]["eng"]].append(j)
        assert len(order) == n
        pos = {old: new for new, old in enumerate(order)}
        newops = []
        for old_i in order:
            o = ops[old_i]
            o["deps"] = {pos[d] for d in o["deps"]}
            o["odeps"] = {pos[d] for d in o["odeps"]}
            newops.append(o)
        self.ops = newops
        return [pos[i] for i in final_wait_ops]

    def emit(self, final_wait_ops=()):
        nc = self.nc
        if self.resched:
            final_wait_ops = self.reschedule(list(final_wait_ops))
        ops = self.ops
        for i in final_wait_ops:
            ops[i]["used"] = True
        cnt = {e: 0 for e in ENGS}
        dma_cnt = {}
        dma_rr = {e: 0 for e in ENGS}
        for o in ops:
            if o["dma"]:
                q = o["eng"]
                s = (q, dma_rr[q] % self.n_dma_sems)
                dma_rr[q] += 1
                dma_cnt[s] = dma_cnt.get(s, 0) + 1
                o["sig"] = ("dma", s, 16 * dma_cnt[s])
            elif o["used"]:
                cnt[o["eng"]] += 1
                o["sig"] = ("eng", o["eng"], cnt[o["eng"]])
            else:
                o["sig"] = None
        with contextlib.ExitStack() as st:
            esem = {e: st.enter_context(nc.semaphore("s_" + e)) for e in ENGS}
            dsem = {}
            for s in dma_cnt:
                dsem[s] = st.enter_context(nc.semaphore("d_%s_%d" % s))
            block = st.enter_context(nc.Block())

            def semof(sig):
                return esem[sig[1]] if sig[0] == "eng" else dsem[sig[1]]

            def run(e, engobj):
                seen = {}
                for o in ops:
                    if o["eng"] != e:
                        continue
                    need = {}
                    for d in o["deps"]:
                        sg = ops[d]["sig"]
                        key = (sg[0], sg[1])
                        need[key] = max(need.get(key, 0), sg[2])
                    if o["dma"]:
                        sg = o["sig"]
                        key = (sg[0], sg[1])
                        if sg[2] > 16:
                            need[key] = max(need.get(key, 0), sg[2] - 16)
                    waits = []
                    for key, v in need.items():
                        if seen.get(key, 0) < v:
                            waits.append((esem[key[1]] if key[0] == "eng" else dsem[key[1]], v))
                            seen[key] = v
                    emb = []
                    if not o["dma"] and (o["single"] or e == "pe"):
                        emb = waits[-1:]
                        waits = waits[:-1]
                    for sm, v in waits:
                        engobj.wait_ge(sm, v)
                    if emb and not o["single"]:
                        ins = o["fn"](_FirstHook(engobj, emb[0]))
                    else:
                        ins = o["fn"](engobj)
                        for sm, v in emb:
                            ins._wait_ge(sm, v)
                    sg = o["sig"]
                    if sg is not None:
                        ins.then_inc(semof(sg), 16 if sg[0] == "dma" else 1)
                if e == "sp":
                    for i in final_wait_ops:
                        sg = ops[i]["sig"]
                        engobj.wait_ge(semof(sg), sg[2])

            @block.tensor
            def _(eng):
                run("pe", eng)

            @block.scalar
            def _(eng):
                run("act", eng)

            @block.vector
            def _(eng):
                run("dve", eng)

            @block.gpsimd
            def _(eng):
                run("pool", eng)

            @block.sync
            def _(eng):
                run("sp", eng)


def w_groups():
    g = []
    g.append(("rwx", "w_in", 1024, 2304, 128))
    g.append(("rwk0", "w_in", 1024, 768, 512))
    g.append(("rwk1", "w_in", 1024, 1280, 256))
    g.append(("rwr0", "w_in", 1024, 0, 512))
    g.append(("rwr1", "w_in", 1024, 512, 256))
    g.append(("rwv0", "w_in", 1024, 1536, 512))
    g.append(("rwv1", "w_in", 1024, 2048, 256))
    g.append(("rwg0", "w_in", 1024, C_RWG, 512))
    g.append(("rwg1", "w_in", 1024, C_RWG + 512, 256))
    g.append(("s5u", "w_in", 1024, C_S5U, 512))
    g.append(("s5g", "w_in", 1024, C_S5G, 512))
    for i in range(4):
        g.append(("mlqk%d" % i, "w_in", 1024, C_MLQK + 384 * i, 384))
    g.append(("mlv0", "w_in", 1024, C_MLV, 512))
    g.append(("mlv1", "w_in", 1024, C_MLV + 512, 264))
    g.append(("mlo0", "w_in", 1024, C_MLO, 512))
    g.append(("mlo1", "w_in", 1024, C_MLO + 512, 256))
    g.append(("mlz0", "w_in", 1024, C_MLZ, 512))
    g.append(("mlz1", "w_in", 1024, C_MLZ + 512, 256))
    for jg in range(2):
        for b, (nm, k) in enumerate((("w_br_rw", 768), ("w_br_s5", 512), ("w_br_ml", 768))):
            g.append(("mg%d%d" % (b, jg), "w_in", 1024, C_MRG + b * 1024 + jg * 512, 512))
            g.append(("br%d%d" % (b, jg), nm, k, jg * 512, 512))
    for jg in range(2):
        g.append(("wo%d" % jg, "w_out", 1024, jg * 512, 512))
    return g


GROUPS = w_groups()
GIDX = {g[0]: i for i, g in enumerate(GROUPS)}

PP = {}
_o = 0
for _n, _w in (("mu", 19), ("w0", 6), ("a0", 6), ("kk", 6), ("ka", 6), ("rk", 6), ("s5d", 4), ("bglu", 4),
               ("mlg", 6), ("bmrg", 24), ("are", 16), ("aim", 16), ("ldt", 16)):
    PP[_n] = (_o, _w)
    _o += _w
NPP = _o


def build(stage=99):
    nc = bass.Bass("TRN2", target_bir_lowering=False)
    es = contextlib.ExitStack()
    S = Sched(nc)

    def din(name, shape):
        return nc.dram_tensor(name, list(shape), F32, kind="ExternalInput").ap()

    def dout(name, shape):
        return nc.dram_tensor(name, list(shape), F32, kind="ExternalOutput").ap()

    def dscr(name, shape, dt):
        return nc.dram_tensor(name, list(shape), dt, kind="Internal").ap()

    xp = din("xp", [SEQ, D])
    xs = din("xs", [NSAMP * 64, D])
    meta = din("meta", [NMETA, D])
    st_shift = din("st_shift", [DEPTH, NSAMP, RWS])
    st_wkv = din("st_wkv", [DEPTH, NSAMP, 12, 64, 64])
    st_s5re = din("st_s5re", [DEPTH, NSAMP, 32, 64])
    st_s5im = din("st_s5im", [DEPTH, NSAMP, 32, 64])
    st_conv = din("st_conv", [DEPTH, NSAMP, 3, 1536])
    st_c = din("st_c", [DEPTH, NSAMP, 4, 192, 192])
    st_n = din("st_n", [DEPTH, NSAMP, 4, 192])
    st_m = din("st_m", [DEPTH, NSAMP, 4])
    w_in = din("w_in", [DEPTH, D, INC])
    wsrc = dict(w_in=w_in, w_br_rw=din("w_br_rw", [DEPTH, 768, D]), w_br_s5=din("w_br_s5", [DEPTH, 512, D]),
                w_br_ml=din("w_br_ml", [DEPTH, 768, D]), w_out=din("w_out", [DEPTH, D, D]))
    w_glu = din("s5_w_glu", [DEPTH, 512, 512])
    rw_w2 = din("rw_w2", [DEPTH, 64, 768])
    rw_a2 = din("rw_a2", [DEPTH, 64, 768])
    pp_d = din("pp", [DEPTH, 128, NPP])
    pq_d = din("pq", [DEPTH, 96, 16 * 5])
    bc_d = din("bc", [6, D])
    rwln_d = din("rwln", [DEPTH, 2, 2, 384])
    bif_d = din("bif", [DEPTH, 8])
    bblk_d = din("bblk", [DEPTH, 128, 4, 1024])
    cpad_d = din("cpad", [DEPTH, 128, 16, 2, 64])

    yp = dout("yp", [SEQ, D])
    ys = dout("ys", [NSAMP * 64, D])
    o_shift = {"p": dout("p_shift", [DEPTH, 1, RWS]), "s": dout("s_shift", [DEPTH, NSAMP, RWS])}
    o_wkv = {"p": dout("p_wkv", [DEPTH, 1, 12, 64, 64]), "s": dout("s_wkv", [DEPTH, NSAMP, 12, 64, 64])}
    o_s5re = {"p": dout("p_s5re", [DEPTH, 1, 32, 64]), "s": dout("s_s5re", [DEPTH, NSAMP, 32, 64])}
    o_s5im = {"p": dout("p_s5im", [DEPTH, 1, 32, 64]), "s": dout("s_s5im", [DEPTH, NSAMP, 32, 64])}
    o_conv = {"p": dout("p_conv", [DEPTH, 1, 3, 1536]), "s": dout("s_conv", [DEPTH, NSAMP, 3, 1536])}
    o_c = {"p": dout("p_c", [DEPTH, 1, 4, 192, 192]), "s": dout("s_c", [DEPTH, NSAMP, 4, 192, 192])}
    o_n = {"p": dout("p_n", [DEPTH, 1, 4, 192]), "s": dout("s_n", [DEPTH, NSAMP, 4, 192])}
    o_m = {"p": dout("p_m", [DEPTH, 1, 4]), "s": dout("s_m", [DEPTH, NSAMP, 4])}

    wq = [[dscr("wq%d_%d" % (l, i), [128, g[2] // 128, g[4]], BF16) for i, g in enumerate(GROUPS)]
          for l in range(DEPTH)]

    def sb(name, shape, dt=F32):
        return es.enter_context(nc.sbuf_tensor(name, list(shape), dt))

    def psum(name, shape, dt):
        return es.enter_context(nc.psum_tensor(name, list(shape), dt))

    final_ops = []

    def ACT(fn, r, w):
        tbl = None
        names = fn.__code__.co_names
        for nm_, t_ in (("Sigmoid", "sig"), ("Exp", "exp"), ("Ln", "ln"), ("Sqrt", "sqrt"), ("Silu", "silu"), ("Sin", "silu")):
            if nm_ in names:
                tbl = t_
        S.tbl = tbl
        i_ = S.op("act", fn, r, w, single=True)
        S.tbl = None
        return i_

    def DVE(fn, r, w):
        return S.op("dve", fn, r, w, single=True)

    def POOL(fn, r, w):
        return S.op("pool", fn, r, w, single=True)

    def PE(fn, r, w):
        return S.op("pe", fn, r, w)

    def LOAD(out, in_, r, w, slow=False):
        return S.op("sp", lambda e: e.dma_start(out=out, in_=in_, allow_slow_non_contiguous=slow), r, w, dma=True)

    def STORE(out, in_, r, slow=False):
        i = S.op("pool", lambda e: e.dma_start(out=out, in_=in_, allow_slow_non_contiguous=slow), r, (), dma=True)
        final_ops.append(i)
        return i

    def MM(out, pairs, r, w):
        S.dur = 0.1 + 0.07 * len(pairs)
        def fn(e):
            n = len(pairs)
            for i, (l_, r_) in enumerate(pairs):
                ins = e.matmul(out, lhsT=l_, rhs=r_, start=(i == 0), stop=(i == n - 1))
            return ins
        i_ = PE(fn, r, w)
        S.dur = None
        return i_

    def TR(out, in_, ident, r, w):
        return S.op("pe", lambda e: e.transpose(out=out, in_=in_, identity=ident), r, w, single=True)

    PSF = [psum("psf%d" % i, [128, 2, 512], F32) for i in range(3)]
    PSB = psum("psb", [128, 2, 1024], BF16)
    rr = {"b": 0, "p": 0, "t": 0}

    def fbank():
        i = rr["b"] % 6
        rr["b"] += 1
        return PSF[i // 2][:, i % 2, :], ["pf%d" % i]

    def fpair():
        i = rr["p"] % 3
        rr["p"] += 1
        return PSF[i], ["pf%d" % (2 * i), "pf%d" % (2 * i + 1)]

    def bbank():
        i = rr["t"] % 2
        rr["t"] += 1
        return PSB[:, i, :], ["pb%d" % i]

    ident_b = sb("ident_b", [128, 128], BF16)
    ident_f = sb("ident_f", [128, 128], F32)
    m_strict = sb("m_strict", [128, 6, 64], F32)
    m_incl = sb("m_incl", [128, 6, 64], F32)
    m_lower = sb("m_lower", [128, 6, 64], F32)
    tri_b = sb("tri_b", [64, 64], BF16)
    tri2 = sb("tri2", [128, 64], BF16)
    tri_f = sb("tri_f", [64, 64], F32)
    blk1 = sb("blk1", [128, 128], BF16)
    bsel = sb("bsel", [128, 2], BF16)
    ones_f = sb("ones_f", [128, 128], F32)
    onesb = sb("onesb", [128, 1], BF16)
    scanm = sb("scanm", [128, NT], F32)

    def mk_sel(t, pattern, base, cm, op, key):
        POOL(lambda e: e.memset(t, 1.0), [], [key])
        POOL(lambda e: e.affine_select(out=t, in_=t, pattern=pattern, compare_op=op, fill=0.0, base=base,
                                       channel_multiplier=cm), [key], [key])

    mk_sel(ident_f[:], [[-1, 128]], 0, 1, ALU.is_equal, "ident_f")
    POOL(lambda e: e.tensor_copy(out=ident_b[:], in_=ident_f[:]), ["ident_f"], ["ident_b"])
    for hf_ in range(2):
        ps_ = slice(hf_ * 64, hf_ * 64 + 64)
        mk_sel(m_strict[ps_], [[0, 6], [1, 64]], -1, -1, ALU.is_ge, "m_strict")
        mk_sel(m_incl[ps_], [[0, 6], [1, 64]], 0, -1, ALU.is_ge, "m_incl")
        mk_sel(m_lower[ps_], [[0, 6], [-1, 64]], -1, 1, ALU.is_ge, "m_lower")
    mk_sel(tri_f[:], [[1, 64]], 0, -1, ALU.is_ge, "tri_f")
    POOL(lambda e: e.tensor_copy(out=tri_b[:], in_=tri_f[:]), ["tri_f"], ["tri_b"])
    POOL(lambda e: e.tensor_copy(out=tri2[0:64, :], in_=tri_f[:]), ["tri_f"], ["tri2"])
    POOL(lambda e: e.tensor_copy(out=tri2[64:128, :], in_=m_incl[64:128, 0, :]), ["m_incl", "tri2"], ["tri2"])
    POOL(lambda e: e.memset(ones_f[:], 1.0), [], ["ones_f"])
    POOL(lambda e: e.memset(onesb[:], 1.0), [], ["onesb"])
    POOL(lambda e: e.memset(blk1[:], 0.0), [], ["blk1"])
    POOL(lambda e: e.memset(blk1[0:64, 0:64], 1.0), ["blk1"], ["blk1"])
    POOL(lambda e: e.memset(blk1[64:128, 64:128], 1.0), ["blk1"], ["blk1"])
    POOL(lambda e: e.memset(bsel[:], 0.0), [], ["bsel"])
    POOL(lambda e: e.memset(bsel[0:64, 0:1], 1.0), ["bsel"], ["bsel"])
    POOL(lambda e: e.memset(bsel[64:128, 1:2], 1.0), ["bsel"], ["bsel"])
    POOL(lambda e: e.memset(scanm[:], 1.0), [], ["scanm"])
    POOL(lambda e: e.memset(scanm[:].rearrange("p (c t) -> p c t", t=64)[:, :, 0:1], 0.0), ["scanm"], ["scanm"])

    if stage == -4:
        S.emit(final_wait_ops=final_ops); es.close(); return nc
    for l in range(DEPTH):
        for i, (nm, src, K, c0, wd) in enumerate(GROUPS):
            srcap = wsrc[src][l, :, c0:c0 + wd].rearrange("(kc p) c -> p kc c", p=128)
            S.op("pool", lambda e, o_=wq[l][i], s_=srcap: e.dma_start(out=o_, in_=s_), [], ["wq%d_%d" % (l, i)], dma=True)

    if stage == -3:
        S.emit(final_wait_ops=final_ops); es.close(); return nc
    ARW = 15360
    arena_t = sb("arena", [128, ARW], F32)
    arp = {"o": 0}

    def ar_reset(o=0):
        arp["o"] = o

    def ar(name, shape, dt=F32):
        P = shape[0]
        n = int(np.prod(shape[1:]))
        words = n if dt == F32 else (n + 1) // 2
        words = (words + 15) // 16 * 16
        o = arp["o"]
        assert o + words <= ARW, (name, o, words)
        arp["o"] = o + words
        v = arena_t[0:P, o:o + words]
        if dt != F32:
            v = v.bitcast(BF16)
        v = v[:, 0:n]
        if len(shape) == 3:
            v = v.rearrange("p (a b) -> p a b", b=shape[2])
        elif len(shape) == 4:
            v = v.rearrange("p (a b c) -> p a b c", b=shape[2], c=shape[3])
        S.arena[name] = (o * 4, (o + words) * 4)
        if len(shape) >= 3:
            esz = 4 if dt == F32 else 2
            sub = int(np.prod(shape[2:])) * esz
            for a_ in range(shape[1]):
                S.arena["%s_%d" % (name, a_)] = (o * 4 + a_ * sub, o * 4 + (a_ + 1) * sub)
        return v

    pp = [sb("pp%d" % l, [128, NPP]) for l in range(DEPTH)]
    pq = [sb("pq%d" % l, [96, 16, 5]) for l in range(DEPTH)]
    omka = [sb("omka%d" % l, [128, 6]) for l in range(DEPTH)]
    lora = [sb("lora%d" % l, [128, 768], BF16) for l in range(DEPTH)]
    wglu = [sb("wglu%d" % l, [128, 4, 512], BF16) for l in range(DEPTH)]
    bcg = sb("bcg", [128, 2, D])
    rwln = [sb("rwln%d" % l, [128, 2, 384]) for l in range(DEPTH)]
    bif = [sb("bif%d" % l, [64, 8]) for l in range(DEPTH)]
    EpB = sb("EpB", [128, 2, 16, 64])
    EnB = sb("EnB", [128, 16, 2, 128], BF16)
    bblkB = sb("bblkB", [128, 4, 1024], BF16)
    cpadB = sb("cpadB", [128, 16, 2, 64], BF16)
    S5K = ["EpB", "EnB", "bblkB", "cpadB"]
    epd = [dscr("epd%d" % l, [128, 2, 16, 64], F32) for l in range(DEPTH)]
    end_ = [dscr("end%d" % l, [64, 16, 2, 128], BF16) for l in range(DEPTH)]
    bblkq = [dscr("bblkq%d" % l, [128, 4, 1024], BF16) for l in range(DEPTH)]
    cpadq = [dscr("cpadq%d" % l, [128, 16, 2, 64], BF16) for l in range(DEPTH)]
    for l in range(DEPTH):
        LOAD(pp[l][:], pp_d[l], [], ["pp%d" % l])
        LOAD(pq[l][:], pq_d[l].rearrange("p (t j) -> p t j", j=5), [], ["pq%d" % l])
        for a_ in range(2):
            for r_ in range(2):
                LOAD(rwln[l][r_ * 64:(r_ + 1) * 64, a_, :], rwln_d[l, a_, r_].partition_broadcast(64), ["rwln%d" % l], ["rwln%d" % l])
        LOAD(bif[l][:], bif_d[l].partition_broadcast(64), [], ["bif%d" % l])
        S.op("pool", lambda e, l=l: e.dma_start(out=lora[l][0:64, :], in_=rw_w2[l]), [], ["lora%d" % l], dma=True)
        S.op("pool", lambda e, l=l: e.dma_start(out=lora[l][64:128, :], in_=rw_a2[l]), [], ["lora%da" % l], dma=True)
        S.op("pool", lambda e, l=l: e.dma_start(out=wglu[l][:], in_=w_glu[l].rearrange("(kc p) c -> p kc c", p=128)),
             [], ["wglu%d" % l], dma=True)
        S.op("pool", lambda e, l=l: e.dma_start(out=bblkq[l], in_=bblk_d[l]), [], ["bblkq%d" % l], dma=True)
        S.op("pool", lambda e, l=l: e.dma_start(out=cpadq[l], in_=cpad_d[l]), [], ["cpadq%d" % l], dma=True)
        o_, w_ = PP["ka"]
        DVE(lambda e, l=l, o_=o_: e.tensor_scalar(out=omka[l][:], in0=pp[l][:, o_:o_ + 6], scalar1=-1.0, scalar2=1.0,
                                                  op0=ALU.mult, op1=ALU.add), ["pp%d" % l], ["omka%d" % l])

    if stage == -2:
        S.emit(final_wait_ops=final_ops); es.close(); return nc

    def load_bc(i):
        LOAD(bcg[:].rearrange("p a b -> p (a b)"), bc_d[2 * i:2 * i + 2, :].rearrange("a b -> (a b)").partition_broadcast(128),
             [], ["bcg"])

    def load_s5(l):
        LOAD(EpB[:], epd[l], ["epd%d" % l], ["EpB"])
        LOAD(EnB[0:64], end_[l], ["end%d" % l, "EnB"], ["EnB"])
        LOAD(EnB[64:128], end_[l], ["end%d" % l, "EnB"], ["EnB"])
        LOAD(bblkB[:], bblkq[l], ["bblkq%d" % l], ["bblkB"])
        LOAD(cpadB[:], cpadq[l], ["cpadq%d" % l], ["cpadB"])

    def PPc(l, name, j=None):
        o_, w_ = PP[name]
        if j is None:
            return pp[l][:, o_:o_ + w_]
        return pp[l][:, o_ + j:o_ + j + 1]

    ar_reset()
    zr = ar("zr", [128, 16]); zi = ar("zi", [128, 16]); dtt = ar("dtt", [128, 16])
    t1 = ar("t1", [128, 16]); t2 = ar("t2", [128, 16]); t3 = ar("t3", [128, 16]); mg = ar("mg", [128, 16])
    lr = ar("lr", [128, 16])
    Epf = ar("Epf", [128, 2, 16, 64])
    Enf = [ar("enf%d" % c, [128, 16, 64]) for c in range(2)]
    ta = ar("ta", [128, 16, 32]); tb_ = ar("tb", [128, 16, 32])
    cfr = ar("cfr", [128, 16]); cfi = ar("cfi", [128, 16]); den = ar("den", [128, 16])
    enc = [ar("enc%d" % c, [128, 16, 64]) for c in range(2)]
    Ent = ar("Ent", [64, 16, 2, 128])
    KT = ["zr", "zi", "dtt", "t1", "t2", "t3", "mg", "lr", "Epf", "enf0", "enf1", "ta", "tb", "cfr", "cfi", "den", "enc0", "enc1"]

    def cexp32(sign, outr, outi):
        ACT(lambda e: e.activation(out=mg[:], in_=zr[:], func=AF.Exp, scale=sign / 32.0), KT, KT)
        ACT(lambda e: e.activation(out=t1[:], in_=zi[:], func=AF.Sin, scale=sign / 32.0), KT, KT)
        ACT(lambda e: e.activation(out=t2[:], in_=zi[:], func=AF.Sin, scale=sign / 32.0, bias=hpi[:, 0:1]), KT + ["hpi"], KT)
        DVE(lambda e: e.tensor_tensor(out=outr, in0=mg[:], in1=t2[:], op=ALU.mult), KT, KT)
        DVE(lambda e: e.tensor_tensor(out=outi, in0=mg[:], in1=t1[:], op=ALU.mult), KT, KT)
        for _ in range(5):
            DVE(lambda e: e.tensor_tensor(out=t1[:], in0=outr, in1=outr, op=ALU.mult), KT, KT)
            DVE(lambda e: e.tensor_tensor(out=t2[:], in0=outi, in1=outi, op=ALU.mult), KT, KT)
            DVE(lambda e: e.tensor_tensor(out=t3[:], in0=outr, in1=outi, op=ALU.mult), KT, KT)
            DVE(lambda e: e.tensor_tensor(out=outr, in0=t1[:], in1=t2[:], op=ALU.subtract), KT, KT)
            DVE(lambda e: e.tensor_scalar(out=outi, in0=t3[:], scalar1=2.0, scalar2=None, op0=ALU.mult), KT, KT)

    def powers(tabr, tabi):
        ln_ = 1
        while ln_ < 64:
            Lr = tabr[:, :, ln_ - 1:ln_].broadcast_to([128, 16, ln_])
            Li = tabi[:, :, ln_ - 1:ln_].broadcast_to([128, 16, ln_])
            a_ = ta[:, :, 0:ln_]
            b_ = tb_[:, :, 0:ln_]
            sr = tabr[:, :, 0:ln_]
            si = tabi[:, :, 0:ln_]
            dr = tabr[:, :, ln_:2 * ln_]
            di = tabi[:, :, ln_:2 * ln_]
            DVE(lambda e, a_=a_, sr=sr, Lr=Lr: e.tensor_tensor(out=a_, in0=sr, in1=Lr, op=ALU.mult), KT, KT)
            DVE(lambda e, b_=b_, si=si, Li=Li: e.tensor_tensor(out=b_, in0=si, in1=Li, op=ALU.mult), KT, KT)
            DVE(lambda e, a_=a_, b_=b_, dr=dr: e.tensor_tensor(out=dr, in0=a_, in1=b_, op=ALU.subtract), KT, KT)
            DVE(lambda e, a_=a_, sr=sr, Li=Li: e.tensor_tensor(out=a_, in0=sr, in1=Li, op=ALU.mult), KT, KT)
            DVE(lambda e, b_=b_, si=si, Lr=Lr: e.tensor_tensor(out=b_, in0=si, in1=Lr, op=ALU.mult), KT, KT)
            DVE(lambda e, a_=a_, b_=b_, di=di: e.tensor_tensor(out=di, in0=a_, in1=b_, op=ALU.add), KT, KT)
            ln_ *= 2

    hpi = sb("hpi", [128, 1])
    POOL(lambda e: e.memset(hpi[:], math.pi / 2), [], ["hpi"])
    for l in range(DEPTH):
        KP = ["pp%d" % l]
        ACT(lambda e, l=l: e.activation(out=dtt[:], in_=PPc(l, "ldt"), func=AF.Exp), KT + KP, KT)
        DVE(lambda e, l=l: e.tensor_tensor(out=zr[:], in0=PPc(l, "are"), in1=dtt[:], op=ALU.mult), KT + KP, KT)
        DVE(lambda e, l=l: e.tensor_tensor(out=zi[:], in0=PPc(l, "aim"), in1=dtt[:], op=ALU.mult), KT + KP, KT)

        if stage == -10:
            S.emit(final_wait_ops=final_ops); es.close(); return nc
        cexp32(1.0, Epf[:, 0, :, 0], Epf[:, 1, :, 0])

        if stage == -9:
            S.emit(final_wait_ops=final_ops); es.close(); return nc
        powers(Epf[:, 0], Epf[:, 1])

        if stage == -8:
            S.emit(final_wait_ops=final_ops); es.close(); return nc
        cexp32(-1.0, Enf[0][:, :, 0], Enf[1][:, :, 0])
        powers(Enf[0], Enf[1])

        if stage == -7:
            S.emit(final_wait_ops=final_ops); es.close(); return nc
        DVE(lambda e: e.tensor_scalar(out=lr[:], in0=Epf[:, 0, :, 0], scalar1=-1.0, scalar2=None, op0=ALU.add), KT, KT)
        DVE(lambda e, l=l: e.tensor_tensor(out=t1[:], in0=PPc(l, "are"), in1=PPc(l, "are"), op=ALU.mult), KT + KP, KT)
        DVE(lambda e, l=l: e.tensor_tensor(out=t2[:], in0=PPc(l, "aim"), in1=PPc(l, "aim"), op=ALU.mult), KT + KP, KT)
        DVE(lambda e: e.tensor_tensor(out=den[:], in0=t1[:], in1=t2[:], op=ALU.add), KT, KT)
        DVE(lambda e: e.reciprocal(out=den[:], in_=den[:]), KT, KT)
        DVE(lambda e, l=l: e.tensor_tensor(out=t1[:], in0=lr[:], in1=PPc(l, "are"), op=ALU.mult), KT + KP, KT)
        DVE(lambda e, l=l: e.tensor_tensor(out=t2[:], in0=Epf[:, 1, :, 0], in1=PPc(l, "aim"), op=ALU.mult), KT + KP, KT)
        DVE(lambda e: e.tensor_tensor(out=cfr[:], in0=t1[:], in1=t2[:], op=ALU.add), KT, KT)
        DVE(lambda e: e.tensor_tensor(out=cfr[:], in0=cfr[:], in1=den[:], op=ALU.mult), KT, KT)
        DVE(lambda e, l=l: e.tensor_tensor(out=t1[:], in0=Epf[:, 1, :, 0], in1=PPc(l, "are"), op=ALU.mult), KT + KP, KT)
        DVE(lambda e, l=l: e.tensor_tensor(out=t2[:], in0=lr[:], in1=PPc(l, "aim"), op=ALU.mult), KT + KP, KT)
        DVE(lambda e: e.tensor_tensor(out=cfi[:], in0=t1[:], in1=t2[:], op=ALU.subtract), KT, KT)
        DVE(lambda e: e.tensor_tensor(out=cfi[:], in0=cfi[:], in1=den[:], op=ALU.mult), KT, KT)
        CR = cfr[:].unsqueeze(2).broadcast_to([128, 16, 64])
        CI = cfi[:].unsqueeze(2).broadcast_to([128, 16, 64])
        DVE(lambda e, CR=CR: e.tensor_tensor(out=enc[0][:], in0=Enf[0][:], in1=CR, op=ALU.mult), KT, KT)
        DVE(lambda e, CI=CI: e.tensor_tensor(out=enc[1][:], in0=Enf[1][:], in1=CI, op=ALU.mult), KT, KT)
        DVE(lambda e: e.tensor_tensor(out=enc[0][:], in0=enc[0][:], in1=enc[1][:], op=ALU.subtract), KT, KT)
        DVE(lambda e, CI=CI: e.tensor_tensor(out=enc[1][:], in0=Enf[0][:], in1=CI, op=ALU.mult), KT, KT)
        DVE(lambda e, CR=CR: e.tensor_tensor(out=Enf[0][:], in0=Enf[1][:], in1=CR, op=ALU.mult), KT, KT)
        DVE(lambda e: e.tensor_tensor(out=enc[1][:], in0=enc[1][:], in1=Enf[0][:], op=ALU.add), KT, KT)

        if stage == -6:
            S.emit(final_wait_ops=final_ops); es.close(); return nc
        for c in range(2):
            for i in range(16):
                ps_, pk = fbank()
                TR(ps_[0:64, 0:128], enc[c][:, i, :], ident_f[:], KT + ["ident_f"], pk)
                ACT(lambda e, ps_=ps_, i=i, c=c: e.copy(out=Ent[:, i, c, :], in_=ps_[0:64, 0:128]), pk, ["Ent"])

        if stage == -5:
            S.emit(final_wait_ops=final_ops); es.close(); return nc
        S.op("pool", lambda e, l=l: e.dma_start(out=epd[l], in_=Epf), KT, ["epd%d" % l], dma=True)
        S.op("pool", lambda e, l=l: e.dma_start(out=end_[l], in_=Ent), ["Ent"], ["end%d" % l], dma=True)

    x_tok = sb("x_tok", [128, 1, D])
    xT = sb("xT", [128, 8, NT], BF16)
    wbuf = [sb("wbuf%d" % i, [128, 8, 512], BF16) for i in range(4)]
    wr = {"i": 0}

    def load_group(l, name):
        gi = GIDX[name]
        g = GROUPS[gi]
        bi = wr["i"] % 4
        wr["i"] += 1
        kc = g[2] // 128
        LOAD(wbuf[bi][:, 0:kc, 0:g[4]], wq[l][gi], ["wq%d_%d" % (l, gi)], ["wbuf%d" % bi])
        return wbuf[bi], "wbuf%d" % bi

    shiftst = [sb("shiftst%d" % l, [128, 19]) for l in range(DEPTH)]
    sshift = sb("sshift", [128, 19, 2])
    sshift_o = sb("sshift_o", [128, 19, 2])
    S0T = [sb("s0t%d" % l, [128, 6, 64]) for l in range(DEPTH)]
    S0Tb = sb("s0tb", [128, 6, 64], BF16)
    h0 = [sb("h0%d" % l, [128, 2, 16]) for l in range(DEPTH)]
    convst = [sb("convst%d" % l, [96, 16, 3]) for l in range(DEPTH)]
    sconv = sb("sconv", [96, 16, 2, 3])
    Cst = [sb("cst%d" % l, [96, 2, 4, 193]) for l in range(DEPTH)]
    Cstb = sb("cstb", [96, 2, 4, 193], BF16)
    mst = [sb("mst%d" % l, [4, 1]) for l in range(DEPTH)]
    stg = sb("stg", [64, 12, 64])

    yrwT = sb("yrwT", [128, 6, NT], BF16)
    ys5T = sb("ys5T", [128, 4, NT], BF16)
    ymlT = sb("ymlT", [128, 6, NT], BF16)
    gate = [sb("gate%d" % i, [128, NT], BF16) for i in range(2)]
    mrg = sb("mrg", [128, 8, NT]); mrgb = sb("mrgb", [128, 8, NT], BF16)
    lnt = sb("lnt", [128, D]); lnb = sb("lnb", [128, D], BF16); lnx = sb("lnx", [128, D]); ctmp2 = [sb("ctmp%d" % i, [128, NT]) for i in range(2)]
    lst = sb("lst", [128, 2, 6]); lmv = sb("lmv", [128, 2]); lrs = sb("lrs", [128, 1])
    gC = sb("gC", [128, 6, 2])

    ar_reset()
    U = [ar("U%d" % i, [128, NT + 1]) for i in range(2)]
    dtmp = [ar("dtmp%d" % i, [128, NT]) for i in range(2)]
    xs18 = ar("xs18", [128, NT]); txw = ar("txw", [128, NT], BF16)
    ldec_2 = [ar("ldec%d" % i_, [128, NT]) for i_ in range(2)]; Gc_2 = [ar("Gc%d" % i_, [128, NT]) for i_ in range(2)]; aa_2 = [ar("aa%d" % i_, [128, NT]) for i_ in range(2)]
    eneg_2 = [ar("eneg%d" % i_, [128, NT]) for i_ in range(2)]; eprev_2 = [ar("eprev%d" % i_, [128, NT]) for i_ in range(2)]; ehat_2 = [ar("ehat%d" % i_, [128, NT]) for i_ in range(2)]; epos_2 = [ar("epos%d" % i_, [128, NT]) for i_ in range(2)]
    kx_2 = [ar("kx%d" % i_, [128, NT]) for i_ in range(2)]; kkr_2 = [ar("kkr%d" % i_, [128, NT]) for i_ in range(2)]; kksq_2 = [ar("kksq%d" % i_, [128, NT], BF16) for i_ in range(2)]
    rn_2 = [ar("rn%d" % i_, [128, NT]) for i_ in range(2)]; kkn_2 = [ar("kkn%d" % i_, [128, NT]) for i_ in range(2)]; tk_2 = [ar("tk%d" % i_, [128, NT]) for i_ in range(2)]
    kmod_2 = [ar("kmod%d" % i_, [128, NT]) for i_ in range(2)]; bb_2 = [ar("bb%d" % i_, [128, NT]) for i_ in range(2)]; rx_2 = [ar("rx%d" % i_, [128, NT]) for i_ in range(2)]
    rkp = ar("rkp", [128, 6, NT], BF16)
    rt_ = ar("rt_", [128, 6, NT], BF16); kt_ = ar("kt_", [128, 6, NT], BF16); bt_ = ar("bt_", [128, 6, NT], BF16)
    at_ = ar("at_", [128, 6, NT], BF16); khat = ar("khat", [128, 6, NT], BF16); bhat = ar("bhat", [128, 6, NT], BF16)
    vT = ar("vT", [128, 6, NT], BF16); grw = ar("grw", [128, 6, NT], BF16)
    TB = []
    for pz in ("A", "B"):
        TB.append(dict(
            vtok=ar("vtok" + pz, [128, 6, 64], BF16), khtok=ar("khtok" + pz, [128, 6, 64], BF16), bhtok=ar("bhtok" + pz, [128, 6, 64], BF16),
            Nsb=[ar("Nsb%d%s" % (i, pz), [128, 6, 64], BF16) for i in range(2)],
            NTsb=[ar("NTsb%d%s" % (i, pz), [128, 6, 64], BF16) for i in range(2)],
            Msb=ar("Msb" + pz, [128, 6, 64], BF16), P1sb=ar("P1sb" + pz, [128, 6, 64], BF16), P2sb=ar("P2sb" + pz, [128, 6, 64], BF16), z=pz))
    Ysb = [ar("Ysb%d" % i, [128, 6, 64], BF16) for i in range(2)]
    yo = ar("yo", [128, 6, 64]); ycen = ar("ycen", [128, 6, 64]); ysq = ar("ysq", [128, 6, 64])
    ystat = ar("ystat", [128, 4, 6]); rkd = ar("rkd", [128, 6]); ytb = ar("ytb", [128, 6, 64], BF16)
    RW_END = arp["o"]
    ar_reset()
    uT = ar("uT", [128, 4, NT], BF16); u32 = ar("u32", [128, 4, NT]); gs5 = ar("gs5", [128, 4, NT], BF16)
    wtok = ar("wtok", [128, 16, 2, 128], BF16)
    s5a = ar("s5a", [128, 4, 128]); s5b = ar("s5b", [128, 4, 128])
    Gr = ar("Gr", [128, 16, 64]); Gi = ar("Gi", [128, 16, 64])
    hA = ar("hA", [128, 16, 64]); hB = ar("hB", [128, 16, 64]); hC = ar("hC", [128, 16, 64]); hD = ar("hD", [128, 16, 64])
    hre = ar("hre", [128, 16, NT], BF16); himn = ar("himn", [128, 16, NT], BF16)
    yv = ar("yv", [128, 4, NT]); gt = ar("gt", [128, 4, NT]); gsg = ar("gsg", [128, 4, NT])
    gl = ar("gl", [128, 4, NT]); glb = ar("glb", [128, 4, NT], BF16); sgl = ar("sgl", [128, 4, NT])
    ar_reset()
    qkraw = ar("qkraw", [96, 16, NT + 3]); cvacc = ar("cvacc", [96, 16, NT]); cvtmp = ar("cvtmp", [96, 16, NT])
    hbuf = ar("hbuf0", [96, 16, 6])
    qkT = ar("qkT", [96, 16, NT], BF16)
    vaug = ar("vaug", [64, NT // 64, 4, 193], BF16)
    iftok = ar("iftok", [64, NT // 64, 8])
    zsl_2 = [ar("zsl%d" % i_, [128, NT]) for i_ in range(2)]; ogT = ar("ogT", [128, 6, NT], BF16)
    lfi = ar("lfi", [64, 8]); bcs = ar("bcs", [64, 4]); zz = ar("zz", [64, 4]); ee = ar("ee", [64, 4]); clampt = ar("clampt", [64, 4])
    zmax = ar("zmax", [4, 1]); mu4 = ar("mu4", [4, 1]); f4 = ar("f4", [4, 1]); bend = ar("bend", [4, 1]); dg = ar("dg", [4, 8])
    bc8 = ar("bc8", [128, 8])
    PTs = ar("PTs", [64, 4, 64]); PTb = ar("PTb", [64, 4, 64], BF16)
    hh = ar("hh", [64, 8, 192]); hcen = ar("hcen", [64, 8, 192]); hsq = ar("hsq", [64, 8, 192]); hst = ar("hst", [64, 4, 4]); hs2 = ar("hs2", [64, 3, 8])
    hnb = ar("hnb", [64, 2, 768], BF16); khm = ar("khm", [64, 4, 192], BF16)

    BR_T = {0: yrwT, 1: ys5T, 2: ymlT}
    BR_K = {0: "yrwT", 1: "ys5T", 2: "ymlT"}
    BR_KC = {0: 6, 1: 4, 2: 6}

    def ln_block(src, srckeys, rows, out_dram=None):
        for hf in range(2):
            DVE(lambda e, hf=hf: e.bn_stats(out=lst[0:rows, hf, :], in_=src[0:rows, hf * 512:(hf + 1) * 512]),
                srckeys, ["lst"])
        DVE(lambda e: e.bn_aggr(out=lmv[0:rows, :], in_=lst[0:rows].rearrange("p a b -> p (a b)")), ["lst"], ["lmv"])
        ACT(lambda e: e.activation(out=lrs[0:rows, :], in_=lmv[0:rows, 1:2], func=AF.Sqrt, bias=LN_EPS, scale=1.0),
            ["lmv"], ["lrs"])
        DVE(lambda e: e.reciprocal(out=lrs[0:rows, :], in_=lrs[0:rows, :]), ["lrs"], ["lrs"])
        DVE(lambda e: e.tensor_scalar(out=lnt[0:rows, :], in0=src[0:rows, :], scalar1=lmv[0:rows, 0:1],
                                      scalar2=lrs[0:rows, 0:1], op0=ALU.subtract, op1=ALU.mult),
            srckeys + ["lmv", "lrs"], ["lnt"])
        DVE(lambda e: e.tensor_tensor(out=lnt[0:rows, :], in0=lnt[0:rows, :], in1=bcg[0:rows, 0, :], op=ALU.mult),
            ["lnt", "bcg"], ["lnt"])
        POOL(lambda e: e.tensor_tensor(out=x_tok[0:rows, 0, :], in0=lnt[0:rows, :], in1=bcg[0:rows, 1, :],
                                       op=ALU.add), ["lnt", "bcg"], ["x_tok"])
        if out_dram is not None:
            STORE(out_dram, x_tok[0:rows, 0, :], ["x_tok"])
        ACT(lambda e: e.copy(out=lnb[0:rows, :], in_=x_tok[0:rows, 0, :]), ["x_tok"], ["lnb"])
        pt, pk = bbank()
        for kc in range(8):
            TR(pt[:, kc * 128:kc * 128 + rows], lnb[0:rows, kc * 128:(kc + 1) * 128], ident_b[0:rows, 0:rows],
               ["lnb", "ident_b"], pk)
        DVE(lambda e: e.tensor_copy(out=xT[:, :, 0:rows],
                                    in_=pt.rearrange("p (k t) -> p k t", t=128)[:, :, 0:rows]), pk, ["xT"])

    def proj(wb, wk, c0, M, N):
        ps_, pk = fbank()
        MM(ps_[0:M, 0:N], [(wb[:, kc, c0:c0 + M], xT[:, kc, 0:N]) for kc in range(8)], [wk, "xT"], pk)
        return ps_[0:M, 0:N], pk

    s5cur = {"l": None}

    def ensure_s5(l):
        if s5cur["l"] != l:
            load_s5(l)
            s5cur["l"] = l

    def run_pass(kind, N, tiles, xsrc, ydst, first, last):
        og = "p" if kind != "sample" else "s"
        load_bc(0)
        LOAD(lnx[0:N, :], xsrc, [], ["lnx"])
        ln_block(lnx, ["lnx"], N)
        for l in range(DEPTH):
            if first:
                for t_, k_ in ((shiftst[l], "shiftst%d" % l), (S0T[l], "s0t%d" % l), (h0[l], "h0%d" % l),
                               (convst[l], "convst%d" % l), (Cst[l], "cst%d" % l), (mst[l], "mst%d" % l)):
                    POOL(lambda e, t_=t_: e.memset(t_[:], 0.0), [], [k_])
            layer(kind, N, tiles, l, og, last)
            phase_c(kind, N, l, ydst if l == DEPTH - 1 else None)

    def layer(kind, N, tiles, l, og, last):
        pl = "pp%d" % l
        samp = kind == "sample"
        nq = len(tiles)
        if samp:
            for q, (o, C, sq) in enumerate(tiles):
                LOAD(lnx[0:19, 0:128], st_shift[l, sq].rearrange("(t p) -> t p", p=128), [], ["lnx"])
                ps_, pk = fbank()
                TR(ps_[:, 0:19], lnx[0:19, 0:128], ident_f[0:19, 0:19], ["lnx", "ident_f"], pk)
                ACT(lambda e, ps_=ps_, q=q: e.copy(out=sshift[:, :, q], in_=ps_[:, 0:19]), pk + ["sshift"], ["sshift"])
                LOAD(lnx[0:48, 128:224], st_conv[l, sq].rearrange("j (t p) -> (j t) p", p=96), [], ["lnx"])
                ps_, pk = fbank()
                TR(ps_[0:96, 0:48], lnx[0:48, 128:224], ident_f[0:48, 0:48], ["lnx", "ident_f"], pk)
                ACT(lambda e, ps_=ps_, q=q: e.copy(out=sconv[:, :, q, :], in_=ps_[0:96, 0:48].rearrange("p (j t) -> p t j", j=3)),
                    pk + ["sconv"], ["sconv"])

        def shift_tile(ps_, pk, ct, out_ap, outkeys, ui):
            Ub = U[ui]
            uk = "U%d" % ui
            ACT(lambda e: e.copy(out=Ub[:, 1:N + 1], in_=ps_), pk, [uk])
            if not samp:
                ACT(lambda e: e.copy(out=Ub[:, 0:1], in_=shiftst[l][:, ct:ct + 1]), ["shiftst%d" % l, uk], [uk])
            else:
                ACT(lambda e: e.copy(out=Ub[:, 0:1], in_=sshift[:, ct, 0:1]), ["sshift", uk], [uk])
            DVE(lambda e: e.tensor_tensor(out=dtmp[ui][:, 0:N], in0=Ub[:, 0:N], in1=Ub[:, 1:N + 1], op=ALU.subtract),
                [uk], ["dtmp%d" % ui])
            DVE(lambda e: e.scalar_tensor_tensor(out=out_ap, in0=dtmp[ui][:, 0:N], scalar=PPc(l, "mu", ct),
                                                 in1=Ub[:, 1:N + 1], op0=ALU.mult, op1=ALU.add),
                ["dtmp%d" % ui, uk, pl], outkeys)
            if samp:
                DVE(lambda e: e.tensor_tensor(out=dtmp[ui][:, 0:1], in0=sshift[:, ct, 1:2], in1=Ub[:, 65:66], op=ALU.subtract),
                    [uk, "sshift", "dtmp%d" % ui], ["dtmp%d" % ui])
                DVE(lambda e: e.scalar_tensor_tensor(out=out_ap[:, 64:65], in0=dtmp[ui][:, 0:1], scalar=PPc(l, "mu", ct),
                                                     in1=Ub[:, 65:66], op0=ALU.mult, op1=ALU.add),
                    ["dtmp%d" % ui, uk, pl] + outkeys, outkeys)
                ACT(lambda e: e.copy(out=sshift_o[:, ct, 0:1], in_=Ub[:, 64:65]), [uk], ["sshift_o"])
                ACT(lambda e: e.copy(out=sshift_o[:, ct, 1:2], in_=Ub[:, 128:129]), [uk, "sshift_o"], ["sshift_o"])
            else:
                ACT(lambda e: e.copy(out=shiftst[l][:, ct:ct + 1], in_=Ub[:, N:N + 1]), [uk], ["shiftst%d" % l])

        ui = [0]

        def nui():
            ui[0] ^= 1
            return ui[0]

        S.label = 'rwA'
        wb, wk = load_group(l, "rwx")
        ps_, pk = proj(wb, wk, 0, 128, N)
        shift_tile(ps_, pk, 18, xs18[:, 0:N], ["xs18"], nui())
        ACT(lambda e: e.activation(out=txw[0:64, 0:N], in_=xs18[0:64, 0:N], func=AF.Tanh), ["xs18"], ["txw"])
        ACT(lambda e: e.copy(out=txw[64:128, 0:N], in_=xs18[64:128, 0:N]), ["xs18", "txw"], ["txw"])
        lk = ["lora%d" % l, "lora%da" % l]

        def jtile(j, wbk, wkk, wbr, wkr, wbv, wkv, c0):
            jp = j % 2
            ldec = ldec_2[jp]
            Gc = Gc_2[jp]
            aa = aa_2[jp]
            eneg = eneg_2[jp]
            eprev = eprev_2[jp]
            ehat = ehat_2[jp]
            epos = epos_2[jp]
            kx = kx_2[jp]
            kkr = kkr_2[jp]
            rn = rn_2[jp]
            kkn = kkn_2[jp]
            tk = tk_2[jp]
            kmod = kmod_2[jp]
            bb = bb_2[jp]
            rx = rx_2[jp]
            kksq = kksq_2[jp]
            ps_, pk = fbank()
            MM(ps_[:, 0:N], [(lora[l][0:64, j * 128:(j + 1) * 128], txw[0:64, 0:N])], lk + ["txw"], pk)
            ACT(lambda e, ps_=ps_: e.activation(out=ldec[:, 0:N], in_=ps_[:, 0:N], func=AF.Sigmoid, bias=PPc(l, "w0", j), scale=1.0),
                pk + [pl], ["ldec%d" % jp])
            POOL(lambda e: e.tensor_scalar(out=ldec[:, 0:N], in0=ldec[:, 0:N], scalar1=-EXPM05, scalar2=None, op0=ALU.mult),
                 ["ldec%d" % jp], ["ldec%d" % jp])
            ps2, pk2 = fbank()
            MM(ps2[:, 0:N], [(lora[l][64:128, j * 128:(j + 1) * 128], txw[64:128, 0:N])], lk + ["txw"], pk2)
            ACT(lambda e, ps2=ps2: e.activation(out=aa[:, 0:N], in_=ps2[:, 0:N], func=AF.Sigmoid, bias=PPc(l, "a0", j), scale=1.0),
                pk2 + [pl], ["aa%d" % jp])
            DVE(lambda e: e.tensor_tensor_scan(out=Gc[:, 0:N], data0=scanm[:, 0:N], data1=ldec[:, 0:N], initial=0.0,
                                               op0=ALU.mult, op1=ALU.add), ["ldec%d" % jp, "scanm"], ["Gc%d" % jp])
            for q, (o, C, sq) in enumerate(tiles):
                ACT(lambda e, q=q, o=o, C=C: e.activation(out=gC[:, j, q:q + 1], in_=Gc[:, o + C - 1:o + C], func=AF.Exp),
                    ["Gc%d" % jp, "gC"], ["gC"])
            ps_, pk = proj(wbk, wkk, c0, 128, N)
            shift_tile(ps_, pk, 6 + j, kx[:, 0:N], ["kx%d" % jp], nui())
            ACT(lambda e: e.activation(out=eneg[:, 0:N], in_=Gc[:, 0:N], func=AF.Exp, scale=-1.0), ["Gc%d" % jp], ["eneg%d" % jp])
            DVE(lambda e: e.tensor_tensor(out=eprev[:, 0:N], in0=Gc[:, 0:N], in1=ldec[:, 0:N], op=ALU.subtract),
                ["Gc%d" % jp, "ldec%d" % jp], ["eprev%d" % jp])
            ACT(lambda e: e.activation(out=eprev[:, 0:N], in_=eprev[:, 0:N], func=AF.Exp), ["eprev%d" % jp], ["eprev%d" % jp])
            for q, (o, C, sq) in enumerate(tiles):
                ACT(lambda e, o=o, C=C: e.activation(out=ehat[:, o:o + C], in_=Gc[:, o:o + C], func=AF.Exp, scale=-1.0),
                    ["Gc%d" % jp, "ehat%d" % jp], ["ehat%d" % jp])
                DVE(lambda e, o=o, C=C, q=q: e.tensor_scalar(out=ehat[:, o:o + C], in0=ehat[:, o:o + C],
                                                             scalar1=gC[:, j, q:q + 1], scalar2=None, op0=ALU.mult),
                    ["ehat%d" % jp, "gC"], ["ehat%d" % jp])
            DVE(lambda e: e.tensor_scalar(out=kkr[:, 0:N], in0=kx[:, 0:N], scalar1=PPc(l, "kk", j), scalar2=None,
                                          op0=ALU.mult), ["kx%d" % jp, pl], ["kkr%d" % jp])
            POOL(lambda e: e.tensor_tensor(out=kksq[:, 0:N], in0=kkr[:, 0:N], in1=kkr[:, 0:N], op=ALU.mult), ["kkr%d" % jp], ["kksq%d" % jp])
            ps2, pk2 = fbank()
            MM(ps2[:, 0:N], [(blk1[:], kksq[:, 0:N])], ["blk1", "kksq%d" % jp], pk2)
            ACT(lambda e, ps2=ps2: e.activation(out=rn[:, 0:N], in_=ps2[:, 0:N], func=AF.Sqrt, bias=1e-12, scale=1.0), pk2, ["rn%d" % jp])
            DVE(lambda e: e.reciprocal(out=rn[:, 0:N], in_=rn[:, 0:N]), ["rn%d" % jp], ["rn%d" % jp])
            DVE(lambda e: e.tensor_tensor(out=kkn[:, 0:N], in0=kkr[:, 0:N], in1=rn[:, 0:N], op=ALU.mult), ["kkr%d" % jp, "rn%d" % jp], ["kkn%d" % jp])
            DVE(lambda e: e.tensor_scalar(out=tk[:, 0:N], in0=aa[:, 0:N], scalar1=PPc(l, "ka", j),
                                          scalar2=omka[l][:, j:j + 1], op0=ALU.mult, op1=ALU.add),
                ["aa%d" % jp, pl, "omka%d" % l], ["tk%d" % jp])
            DVE(lambda e: e.tensor_tensor(out=kmod[:, 0:N], in0=kx[:, 0:N], in1=tk[:, 0:N], op=ALU.mult), ["kx%d" % jp, "tk%d" % jp], ["kmod%d" % jp])
            POOL(lambda e: e.tensor_tensor(out=bb[:, 0:N], in0=kkn[:, 0:N], in1=aa[:, 0:N], op=ALU.mult), ["kkn%d" % jp, "aa%d" % jp], ["bb%d" % jp])
            DVE(lambda e: e.tensor_tensor(out=kt_[:, j, 0:N], in0=kmod[:, 0:N], in1=eneg[:, 0:N], op=ALU.mult),
                ["kmod%d" % jp, "eneg%d" % jp, "kt__%d" % j], ["kt__%d" % j])
            POOL(lambda e: e.tensor_tensor(out=bt_[:, j, 0:N], in0=bb[:, 0:N], in1=eneg[:, 0:N], op=ALU.mult),
                 ["bb%d" % jp, "eneg%d" % jp, "bt__%d" % j], ["bt__%d" % j])
            DVE(lambda e: e.scalar_tensor_tensor(out=at_[:, j, 0:N], in0=kkn[:, 0:N], scalar=-1.0, in1=eprev[:, 0:N],
                                                 op0=ALU.mult, op1=ALU.mult), ["kkn%d" % jp, "eprev%d" % jp, "at__%d" % j], ["at__%d" % j])
            POOL(lambda e: e.tensor_tensor(out=khat[:, j, 0:N], in0=kmod[:, 0:N], in1=ehat[:, 0:N], op=ALU.mult),
                 ["kmod%d" % jp, "ehat%d" % jp, "khat_%d" % j], ["khat_%d" % j])
            DVE(lambda e: e.tensor_tensor(out=bhat[:, j, 0:N], in0=bb[:, 0:N], in1=ehat[:, 0:N], op=ALU.mult),
                ["bb%d" % jp, "ehat%d" % jp, "bhat_%d" % j], ["bhat_%d" % j])
            ps_, pk = proj(wbr, wkr, c0, 128, N)
            shift_tile(ps_, pk, j, rx[:, 0:N], ["rx%d" % jp], nui())
            ACT(lambda e: e.activation(out=epos[:, 0:N], in_=Gc[:, 0:N], func=AF.Exp), ["Gc%d" % jp], ["epos%d" % jp])
            DVE(lambda e: e.tensor_tensor(out=rt_[:, j, 0:N], in0=rx[:, 0:N], in1=epos[:, 0:N], op=ALU.mult),
                ["rx%d" % jp, "epos%d" % jp, "rt__%d" % j], ["rt__%d" % j])
            DVE(lambda e: e.scalar_tensor_tensor(out=rkp[:, j, 0:N], in0=rx[:, 0:N], scalar=PPc(l, "rk", j),
                                                 in1=kmod[:, 0:N], op0=ALU.mult, op1=ALU.mult),
                ["rx%d" % jp, pl, "kmod%d" % jp, "rkp_%d" % j], ["rkp_%d" % j])
            ps_, pk = proj(wbv, wkv, c0, 128, N)
            shift_tile(ps_, pk, 12 + j, vT[:, j, 0:N], ["vT_%d" % j], nui())

        for part, js in (("0", range(4)), ("1", range(4, 6))):
            wbk, wkk = load_group(l, "rwk" + part)
            wbr, wkr = load_group(l, "rwr" + part)
            wbv, wkv = load_group(l, "rwv" + part)
            for j in js:
                jtile(j, wbk, wkk, wbr, wkr, wbv, wkv, (j % 4) * 128)

        for gn, js in (("rwg0", range(4)), ("rwg1", range(4, 6))):
            wb, wk = load_group(l, gn)
            for j in js:
                ps_, pk = proj(wb, wk, (j % 4) * 128, 128, N)
                ACT(lambda e, ps_=ps_, j=j: e.activation(out=grw[:, j, 0:N], in_=ps_, func=AF.Silu), pk + ["grw_%d" % j], ["grw_%d" % j])

        for q, (o, C, sq) in enumerate(tiles):
            rwkv_tile(l, q, o, C, sq, kind, og, last, nq)
        if samp:
            for q, (o, C, sq) in enumerate(tiles):
                ps_, pk = fbank()
                TR(ps_[0:19, 0:128], sshift_o[:, :, q], ident_f[:], ["sshift_o", "ident_f"], pk)
                ACT(lambda e, ps_=ps_: e.copy(out=lnx[0:19, 0:128], in_=ps_[0:19, 0:128]), pk + ["lnx"], ["lnx"])
                STORE(o_shift["s"][l, sq].rearrange("(t p) -> t p", p=128), lnx[0:19, 0:128], ["lnx"])
        elif last:
            ps_, pk = fbank()
            TR(ps_[0:19, 0:128], shiftst[l][:, :], ident_f[:], ["shiftst%d" % l, "ident_f"], pk)
            ACT(lambda e, ps_=ps_: e.copy(out=lnx[0:19, 0:128], in_=ps_[0:19, 0:128]), pk + ["lnx"], ["lnx"])
            STORE(o_shift["p"][l, 0].rearrange("(t p) -> t p", p=128), lnx[0:19, 0:128], ["lnx"])

        S.label = 's5A'
        ensure_s5(l)
        wb, wk = load_group(l, "s5u")
        for j in range(4):
            ps_, pk = proj(wb, wk, j * 128, 128, N)
            ACT(lambda e, ps_=ps_, j=j: e.copy(out=u32[:, j, 0:N], in_=ps_), pk + ["u32_%d" % j], ["u32_%d" % j])
            DVE(lambda e, j=j: e.tensor_copy(out=uT[:, j, 0:N], in_=u32[:, j, 0:N]), ["u32_%d" % j, "uT_%d" % j], ["uT_%d" % j])
        wb, wk = load_group(l, "s5g")
        for j in range(4):
            ps_, pk = proj(wb, wk, j * 128, 128, N)
            ACT(lambda e, ps_=ps_, j=j: e.activation(out=gs5[:, j, 0:N], in_=ps_, func=AF.Silu), pk + ["gs5_%d" % j], ["gs5_%d" % j])
        s5_en(l, N)
        for q, (o, C, sq) in enumerate(tiles):
            s5_tile(l, q, o, C, sq, kind, og, last, nq)
        s5_out(l, N)
        ensure_s5(1 - l)

        S.label = 'mlA'
        S.label = 'mlA'
        for gi_ in range(4):
            wb, wk = load_group(l, "mlqk%d" % gi_)
            for jj in range(4):
                t = gi_ * 4 + jj
                ps_, pk = proj(wb, wk, jj * 96, 96, N)
                ACT(lambda e, ps_=ps_, t=t: e.copy(out=qkraw[:, t, 3:N + 3], in_=ps_), pk + ["qkraw_%d" % t], ["qkraw_%d" % t])
        qk = ["qkraw"]
        if not samp:
            ACT(lambda e: e.copy(out=qkraw[:, :, 0:3], in_=convst[l][:, :, :]), ["convst%d" % l] + qk, qk)
        else:
            ACT(lambda e: e.copy(out=qkraw[:, :, 0:3], in_=sconv[:, :, 0, :]), ["sconv"] + qk, qk)

        def wbc(jt, n_):
            return pq[l][:, :, jt:jt + 1].broadcast_to([96, 16, n_])

        pqk = "pq%d" % l
        DVE(lambda e: e.tensor_tensor(out=cvacc[:, :, 0:N], in0=qkraw[:, :, 0:N], in1=wbc(0, N), op=ALU.mult), qk + [pqk], ["cvacc"])
        POOL(lambda e: e.tensor_tensor(out=cvacc[:, :, 0:N], in0=cvacc[:, :, 0:N], in1=wbc(4, N), op=ALU.add), ["cvacc", pqk], ["cvacc"])
        for jt in range(1, 4):
            POOL(lambda e, jt=jt: e.tensor_tensor(out=cvtmp[:, :, 0:N], in0=qkraw[:, :, jt:N + jt], in1=wbc(jt, N), op=ALU.mult),
                 qk + [pqk, "cvtmp"], ["cvtmp"])
            DVE(lambda e: e.tensor_tensor(out=cvacc[:, :, 0:N], in0=cvacc[:, :, 0:N], in1=cvtmp[:, :, 0:N], op=ALU.add),
                ["cvacc", "cvtmp"], ["cvacc"])
        if samp:
            hk = "hbuf0"
            hb = hbuf
            POOL(lambda e: e.tensor_copy(out=hb[:, :, 0:3], in_=sconv[:, :, 1, :]), ["sconv", hk], [hk])
            POOL(lambda e: e.tensor_copy(out=hb[:, :, 3:6], in_=qkraw[:, :, 67:70]), qk + [hk], [hk])
            f3 = cvacc[:, :, 64:67]
            DVE(lambda e: e.tensor_tensor(out=f3, in0=hb[:, :, 0:3], in1=wbc(0, 3), op=ALU.mult), [hk, pqk, "cvacc"], ["cvacc"])
            DVE(lambda e: e.tensor_tensor(out=f3, in0=f3, in1=wbc(4, 3), op=ALU.add), [pqk, "cvacc"], ["cvacc"])
            for jt in range(1, 4):
                DVE(lambda e, jt=jt: e.tensor_tensor(out=cvtmp[:, :, 0:3], in0=hb[:, :, jt:jt + 3], in1=wbc(jt, 3), op=ALU.mult),
                    [hk, pqk, "cvtmp"], ["cvtmp"])
                DVE(lambda e: e.tensor_tensor(out=f3, in0=f3, in1=cvtmp[:, :, 0:3], op=ALU.add), ["cvacc", "cvtmp"], ["cvacc"])
        ACT(lambda e: e.activation(out=qkT[:, :, 0:N], in_=cvacc[:, :, 0:N], func=AF.Silu), ["cvacc", "qkT"], ["qkT"])
        POOL(lambda e: e.tensor_scalar(out=qkT[:, 8:16, 0:N], in0=qkT[:, 8:16, 0:N], scalar1=1.0 / math.sqrt(192.0), scalar2=None,
                                       op0=ALU.mult), ["qkT"], ["qkT"])
        if not samp:
            ACT(lambda e: e.copy(out=convst[l][:, :, :], in_=qkraw[:, :, N:N + 3]), qk + ["convst%d" % l], ["convst%d" % l])
        conv_out(l, N, tiles, kind, og, last)
        wb0, wk0 = load_group(l, "mlv0")
        wb1, wk1 = load_group(l, "mlv1")
        for q, (o, C, sq) in enumerate(tiles):
            ps_, pk = fpair()
            MM(ps_[0:C, 0, 0:512], [(xT[:, kc, o:o + C], wb0[:, kc, 0:512]) for kc in range(8)], [wk0, "xT"], [pk[0]])
            MM(ps_[0:C, 1, 0:264], [(xT[:, kc, o:o + C], wb1[:, kc, 0:264]) for kc in range(8)], [wk1, "xT"], [pk[1]])
            vk = ["vaug"]
            ACT(lambda e, ps_=ps_, q=q, C=C: e.copy(out=vaug[0:C, q, 0:2, 0:192],
                                                    in_=ps_[0:C, 0, 0:384].rearrange("p (h d) -> p h d", d=192)), [pk[0]] + vk, vk)
            ACT(lambda e, ps_=ps_, q=q, C=C: e.copy(out=vaug[0:C, q, 2, 0:128], in_=ps_[0:C, 0, 384:512]), [pk[0]] + vk, vk)
            DVE(lambda e, ps_=ps_, q=q, C=C: e.tensor_copy(out=vaug[0:C, q, 2, 128:192], in_=ps_[0:C, 1, 0:64]), [pk[1]] + vk, vk)
            DVE(lambda e, ps_=ps_, q=q, C=C: e.tensor_copy(out=vaug[0:C, q, 3, 0:192], in_=ps_[0:C, 1, 64:256]), [pk[1]] + vk, vk)
            POOL(lambda e, q=q, C=C: e.memset(vaug[0:C, q, :, 192:193], 1.0), vk, vk)
            DVE(lambda e, ps_=ps_, q=q, C=C: e.tensor_tensor(out=iftok[0:C, q, :], in0=ps_[0:C, 1, 256:264], in1=bif[l][0:C, :],
                                                             op=ALU.add), [pk[1], "bif%d" % l, "iftok"], ["iftok"])
        wbo0, wko0 = load_group(l, "mlo0")
        wbo1, wko1 = load_group(l, "mlo1")
        for j in range(6):
            wb, wk = (wbo0, wko0) if j < 4 else (wbo1, wko1)
            ps_, pk = proj(wb, wk, (j % 4) * 128, 128, N)
            ACT(lambda e, ps_=ps_, j=j: e.activation(out=ogT[:, j, 0:N], in_=ps_, func=AF.Sigmoid), pk + ["ogT_%d" % j], ["ogT_%d" % j])
        wbz0, wkz0 = load_group(l, "mlz0")
        wbz1, wkz1 = load_group(l, "mlz1")
        for j in range(6):
            wb, wk = (wbz0, wkz0) if j < 4 else (wbz1, wkz1)
            ps_, pk = proj(wb, wk, (j % 4) * 128, 128, N)
            zsl = zsl_2[j % 2]
            ACT(lambda e, ps_=ps_, zsl=zsl: e.activation(out=zsl[:, 0:N], in_=ps_, func=AF.Silu), pk, ["zsl%d" % (j % 2)])
            DVE(lambda e, j=j, zsl=zsl: e.scalar_tensor_tensor(out=ogT[:, j, 0:N], in0=ogT[:, j, 0:N], scalar=PPc(l, "mlg", j),
                                                               in1=zsl[:, 0:N], op0=ALU.mult, op1=ALU.mult),
                ["ogT_%d" % j, "zsl%d" % (j % 2), pl], ["ogT_%d" % j])
        for q, (o, C, sq) in enumerate(tiles):
            ml_tile(l, q, o, C, sq, kind, og, last, nq)
        ml_out(l, N, tiles)

    def conv_out(l, N, tiles, kind, og, last):
        if kind == "sample":
            ends = [(o + C - 3, sq) for (o, C, sq) in tiles]
        elif last:
            ends = [(N - 3, 0)]
        else:
            return
        for (e0, sq) in ends:
            for hf in range(2):
                for i2 in range(2):
                    i = hf * 2 + i2
                    wbi, wki = load_group(l, "mlqk%d" % i)
                    ps_, pk = fbank()
                    MM(ps_[0:3, 0:384], [(xT[:, kc, e0:e0 + 3], wbi[:, kc, 0:384]) for kc in range(8)], [wki, "xT"], pk)
                    ACT(lambda e, i2=i2, ps_=ps_: e.copy(out=lnt[0:3, i2 * 384:(i2 + 1) * 384], in_=ps_[0:3, 0:384]), pk + ["lnt"], ["lnt"])
                STORE(o_conv[og][l, sq, :, hf * 768:(hf + 1) * 768], lnt[0:3, 0:768], ["lnt"])

    def rwkv_tile(l, q, o, C, sq, kind, og, last, nq):
        S.label = 'rwT'
        tb_ = TB[q % 2]
        pz = tb_['z']
        vtok, khtok, bhtok, Nsb, NTsb, Msb, P1sb, P2sb = (tb_[k_] for k_ in ('vtok', 'khtok', 'bhtok', 'Nsb', 'NTsb', 'Msb', 'P1sb', 'P2sb'))
        samp = kind == "sample"
        sk = "s0t%d" % l
        sl = slice(o, o + C)
        PARTS = [(0, 0), (1, 64)]
        if samp:
            LOAD(stg[:], st_wkv[l, sq].rearrange("h v k -> v h k"), [], ["stg"])
            for j in range(6):
                ps_, pk = fbank()
                TR(ps_[:, 0:64], stg[:, 2 * j:2 * j + 2, :].rearrange("p a b -> p (a b)"), ident_f[0:64, 0:64], ["stg", "ident_f"], pk)
                ACT(lambda e, ps_=ps_, j=j: e.copy(out=S0T[l][:, j, :], in_=ps_[:, 0:64]), pk + [sk], [sk])
        ACT(lambda e: e.copy(out=S0Tb[:], in_=S0T[l][:]), [sk], ["s0tb"])

        def both(fn_):
            if C == 64:
                fn_(slice(0, 128))
            else:
                for par, pb in PARTS:
                    fn_(slice(pb, pb + C))

        for src, srck, dst, dk in ((vT, "vT", vtok, "vtok" + pz), (khat, "khat", khtok, "khtok" + pz), (bhat, "bhat", bhtok, "bhtok" + pz)):
            def fnt(e, src=src):
                for j in range(6):
                    for par, pb in PARTS:
                        ins = e.transpose(out=PSB[pb:pb + C, par, j * 64:(j + 1) * 64], in_=src[pb:pb + 64, j, sl],
                                          identity=ident_b[pb:pb + 64, pb:pb + 64])
                return ins
            PE(fnt, [srck, "ident_b"], ["pb0", "pb1"])
            for par, pb in PARTS:
                ACT(lambda e, dst=dst, par=par, pb=pb: e.copy(out=dst[pb:pb + C, :, :],
                                                                in_=PSB[pb:pb + C, par, 0:384].rearrange("p (j d) -> p j d", d=64)),
                    ["pb%d" % par, dk], [dk])

        def score(lt, lk_, rt2, rk_, mask, mk, dst, dk):
            ps_, pk = fpair()
            def fn(e, ps_=ps_):
                for j in range(6):
                    for par, pb in PARTS:
                        ins = e.matmul(ps_[pb:pb + C, par, j * 64:j * 64 + C], lhsT=lt[pb:pb + 64, j, sl],
                                       rhs=rt2[pb:pb + 64, j, sl], start=True, stop=True)
                return ins
            PE(fn, [lk_, rk_], pk)
            for par, pb in PARTS:
                DVE(lambda e, ps_=ps_, par=par, pb=pb: e.tensor_tensor(
                    out=dst[pb:pb + C, :, 0:C], in0=ps_[pb:pb + C, par, 0:384].rearrange("p (j t) -> p j t", t=64)[:, :, 0:C],
                    in1=mask[pb:pb + C, :, 0:C], op=ALU.mult), [pk[par], mk, dk], [dk])

        def evac(ps_, pk, dst, dk, eng=ACT):
            for par, pb in PARTS:
                if par == 0:
                    ACT(lambda e, par=par, pb=pb: e.copy(out=dst[pb:pb + C, :, :],
                                                         in_=ps_[pb:pb + C, par, 0:384].rearrange("p (j d) -> p j d", d=64)),
                        [pk[par], dk], [dk])
                else:
                    DVE(lambda e, par=par, pb=pb: e.tensor_copy(out=dst[pb:pb + C, :, :],
                                                                in_=ps_[pb:pb + C, par, 0:384].rearrange("p (j d) -> p j d", d=64)),
                        [pk[par], dk], [dk])

        def evac_sq(ps_, pk, dst, dk):
            for par, pb in PARTS:
                DVE(lambda e, par=par, pb=pb: e.tensor_copy(
                    out=dst[pb:pb + C, :, 0:C], in_=ps_[pb:pb + C, par, 0:384].rearrange("p (j t) -> p j t", t=64)[:, :, 0:C]),
                    [pk[par], dk], [dk])

        score(bt_, "bt_", at_, "at_", m_strict, "m_strict", Nsb[0], "Nsb0" + pz)
        score(at_, "at_", bt_, "bt_", m_lower, "m_lower", NTsb[0], "NTsb0" + pz)
        score(kt_, "kt_", at_, "at_", m_strict, "m_strict", Msb, "Msb" + pz)

        ps_, pk = fpair()
        def fnx(e, ps_=ps_):
            for j in range(6):
                for par, pb in PARTS:
                    out = ps_[pb:pb + C, par, j * 64:(j + 1) * 64]
                    e.matmul(out, lhsT=at_[pb:pb + 64, j, sl], rhs=S0Tb[pb:pb + 64, j, :], start=True, stop=False)
                    ins = e.matmul(out, lhsT=Msb[pb:pb + C, j, 0:C], rhs=vtok[pb:pb + C, j, :], start=False, stop=True)
            return ins
        PE(fnx, ["at_", "s0tb", "Msb" + pz, "vtok" + pz], pk)
        evac(ps_, pk, Ysb[0], "Ysb0")
        lev = int(round(math.log2(C)))
        cur = 0
        for lv in range(lev):
            Pc, PTc, Yc = Nsb[cur], NTsb[cur], Ysb[cur]
            Pn, PTn, Yn = Nsb[1 - cur], NTsb[1 - cur], Ysb[1 - cur]
            ps_, pk = fpair()
            def fny(e, ps_=ps_, Pc=Pc, Yc=Yc):
                for j in range(6):
                    for par, pb in PARTS:
                        out = ps_[pb:pb + C, par, j * 64:(j + 1) * 64]
                        e.matmul(out, lhsT=ident_b[pb:pb + C, pb:pb + C], rhs=Yc[pb:pb + C, j, :], start=True, stop=False)
                        ins = e.matmul(out, lhsT=Pc[pb:pb + C, j, 0:C], rhs=Yc[pb:pb + C, j, :], start=False, stop=True)
                return ins
            PE(fny, ["Nsb%d" % cur + pz, "Ysb%d" % cur, "ident_b"], pk)
            evac(ps_, pk, Yn, "Ysb%d" % (1 - cur))
            if lv < lev - 1:
                ps2, pk2 = fpair()
                def fnp(e, ps2=ps2, Pc=Pc, PTc=PTc):
                    for j in range(6):
                        for par, pb in PARTS:
                            ins = e.matmul(ps2[pb:pb + C, par, j * 64:j * 64 + C], lhsT=PTc[pb:pb + C, j, 0:C],
                                           rhs=Pc[pb:pb + C, j, 0:C], start=True, stop=True)
                    return ins
                PE(fnp, ["Nsb%d" % cur + pz, "NTsb%d" % cur + pz], pk2)
                evac_sq(ps2, pk2, Pn, "Nsb%d" % (1 - cur) + pz)
                if lv < lev - 2:
                    ps3, pk3 = fpair()
                    def fnq(e, ps3=ps3, Pc=Pc, PTc=PTc):
                        for j in range(6):
                            for par, pb in PARTS:
                                ins = e.matmul(ps3[pb:pb + C, par, j * 64:j * 64 + C], lhsT=Pc[pb:pb + C, j, 0:C],
                                               rhs=PTc[pb:pb + C, j, 0:C], start=True, stop=True)
                        return ins
                    PE(fnq, ["Nsb%d" % cur + pz, "NTsb%d" % cur + pz], pk3)
                    evac_sq(ps3, pk3, PTn, "NTsb%d" % (1 - cur) + pz)
            cur = 1 - cur
        UT = Ysb[cur]
        uk = "Ysb%d" % cur
        score(kt_, "kt_", rt_, "rt_", m_incl, "m_incl", P1sb, "P1sb" + pz)
        score(bt_, "bt_", rt_, "rt_", m_incl, "m_incl", P2sb, "P2sb" + pz)
        ps_, pk = fpair()
        def fno(e, ps_=ps_, UT=UT):
            for j in range(6):
                for par, pb in PARTS:
                    out = ps_[pb:pb + C, par, j * 64:(j + 1) * 64]
                    e.matmul(out, lhsT=rt_[pb:pb + 64, j, sl], rhs=S0Tb[pb:pb + 64, j, :], start=True, stop=False)
                    e.matmul(out, lhsT=P1sb[pb:pb + C, j, 0:C], rhs=vtok[pb:pb + C, j, :], start=False, stop=False)
                    ins = e.matmul(out, lhsT=P2sb[pb:pb + C, j, 0:C], rhs=UT[pb:pb + C, j, :], start=False, stop=True)
            return ins
        PE(fno, ["rt_", "s0tb", "P1sb" + pz, "P2sb" + pz, "vtok" + pz, uk], pk)
        evac(ps_, pk, yo, "yo")
        psr, pkr = fpair()
        def fnr(e, psr=psr):
            for j in range(6):
                for par, pb in PARTS:
                    ins = e.matmul(psr[pb:pb + C, par, j:j + 1], lhsT=rkp[pb:pb + 64, j, sl], rhs=onesb[pb:pb + 64, 0:1],
                                   start=True, stop=True)
            return ins
        PE(fnr, ["rkp", "onesb"], pkr)
        for par, pb in PARTS:
            ACT(lambda e, par=par, pb=pb, psr=psr: e.copy(out=rkd[pb:pb + C, :], in_=psr[pb:pb + C, par, 0:6]), [pkr[par], "rkd"], ["rkd"])
        both(lambda P: DVE(lambda e: e.tensor_reduce(out=ystat[P, 0, :], in_=yo[P], axis=AX.X, op=ALU.add), ["yo", "ystat"], ["ystat"]))
        both(lambda P: DVE(lambda e: e.tensor_scalar(out=ystat[P, 0, :], in0=ystat[P, 0, :], scalar1=1.0 / 64, scalar2=None, op0=ALU.mult),
                           ["ystat"], ["ystat"]))
        both(lambda P: DVE(lambda e: e.tensor_tensor(out=ycen[P], in0=yo[P], in1=ystat[P, 0, :].unsqueeze(2).broadcast_to([P.stop - P.start, 6, 64]),
                                                     op=ALU.subtract), ["yo", "ystat", "ycen"], ["ycen"]))
        both(lambda P: POOL(lambda e: e.tensor_tensor(out=ysq[P], in0=ycen[P], in1=ycen[P], op=ALU.mult), ["ycen", "ysq"], ["ysq"]))
        both(lambda P: DVE(lambda e: e.tensor_reduce(out=ystat[P, 1, :], in_=ysq[P], axis=AX.X, op=ALU.add), ["ysq", "ystat"], ["ystat"]))
        both(lambda P: ACT(lambda e: e.activation(out=ystat[P, 2, :], in_=ystat[P, 1, :], func=AF.Sqrt, bias=RW_GN_EPS, scale=1.0 / 64),
                           ["ystat"], ["ystat"]))
        both(lambda P: DVE(lambda e: e.reciprocal(out=ystat[P, 2, :], in_=ystat[P, 2, :]), ["ystat"], ["ystat"]))
        both(lambda P: DVE(lambda e: e.tensor_tensor(out=ycen[P], in0=ycen[P], in1=ystat[P, 2, :].unsqueeze(2).broadcast_to([P.stop - P.start, 6, 64]),
                                                     op=ALU.mult), ["ycen", "ystat"], ["ycen"]))
        both(lambda P: DVE(lambda e: e.tensor_tensor(out=ycen[P].rearrange("p j d -> p (j d)"), in0=ycen[P].rearrange("p j d -> p (j d)"),
                                                     in1=rwln[l][P, 0, :], op=ALU.mult), ["ycen", "rwln%d" % l], ["ycen"]))
        both(lambda P: POOL(lambda e: e.tensor_tensor(out=ycen[P].rearrange("p j d -> p (j d)"), in0=ycen[P].rearrange("p j d -> p (j d)"),
                                                      in1=rwln[l][P, 1, :], op=ALU.add), ["ycen", "rwln%d" % l], ["ycen"]))
        both(lambda P: POOL(lambda e: e.tensor_tensor(out=ysq[P], in0=vtok[P], in1=rkd[P, :].unsqueeze(2).broadcast_to([P.stop - P.start, 6, 64]),
                                                      op=ALU.mult), ["vtok" + pz, "rkd", "ysq"], ["ysq"]))
        both(lambda P: DVE(lambda e: e.tensor_tensor(out=ytb[P], in0=ycen[P], in1=ysq[P], op=ALU.add), ["ycen", "ysq", "ytb"], ["ytb"]))
        def fnb(e):
            for j in range(6):
                for par, pb in PARTS:
                    ins = e.transpose(out=PSB[pb:pb + 64, par, j * 64:j * 64 + C], in_=ytb[pb:pb + C, j, :],
                                      identity=ident_b[pb:pb + C, pb:pb + C])
            return ins
        PE(fnb, ["ytb", "ident_b"], ["pb0", "pb1"])
        for par, pb in PARTS:
            DVE(lambda e, par=par, pb=pb: e.tensor_tensor(out=yrwT[pb:pb + 64, :, sl],
                                                          in0=PSB[pb:pb + 64, par, 0:384].rearrange("p (j t) -> p j t", t=64)[:, :, 0:C],
                                                          in1=grw[pb:pb + 64, :, sl], op=ALU.mult), ["pb%d" % par, "grw", "yrwT"], ["yrwT"])
        ps_, pk = fpair()
        def fns(e, ps_=ps_, UT=UT):
            for j in range(6):
                for par, pb in PARTS:
                    out = ps_[pb:pb + 64, par, j * 64:(j + 1) * 64]
                    e.matmul(out, lhsT=khtok[pb:pb + C, j, :], rhs=vtok[pb:pb + C, j, :], start=True, stop=False)
                    ins = e.matmul(out, lhsT=bhtok[pb:pb + C, j, :], rhs=UT[pb:pb + C, j, :], start=False, stop=True)
            return ins
        PE(fns, ["khtok" + pz, "bhtok" + pz, "vtok" + pz, uk], pk)
        for j in range(6):
            for par, pb in PARTS:
                DVE(lambda e, j=j, ps_=ps_, par=par, pb=pb: e.scalar_tensor_tensor(
                    out=S0T[l][pb:pb + 64, j, :], in0=S0T[l][pb:pb + 64, j, :], scalar=gC[pb:pb + 64, j, q:q + 1],
                    in1=ps_[pb:pb + 64, par, j * 64:(j + 1) * 64], op0=ALU.mult, op1=ALU.add), [pk[par], sk, "gC"], [sk])
        if samp or (last and q == nq - 1):
            b_ = sq if samp else 0
            for j in range(6):
                ps2, pk2 = fbank()
                TR(ps2[0:64, 0:128], S0T[l][:, j, :], ident_f[:], [sk, "ident_f"], pk2)
                ACT(lambda e, ps2=ps2, j=j: e.copy(out=stg[:, 2 * j:2 * j + 2, :].rearrange("p a b -> p (a b)"), in_=ps2[0:64, 0:128]),
                    pk2 + ["stg"], ["stg"])
            STORE(o_wkv[og][l, b_].rearrange("h v k -> v h k"), stg[:], ["stg"])

    def s5_en(l, N):
        S.label = 's5T'
        C = N
        sl = slice(0, N)
        for ut in range(4):
            ps_, pk = fpair()
            MM(ps_[0:C, 0, :], [(uT[:, ut, sl], bblkB[:, ut, 0:512])], ["uT", "bblkB"], [pk[0]])
            MM(ps_[0:C, 1, :], [(uT[:, ut, sl], bblkB[:, ut, 512:1024])], ["uT", "bblkB"], [pk[1]])
            pvv = ps_[0:C].rearrange("p b (i c q) -> p (b i) c q", i=2, c=2)
            bur, bui = pvv[:, :, 0, :], pvv[:, :, 1, :]
            enr, eni = EnB[0:C, ut * 4:(ut + 1) * 4, 0, :], EnB[0:C, ut * 4:(ut + 1) * 4, 1, :]
            wr_, wi_ = wtok[0:C, ut * 4:(ut + 1) * 4, 0, :], wtok[0:C, ut * 4:(ut + 1) * 4, 1, :]
            ek = ["EnB"]
            DVE(lambda e, bur=bur, enr=enr: e.tensor_tensor(out=s5a[0:C], in0=bur, in1=enr, op=ALU.mult), pk + ek, ["s5a"])
            DVE(lambda e, bui=bui, eni=eni: e.tensor_tensor(out=s5b[0:C], in0=bui, in1=eni, op=ALU.mult), pk + ek, ["s5b"])
            POOL(lambda e, wr_=wr_: e.tensor_tensor(out=wr_, in0=s5a[0:C], in1=s5b[0:C], op=ALU.subtract), ["s5a", "s5b", "wtok"], ["wtok"])
            DVE(lambda e, bur=bur, eni=eni: e.tensor_tensor(out=s5a[0:C], in0=bur, in1=eni, op=ALU.mult), pk + ek + ["s5a"], ["s5a"])
            DVE(lambda e, bui=bui, enr=enr: e.tensor_tensor(out=s5b[0:C], in0=bui, in1=enr, op=ALU.mult), pk + ek + ["s5b"], ["s5b"])
            POOL(lambda e, wi_=wi_: e.tensor_tensor(out=wi_, in0=s5a[0:C], in1=s5b[0:C], op=ALU.add), ["s5a", "s5b", "wtok"], ["wtok"])

    def s5_tile(l, q, o, C, sq, kind, og, last, nq):
        S.label = 's5T'
        samp = kind == "sample"
        hk = "h0%d" % l
        sl = slice(o, o + C)
        if samp:
            for c, srcd in ((0, st_s5re), (1, st_s5im)):
                LOAD(lnx[0:16, c * 128:(c + 1) * 128], srcd[l, sq].rearrange("(i g) p -> i (g p)", g=2), [], ["lnx"])
            for c in range(2):
                ps_, pk = fbank()
                TR(ps_[:, 0:16], lnx[0:16, c * 128:(c + 1) * 128], ident_f[0:16, 0:16], ["lnx", "ident_f"], pk)
                ACT(lambda e, ps_=ps_, c=c: e.copy(out=h0[l][:, c, :], in_=ps_[:, 0:16]), pk + [hk], [hk])
        for c, Gd, gk in ((0, Gr, "Gr"), (1, Gi, "Gi")):
            ps_, pk = fpair()
            def fnc(e, ps_=ps_, c=c):
                for i in range(16):
                    ins = e.matmul(ps_[:, i // 8, (i % 8) * 64:(i % 8) * 64 + C], lhsT=wtok[o:o + C, i, c, :], rhs=tri2[o:o + C, 0:C],
                                   start=True, stop=True)
                return ins
            PE(fnc, ["wtok", "tri2"], pk)
            DVE(lambda e, ps_=ps_, Gd=Gd, c=c: e.tensor_tensor(
                out=Gd[:, :, 0:C].rearrange("p (b i) t -> p b i t", b=2),
                in0=ps_[:, :, :].rearrange("p b (i t) -> p b i t", t=64)[:, :, :, 0:C],
                in1=h0[l][:, c, :].rearrange("p (b i) -> p b i", b=2).unsqueeze(3).broadcast_to([128, 2, 8, C]), op=ALU.add),
                pk + [hk], [gk])
        er, ei = EpB[:, 0, :, 0:C], EpB[:, 1, :, 0:C]
        ek = ["EpB"]
        DVE(lambda e: e.tensor_tensor(out=hA[:, :, 0:C], in0=Gr[:, :, 0:C], in1=er, op=ALU.mult), ["Gr"] + ek, ["hA"])
        DVE(lambda e: e.tensor_tensor(out=hB[:, :, 0:C], in0=Gi[:, :, 0:C], in1=ei, op=ALU.mult), ["Gi"] + ek, ["hB"])
        DVE(lambda e: e.tensor_tensor(out=hre[:, :, sl], in0=hA[:, :, 0:C], in1=hB[:, :, 0:C], op=ALU.subtract), ["hA", "hB", "hre"], ["hre"])
        POOL(lambda e: e.tensor_tensor(out=hC[:, :, 0:C], in0=Gi[:, :, 0:C], in1=er, op=ALU.mult), ["Gi"] + ek, ["hC"])
        POOL(lambda e: e.tensor_tensor(out=hD[:, :, 0:C], in0=Gr[:, :, 0:C], in1=ei, op=ALU.mult), ["Gr"] + ek, ["hD"])
        DVE(lambda e: e.scalar_tensor_tensor(out=himn[:, :, sl], in0=hC[:, :, 0:C], scalar=-1.0, in1=hD[:, :, 0:C],
                                             op0=ALU.mult, op1=ALU.subtract), ["hC", "hD", "himn"], ["himn"])
        DVE(lambda e: e.tensor_tensor(out=h0[l][:, 0, :], in0=hA[:, :, C - 1], in1=hB[:, :, C - 1], op=ALU.subtract),
            ["hA", "hB", hk, "Gr", "Gi"], [hk])
        DVE(lambda e: e.tensor_tensor(out=h0[l][:, 1, :], in0=hC[:, :, C - 1], in1=hD[:, :, C - 1], op=ALU.add), ["hC", "hD", hk], [hk])
        if samp or (last and q == nq - 1):
            b_ = sq if samp else 0
            for c, dd in ((0, o_s5re), (1, o_s5im)):
                ps3, pk3 = fbank()
                TR(ps3[0:16, 0:128], h0[l][:, c, :], ident_f[:], [hk, "ident_f"], pk3)
                ACT(lambda e, ps3=ps3, c=c: e.copy(out=lnx[0:16, c * 128:(c + 1) * 128], in_=ps3[0:16, 0:128]), pk3 + ["lnx"], ["lnx"])
                STORE(dd[og][l, b_].rearrange("(i g) p -> i (g p)", g=2), lnx[0:16, c * 128:(c + 1) * 128], ["lnx"])

    def s5_out(l, N):
        S.label = 's5T'
        C = N
        sl = slice(0, N)
        ps_, pk = fbank()
        def fny(e, ps_=ps_):
            for ut in range(4):
                for hf in range(2):
                    out = ps_[hf * 64:(hf + 1) * 64, ut * 128:ut * 128 + C]
                    n_ = 0
                    for ii in range(2):
                        i = ut * 4 + hf * 2 + ii
                        for c, hsrc in ((0, hre), (1, himn)):
                            ins = e.matmul(out, lhsT=cpadB[:, i, c, :], rhs=hsrc[:, i, 0:C], start=(n_ == 0), stop=(n_ == 3))
                            n_ += 1
            return ins
        PE(fny, ["cpadB", "hre", "himn"], pk)
        for ut in range(4):
            DVE(lambda e, ut=ut, ps_=ps_: e.scalar_tensor_tensor(out=yv[:, ut, 0:C], in0=u32[:, ut, sl], scalar=PPc(l, "s5d", ut),
                                                                 in1=ps_[:, ut * 128:ut * 128 + C], op0=ALU.mult, op1=ALU.add),
                pk + ["u32", "pp%d" % l, "yv"], ["yv"])
        POOL(lambda e: e.tensor_tensor(out=gt[:, :, 0:C], in0=yv[:, :, 0:C], in1=yv[:, :, 0:C], op=ALU.mult), ["yv"], ["gt"])
        DVE(lambda e: e.tensor_scalar(out=gt[:, :, 0:C], in0=gt[:, :, 0:C], scalar1=0.044715, scalar2=1.0, op0=ALU.mult, op1=ALU.add),
            ["gt"], ["gt"])
        DVE(lambda e: e.tensor_tensor(out=gt[:, :, 0:C], in0=gt[:, :, 0:C], in1=yv[:, :, 0:C], op=ALU.mult), ["gt", "yv"], ["gt"])
        ACT(lambda e: e.activation(out=gsg[:, :, 0:C], in_=gt[:, :, 0:C], func=AF.Sigmoid, scale=1.5957691216057308), ["gt"], ["gsg"])
        DVE(lambda e: e.tensor_tensor(out=gl[:, :, 0:C], in0=yv[:, :, 0:C], in1=gsg[:, :, 0:C], op=ALU.mult), ["yv", "gsg"], ["gl"])
        ACT(lambda e: e.copy(out=glb[:, :, 0:C], in_=gl[:, :, 0:C]), ["gl"], ["glb"])
        ps2, pk2 = fbank()
        def fng(e):
            for ct in range(4):
                for kc in range(4):
                    ins = e.matmul(ps2[:, ct * 128:ct * 128 + C], lhsT=wglu[l][:, kc, ct * 128:(ct + 1) * 128], rhs=glb[:, kc, 0:C],
                                   start=(kc == 0), stop=(kc == 3))
            return ins
        PE(fng, ["wglu%d" % l, "glb"], pk2)
        for ct in range(4):
            ACT(lambda e, ct=ct: e.activation(out=sgl[:, ct, 0:C], in_=ps2[:, ct * 128:ct * 128 + C], func=AF.Sigmoid,
                                              bias=PPc(l, "bglu", ct), scale=1.0), pk2 + ["pp%d" % l, "sgl"], ["sgl"])
        POOL(lambda e: e.tensor_tensor(out=gl[:, :, 0:C], in0=gl[:, :, 0:C], in1=sgl[:, :, 0:C], op=ALU.mult), ["gl", "sgl"], ["gl"])
        DVE(lambda e: e.tensor_tensor(out=ys5T[:, :, sl], in0=gl[:, :, 0:C], in1=gs5[:, :, sl], op=ALU.mult),
            ["gl", "gs5", "ys5T"], ["ys5T"])

    def ml_tile(l, q, o, C, sq, kind, og, last, nq):
        S.label = 'mlT'
        samp = kind == "sample"
        ck, mk_ = "cst%d" % l, "mst%d" % l
        sl = slice(o, o + C)
        if samp:
            for kt in range(2):
                LOAD(Cst[l][:, kt, :, 0:192], st_c[l, sq, :, kt * 96:(kt + 1) * 96, :].rearrange("h p v -> p h v"), [ck], [ck])
                LOAD(Cst[l][:, kt, :, 192:193], st_n[l, sq, :, kt * 96:(kt + 1) * 96].rearrange("h (p o) -> p h o", o=1), [ck], [ck], slow=True)
            LOAD(mst[l][:], st_m[l, sq].rearrange("(h o) -> h o", o=1), [], [mk_], slow=True)
        ik = "iftok"
        ACT(lambda e: e.activation(out=lfi[0:C, 4:8], in_=iftok[0:C, q, 4:8], func=AF.Sigmoid), [ik], ["lfi"])
        ACT(lambda e: e.activation(out=lfi[0:C, 4:8], in_=lfi[0:C, 4:8], func=AF.Ln), ["lfi"], ["lfi"])
        ps_, pk = fbank()
        MM(ps_[0:C, 0:4], [(tri_f[0:C, 0:C], lfi[0:C, 4:8])], ["tri_f", "lfi"], pk)
        ps7, pk7 = fbank()
        MM(ps7[0:4, 0:1], [(lfi[0:C, 4:8], ones_f[0:C, 0:1])], ["ones_f", "lfi"], pk7)
        ACT(lambda e: e.copy(out=bcs[0:C, :], in_=ps_[0:C, 0:4]), pk, ["bcs"])
        ACT(lambda e: e.copy(out=bend[:], in_=ps7[0:4, 0:1]), pk7, ["bend"])
        DVE(lambda e: e.tensor_tensor(out=zz[0:C, :], in0=iftok[0:C, q, 0:4], in1=bcs[0:C, :], op=ALU.subtract), [ik, "bcs"], ["zz"])
        ps2, pk2 = fbank()
        TR(ps2[0:4, 0:C], zz[0:C, :], ident_f[0:C, 0:C], ["zz", "ident_f"], pk2)
        DVE(lambda e: e.tensor_reduce(out=zmax[:], in_=ps2[0:4, 0:C], axis=AX.X, op=ALU.max), pk2, ["zmax"])
        DVE(lambda e: e.tensor_tensor(out=mu4[:], in0=zmax[:], in1=mst[l][:], op=ALU.max), ["zmax", mk_], ["mu4"])
        DVE(lambda e: e.tensor_tensor(out=f4[:], in0=mst[l][:], in1=mu4[:], op=ALU.subtract), [mk_, "mu4"], ["f4"])
        ACT(lambda e: e.activation(out=f4[:], in_=f4[:], func=AF.Exp), ["f4"], ["f4"])
        DVE(lambda e: e.tensor_tensor(out=mst[l][:], in0=bend[:], in1=mu4[:], op=ALU.add), ["bend", "mu4", "f4", mk_], [mk_])
        DVE(lambda e: e.tensor_scalar(out=dg[:, 0:4], in0=ident_f[0:4, 0:4], scalar1=mu4[:, 0:1], scalar2=None, op0=ALU.mult),
            ["ident_f", "mu4"], ["dg"])
        DVE(lambda e: e.tensor_scalar(out=dg[:, 4:8], in0=ident_f[0:4, 0:4], scalar1=f4[:, 0:1], scalar2=None, op0=ALU.mult),
            ["ident_f", "f4", "dg"], ["dg"])
        ps3, pk3 = fbank()
        MM(ps3[:, 0:8], [(ones_f[0:4, :], dg[:, :])], ["ones_f", "dg"], pk3)
        ACT(lambda e: e.copy(out=bc8[:], in_=ps3[:, 0:8]), pk3, ["bc8"])
        DVE(lambda e: e.tensor_tensor(out=ee[0:C, :], in0=zz[0:C, :], in1=bc8[0:C, 0:4], op=ALU.subtract), ["zz", "bc8"], ["ee"])
        ACT(lambda e: e.activation(out=ee[0:C, :], in_=ee[0:C, :], func=AF.Exp), ["ee"], ["ee"])
        DVE(lambda e: e.tensor_tensor(out=clampt[0:C, :], in0=bcs[0:C, :], in1=bc8[0:C, 0:4], op=ALU.add), ["bcs", "bc8"], ["clampt"])
        ACT(lambda e: e.activation(out=clampt[0:C, :], in_=clampt[0:C, :], func=AF.Exp, scale=-1.0), ["clampt"], ["clampt"])
        for hd in range(4):
            DVE(lambda e, hd=hd: e.tensor_scalar(out=Cst[l][:, :, hd, :], in0=Cst[l][:, :, hd, :], scalar1=bc8[0:96, 4 + hd:5 + hd],
                                                 scalar2=None, op0=ALU.mult), [ck, "bc8"], [ck])
        ACT(lambda e: e.copy(out=Cstb[:], in_=Cst[l][:]), [ck], ["cstb"])
        ps4, pk4 = fbank()
        def fnsc(e):
            for hd in range(4):
                for kt in range(2):
                    ins = e.matmul(ps4[0:C, hd * 64:hd * 64 + C], lhsT=qkT[:, 8 + 2 * hd + kt, sl], rhs=qkT[:, 2 * hd + kt, sl],
                                   start=(kt == 0), stop=(kt == 1))
            return ins
        PE(fnsc, ["qkT"], pk4)
        p4 = ps4[0:C, 0:256].rearrange("p (h t) -> p h t", t=64)[:, :, 0:C]
        DVE(lambda e: e.tensor_tensor(out=PTs[0:C, :, 0:C], in0=p4, in1=m_incl[0:C, 0:4, 0:C], op=ALU.mult), pk4 + ["m_incl"], ["PTs"])
        DVE(lambda e: e.tensor_tensor(out=PTb[0:C, :, 0:C], in0=PTs[0:C, :, 0:C], in1=ee[0:C, :].unsqueeze(2).broadcast_to([C, 4, C]),
                                      op=ALU.mult), ["PTs", "ee"], ["PTb"])
        ps5, pk5 = fpair()
        def fnnd(e):
            for hd in range(4):
                out = ps5[0:C, hd // 2, (hd % 2) * 193:(hd % 2) * 193 + 193]
                e.matmul(out, lhsT=PTb[0:C, hd, 0:C], rhs=vaug[0:C, q, hd, :], start=True, stop=False)
                e.matmul(out, lhsT=qkT[:, 2 * hd, sl], rhs=Cstb[:, 0, hd, :], start=False, stop=False)
                ins = e.matmul(out, lhsT=qkT[:, 2 * hd + 1, sl], rhs=Cstb[:, 1, hd, :], start=False, stop=True)
            return ins
        PE(fnnd, ["PTb", "vaug", "cstb", "qkT"], pk5)
        nd = ps5[0:C, :, 0:386].rearrange("p b (h d) -> p b h d", d=193)
        hv4 = hst[0:C, 0, :].rearrange("p (b h) -> p b h", b=2)
        ACT(lambda e: e.activation(out=hv4.unsqueeze(3), in_=nd[:, :, :, 192:193], func=AF.Abs), pk5, ["hst"])
        DVE(lambda e: e.tensor_tensor(out=hst[0:C, 0, :], in0=hst[0:C, 0, :], in1=clampt[0:C, :], op=ALU.max),
            ["hst", "clampt"], ["hst"])
        DVE(lambda e: e.reciprocal(out=hst[0:C, 0, :], in_=hst[0:C, 0, :]), ["hst"], ["hst"])
        DVE(lambda e: e.tensor_tensor(out=hh[0:C, 4 * q:4 * q + 4, :].rearrange("p (b h) d -> p b h d", b=2), in0=nd[:, :, :, 0:192],
                                      in1=hv4.unsqueeze(3).broadcast_to([C, 2, 2, 192]), op=ALU.mult), pk5 + ["hst", "hh"], ["hh"])
        pt2, pk2_ = bbank()
        for t in range(8):
            TR(pt2[0:C, t * 96:(t + 1) * 96], qkT[:, 8 + t, sl], ident_b[0:96, 0:96], ["qkT", "ident_b"], pk2_)
        DVE(lambda e, pt2=pt2: e.tensor_tensor(out=khm[0:C], in0=pt2[0:C, 0:768].rearrange("p (h d) -> p h d", d=192),
                                               in1=ee[0:C, :].unsqueeze(2).broadcast_to([C, 4, 192]), op=ALU.mult), pk2_ + ["ee"], ["khm"])
        for kt in range(2):
            ps6, pk6 = fpair()
            def fncu(e, ps6=ps6, kt=kt):
                for hd in range(4):
                    ins = e.matmul(ps6[0:96, hd // 2, (hd % 2) * 193:(hd % 2) * 193 + 193], lhsT=khm[0:C, hd, kt * 96:(kt + 1) * 96],
                                   rhs=vaug[0:C, q, hd, :], start=True, stop=True)
                return ins
            PE(fncu, ["khm", "vaug"], pk6)
            DVE(lambda e, ps6=ps6, kt=kt: e.tensor_tensor(out=Cst[l][:, kt, :, :].rearrange("p (b h) d -> p b h d", b=2),
                                                          in0=Cst[l][:, kt, :, :].rearrange("p (b h) d -> p b h d", b=2),
                                                          in1=ps6[0:96, :, 0:386].rearrange("p b (h d) -> p b h d", d=193), op=ALU.add),
                pk6 + [ck], [ck])
        if samp or (last and q == nq - 1):
            b_ = sq if samp else 0
            for kt in range(2):
                STORE(o_c[og][l, b_, :, kt * 96:(kt + 1) * 96, :].rearrange("h p v -> p h v"), Cst[l][:, kt, :, 0:192], [ck])
                STORE(o_n[og][l, b_, :, kt * 96:(kt + 1) * 96].rearrange("h (p o) -> p h o", o=1), Cst[l][:, kt, :, 192:193], [ck], slow=True)
            STORE(o_m[og][l, b_].rearrange("(h o) -> h o", o=1), mst[l][:], [mk_], slow=True)

    def ml_out(l, N, tiles):
        S.label = 'mlT'
        C = tiles[0][1]
        nh = 4 * len(tiles)
        H = hh[0:C, 0:nh, :]
        DVE(lambda e: e.tensor_reduce(out=hs2[0:C, 0, 0:nh], in_=H, axis=AX.X, op=ALU.add), ["hh", "hs2"], ["hs2"])
        DVE(lambda e: e.tensor_scalar(out=hs2[0:C, 0, 0:nh], in0=hs2[0:C, 0, 0:nh], scalar1=1.0 / 192, scalar2=None, op0=ALU.mult), ["hs2"], ["hs2"])
        DVE(lambda e: e.tensor_tensor(out=hcen[0:C, 0:nh, :], in0=H, in1=hs2[0:C, 0, 0:nh].unsqueeze(2).broadcast_to([C, nh, 192]),
                                      op=ALU.subtract), ["hh", "hs2"], ["hcen"])
        POOL(lambda e: e.tensor_tensor(out=hsq[0:C, 0:nh, :], in0=hcen[0:C, 0:nh, :], in1=hcen[0:C, 0:nh, :], op=ALU.mult), ["hcen"], ["hsq"])
        DVE(lambda e: e.tensor_reduce(out=hs2[0:C, 1, 0:nh], in_=hsq[0:C, 0:nh, :], axis=AX.X, op=ALU.add), ["hsq", "hs2"], ["hs2"])
        ACT(lambda e: e.activation(out=hs2[0:C, 2, 0:nh], in_=hs2[0:C, 1, 0:nh], func=AF.Sqrt, bias=LN_EPS, scale=1.0 / 192), ["hs2"], ["hs2"])
        DVE(lambda e: e.reciprocal(out=hs2[0:C, 2, 0:nh], in_=hs2[0:C, 2, 0:nh]), ["hs2"], ["hs2"])
        DVE(lambda e: e.tensor_tensor(out=hnb[0:C].rearrange("p q (h d) -> p (q h) d", d=192)[:, 0:nh, :], in0=hcen[0:C, 0:nh, :],
                                      in1=hs2[0:C, 2, 0:nh].unsqueeze(2).broadcast_to([C, nh, 192]), op=ALU.mult), ["hcen", "hs2"], ["hnb"])
        pt, pk = bbank()
        for q, (o, C_, sq) in enumerate(tiles):
            for j in range(6):
                TR(pt[:, j * 128 + o:j * 128 + o + C], hnb[0:C, q, j * 128:(j + 1) * 128], ident_b[0:C, 0:C], ["hnb", "ident_b"], pk)
        DVE(lambda e, pt=pt: e.tensor_tensor(out=ymlT[:, :, 0:N], in0=pt[:, 0:768].rearrange("p (j t) -> p j t", t=128)[:, :, 0:N],
                                             in1=ogT[:, :, 0:N], op=ALU.mult), pk + ["ogT", "ymlT"], ["ymlT"])

    def phase_c(kind, N, l, ydst):
        S.label = 'C'
        pl = "pp%d" % l
        for jg in range(2):
            for b in range(3):
                wbm, wkm = load_group(l, "mg%d%d" % (b, jg))
                wbb, wkb = load_group(l, "br%d%d" % (b, jg))
                for jj in range(4):
                    j = jg * 4 + jj
                    ps_, pk = proj(wbm, wkm, jj * 128, 128, N)
                    gi_ = (b + jj) % 2
                    ACT(lambda e, ps_=ps_, gi_=gi_, b=b, j=j: e.activation(out=gate[gi_][:, 0:N], in_=ps_, func=AF.Sigmoid,
                                                                           bias=PPc(l, "bmrg", b * 8 + j), scale=1.0),
                        pk + [pl], ["gate%d" % gi_])
                    ps2, pk2 = fbank()
                    nk = BR_KC[b]
                    MM(ps2[:, 0:N], [(wbb[:, kc, jj * 128:(jj + 1) * 128], BR_T[b][:, kc, 0:N]) for kc in range(nk)],
                       [wkb, BR_K[b]], pk2)
                    if b == 0:
                        DVE(lambda e, ps2=ps2, gi_=gi_, j=j: e.tensor_tensor(out=mrg[:, j, 0:N], in0=ps2[:, 0:N], in1=gate[gi_][:, 0:N],
                                                                             op=ALU.mult), pk2 + ["gate%d" % gi_, "mrg%d" % j], ["mrg%d" % j])
                    else:
                        ctmp = ctmp2[jj % 2]
                        DVE(lambda e, ps2=ps2, gi_=gi_, ctmp=ctmp: e.tensor_tensor(out=ctmp[:, 0:N], in0=ps2[:, 0:N], in1=gate[gi_][:, 0:N],
                                                                        op=ALU.mult), pk2 + ["gate%d" % gi_], ["ctmp%d" % (jj % 2)])
                        if b == 1:
                            POOL(lambda e, j=j, ctmp=ctmp: e.tensor_tensor(out=mrg[:, j, 0:N], in0=mrg[:, j, 0:N], in1=ctmp[:, 0:N], op=ALU.add),
                                 ["mrg%d" % j, "ctmp%d" % (jj % 2)], ["mrg%d" % j])
                        else:
                            POOL(lambda e, j=j, ctmp=ctmp: e.tensor_tensor(out=mrgb[:, j, 0:N], in0=mrg[:, j, 0:N], in1=ctmp[:, 0:N], op=ALU.add),
                                 ["mrg%d" % j, "ctmp%d" % (jj % 2), "mrgb%d" % j], ["mrgb%d" % j])
        load_bc(1 + l)
        wo0, wok0 = load_group(l, "wo0")
        wo1, wok1 = load_group(l, "wo1")
        rows = N
        ps_, pk = fpair()
        MM(ps_[0:rows, 0, :], [(mrgb[:, kc, 0:rows], wo0[:, kc, 0:512]) for kc in range(8)], [wok0] + ["mrgb%d" % j_ for j_ in range(8)], [pk[0]])
        MM(ps_[0:rows, 1, :], [(mrgb[:, kc, 0:rows], wo1[:, kc, 0:512]) for kc in range(8)], [wok1] + ["mrgb%d" % j_ for j_ in range(8)], [pk[1]])
        DVE(lambda e, ps_=ps_: e.scalar_tensor_tensor(
            out=lnx[0:rows, :].rearrange("p (b n) -> p b n", b=2), in0=x_tok[0:rows, 0, :].rearrange("p (b n) -> p b n", b=2),
            scalar=DN_ALPHA, in1=ps_[0:rows, :, :], op0=ALU.mult, op1=ALU.add), pk + ["x_tok", "lnx"], ["lnx"])
        ln_block(lnx, ["lnx"], rows, out_dram=ydst)

    pass
    if stage >= 1000:
        S.limit = stage - 1000
        stage = 3
    try:
        if stage >= 1:
            run_pass("meta", 16, [(0, 16, 0)], meta, None, True, False)
        npp = SEQ // NT
        tl2 = [(0, 64, 0), (64, 64, 1)]
        for p in range(npp):
            if stage >= 2 and (stage >= 99 or p < stage - 1):
                run_pass("prompt", NT, tl2, xp[p * NT:(p + 1) * NT, :], yp[p * NT:(p + 1) * NT, :], False, p == npp - 1)
        for sp_ in range(2):
            if stage >= 99:
                run_pass("sample", NT, [(0, 64, 2 * sp_), (64, 64, 2 * sp_ + 1)], xs[sp_ * NT:(sp_ + 1) * NT, :],
                         ys[sp_ * NT:(sp_ + 1) * NT, :], False, False)


    except StopBuild:
        pass
    pass
    S.emit(final_wait_ops=final_ops)
    es.close()
    return nc


_NC = None


def _get_nc():
    global _NC
    if _NC is None:
        _NC = build()
    return _NC


def _host_inputs(inp, c):
    f = lambda a: np.ascontiguousarray(a, dtype=np.float32)
    p = c % 4
    sl = slice(4 * c, 4 * c + 4)
    m = {}
    m["xp"] = f(inp["x_prompt"][p])
    m["xs"] = f(inp["x_sample"][sl].reshape(NSAMP * 64, D))
    m["meta"] = f(inp["meta"])
    m["st_shift"] = f(inp["state_rwkv_shift"][:, sl])
    m["st_wkv"] = f(inp["state_rwkv_wkv"][:, sl])
    m["st_s5re"] = f(inp["state_s5_re"][:, sl])
    m["st_s5im"] = f(inp["state_s5_im"][:, sl])
    m["st_conv"] = f(inp["state_mlstm_conv"][:, sl])
    m["st_c"] = f(inp["state_mlstm_c"][:, sl])
    m["st_n"] = f(inp["state_mlstm_n"][:, sl])
    m["st_m"] = f(inp["state_mlstm_m"][:, sl])
    for k in ("w_in", "w_br_rw", "w_br_s5", "w_br_ml", "w_out", "s5_w_glu", "rw_w2", "rw_a2"):
        m[k] = f(inp[k])
    return m


def _shared_inputs(inp):
    f32 = np.float32
    pp = np.zeros((DEPTH, 128, NPP), f32)
    pq = np.zeros((DEPTH, 96, 80), f32)

    def cols(v, n):
        return np.asarray(v, f32).reshape(n, 128).T

    for l in range(DEPTH):
        def put(name, arr):
            o, w = PP[name]
            pp[l, :, o:o + w] = arr
        put("mu", cols(inp["rw_mu"][l], 19))
        put("w0", cols(inp["rw_w0"][l], 6))
        put("a0", cols(inp["rw_a0"][l], 6))
        put("kk", cols(inp["rw_kk"][l], 6))
        put("ka", cols(inp["rw_ka"][l], 6))
        put("rk", cols(np.asarray(inp["rw_rk"][l]).reshape(768), 6))
        put("s5d", cols(inp["s5_d"][l], 4))
        put("bglu", cols(inp["s5_b_glu"][l], 4))
        put("mlg", cols(inp["ml_ln_g"][l], 6))
        put("bmrg", cols(inp["b_merge"][l], 24))
        are = np.asarray(inp["s5_a_re"][l], f32).reshape(16, 2, 64).transpose(1, 2, 0).reshape(128, 16)
        aim = np.asarray(inp["s5_a_im"][l], f32).reshape(16, 2, 64).transpose(1, 2, 0).reshape(128, 16)
        ldt = np.repeat(np.asarray(inp["s5_log_dt"][l], f32).reshape(16, 2, 1), 64, axis=2).transpose(1, 2, 0).reshape(128, 16)
        put("are", are)
        put("aim", aim)
        put("ldt", ldt)
        cw = np.asarray(inp["ml_conv_w"][l], f32).reshape(4, 16, 96)
        cb = np.asarray(inp["ml_conv_b"][l], f32).reshape(16, 96)
        pqv = np.zeros((96, 16, 5), f32)
        pqv[:, :, 0:4] = cw.transpose(2, 1, 0)
        pqv[:, :, 4] = cb.T
        pq[l] = pqv.reshape(96, 80)
    bc = np.stack([inp["in_ln_g"], inp["in_ln_b"], inp["ln_g"][0], inp["ln_b"][0], inp["ln_g"][1], inp["ln_b"][1]]).astype(f32)
    rwln = np.stack([np.asarray(inp["rw_ln_g"], f32), np.asarray(inp["rw_ln_b"], f32)], axis=1)
    rwln = np.ascontiguousarray(rwln.reshape(DEPTH, 2, 6, 2, 64).transpose(0, 1, 3, 2, 4).reshape(DEPTH, 2, 2, 384))
    bif = np.asarray(inp["ml_b_if"], f32)
    bblk = np.zeros((DEPTH, 128, 4, 1024), f32)
    cpad = np.zeros((DEPTH, 128, 16, 2, 64), f32)
    for l in range(DEPTH):
        for c, (bk, ck) in enumerate((("s5_b_re", "s5_c_re"), ("s5_b_im", "s5_c_im"))):
            B = np.asarray(inp[bk][l], f32)
            Cm = np.asarray(inp[ck][l], f32)
            for g in range(32):
                i = g // 2
                ut = g // 8
                col0 = (i % 4) * 256 + c * 128 + (g % 2) * 64
                bblk[l, (g % 8) * 16:(g % 8) * 16 + 16, ut, col0:col0 + 64] = B[g].T
                oc = (i % 2) * 32 + (g % 2) * 16
                cpad[l, (g % 2) * 64:(g % 2) * 64 + 64, i, c, oc:oc + 16] = Cm[g].T
    return dict(pp=pp, pq=pq, bc=bc, rwln=rwln, bif=bif, bblk=bblk, cpad=cpad)


def kernel(**inp):
    inp = {k: np.asarray(v) for k, v in inp.items()}
    nc = _get_nc()
    shared = _shared_inputs(inp)
    in_maps = []
    for c in range(8):
        m = _host_inputs(inp, c)
        m.update(shared)
        in_maps.append(m)
    res = run_bass_kernel_spmd(nc, in_maps, core_ids=list(range(8)))
    R = res.results
    y_prompt = np.stack([R[c]["yp"] for c in range(4)], 0)
    y_sample = np.concatenate([R[c]["ys"].reshape(NSAMP, 64, D) for c in range(8)], 0)
    outs = [y_prompt, y_sample]
    for nm in ("shift", "wkv", "s5re", "s5im", "conv", "c", "n", "m"):
        outs.append(np.concatenate([R[c]["p_" + nm] for c in range(4)], 1))
    for nm in ("shift", "wkv", "s5re", "s5im", "conv", "c", "n", "m"):
        outs.append(np.concatenate([R[c]["s_" + nm] for c in range(8)], 1))
    return tuple(np.ascontiguousarray(o, dtype=np.float32) for o in outs)
```

```python
import contextlib
import math
import numpy as np
import concourse.bass as bass
import concourse.mybir as mybir
from concourse.bass_utils import run_bass_kernel_spmd

F32 = mybir.dt.float32
BF16 = mybir.dt.bfloat16
ALU = mybir.AluOpType
AF = mybir.ActivationFunctionType
AX = mybir.AxisListType

ENGS = ("pe", "act", "dve", "pool", "sp")
D = 1024
DEPTH = 2
NT = 128
SEQ = 4096
NMETA = 16
NSAMP = 4
RWW = 768
RWS = 2432
S5W = 512
MLW = 768
INC = 11144
DN_ALPHA = (2 * DEPTH) ** 0.25
LN_EPS = 1e-5
RW_GN_EPS = 64e-5
C_RW = 0
C_RWG = 2432
C_S5U = 3200
C_S5G = 3712
C_MLQK = 4224
C_MLV = 5760
C_MLIF = 6528
C_MLO = 6536
C_MLZ = 7304
C_MRG = 8072
EXPM05 = math.exp(-0.5)


class StopBuild(Exception):
    pass


class _FirstHook:
    def __init__(self, eng, wait):
        self._e = eng
        self._w = wait

    def _wrap(self, f):
        def g(*a, **k):
            ins = f(*a, **k)
            if self._w is not None:
                ins._wait_ge(*self._w)
                self._w = None
            return ins
        return g

    def __getattr__(self, name):
        v = getattr(self._e, name)
        if name in ("matmul", "transpose"):
            return self._wrap(v)
        return v


class Sched:
    limit = None
    resched = True
    def __init__(self, nc, n_dma_sems=12):
        self.nc = nc
        self.ops = []
        self.last_w = {}
        self.readers = {}
        self.n_dma_sems = n_dma_sems
        self.arena = {}

    def xl(self, keys):
        out = []
        for k in keys:
            r = self.arena.get(k)
            if r is None:
                r = self.arena.get(k.rstrip('0123456789'))
            if r is None:
                assert not ('_' in k and k.rsplit('_', 1)[1].isdigit() and k.rsplit('_', 1)[0] in self.arena), k
                out.append(k)
            else:
                out.extend("ar%d" % u for u in range(r[0] // 256, (r[1] + 255) // 256))
        return out

    def op(self, eng, fn, reads=(), writes=(), dma=False, single=False):
        if self.limit is not None and len(self.ops) >= self.limit:
            raise StopBuild()
        reads = self.xl(reads)
        writes = self.xl(writes)
        i = len(self.ops)
        deps = set()
        for k in reads:
            w = self.last_w.get(k)
            if w is not None:
                deps.add(w)
        for k in writes:
            w = self.last_w.get(k)
            if w is not None:
                deps.add(w)
            for r in self.readers.get(k, ()):
                deps.add(r)
        deps.discard(i)
        odeps = set()
        if eng == "pe" and not dma:
            odeps = {d for d in deps if (self.ops[d]["eng"] == "pe" and not self.ops[d]["dma"])}
            deps = deps - odeps
        self.ops.append(dict(eng=eng, fn=fn, deps=deps, odeps=odeps, dma=dma, used=False, single=single, label=getattr(self, 'label', ''),
                             dur=getattr(self, 'dur', None), tbl=getattr(self, 'tbl', None)))
        for d in deps:
            self.ops[d]["used"] = True
        for k in writes:
            self.last_w[k] = i
            self.readers[k] = []
        for k in reads:
            if k not in writes:
                self.readers.setdefault(k, []).append(i)
        return i

    def reschedule(self, final_wait_ops):
        ops = self.ops
        n = len(ops)
        DUR = {"pe": 0.35, "act": 0.3, "dve": 0.3, "pool": 0.45, "sp": 0.2}
        succ = [[] for _ in range(n)]
        indeg = [0] * n
        for i, o in enumerate(ops):
            for d in (o["deps"] | o["odeps"]):
                succ[d].append(i)
                indeg[i] += 1
        lastq = {}
        for i, o in enumerate(ops):
            if o["dma"]:
                q = o["eng"]
                if q in lastq:
                    succ[lastq[q]].append(i)
                    indeg[i] += 1
                lastq[q] = i
        fin = [0.0] * n
        ready_t = [0.0] * n
        cur = {e: 0.0 for e in ENGS}
        ready = {e: [] for e in ENGS}
        for i in range(n):
            if indeg[i] == 0:
                ready[ops[i]["eng"]].append(i)
        order = []
        acttbl = [None]
        done = 0
        while done < n:
            best = None
            for e in ENGS:
                lst = ready[e]
                if not lst:
                    continue
                bi_, bs_ = None, None
                for i in lst[:24]:
                    st = max(ready_t[i], cur[e])
                    if e == "act" and ops[i]["tbl"] is not None and ops[i]["tbl"] != acttbl[0]:
                        st += 1.3
                    key = (st, i)
                    if bs_ is None or key < bs_:
                        bi_, bs_ = i, key
                if best is None or bs_ < best[0]:
                    best = (bs_, bi_, e)
            (st, _), i, e = best
            ready[e].remove(i)
            o = ops[i]
            if o["dma"]:
                cur[e] = st + 0.1
                fin[i] = st + 2.5
            else:
                if e == "act" and o["tbl"] is not None:
                    acttbl[0] = o["tbl"]
                fin[i] = st + (o["dur"] or DUR[e])
                cur[e] = fin[i]
            order.append(i)
            done += 1
            for j in succ[i]:
                ready_t[j] = max(ready_t[j], fin[i] + 0.15)
                indeg[j] -= 1
                if indeg[j] == 0:
                    ready[ops[j]["eng"]].append(j)
        assert len(order) == n
        pos = {old: new for new, old in enumerate(order)}
        newops = []
        for old_i in order:
            o = ops[old_i]
            o["deps"] = {pos[d] for d in o["deps"]}
            o["odeps"] = {pos[d] for d in o["odeps"]}
            newops.append(o)
        self.ops = newops
        return [pos[i] for i in final_wait_ops]

    def emit(self, final_wait_ops=()):
        nc = self.nc
        if self.resched:
            final_wait_ops = self.reschedule(list(final_wait_ops))
        ops = self.ops
        for i in final_wait_ops:
            ops[i]["used"] = True
        cnt = {e: 0 for e in ENGS}
        dma_cnt = {}
        dma_rr = {e: 0 for e in ENGS}
        for o in ops:
            if o["dma"]:
                q = o["eng"]
                s = (q, dma_rr[q] % self.n_dma_sems)
                dma_rr[q] += 1
                dma_cnt[s] = dma_cnt.get(s, 0) + 1
                o["sig"] = ("dma", s, 16 * dma_cnt[s])
            elif o["used"]:
                cnt[o["eng"]] += 1
                o["sig"] = ("eng", o["eng"], cnt[o["eng"]])
            else:
                o["sig"] = None
        with contextlib.ExitStack() as st:
            esem = {e: st.enter_context(nc.semaphore("s_" + e)) for e in ENGS}
            dsem = {}
            for s in dma_cnt:
                dsem[s] = st.enter_context(nc.semaphore("d_%s_%d" % s))
            block = st.enter_context(nc.Block())

            def semof(sig):
                return esem[sig[1]] if sig[0] == "eng" else dsem[sig[1]]

            def run(e, engobj):
                seen = {}
                for o in ops:
                    if o["eng"] != e:
                        continue
                    need = {}
                    for d in o["deps"]:
                        sg = ops[d]["sig"]
                        key = (sg[0], sg[1])
                        need[key] = max(need.get(key, 0), sg[2])
                    if o["dma"]:
                        sg = o["sig"]
                        key = (sg[0], sg[1])
                        if sg[2] > 16:
                            need[key] = max(need.get(key, 0), sg[2] - 16)
                    waits = []
                    for key, v in need.items():
                        if seen.get(key, 0) < v:
                            waits.append((esem[key[1]] if key[0] == "eng" else dsem[key[1]], v))
                            seen[key] = v
                    emb = []
                    if not o["dma"] and (o["single"] or e == "pe"):
                        emb = waits[-1:]
                        waits = waits[:-1]
                    for sm, v in waits:
                        engobj.wait_ge(sm, v)
                    if emb and not o["single"]:
                        ins = o["fn"](_FirstHook(engobj, emb[0]))
                    else:
                        ins = o["fn"](engobj)
                        for sm, v in emb:
                            ins._wait_ge(sm, v)
                    sg = o["sig"]
                    if sg is not None:
                        ins.then_inc(semof(sg), 16 if sg[0] == "dma" else 1)
                if e == "sp":
                    for i in final_wait_ops:
                        sg = ops[i]["sig"]
                        engobj.wait_ge(semof(sg), sg[2])

            @block.tensor
            def _(eng):
                run("pe", eng)

            @block.scalar
            def _(eng):
                run("act", eng)

            @block.vector
            def _(eng):
                run("dve", eng)

            @block.gpsimd
            def _(eng):
                run("pool", eng)

            @block.sync
            def _(eng):
                run("sp", eng)


def w_groups():
    g = []
    g.append(("rwx", "w_in", 1024, 2304, 128))
    g.append(("rwk0", "w_in", 1024, 768, 512))
    g.append(("rwk1", "w_in", 1024, 1280, 256))
    g.append(("rwr0", "w_in", 1024, 0, 512))
    g.append(("rwr1", "w_in", 1024, 512, 256))
    g.append(("rwv0", "w_in", 1024, 1536, 512))
    g.append(("rwv1", "w_in", 1024, 2048, 256))
    g.append(("rwg0", "w_in", 1024, C_RWG, 512))
    g.append(("rwg1", "w_in", 1024, C_RWG + 512, 256))
    g.append(("s5u", "w_in", 1024, C_S5U, 512))
    g.append(("s5g", "w_in", 1024, C_S5G, 512))
    for i in range(4):
        g.append(("mlqk%d" % i, "w_in", 1024, C_MLQK + 384 * i, 384))
    g.append(("mlv0", "w_in", 1024, C_MLV, 512))
    g.append(("mlv1", "w_in", 1024, C_MLV + 512, 264))
    g.append(("mlo0", "w_in", 1024, C_MLO, 512))
    g.append(("mlo1", "w_in", 1024, C_MLO + 512, 256))
    g.append(("mlz0", "w_in", 1024, C_MLZ, 512))
    g.append(("mlz1", "w_in", 1024, C_MLZ + 512, 256))
    for jg in range(2):
        for b, (nm, k) in enumerate((("w_br_rw", 768), ("w_br_s5", 512), ("w_br_ml", 768))):
            g.append(("mg%d%d" % (b, jg), "w_in", 1024, C_MRG + b * 1024 + jg * 512, 512))
            g.append(("br%d%d" % (b, jg), nm, k, jg * 512, 512))
    for jg in range(2):
        g.append(("wo%d" % jg, "w_out", 1024, jg * 512, 512))
    return g


GROUPS = w_groups()
GIDX = {g[0]: i for i, g in enumerate(GROUPS)}

PP = {}
_o = 0
for _n, _w in (("mu", 19), ("w0", 6), ("a0", 6), ("kk", 6), ("ka", 6), ("rk", 6), ("s5d", 4), ("bglu", 4),
               ("mlg", 6), ("bmrg", 24), ("are", 16), ("aim", 16), ("ldt", 16)):
    PP[_n] = (_o, _w)
    _o += _w
NPP = _o


def build(stage=99):
    nc = bass.Bass("TRN2", target_bir_lowering=False)
    es = contextlib.ExitStack()
    S = Sched(nc)

    def din(name, shape):
        return nc.dram_tensor(name, list(shape), F32, kind="ExternalInput").ap()

    def dout(name, shape):
        return nc.dram_tensor(name, list(shape), F32, kind="ExternalOutput").ap()

    def dscr(name, shape, dt):
        return nc.dram_tensor(name, list(shape), dt, kind="Internal").ap()

    xp = din("xp", [SEQ, D])
    xs = din("xs", [NSAMP * 64, D])
    meta = din("meta", [NMETA, D])
    st_shift = din("st_shift", [DEPTH, NSAMP, RWS])
    st_wkv = din("st_wkv", [DEPTH, NSAMP, 12, 64, 64])
    st_s5re = din("st_s5re", [DEPTH, NSAMP, 32, 64])
    st_s5im = din("st_s5im", [DEPTH, NSAMP, 32, 64])
    st_conv = din("st_conv", [DEPTH, NSAMP, 3, 1536])
    st_c = din("st_c", [DEPTH, NSAMP, 4, 192, 192])
    st_n = din("st_n", [DEPTH, NSAMP, 4, 192])
    st_m = din("st_m", [DEPTH, NSAMP, 4])
    w_in = din("w_in", [DEPTH, D, INC])
    wsrc = dict(w_in=w_in, w_br_rw=din("w_br_rw", [DEPTH, 768, D]), w_br_s5=din("w_br_s5", [DEPTH, 512, D]),
                w_br_ml=din("w_br_ml", [DEPTH, 768, D]), w_out=din("w_out", [DEPTH, D, D]))
    w_glu = din("s5_w_glu", [DEPTH, 512, 512])
    rw_w2 = din("rw_w2", [DEPTH, 64, 768])
    rw_a2 = din("rw_a2", [DEPTH, 64, 768])
    pp_d = din("pp", [DEPTH, 128, NPP])
    pq_d = din("pq", [DEPTH, 96, 16 * 5])
    bc_d = din("bc", [6, D])
    rwln_d = din("rwln", [DEPTH, 2, 2, 384])
    bif_d = din("bif", [DEPTH, 8])
    bblk_d = din("bblk", [DEPTH, 128, 4, 1024])
    cpad_d = din("cpad", [DEPTH, 128, 16, 2, 64])

    yp = dout("yp", [SEQ, D])
    ys = dout("ys", [NSAMP * 64, D])
    o_shift = {"p": dout("p_shift", [DEPTH, 1, RWS]), "s": dout("s_shift", [DEPTH, NSAMP, RWS])}
    o_wkv = {"p": dout("p_wkv", [DEPTH, 1, 12, 64, 64]), "s": dout("s_wkv", [DEPTH, NSAMP, 12, 64, 64])}
    o_s5re = {"p": dout("p_s5re", [DEPTH, 1, 32, 64]), "s": dout("s_s5re", [DEPTH, NSAMP, 32, 64])}
    o_s5im = {"p": dout("p_s5im", [DEPTH, 1, 32, 64]), "s": dout("s_s5im", [DEPTH, NSAMP, 32, 64])}
    o_conv = {"p": dout("p_conv", [DEPTH, 1, 3, 1536]), "s": dout("s_conv", [DEPTH, NSAMP, 3, 1536])}
    o_c = {"p": dout("p_c", [DEPTH, 1, 4, 192, 192]), "s": dout("s_c", [DEPTH, NSAMP, 4, 192, 192])}
    o_n = {"p": dout("p_n", [DEPTH, 1, 4, 192]), "s": dout("s_n", [DEPTH, NSAMP, 4, 192])}
    o_m = {"p": dout("p_m", [DEPTH, 1, 4]), "s": dout("s_m", [DEPTH, NSAMP, 4])}

    wq = [[dscr("wq%d_%d" % (l, i), [128, g[2] // 128, g[4]], BF16) for i, g in enumerate(GROUPS)]
          for l in range(DEPTH)]

    def sb(name, shape, dt=F32):
        return es.enter_context(nc.sbuf_tensor(name, list(shape), dt))

    def psum(name, shape, dt):
        return es.enter_context(nc.psum_tensor(name, list(shape), dt))

    final_ops = []

    def ACT(fn, r, w):
        tbl = None
        names = fn.__code__.co_names
        for nm_, t_ in (("Sigmoid", "sig"), ("Exp", "exp"), ("Ln", "ln"), ("Sqrt", "sqrt"), ("Silu", "silu"), ("Sin", "silu")):
            if nm_ in names:
                tbl = t_
        S.tbl = tbl
        i_ = S.op("act", fn, r, w, single=True)
        S.tbl = None
        return i_

    def DVE(fn, r, w):
        return S.op("dve", fn, r, w, single=True)

    def POOL(fn, r, w):
        return S.op("pool", fn, r, w, single=True)

    def PE(fn, r, w):
        return S.op("pe", fn, r, w)

    def LOAD(out, in_, r, w, slow=False):
        return S.op("sp", lambda e: e.dma_start(out=out, in_=in_, allow_slow_non_contiguous=slow), r, w, dma=True)

    def STORE(out, in_, r, slow=False):
        i = S.op("pool", lambda e: e.dma_start(out=out, in_=in_, allow_slow_non_contiguous=slow), r, (), dma=True)
        final_ops.append(i)
        return i

    def MM(out, pairs, r, w):
        S.dur = 0.1 + 0.07 * len(pairs)
        def fn(e):
            n = len(pairs)
            for i, (l_, r_) in enumerate(pairs):
                ins = e.matmul(out, lhsT=l_, rhs=r_, start=(i == 0), stop=(i == n - 1))
            return ins
        i_ = PE(fn, r, w)
        S.dur = None
        return i_

    def TR(out, in_, ident, r, w):
        return S.op("pe", lambda e: e.transpose(out=out, in_=in_, identity=ident), r, w, single=True)

    PSF = [psum("psf%d" % i, [128, 2, 512], F32) for i in range(3)]
    PSB = psum("psb", [128, 2, 1024], BF16)
    rr = {"b": 0, "p": 0, "t": 0}

    def fbank():
        i = rr["b"] % 6
        rr["b"] += 1
        return PSF[i // 2][:, i % 2, :], ["pf%d" % i]

    def fpair():
        i = rr["p"] % 3
        rr["p"] += 1
        return PSF[i], ["pf%d" % (2 * i), "pf%d" % (2 * i + 1)]

    def bbank():
        i = rr["t"] % 2
        rr["t"] += 1
        return PSB[:, i, :], ["pb%d" % i]

    ident_b = sb("ident_b", [128, 128], BF16)
    ident_f = sb("ident_f", [128, 128], F32)
    m_strict = sb("m_strict", [128, 6, 64], F32)
    m_incl = sb("m_incl", [128, 6, 64], F32)
    m_lower = sb("m_lower", [128, 6, 64], F32)
    tri_b = sb("tri_b", [64, 64], BF16)
    tri2 = sb("tri2", [128, 64], BF16)
    tri_f = sb("tri_f", [64, 64], F32)
    blk1 = sb("blk1", [128, 128], BF16)
    bsel = sb("bsel", [128, 2], BF16)
    ones_f = sb("ones_f", [128, 128], F32)
    onesb = sb("onesb", [128, 1], BF16)
    scanm = sb("scanm", [128, NT], F32)

    def mk_sel(t, pattern, base, cm, op, key):
        POOL(lambda e: e.memset(t, 1.0), [], [key])
        POOL(lambda e: e.affine_select(out=t, in_=t, pattern=pattern, compare_op=op, fill=0.0, base=base,
                                       channel_multiplier=cm), [key], [key])

    mk_sel(ident_f[:], [[-1, 128]], 0, 1, ALU.is_equal, "ident_f")
    POOL(lambda e: e.tensor_copy(out=ident_b[:], in_=ident_f[:]), ["ident_f"], ["ident_b"])
    for hf_ in range(2):
        ps_ = slice(hf_ * 64, hf_ * 64 + 64)
        mk_sel(m_strict[ps_], [[0, 6], [1, 64]], -1, -1, ALU.is_ge, "m_strict")
        mk_sel(m_incl[ps_], [[0, 6], [1, 64]], 0, -1, ALU.is_ge, "m_incl")
        mk_sel(m_lower[ps_], [[0, 6], [-1, 64]], -1, 1, ALU.is_ge, "m_lower")
    mk_sel(tri_f[:], [[1, 64]], 0, -1, ALU.is_ge, "tri_f")
    POOL(lambda e: e.tensor_copy(out=tri_b[:], in_=tri_f[:]), ["tri_f"], ["tri_b"])
    POOL(lambda e: e.tensor_copy(out=tri2[0:64, :], in_=tri_f[:]), ["tri_f"], ["tri2"])
    POOL(lambda e: e.tensor_copy(out=tri2[64:128, :], in_=m_incl[64:128, 0, :]), ["m_incl", "tri2"], ["tri2"])
    POOL(lambda e: e.memset(ones_f[:], 1.0), [], ["ones_f"])
    POOL(lambda e: e.memset(onesb[:], 1.0), [], ["onesb"])
    POOL(lambda e: e.memset(blk1[:], 0.0), [], ["blk1"])
    POOL(lambda e: e.memset(blk1[0:64, 0:64], 1.0), ["blk1"], ["blk1"])
    POOL(lambda e: e.memset(blk1[64:128, 64:128], 1.0), ["blk1"], ["blk1"])
    POOL(lambda e: e.memset(bsel[:], 0.0), [], ["bsel"])
    POOL(lambda e: e.memset(bsel[0:64, 0:1], 1.0), ["bsel"], ["bsel"])
    POOL(lambda e: e.memset(bsel[64:128, 1:2], 1.0), ["bsel"], ["bsel"])
    POOL(lambda e: e.memset(scanm[:], 1.0), [], ["scanm"])
    POOL(lambda e: e.memset(scanm[:].rearrange("p (c t) -> p c t", t=64)[:, :, 0:1], 0.0), ["scanm"], ["scanm"])

    if stage == -4:
        S.emit(final_wait_ops=final_ops); es.close(); return nc
    for l in range(DEPTH):
        for i, (nm, src, K, c0, wd) in enumerate(GROUPS):
            srcap = wsrc[src][l, :, c0:c0 + wd].rearrange("(kc p) c -> p kc c", p=128)
            S.op("pool", lambda e, o_=wq[l][i], s_=srcap: e.dma_start(out=o_, in_=s_), [], ["wq%d_%d" % (l, i)], dma=True)

    if stage == -3:
        S.emit(final_wait_ops=final_ops); es.close(); return nc
    ARW = 15360
    arena_t = sb("arena", [128, ARW], F32)
    arp = {"o": 0}

    def ar_reset(o=0):
        arp["o"] = o

    def ar(name, shape, dt=F32):
        P = shape[0]
        n = int(np.prod(shape[1:]))
        words = n if dt == F32 else (n + 1) // 2
        words = (words + 15) // 16 * 16
        o = arp["o"]
        assert o + words <= ARW, (name, o, words)
        arp["o"] = o + words
        v = arena_t[0:P, o:o + words]
        if dt != F32:
            v = v.bitcast(BF16)
        v = v[:, 0:n]
        if len(shape) == 3:
            v = v.rearrange("p (a b) -> p a b", b=shape[2])
        elif len(shape) == 4:
            v = v.rearrange("p (a b c) -> p a b c", b=shape[2], c=shape[3])
        S.arena[name] = (o * 4, (o + words) * 4)
        if len(shape) >= 3:
            esz = 4 if dt == F32 else 2
            sub = int(np.prod(shape[2:])) * esz
            for a_ in range(shape[1]):
                S.arena["%s_%d" % (name, a_)] = (o * 4 + a_ * sub, o * 4 + (a_ + 1) * sub)
        return v

    pp = [sb("pp%d" % l, [128, NPP]) for l in range(DEPTH)]
    pq = [sb("pq%d" % l, [96, 16, 5]) for l in range(DEPTH)]
    omka = [sb("omka%d" % l, [128, 6]) for l in range(DEPTH)]
    lora = [sb("lora%d" % l, [128, 768], BF16) for l in range(DEPTH)]
    wglu = [sb("wglu%d" % l, [128, 4, 512], BF16) for l in range(DEPTH)]
    bcg = sb("bcg", [128, 2, D])
    rwln = [sb("rwln%d" % l, [128, 2, 384]) for l in range(DEPTH)]
    bif = [sb("bif%d" % l, [64, 8]) for l in range(DEPTH)]
    EpB = sb("EpB", [128, 2, 16, 64])
    EnB = sb("EnB", [128, 16, 2, 128], BF16)
    bblkB = sb("bblkB", [128, 4, 1024], BF16)
    cpadB = sb("cpadB", [128, 16, 2, 64], BF16)
    S5K = ["EpB", "EnB", "bblkB", "cpadB"]
    epd = [dscr("epd%d" % l, [128, 2, 16, 64], F32) for l in range(DEPTH)]
    end_ = [dscr("end%d" % l, [64, 16, 2, 128], BF16) for l in range(DEPTH)]
    bblkq = [dscr("bblkq%d" % l, [128, 4, 1024], BF16) for l in range(DEPTH)]
    cpadq = [dscr("cpadq%d" % l, [128, 16, 2, 64], BF16) for l in range(DEPTH)]
    for l in range(DEPTH):
        LOAD(pp[l][:], pp_d[l], [], ["pp%d" % l])
        LOAD(pq[l][:], pq_d[l].rearrange("p (t j) -> p t j", j=5), [], ["pq%d" % l])
        for a_ in range(2):
            for r_ in range(2):
                LOAD(rwln[l][r_ * 64:(r_ + 1) * 64, a_, :], rwln_d[l, a_, r_].partition_broadcast(64), ["rwln%d" % l], ["rwln%d" % l])
        LOAD(bif[l][:], bif_d[l].partition_broadcast(64), [], ["bif%d" % l])
        S.op("pool", lambda e, l=l: e.dma_start(out=lora[l][0:64, :], in_=rw_w2[l]), [], ["lora%d" % l], dma=True)
        S.op("pool", lambda e, l=l: e.dma_start(out=lora[l][64:128, :], in_=rw_a2[l]), [], ["lora%da" % l], dma=True)
        S.op("pool", lambda e, l=l: e.dma_start(out=wglu[l][:], in_=w_glu[l].rearrange("(kc p) c -> p kc c", p=128)),
             [], ["wglu%d" % l], dma=True)
        S.op("pool", lambda e, l=l: e.dma_start(out=bblkq[l], in_=bblk_d[l]), [], ["bblkq%d" % l], dma=True)
        S.op("pool", lambda e, l=l: e.dma_start(out=cpadq[l], in_=cpad_d[l]), [], ["cpadq%d" % l], dma=True)
        o_, w_ = PP["ka"]
        DVE(lambda e, l=l, o_=o_: e.tensor_scalar(out=omka[l][:], in0=pp[l][:, o_:o_ + 6], scalar1=-1.0, scalar2=1.0,
                                                  op0=ALU.mult, op1=ALU.add), ["pp%d" % l], ["omka%d" % l])

    if stage == -2:
        S.emit(final_wait_ops=final_ops); es.close(); return nc

    def load_bc(i):
        LOAD(bcg[:].rearrange("p a b -> p (a b)"), bc_d[2 * i:2 * i + 2, :].rearrange("a b -> (a b)").partition_broadcast(128),
             [], ["bcg"])

    def load_s5(l):
        LOAD(EpB[:], epd[l], ["epd%d" % l], ["EpB"])
        LOAD(EnB[0:64], end_[l], ["end%d" % l, "EnB"], ["EnB"])
        LOAD(EnB[64:128], end_[l], ["end%d" % l, "EnB"], ["EnB"])
        LOAD(bblkB[:], bblkq[l], ["bblkq%d" % l], ["bblkB"])
        LOAD(cpadB[:], cpadq[l], ["cpadq%d" % l], ["cpadB"])

    def PPc(l, name, j=None):
        o_, w_ = PP[name]
        if j is None:
            return pp[l][:, o_:o_ + w_]
        return pp[l][:, o_ + j:o_ + j + 1]

    ar_reset()
    zr = ar("zr", [128, 16]); zi = ar("zi", [128, 16]); dtt = ar("dtt", [128, 16])
    t1 = ar("t1", [128, 16]); t2 = ar("t2", [128, 16]); t3 = ar("t3", [128, 16]); mg = ar("mg", [128, 16])
    lr = ar("lr", [128, 16])
    Epf = ar("Epf", [128, 2, 16, 64])
    Enf = [ar("enf%d" % c, [128, 16, 64]) for c in range(2)]
    ta = ar("ta", [128, 16, 32]); tb_ = ar("tb", [128, 16, 32])
    cfr = ar("cfr", [128, 16]); cfi = ar("cfi", [128, 16]); den = ar("den", [128, 16])
    enc = [ar("enc%d" % c, [128, 16, 64]) for c in range(2)]
    Ent = ar("Ent", [64, 16, 2, 128])
    KT = ["zr", "zi", "dtt", "t1", "t2", "t3", "mg", "lr", "Epf", "enf0", "enf1", "ta", "tb", "cfr", "cfi", "den", "enc0", "enc1"]

    def cexp32(sign, outr, outi):
        ACT(lambda e: e.activation(out=mg[:], in_=zr[:], func=AF.Exp, scale=sign / 32.0), KT, KT)
        ACT(lambda e: e.activation(out=t1[:], in_=zi[:], func=AF.Sin, scale=sign / 32.0), KT, KT)
        ACT(lambda e: e.activation(out=t2[:], in_=zi[:], func=AF.Sin, scale=sign / 32.0, bias=hpi[:, 0:1]), KT + ["hpi"], KT)
        DVE(lambda e: e.tensor_tensor(out=outr, in0=mg[:], in1=t2[:], op=ALU.mult), KT, KT)
        DVE(lambda e: e.tensor_tensor(out=outi, in0=mg[:], in1=t1[:], op=ALU.mult), KT, KT)
        for _ in range(5):
            DVE(lambda e: e.tensor_tensor(out=t1[:], in0=outr, in1=outr, op=ALU.mult), KT, KT)
            DVE(lambda e: e.tensor_tensor(out=t2[:], in0=outi, in1=outi, op=ALU.mult), KT, KT)
            DVE(lambda e: e.tensor_tensor(out=t3[:], in0=outr, in1=outi, op=ALU.mult), KT, KT)
            DVE(lambda e: e.tensor_tensor(out=outr, in0=t1[:], in1=t2[:], op=ALU.subtract), KT, KT)
            DVE(lambda e: e.tensor_scalar(out=outi, in0=t3[:], scalar1=2.0, scalar2=None, op0=ALU.mult), KT, KT)

    def powers(tabr, tabi):
        ln_ = 1
        while ln_ < 64:
            Lr = tabr[:, :, ln_ - 1:ln_].broadcast_to([128, 16, ln_])
            Li = tabi[:, :, ln_ - 1:ln_].broadcast_to([128, 16, ln_])
            a_ = ta[:, :, 0:ln_]
            b_ = tb_[:, :, 0:ln_]
            sr = tabr[:, :, 0:ln_]
            si = tabi[:, :, 0:ln_]
            dr = tabr[:, :, ln_:2 * ln_]
            di = tabi[:, :, ln_:2 * ln_]
            DVE(lambda e, a_=a_, sr=sr, Lr=Lr: e.tensor_tensor(out=a_, in0=sr, in1=Lr, op=ALU.mult), KT, KT)
            DVE(lambda e, b_=b_, si=si, Li=Li: e.tensor_tensor(out=b_, in0=si, in1=Li, op=ALU.mult), KT, KT)
            DVE(lambda e, a_=a_, b_=b_, dr=dr: e.tensor_tensor(out=dr, in0=a_, in1=b_, op=ALU.subtract), KT, KT)
            DVE(lambda e, a_=a_, sr=sr, Li=Li: e.tensor_tensor(out=a_, in0=sr, in1=Li, op=ALU.mult), KT, KT)
            DVE(lambda e, b_=b_, si=si, Lr=Lr: e.tensor_tensor(out=b_, in0=si, in1=Lr, op=ALU.mult), KT, KT)
            DVE(lambda e, a_=a_, b_=b_, di=di: e.tensor_tensor(out=di, in0=a_, in1=b_, op=ALU.add), KT, KT)
            ln_ *= 2

    hpi = sb("hpi", [128, 1])
    POOL(lambda e: e.memset(hpi[:], math.pi / 2), [], ["hpi"])
    for l in range(DEPTH):
        KP = ["pp%d" % l]
        ACT(lambda e, l=l: e.activation(out=dtt[:], in_=PPc(l, "ldt"), func=AF.Exp), KT + KP, KT)
        DVE(lambda e, l=l: e.tensor_tensor(out=zr[:], in0=PPc(l, "are"), in1=dtt[:], op=ALU.mult), KT + KP, KT)
        DVE(lambda e, l=l: e.tensor_tensor(out=zi[:], in0=PPc(l, "aim"), in1=dtt[:], op=ALU.mult), KT + KP, KT)

        if stage == -10:
            S.emit(final_wait_ops=final_ops); es.close(); return nc
        cexp32(1.0, Epf[:, 0, :, 0], Epf[:, 1, :, 0])

        if stage == -9:
            S.emit(final_wait_ops=final_ops); es.close(); return nc
        powers(Epf[:, 0], Epf[:, 1])

        if stage == -8:
            S.emit(final_wait_ops=final_ops); es.close(); return nc
        cexp32(-1.0, Enf[0][:, :, 0], Enf[1][:, :, 0])
        powers(Enf[0], Enf[1])

        if stage == -7:
            S.emit(final_wait_ops=final_ops); es.close(); return nc
        DVE(lambda e: e.tensor_scalar(out=lr[:], in0=Epf[:, 0, :, 0], scalar1=-1.0, scalar2=None, op0=ALU.add), KT, KT)
        DVE(lambda e, l=l: e.tensor_tensor(out=t1[:], in0=PPc(l, "are"), in1=PPc(l, "are"), op=ALU.mult), KT + KP, KT)
        DVE(lambda e, l=l: e.tensor_tensor(out=t2[:], in0=PPc(l, "aim"), in1=PPc(l, "aim"), op=ALU.mult), KT + KP, KT)
        DVE(lambda e: e.tensor_tensor(out=den[:], in0=t1[:], in1=t2[:], op=ALU.add), KT, KT)
        DVE(lambda e: e.reciprocal(out=den[:], in_=den[:]), KT, KT)
        DVE(lambda e, l=l: e.tensor_tensor(out=t1[:], in0=lr[:], in1=PPc(l, "are"), op=ALU.mult), KT + KP, KT)
        DVE(lambda e, l=l: e.tensor_tensor(out=t2[:], in0=Epf[:, 1, :, 0], in1=PPc(l, "aim"), op=ALU.mult), KT + KP, KT)
        DVE(lambda e: e.tensor_tensor(out=cfr[:], in0=t1[:], in1=t2[:], op=ALU.add), KT, KT)
        DVE(lambda e: e.tensor_tensor(out=cfr[:], in0=cfr[:], in1=den[:], op=ALU.mult), KT, KT)
        DVE(lambda e, l=l: e.tensor_tensor(out=t1[:], in0=Epf[:, 1, :, 0], in1=PPc(l, "are"), op=ALU.mult), KT + KP, KT)
        DVE(lambda e, l=l: e.tensor_tensor(out=t2[:], in0=lr[:], in1=PPc(l, "aim"), op=ALU.mult), KT + KP, KT)
        DVE(lambda e: e.tensor_tensor(out=cfi[:], in0=t1[:], in1=t2[:], op=ALU.subtract), KT, KT)
        DVE(lambda e: e.tensor_tensor(out=cfi[:], in0=cfi[:], in1=den[:], op=ALU.mult), KT, KT)
        CR = cfr[:].unsqueeze(2).broadcast_to([128, 16, 64])
        CI = cfi[:].unsqueeze(2).broadcast_to([128, 16, 64])
        DVE(lambda e, CR=CR: e.tensor_tensor(out=enc[0][:], in0=Enf[0][:], in1=CR, op=ALU.mult), KT, KT)
        DVE(lambda e, CI=CI: e.tensor_tensor(out=enc[1][:], in0=Enf[1][:], in1=CI, op=ALU.mult), KT, KT)
        DVE(lambda e: e.tensor_tensor(out=enc[0][:], in0=enc[0][:], in1=enc[1][:], op=ALU.subtract), KT, KT)
        DVE(lambda e, CI=CI: e.tensor_tensor(out=enc[1][:], in0=Enf[0][:], in1=CI, op=ALU.mult), KT, KT)
        DVE(lambda e, CR=CR: e.tensor_tensor(out=Enf[0][:], in0=Enf[1][:], in1=CR, op=ALU.mult), KT, KT)
        DVE(lambda e: e.tensor_tensor(out=enc[1][:], in0=enc[1][:], in1=Enf[0][:], op=ALU.add), KT, KT)

        if stage == -6:
            S.emit(final_wait_ops=final_ops); es.close(); return nc
        for c in range(2):
            for i in range(16):
                ps_, pk = fbank()
                TR(ps_[0:64, 0:128], enc[c][:, i, :], ident_f[:], KT + ["ident_f"], pk)
                ACT(lambda e, ps_=ps_, i=i, c=c: e.copy(out=Ent[:, i, c, :], in_=ps_[0:64, 0:128]), pk, ["Ent"])

        if stage == -5:
            S.emit(final_wait_ops=final_ops); es.close(); return nc
        S.op("pool", lambda e, l=l: e.dma_start(out=epd[l], in_=Epf), KT, ["epd%d" % l], dma=True)
        S.op("pool", lambda e, l=l: e.dma_start(out=end_[l], in_=Ent), ["Ent"], ["end%d" % l], dma=True)

    x_tok = sb("x_tok", [128, 1, D])
    xT = sb("xT", [128, 8, NT], BF16)
    wbuf = [sb("wbuf%d" % i, [128, 8, 512], BF16) for i in range(4)]
    wr = {"i": 0}

    def load_group(l, name):
        gi = GIDX[name]
        g = GROUPS[gi]
        bi = wr["i"] % 4
        wr["i"] += 1
        kc = g[2] // 128
        LOAD(wbuf[bi][:, 0:kc, 0:g[4]], wq[l][gi], ["wq%d_%d" % (l, gi)], ["wbuf%d" % bi])
        return wbuf[bi], "wbuf%d" % bi

    shiftst = [sb("shiftst%d" % l, [128, 19]) for l in range(DEPTH)]
    sshift = sb("sshift", [128, 19, 2])
    sshift_o = sb("sshift_o", [128, 19, 2])
    S0T = [sb("s0t%d" % l, [128, 6, 64]) for l in range(DEPTH)]
    S0Tb = sb("s0tb", [128, 6, 64], BF16)
    h0 = [sb("h0%d" % l, [128, 2, 16]) for l in range(DEPTH)]
    convst = [sb("convst%d" % l, [96, 16, 3]) for l in range(DEPTH)]
    sconv = sb("sconv", [96, 16, 2, 3])
    Cst = [sb("cst%d" % l, [96, 2, 4, 193]) for l in range(DEPTH)]
    Cstb = sb("cstb", [96, 2, 4, 193], BF16)
    mst = [sb("mst%d" % l, [4, 1]) for l in range(DEPTH)]
    stg = sb("stg", [64, 12, 64])

    yrwT = sb("yrwT", [128, 6, NT], BF16)
    ys5T = sb("ys5T", [128, 4, NT], BF16)
    ymlT = sb("ymlT", [128, 6, NT], BF16)
    gate = [sb("gate%d" % i, [128, NT], BF16) for i in range(2)]
    mrg = sb("mrg", [128, 8, NT]); mrgb = sb("mrgb", [128, 8, NT], BF16)
    lnt = sb("lnt", [128, D]); lnb = sb("lnb", [128, D], BF16); lnx = sb("lnx", [128, D]); ctmp2 = [sb("ctmp%d" % i, [128, NT]) for i in range(2)]
    lst = sb("lst", [128, 2, 6]); lmv = sb("lmv", [128, 2]); lrs = sb("lrs", [128, 1])
    gC = sb("gC", [128, 6, 2])

    ar_reset()
    U = [ar("U%d" % i, [128, NT + 1]) for i in range(2)]
    dtmp = [ar("dtmp%d" % i, [128, NT]) for i in range(2)]
    xs18 = ar("xs18", [128, NT]); txw = ar("txw", [128, NT], BF16)
    ldec_2 = [ar("ldec%d" % i_, [128, NT]) for i_ in range(2)]; Gc_2 = [ar("Gc%d" % i_, [128, NT]) for i_ in range(2)]; aa_2 = [ar("aa%d" % i_, [128, NT]) for i_ in range(2)]
    eneg_2 = [ar("eneg%d" % i_, [128, NT]) for i_ in range(2)]; eprev_2 = [ar("eprev%d" % i_, [128, NT]) for i_ in range(2)]; ehat_2 = [ar("ehat%d" % i_, [128, NT]) for i_ in range(2)]; epos_2 = [ar("epos%d" % i_, [128, NT]) for i_ in range(2)]
    kx_2 = [ar("kx%d" % i_, [128, NT]) for i_ in range(2)]; kkr_2 = [ar("kkr%d" % i_, [128, NT]) for i_ in range(2)]; kksq_2 = [ar("kksq%d" % i_, [128, NT], BF16) for i_ in range(2)]
    rn_2 = [ar("rn%d" % i_, [128, NT]) for i_ in range(2)]; kkn_2 = [ar("kkn%d" % i_, [128, NT]) for i_ in range(2)]; tk_2 = [ar("tk%d" % i_, [128, NT]) for i_ in range(2)]
    kmod_2 = [ar("kmod%d" % i_, [128, NT]) for i_ in range(2)]; bb_2 = [ar("bb%d" % i_, [128, NT]) for i_ in range(2)]; rx_2 = [ar("rx%d" % i_, [128, NT]) for i_ in range(2)]
    rkp = ar("rkp", [128, 6, NT], BF16)
    rt_ = ar("rt_", [128, 6, NT], BF16); kt_ = ar("kt_", [128, 6, NT], BF16); bt_ = ar("bt_", [128, 6, NT], BF16)
    at_ = ar("at_", [128, 6, NT], BF16); khat = ar("khat", [128, 6, NT], BF16); bhat = ar("bhat", [128, 6, NT], BF16)
    vT = ar("vT", [128, 6, NT], BF16); grw = ar("grw", [128, 6, NT], BF16)
    TB = []
    for pz in ("A", "B"):
        TB.append(dict(
            vtok=ar("vtok" + pz, [128, 6, 64], BF16), khtok=ar("khtok" + pz, [128, 6, 64], BF16), bhtok=ar("bhtok" + pz, [128, 6, 64], BF16),
            Nsb=[ar("Nsb%d%s" % (i, pz), [128, 6, 64], BF16) for i in range(2)],
            NTsb=[ar("NTsb%d%s" % (i, pz), [128, 6, 64], BF16) for i in range(2)],
            Msb=ar("Msb" + pz, [128, 6, 64], BF16), P1sb=ar("P1sb" + pz, [128, 6, 64], BF16), P2sb=ar("P2sb" + pz, [128, 6, 64], BF16), z=pz))
    Ysb = [ar("Ysb%d" % i, [128, 6, 64], BF16) for i in range(2)]
    yo = ar("yo", [128, 12, 64]); ycen = ar("ycen", [128, 12, 64]); ysq = ar("ysq", [128, 12, 64])
    ystat = ar("ystat", [128, 4, 12]); rkd = ar("rkd", [128, 12]); ytb = ar("ytb", [128, 12, 64], BF16)
    RW_END = arp["o"]
    ar_reset()
    uT = ar("uT", [128, 4, NT], BF16); u32 = ar("u32", [128, 4, NT]); gs5 = ar("gs5", [128, 4, NT], BF16)
    wtok = ar("wtok", [128, 16, 2, 128], BF16)
    s5a = ar("s5a", [128, 4, 128]); s5b = ar("s5b", [128, 4, 128])
    Gr = ar("Gr", [128, 16, 64]); Gi = ar("Gi", [128, 16, 64])
    hA = ar("hA", [128, 16, 64]); hB = ar("hB", [128, 16, 64]); hC = ar("hC", [128, 16, 64]); hD = ar("hD", [128, 16, 64])
    hre = ar("hre", [128, 16, NT], BF16); himn = ar("himn", [128, 16, NT], BF16)
    yv = ar("yv", [128, 4, NT]); gt = ar("gt", [128, 4, NT]); gsg = ar("gsg", [128, 4, NT])
    gl = ar("gl", [128, 4, NT]); glb = ar("glb", [128, 4, NT], BF16); sgl = ar("sgl", [128, 4, NT])
    ar_reset()
    qkraw = ar("qkraw", [96, 16, NT + 3]); cvacc = ar("cvacc", [96, 16, NT]); cvtmp = ar("cvtmp", [96, 16, NT])
    hbuf = ar("hbuf0", [96, 16, 6])
    qkT = ar("qkT", [96, 16, NT], BF16)
    vaug = ar("vaug", [64, NT // 64, 4, 193], BF16)
    iftok = ar("iftok", [64, NT // 64, 8])
    zsl_2 = [ar("zsl%d" % i_, [128, NT]) for i_ in range(2)]; ogT = ar("ogT", [128, 6, NT], BF16)
    lfi = ar("lfi", [64, 8]); bcs = ar("bcs", [64, 4]); zz = ar("zz", [64, 4]); ee = ar("ee", [64, 4]); clampt = ar("clampt", [64, 4])
    zmax = ar("zmax", [4, 1]); mu4 = ar("mu4", [4, 1]); f4 = ar("f4", [4, 1]); bend = ar("bend", [4, 1]); dg = ar("dg", [4, 8])
    bc8 = ar("bc8", [128, 8])
    PTs = ar("PTs", [64, 4, 64]); PTb = ar("PTb", [64, 4, 64], BF16)
    hh = ar("hh", [64, 8, 192]); hcen = ar("hcen", [64, 8, 192]); hsq = ar("hsq", [64, 8, 192]); hst = ar("hst", [64, 4, 4]); hs2 = ar("hs2", [64, 3, 8])
    hnb = ar("hnb", [64, 2, 768], BF16); khm = ar("khm", [64, 4, 192], BF16)

    BR_T = {0: yrwT, 1: ys5T, 2: ymlT}
    BR_K = {0: "yrwT", 1: "ys5T", 2: "ymlT"}
    BR_KC = {0: 6, 1: 4, 2: 6}

    def ln_block(src, srckeys, rows, out_dram=None):
        for hf in range(2):
            DVE(lambda e, hf=hf: e.bn_stats(out=lst[0:rows, hf, :], in_=src[0:rows, hf * 512:(hf + 1) * 512]),
                srckeys, ["lst"])
        DVE(lambda e: e.bn_aggr(out=lmv[0:rows, :], in_=lst[0:rows].rearrange("p a b -> p (a b)")), ["lst"], ["lmv"])
        ACT(lambda e: e.activation(out=lrs[0:rows, :], in_=lmv[0:rows, 1:2], func=AF.Sqrt, bias=LN_EPS, scale=1.0),
            ["lmv"], ["lrs"])
        DVE(lambda e: e.reciprocal(out=lrs[0:rows, :], in_=lrs[0:rows, :]), ["lrs"], ["lrs"])
        DVE(lambda e: e.tensor_scalar(out=lnt[0:rows, :], in0=src[0:rows, :], scalar1=lmv[0:rows, 0:1],
                                      scalar2=lrs[0:rows, 0:1], op0=ALU.subtract, op1=ALU.mult),
            srckeys + ["lmv", "lrs"], ["lnt"])
        DVE(lambda e: e.tensor_tensor(out=lnt[0:rows, :], in0=lnt[0:rows, :], in1=bcg[0:rows, 0, :], op=ALU.mult),
            ["lnt", "bcg"], ["lnt"])
        POOL(lambda e: e.tensor_tensor(out=x_tok[0:rows, 0, :], in0=lnt[0:rows, :], in1=bcg[0:rows, 1, :],
                                       op=ALU.add), ["lnt", "bcg"], ["x_tok"])
        if out_dram is not None:
            STORE(out_dram, x_tok[0:rows, 0, :], ["x_tok"])
        ACT(lambda e: e.copy(out=lnb[0:rows, :], in_=x_tok[0:rows, 0, :]), ["x_tok"], ["lnb"])
        pt, pk = bbank()
        for kc in range(8):
            TR(pt[:, kc * 128:kc * 128 + rows], lnb[0:rows, kc * 128:(kc + 1) * 128], ident_b[0:rows, 0:rows],
               ["lnb", "ident_b"], pk)
        DVE(lambda e: e.tensor_copy(out=xT[:, :, 0:rows],
                                    in_=pt.rearrange("p (k t) -> p k t", t=128)[:, :, 0:rows]), pk, ["xT"])

    def proj(wb, wk, c0, M, N):
        ps_, pk = fbank()
        MM(ps_[0:M, 0:N], [(wb[:, kc, c0:c0 + M], xT[:, kc, 0:N]) for kc in range(8)], [wk, "xT"], pk)
        return ps_[0:M, 0:N], pk

    s5cur = {"l": None}

    def ensure_s5(l):
        if s5cur["l"] != l:
            load_s5(l)
            s5cur["l"] = l

    def run_pass(kind, N, tiles, xsrc, ydst, first, last):
        og = "p" if kind != "sample" else "s"
        load_bc(0)
        LOAD(lnx[0:N, :], xsrc, [], ["lnx"])
        ln_block(lnx, ["lnx"], N)
        for l in range(DEPTH):
            if first:
                for t_, k_ in ((shiftst[l], "shiftst%d" % l), (S0T[l], "s0t%d" % l), (h0[l], "h0%d" % l),
                               (convst[l], "convst%d" % l), (Cst[l], "cst%d" % l), (mst[l], "mst%d" % l)):
                    POOL(lambda e, t_=t_: e.memset(t_[:], 0.0), [], [k_])
            layer(kind, N, tiles, l, og, last)
            phase_c(kind, N, l, ydst if l == DEPTH - 1 else None)

    def layer(kind, N, tiles, l, og, last):
        pl = "pp%d" % l
        samp = kind == "sample"
        nq = len(tiles)
        if samp:
            for q, (o, C, sq) in enumerate(tiles):
                LOAD(lnx[0:19, 0:128], st_shift[l, sq].rearrange("(t p) -> t p", p=128), [], ["lnx"])
                ps_, pk = fbank()
                TR(ps_[:, 0:19], lnx[0:19, 0:128], ident_f[0:19, 0:19], ["lnx", "ident_f"], pk)
                ACT(lambda e, ps_=ps_, q=q: e.copy(out=sshift[:, :, q], in_=ps_[:, 0:19]), pk + ["sshift"], ["sshift"])
                LOAD(lnx[0:48, 128:224], st_conv[l, sq].rearrange("j (t p) -> (j t) p", p=96), [], ["lnx"])
                ps_, pk = fbank()
                TR(ps_[0:96, 0:48], lnx[0:48, 128:224], ident_f[0:48, 0:48], ["lnx", "ident_f"], pk)
                ACT(lambda e, ps_=ps_, q=q: e.copy(out=sconv[:, :, q, :], in_=ps_[0:96, 0:48].rearrange("p (j t) -> p t j", j=3)),
                    pk + ["sconv"], ["sconv"])

        def shift_tile(ps_, pk, ct, out_ap, outkeys, ui):
            Ub = U[ui]
            uk = "U%d" % ui
            ACT(lambda e: e.copy(out=Ub[:, 1:N + 1], in_=ps_), pk, [uk])
            if not samp:
                ACT(lambda e: e.copy(out=Ub[:, 0:1], in_=shiftst[l][:, ct:ct + 1]), ["shiftst%d" % l, uk], [uk])
            else:
                ACT(lambda e: e.copy(out=Ub[:, 0:1], in_=sshift[:, ct, 0:1]), ["sshift", uk], [uk])
            DVE(lambda e: e.tensor_tensor(out=dtmp[ui][:, 0:N], in0=Ub[:, 0:N], in1=Ub[:, 1:N + 1], op=ALU.subtract),
                [uk], ["dtmp%d" % ui])
            DVE(lambda e: e.scalar_tensor_tensor(out=out_ap, in0=dtmp[ui][:, 0:N], scalar=PPc(l, "mu", ct),
                                                 in1=Ub[:, 1:N + 1], op0=ALU.mult, op1=ALU.add),
                ["dtmp%d" % ui, uk, pl], outkeys)
            if samp:
                DVE(lambda e: e.tensor_tensor(out=dtmp[ui][:, 0:1], in0=sshift[:, ct, 1:2], in1=Ub[:, 65:66], op=ALU.subtract),
                    [uk, "sshift", "dtmp%d" % ui], ["dtmp%d" % ui])
                DVE(lambda e: e.scalar_tensor_tensor(out=out_ap[:, 64:65], in0=dtmp[ui][:, 0:1], scalar=PPc(l, "mu", ct),
                                                     in1=Ub[:, 65:66], op0=ALU.mult, op1=ALU.add),
                    ["dtmp%d" % ui, uk, pl] + outkeys, outkeys)
                ACT(lambda e: e.copy(out=sshift_o[:, ct, 0:1], in_=Ub[:, 64:65]), [uk], ["sshift_o"])
                ACT(lambda e: e.copy(out=sshift_o[:, ct, 1:2], in_=Ub[:, 128:129]), [uk, "sshift_o"], ["sshift_o"])
            else:
                ACT(lambda e: e.copy(out=shiftst[l][:, ct:ct + 1], in_=Ub[:, N:N + 1]), [uk], ["shiftst%d" % l])

        ui = [0]

        def nui():
            ui[0] ^= 1
            return ui[0]

        S.label = 'rwA'
        wb, wk = load_group(l, "rwx")
        ps_, pk = proj(wb, wk, 0, 128, N)
        shift_tile(ps_, pk, 18, xs18[:, 0:N], ["xs18"], nui())
        ACT(lambda e: e.activation(out=txw[0:64, 0:N], in_=xs18[0:64, 0:N], func=AF.Tanh), ["xs18"], ["txw"])
        ACT(lambda e: e.copy(out=txw[64:128, 0:N], in_=xs18[64:128, 0:N]), ["xs18", "txw"], ["txw"])
        lk = ["lora%d" % l, "lora%da" % l]

        def jtile(j, wbk, wkk, wbr, wkr, wbv, wkv, c0):
            jp = j % 2
            ldec = ldec_2[jp]
            Gc = Gc_2[jp]
            aa = aa_2[jp]
            eneg = eneg_2[jp]
            eprev = eprev_2[jp]
            ehat = ehat_2[jp]
            epos = epos_2[jp]
            kx = kx_2[jp]
            kkr = kkr_2[jp]
            rn = rn_2[jp]
            kkn = kkn_2[jp]
            tk = tk_2[jp]
            kmod = kmod_2[jp]
            bb = bb_2[jp]
            rx = rx_2[jp]
            kksq = kksq_2[jp]
            ps_, pk = fbank()
            MM(ps_[:, 0:N], [(lora[l][0:64, j * 128:(j + 1) * 128], txw[0:64, 0:N])], lk + ["txw"], pk)
            ACT(lambda e, ps_=ps_: e.activation(out=ldec[:, 0:N], in_=ps_[:, 0:N], func=AF.Sigmoid, bias=PPc(l, "w0", j), scale=1.0),
                pk + [pl], ["ldec%d" % jp])
            POOL(lambda e: e.tensor_scalar(out=ldec[:, 0:N], in0=ldec[:, 0:N], scalar1=-EXPM05, scalar2=None, op0=ALU.mult),
                 ["ldec%d" % jp], ["ldec%d" % jp])
            ps2, pk2 = fbank()
            MM(ps2[:, 0:N], [(lora[l][64:128, j * 128:(j + 1) * 128], txw[64:128, 0:N])], lk + ["txw"], pk2)
            ACT(lambda e, ps2=ps2: e.activation(out=aa[:, 0:N], in_=ps2[:, 0:N], func=AF.Sigmoid, bias=PPc(l, "a0", j), scale=1.0),
                pk2 + [pl], ["aa%d" % jp])
            DVE(lambda e: e.tensor_tensor_scan(out=Gc[:, 0:N], data0=scanm[:, 0:N], data1=ldec[:, 0:N], initial=0.0,
                                               op0=ALU.mult, op1=ALU.add), ["ldec%d" % jp, "scanm"], ["Gc%d" % jp])
            for q, (o, C, sq) in enumerate(tiles):
                ACT(lambda e, q=q, o=o, C=C: e.activation(out=gC[:, j, q:q + 1], in_=Gc[:, o + C - 1:o + C], func=AF.Exp),
                    ["Gc%d" % jp, "gC"], ["gC"])
            ps_, pk = proj(wbk, wkk, c0, 128, N)
            shift_tile(ps_, pk, 6 + j, kx[:, 0:N], ["kx%d" % jp], nui())
            ACT(lambda e: e.activation(out=eneg[:, 0:N], in_=Gc[:, 0:N], func=AF.Exp, scale=-1.0), ["Gc%d" % jp], ["eneg%d" % jp])
            DVE(lambda e: e.tensor_tensor(out=eprev[:, 0:N], in0=Gc[:, 0:N], in1=ldec[:, 0:N], op=ALU.subtract),
                ["Gc%d" % jp, "ldec%d" % jp], ["eprev%d" % jp])
            ACT(lambda e: e.activation(out=eprev[:, 0:N], in_=eprev[:, 0:N], func=AF.Exp), ["eprev%d" % jp], ["eprev%d" % jp])
            for q, (o, C, sq) in enumerate(tiles):
                ACT(lambda e, o=o, C=C: e.activation(out=ehat[:, o:o + C], in_=Gc[:, o:o + C], func=AF.Exp, scale=-1.0),
                    ["Gc%d" % jp, "ehat%d" % jp], ["ehat%d" % jp])
                DVE(lambda e, o=o, C=C, q=q: e.tensor_scalar(out=ehat[:, o:o + C], in0=ehat[:, o:o + C],
                                                             scalar1=gC[:, j, q:q + 1], scalar2=None, op0=ALU.mult),
                    ["ehat%d" % jp, "gC"], ["ehat%d" % jp])
            DVE(lambda e: e.tensor_scalar(out=kkr[:, 0:N], in0=kx[:, 0:N], scalar1=PPc(l, "kk", j), scalar2=None,
                                          op0=ALU.mult), ["kx%d" % jp, pl], ["kkr%d" % jp])
            POOL(lambda e: e.tensor_tensor(out=kksq[:, 0:N], in0=kkr[:, 0:N], in1=kkr[:, 0:N], op=ALU.mult), ["kkr%d" % jp], ["kksq%d" % jp])
            ps2, pk2 = fbank()
            MM(ps2[:, 0:N], [(blk1[:], kksq[:, 0:N])], ["blk1", "kksq%d" % jp], pk2)
            ACT(lambda e, ps2=ps2: e.activation(out=rn[:, 0:N], in_=ps2[:, 0:N], func=AF.Sqrt, bias=1e-12, scale=1.0), pk2, ["rn%d" % jp])
            DVE(lambda e: e.reciprocal(out=rn[:, 0:N], in_=rn[:, 0:N]), ["rn%d" % jp], ["rn%d" % jp])
            DVE(lambda e: e.tensor_tensor(out=kkn[:, 0:N], in0=kkr[:, 0:N], in1=rn[:, 0:N], op=ALU.mult), ["kkr%d" % jp, "rn%d" % jp], ["kkn%d" % jp])
            DVE(lambda e: e.tensor_scalar(out=tk[:, 0:N], in0=aa[:, 0:N], scalar1=PPc(l, "ka", j),
                                          scalar2=omka[l][:, j:j + 1], op0=ALU.mult, op1=ALU.add),
                ["aa%d" % jp, pl, "omka%d" % l], ["tk%d" % jp])
            DVE(lambda e: e.tensor_tensor(out=kmod[:, 0:N], in0=kx[:, 0:N], in1=tk[:, 0:N], op=ALU.mult), ["kx%d" % jp, "tk%d" % jp], ["kmod%d" % jp])
            POOL(lambda e: e.tensor_tensor(out=bb[:, 0:N], in0=kkn[:, 0:N], in1=aa[:, 0:N], op=ALU.mult), ["kkn%d" % jp, "aa%d" % jp], ["bb%d" % jp])
            DVE(lambda e: e.tensor_tensor(out=kt_[:, j, 0:N], in0=kmod[:, 0:N], in1=eneg[:, 0:N], op=ALU.mult),
                ["kmod%d" % jp, "eneg%d" % jp, "kt__%d" % j], ["kt__%d" % j])
            POOL(lambda e: e.tensor_tensor(out=bt_[:, j, 0:N], in0=bb[:, 0:N], in1=eneg[:, 0:N], op=ALU.mult),
                 ["bb%d" % jp, "eneg%d" % jp, "bt__%d" % j], ["bt__%d" % j])
            DVE(lambda e: e.scalar_tensor_tensor(out=at_[:, j, 0:N], in0=kkn[:, 0:N], scalar=-1.0, in1=eprev[:, 0:N],
                                                 op0=ALU.mult, op1=ALU.mult), ["kkn%d" % jp, "eprev%d" % jp, "at__%d" % j], ["at__%d" % j])
            POOL(lambda e: e.tensor_tensor(out=khat[:, j, 0:N], in0=kmod[:, 0:N], in1=ehat[:, 0:N], op=ALU.mult),
                 ["kmod%d" % jp, "ehat%d" % jp, "khat_%d" % j], ["khat_%d" % j])
            DVE(lambda e: e.tensor_tensor(out=bhat[:, j, 0:N], in0=bb[:, 0:N], in1=ehat[:, 0:N], op=ALU.mult),
                ["bb%d" % jp, "ehat%d" % jp, "bhat_%d" % j], ["bhat_%d" % j])
            ps_, pk = proj(wbr, wkr, c0, 128, N)
            shift_tile(ps_, pk, j, rx[:, 0:N], ["rx%d" % jp], nui())
            ACT(lambda e: e.activation(out=epos[:, 0:N], in_=Gc[:, 0:N], func=AF.Exp), ["Gc%d" % jp], ["epos%d" % jp])
            DVE(lambda e: e.tensor_tensor(out=rt_[:, j, 0:N], in0=rx[:, 0:N], in1=epos[:, 0:N], op=ALU.mult),
                ["rx%d" % jp, "epos%d" % jp, "rt__%d" % j], ["rt__%d" % j])
            DVE(lambda e: e.scalar_tensor_tensor(out=rkp[:, j, 0:N], in0=rx[:, 0:N], scalar=PPc(l, "rk", j),
                                                 in1=kmod[:, 0:N], op0=ALU.mult, op1=ALU.mult),
                ["rx%d" % jp, pl, "kmod%d" % jp, "rkp_%d" % j], ["rkp_%d" % j])
            ps_, pk = proj(wbv, wkv, c0, 128, N)
            shift_tile(ps_, pk, 12 + j, vT[:, j, 0:N], ["vT_%d" % j], nui())

        for part, js in (("0", range(4)), ("1", range(4, 6))):
            wbk, wkk = load_group(l, "rwk" + part)
            wbr, wkr = load_group(l, "rwr" + part)
            wbv, wkv = load_group(l, "rwv" + part)
            for j in js:
                jtile(j, wbk, wkk, wbr, wkr, wbv, wkv, (j % 4) * 128)

        for gn, js in (("rwg0", range(4)), ("rwg1", range(4, 6))):
            wb, wk = load_group(l, gn)
            for j in js:
                ps_, pk = proj(wb, wk, (j % 4) * 128, 128, N)
                ACT(lambda e, ps_=ps_, j=j: e.activation(out=grw[:, j, 0:N], in_=ps_, func=AF.Silu), pk + ["grw_%d" % j], ["grw_%d" % j])

        for q, (o, C, sq) in enumerate(tiles):
            rwkv_tile(l, q, o, C, sq, kind, og, last, nq)
        rw_out(l, N, tiles)
        if samp:
            for q, (o, C, sq) in enumerate(tiles):
                ps_, pk = fbank()
                TR(ps_[0:19, 0:128], sshift_o[:, :, q], ident_f[:], ["sshift_o", "ident_f"], pk)
                ACT(lambda e, ps_=ps_: e.copy(out=lnx[0:19, 0:128], in_=ps_[0:19, 0:128]), pk + ["lnx"], ["lnx"])
                STORE(o_shift["s"][l, sq].rearrange("(t p) -> t p", p=128), lnx[0:19, 0:128], ["lnx"])
        elif last:
            ps_, pk = fbank()
            TR(ps_[0:19, 0:128], shiftst[l][:, :], ident_f[:], ["shiftst%d" % l, "ident_f"], pk)
            ACT(lambda e, ps_=ps_: e.copy(out=lnx[0:19, 0:128], in_=ps_[0:19, 0:128]), pk + ["lnx"], ["lnx"])
            STORE(o_shift["p"][l, 0].rearrange("(t p) -> t p", p=128), lnx[0:19, 0:128], ["lnx"])

        S.label = 's5A'
        ensure_s5(l)
        wb, wk = load_group(l, "s5u")
        for j in range(4):
            ps_, pk = proj(wb, wk, j * 128, 128, N)
            ACT(lambda e, ps_=ps_, j=j: e.copy(out=u32[:, j, 0:N], in_=ps_), pk + ["u32_%d" % j], ["u32_%d" % j])
            DVE(lambda e, j=j: e.tensor_copy(out=uT[:, j, 0:N], in_=u32[:, j, 0:N]), ["u32_%d" % j, "uT_%d" % j], ["uT_%d" % j])
        wb, wk = load_group(l, "s5g")
        for j in range(4):
            ps_, pk = proj(wb, wk, j * 128, 128, N)
            ACT(lambda e, ps_=ps_, j=j: e.activation(out=gs5[:, j, 0:N], in_=ps_, func=AF.Silu), pk + ["gs5_%d" % j], ["gs5_%d" % j])
        s5_en(l, N)
        for q, (o, C, sq) in enumerate(tiles):
            s5_tile(l, q, o, C, sq, kind, og, last, nq)
        s5_out(l, N)
        ensure_s5(1 - l)

        S.label = 'mlA'
        S.label = 'mlA'
        for gi_ in range(4):
            wb, wk = load_group(l, "mlqk%d" % gi_)
            for jj in range(4):
                t = gi_ * 4 + jj
                ps_, pk = proj(wb, wk, jj * 96, 96, N)
                ACT(lambda e, ps_=ps_, t=t: e.copy(out=qkraw[:, t, 3:N + 3], in_=ps_), pk + ["qkraw_%d" % t], ["qkraw_%d" % t])
        qk = ["qkraw"]
        if not samp:
            ACT(lambda e: e.copy(out=qkraw[:, :, 0:3], in_=convst[l][:, :, :]), ["convst%d" % l] + qk, qk)
        else:
            ACT(lambda e: e.copy(out=qkraw[:, :, 0:3], in_=sconv[:, :, 0, :]), ["sconv"] + qk, qk)

        def wbc(jt, n_):
            return pq[l][:, :, jt:jt + 1].broadcast_to([96, 16, n_])

        pqk = "pq%d" % l
        DVE(lambda e: e.tensor_tensor(out=cvacc[:, :, 0:N], in0=qkraw[:, :, 0:N], in1=wbc(0, N), op=ALU.mult), qk + [pqk], ["cvacc"])
        POOL(lambda e: e.tensor_tensor(out=cvacc[:, :, 0:N], in0=cvacc[:, :, 0:N], in1=wbc(4, N), op=ALU.add), ["cvacc", pqk], ["cvacc"])
        for jt in range(1, 4):
            POOL(lambda e, jt=jt: e.tensor_tensor(out=cvtmp[:, :, 0:N], in0=qkraw[:, :, jt:N + jt], in1=wbc(jt, N), op=ALU.mult),
                 qk + [pqk, "cvtmp"], ["cvtmp"])
            DVE(lambda e: e.tensor_tensor(out=cvacc[:, :, 0:N], in0=cvacc[:, :, 0:N], in1=cvtmp[:, :, 0:N], op=ALU.add),
                ["cvacc", "cvtmp"], ["cvacc"])
        if samp:
            hk = "hbuf0"
            hb = hbuf
            POOL(lambda e: e.tensor_copy(out=hb[:, :, 0:3], in_=sconv[:, :, 1, :]), ["sconv", hk], [hk])
            POOL(lambda e: e.tensor_copy(out=hb[:, :, 3:6], in_=qkraw[:, :, 67:70]), qk + [hk], [hk])
            f3 = cvacc[:, :, 64:67]
            DVE(lambda e: e.tensor_tensor(out=f3, in0=hb[:, :, 0:3], in1=wbc(0, 3), op=ALU.mult), [hk, pqk, "cvacc"], ["cvacc"])
            DVE(lambda e: e.tensor_tensor(out=f3, in0=f3, in1=wbc(4, 3), op=ALU.add), [pqk, "cvacc"], ["cvacc"])
            for jt in range(1, 4):
                DVE(lambda e, jt=jt: e.tensor_tensor(out=cvtmp[:, :, 0:3], in0=hb[:, :, jt:jt + 3], in1=wbc(jt, 3), op=ALU.mult),
                    [hk, pqk, "cvtmp"], ["cvtmp"])
                DVE(lambda e: e.tensor_tensor(out=f3, in0=f3, in1=cvtmp[:, :, 0:3], op=ALU.add), ["cvacc", "cvtmp"], ["cvacc"])
        ACT(lambda e: e.activation(out=qkT[:, :, 0:N], in_=cvacc[:, :, 0:N], func=AF.Silu), ["cvacc", "qkT"], ["qkT"])
        POOL(lambda e: e.tensor_scalar(out=qkT[:, 8:16, 0:N], in0=qkT[:, 8:16, 0:N], scalar1=1.0 / math.sqrt(192.0), scalar2=None,
                                       op0=ALU.mult), ["qkT"], ["qkT"])
        if not samp:
            ACT(lambda e: e.copy(out=convst[l][:, :, :], in_=qkraw[:, :, N:N + 3]), qk + ["convst%d" % l], ["convst%d" % l])
        conv_out(l, N, tiles, kind, og, last)
        wb0, wk0 = load_group(l, "mlv0")
        wb1, wk1 = load_group(l, "mlv1")
        for q, (o, C, sq) in enumerate(tiles):
            ps_, pk = fpair()
            MM(ps_[0:C, 0, 0:512], [(xT[:, kc, o:o + C], wb0[:, kc, 0:512]) for kc in range(8)], [wk0, "xT"], [pk[0]])
            MM(ps_[0:C, 1, 0:264], [(xT[:, kc, o:o + C], wb1[:, kc, 0:264]) for kc in range(8)], [wk1, "xT"], [pk[1]])
            vk = ["vaug"]
            ACT(lambda e, ps_=ps_, q=q, C=C: e.copy(out=vaug[0:C, q, 0:2, 0:192],
                                                    in_=ps_[0:C, 0, 0:384].rearrange("p (h d) -> p h d", d=192)), [pk[0]] + vk, vk)
            ACT(lambda e, ps_=ps_, q=q, C=C: e.copy(out=vaug[0:C, q, 2, 0:128], in_=ps_[0:C, 0, 384:512]), [pk[0]] + vk, vk)
            DVE(lambda e, ps_=ps_, q=q, C=C: e.tensor_copy(out=vaug[0:C, q, 2, 128:192], in_=ps_[0:C, 1, 0:64]), [pk[1]] + vk, vk)
            DVE(lambda e, ps_=ps_, q=q, C=C: e.tensor_copy(out=vaug[0:C, q, 3, 0:192], in_=ps_[0:C, 1, 64:256]), [pk[1]] + vk, vk)
            POOL(lambda e, q=q, C=C: e.memset(vaug[0:C, q, :, 192:193], 1.0), vk, vk)
            DVE(lambda e, ps_=ps_, q=q, C=C: e.tensor_tensor(out=iftok[0:C, q, :], in0=ps_[0:C, 1, 256:264], in1=bif[l][0:C, :],
                                                             op=ALU.add), [pk[1], "bif%d" % l, "iftok"], ["iftok"])
        wbo0, wko0 = load_group(l, "mlo0")
        wbo1, wko1 = load_group(l, "mlo1")
        for j in range(6):
            wb, wk = (wbo0, wko0) if j < 4 else (wbo1, wko1)
            ps_, pk = proj(wb, wk, (j % 4) * 128, 128, N)
            ACT(lambda e, ps_=ps_, j=j: e.activation(out=ogT[:, j, 0:N], in_=ps_, func=AF.Sigmoid), pk + ["ogT_%d" % j], ["ogT_%d" % j])
        wbz0, wkz0 = load_group(l, "mlz0")
        wbz1, wkz1 = load_group(l, "mlz1")
        for j in range(6):
            wb, wk = (wbz0, wkz0) if j < 4 else (wbz1, wkz1)
            ps_, pk = proj(wb, wk, (j % 4) * 128, 128, N)
            zsl = zsl_2[j % 2]
            ACT(lambda e, ps_=ps_, zsl=zsl: e.activation(out=zsl[:, 0:N], in_=ps_, func=AF.Silu), pk, ["zsl%d" % (j % 2)])
            DVE(lambda e, j=j, zsl=zsl: e.scalar_tensor_tensor(out=ogT[:, j, 0:N], in0=ogT[:, j, 0:N], scalar=PPc(l, "mlg", j),
                                                               in1=zsl[:, 0:N], op0=ALU.mult, op1=ALU.mult),
                ["ogT_%d" % j, "zsl%d" % (j % 2), pl], ["ogT_%d" % j])
        for q, (o, C, sq) in enumerate(tiles):
            ml_tile(l, q, o, C, sq, kind, og, last, nq)
        ml_out(l, N, tiles)

    def conv_out(l, N, tiles, kind, og, last):
        if kind == "sample":
            ends = [(o + C - 3, sq) for (o, C, sq) in tiles]
        elif last:
            ends = [(N - 3, 0)]
        else:
            return
        for (e0, sq) in ends:
            for hf in range(2):
                for i2 in range(2):
                    i = hf * 2 + i2
                    wbi, wki = load_group(l, "mlqk%d" % i)
                    ps_, pk = fbank()
                    MM(ps_[0:3, 0:384], [(xT[:, kc, e0:e0 + 3], wbi[:, kc, 0:384]) for kc in range(8)], [wki, "xT"], pk)
                    ACT(lambda e, i2=i2, ps_=ps_: e.copy(out=lnt[0:3, i2 * 384:(i2 + 1) * 384], in_=ps_[0:3, 0:384]), pk + ["lnt"], ["lnt"])
                STORE(o_conv[og][l, sq, :, hf * 768:(hf + 1) * 768], lnt[0:3, 0:768], ["lnt"])

    def rwkv_tile(l, q, o, C, sq, kind, og, last, nq):
        S.label = 'rwT'
        tb_ = TB[q % 2]
        pz = tb_['z']
        vtok, khtok, bhtok, Nsb, NTsb, Msb, P1sb, P2sb = (tb_[k_] for k_ in ('vtok', 'khtok', 'bhtok', 'Nsb', 'NTsb', 'Msb', 'P1sb', 'P2sb'))
        samp = kind == "sample"
        sk = "s0t%d" % l
        sl = slice(o, o + C)
        PARTS = [(0, 0), (1, 64)]
        if samp:
            LOAD(stg[:], st_wkv[l, sq].rearrange("h v k -> v h k"), [], ["stg"])
            for j in range(6):
                ps_, pk = fbank()
                TR(ps_[:, 0:64], stg[:, 2 * j:2 * j + 2, :].rearrange("p a b -> p (a b)"), ident_f[0:64, 0:64], ["stg", "ident_f"], pk)
                ACT(lambda e, ps_=ps_, j=j: e.copy(out=S0T[l][:, j, :], in_=ps_[:, 0:64]), pk + [sk], [sk])
        ACT(lambda e: e.copy(out=S0Tb[:], in_=S0T[l][:]), [sk], ["s0tb"])

        def both(fn_):
            if C == 64:
                fn_(slice(0, 128))
            else:
                for par, pb in PARTS:
                    fn_(slice(pb, pb + C))

        for src, srck, dst, dk in ((vT, "vT", vtok, "vtok" + pz), (khat, "khat", khtok, "khtok" + pz), (bhat, "bhat", bhtok, "bhtok" + pz)):
            def fnt(e, src=src):
                for j in range(6):
                    for par, pb in PARTS:
                        ins = e.transpose(out=PSB[pb:pb + C, par, j * 64:(j + 1) * 64], in_=src[pb:pb + 64, j, sl],
                                          identity=ident_b[pb:pb + 64, pb:pb + 64])
                return ins
            PE(fnt, [srck, "ident_b"], ["pb0", "pb1"])
            for par, pb in PARTS:
                ACT(lambda e, dst=dst, par=par, pb=pb: e.copy(out=dst[pb:pb + C, :, :],
                                                                in_=PSB[pb:pb + C, par, 0:384].rearrange("p (j d) -> p j d", d=64)),
                    ["pb%d" % par, dk], [dk])

        def score(lt, lk_, rt2, rk_, mask, mk, dst, dk):
            ps_, pk = fpair()
            def fn(e, ps_=ps_):
                for j in range(6):
                    for par, pb in PARTS:
                        ins = e.matmul(ps_[pb:pb + C, par, j * 64:j * 64 + C], lhsT=lt[pb:pb + 64, j, sl],
                                       rhs=rt2[pb:pb + 64, j, sl], start=True, stop=True)
                return ins
            PE(fn, [lk_, rk_], pk)
            for par, pb in PARTS:
                DVE(lambda e, ps_=ps_, par=par, pb=pb: e.tensor_tensor(
                    out=dst[pb:pb + C, :, 0:C], in0=ps_[pb:pb + C, par, 0:384].rearrange("p (j t) -> p j t", t=64)[:, :, 0:C],
                    in1=mask[pb:pb + C, :, 0:C], op=ALU.mult), [pk[par], mk, dk], [dk])

        def evac(ps_, pk, dst, dk, eng=ACT):
            for par, pb in PARTS:
                if par == 0:
                    ACT(lambda e, par=par, pb=pb: e.copy(out=dst[pb:pb + C, :, :],
                                                         in_=ps_[pb:pb + C, par, 0:384].rearrange("p (j d) -> p j d", d=64)),
                        [pk[par], dk], [dk])
                else:
                    DVE(lambda e, par=par, pb=pb: e.tensor_copy(out=dst[pb:pb + C, :, :],
                                                                in_=ps_[pb:pb + C, par, 0:384].rearrange("p (j d) -> p j d", d=64)),
                        [pk[par], dk], [dk])

        def evac_sq(ps_, pk, dst, dk):
            for par, pb in PARTS:
                DVE(lambda e, par=par, pb=pb: e.tensor_copy(
                    out=dst[pb:pb + C, :, 0:C], in_=ps_[pb:pb + C, par, 0:384].rearrange("p (j t) -> p j t", t=64)[:, :, 0:C]),
                    [pk[par], dk], [dk])

        score(bt_, "bt_", at_, "at_", m_strict, "m_strict", Nsb[0], "Nsb0" + pz)
        score(at_, "at_", bt_, "bt_", m_lower, "m_lower", NTsb[0], "NTsb0" + pz)
        score(kt_, "kt_", at_, "at_", m_strict, "m_strict", Msb, "Msb" + pz)

        ps_, pk = fpair()
        def fnx(e, ps_=ps_):
            for j in range(6):
                for par, pb in PARTS:
                    out = ps_[pb:pb + C, par, j * 64:(j + 1) * 64]
                    e.matmul(out, lhsT=at_[pb:pb + 64, j, sl], rhs=S0Tb[pb:pb + 64, j, :], start=True, stop=False)
                    ins = e.matmul(out, lhsT=Msb[pb:pb + C, j, 0:C], rhs=vtok[pb:pb + C, j, :], start=False, stop=True)
            return ins
        PE(fnx, ["at_", "s0tb", "Msb" + pz, "vtok" + pz], pk)
        evac(ps_, pk, Ysb[0], "Ysb0")
        lev = int(round(math.log2(C)))
        cur = 0
        for lv in range(lev):
            Pc, PTc, Yc = Nsb[cur], NTsb[cur], Ysb[cur]
            Pn, PTn, Yn = Nsb[1 - cur], NTsb[1 - cur], Ysb[1 - cur]
            ps_, pk = fpair()
            def fny(e, ps_=ps_, Pc=Pc, Yc=Yc):
                for j in range(6):
                    for par, pb in PARTS:
                        out = ps_[pb:pb + C, par, j * 64:(j + 1) * 64]
                        e.matmul(out, lhsT=ident_b[pb:pb + C, pb:pb + C], rhs=Yc[pb:pb + C, j, :], start=True, stop=False)
                        ins = e.matmul(out, lhsT=Pc[pb:pb + C, j, 0:C], rhs=Yc[pb:pb + C, j, :], start=False, stop=True)
                return ins
            PE(fny, ["Nsb%d" % cur + pz, "Ysb%d" % cur, "ident_b"], pk)
            evac(ps_, pk, Yn, "Ysb%d" % (1 - cur))
            if lv < lev - 1:
                ps2, pk2 = fpair()
                def fnp(e, ps2=ps2, Pc=Pc, PTc=PTc):
                    for j in range(6):
                        for par, pb in PARTS:
                            ins = e.matmul(ps2[pb:pb + C, par, j * 64:j * 64 + C], lhsT=PTc[pb:pb + C, j, 0:C],
                                           rhs=Pc[pb:pb + C, j, 0:C], start=True, stop=True)
                    return ins
                PE(fnp, ["Nsb%d" % cur + pz, "NTsb%d" % cur + pz], pk2)
                evac_sq(ps2, pk2, Pn, "Nsb%d" % (1 - cur) + pz)
                if lv < lev - 2:
                    ps3, pk3 = fpair()
                    def fnq(e, ps3=ps3, Pc=Pc, PTc=PTc):
                        for j in range(6):
                            for par, pb in PARTS:
                                ins = e.matmul(ps3[pb:pb + C, par, j * 64:j * 64 + C], lhsT=Pc[pb:pb + C, j, 0:C],
                                               rhs=PTc[pb:pb + C, j, 0:C], start=True, stop=True)
                        return ins
                    PE(fnq, ["Nsb%d" % cur + pz, "NTsb%d" % cur + pz], pk3)
                    evac_sq(ps3, pk3, PTn, "NTsb%d" % (1 - cur) + pz)
            cur = 1 - cur
        UT = Ysb[cur]
        uk = "Ysb%d" % cur
        score(kt_, "kt_", rt_, "rt_", m_incl, "m_incl", P1sb, "P1sb" + pz)
        score(bt_, "bt_", rt_, "rt_", m_incl, "m_incl", P2sb, "P2sb" + pz)
        ps_, pk = fpair()
        def fno(e, ps_=ps_, UT=UT):
            for j in range(6):
                for par, pb in PARTS:
                    out = ps_[pb:pb + C, par, j * 64:(j + 1) * 64]
                    e.matmul(out, lhsT=rt_[pb:pb + 64, j, sl], rhs=S0Tb[pb:pb + 64, j, :], start=True, stop=False)
                    e.matmul(out, lhsT=P1sb[pb:pb + C, j, 0:C], rhs=vtok[pb:pb + C, j, :], start=False, stop=False)
                    ins = e.matmul(out, lhsT=P2sb[pb:pb + C, j, 0:C], rhs=UT[pb:pb + C, j, :], start=False, stop=True)
            return ins
        PE(fno, ["rt_", "s0tb", "P1sb" + pz, "P2sb" + pz, "vtok" + pz, uk], pk)
        evac(ps_, pk, yo[:, 6 * q:6 * q + 6, :], "yo")
        psr, pkr = fpair()
        def fnr(e, psr=psr):
            for j in range(6):
                for par, pb in PARTS:
                    ins = e.matmul(psr[pb:pb + C, par, j:j + 1], lhsT=rkp[pb:pb + 64, j, sl], rhs=onesb[pb:pb + 64, 0:1],
                                   start=True, stop=True)
            return ins
        PE(fnr, ["rkp", "onesb"], pkr)
        for par, pb in PARTS:
            ACT(lambda e, par=par, pb=pb, psr=psr: e.copy(out=rkd[pb:pb + C, 6 * q:6 * q + 6], in_=psr[pb:pb + C, par, 0:6]), [pkr[par], "rkd"], ["rkd"])
        both(lambda P: POOL(lambda e: e.tensor_tensor(out=ysq[P, 6 * q:6 * q + 6, :], in0=vtok[P],
                                                      in1=rkd[P, 6 * q:6 * q + 6].unsqueeze(2).broadcast_to([P.stop - P.start, 6, 64]),
                                                      op=ALU.mult), ["vtok" + pz, "rkd", "ysq"], ["ysq"]))
        ps_, pk = fpair()
        def fns(e, ps_=ps_, UT=UT):
            for j in range(6):
                for par, pb in PARTS:
                    out = ps_[pb:pb + 64, par, j * 64:(j + 1) * 64]
                    e.matmul(out, lhsT=khtok[pb:pb + C, j, :], rhs=vtok[pb:pb + C, j, :], start=True, stop=False)
                    ins = e.matmul(out, lhsT=bhtok[pb:pb + C, j, :], rhs=UT[pb:pb + C, j, :], start=False, stop=True)
            return ins
        PE(fns, ["khtok" + pz, "bhtok" + pz, "vtok" + pz, uk], pk)
        for j in range(6):
            for par, pb in PARTS:
                DVE(lambda e, j=j, ps_=ps_, par=par, pb=pb: e.scalar_tensor_tensor(
                    out=S0T[l][pb:pb + 64, j, :], in0=S0T[l][pb:pb + 64, j, :], scalar=gC[pb:pb + 64, j, q:q + 1],
                    in1=ps_[pb:pb + 64, par, j * 64:(j + 1) * 64], op0=ALU.mult, op1=ALU.add), [pk[par], sk, "gC"], [sk])
        if samp or (last and q == nq - 1):
            b_ = sq if samp else 0
            for j in range(6):
                ps2, pk2 = fbank()
                TR(ps2[0:64, 0:128], S0T[l][:, j, :], ident_f[:], [sk, "ident_f"], pk2)
                ACT(lambda e, ps2=ps2, j=j: e.copy(out=stg[:, 2 * j:2 * j + 2, :].rearrange("p a b -> p (a b)"), in_=ps2[0:64, 0:128]),
                    pk2 + ["stg"], ["stg"])
            STORE(o_wkv[og][l, b_].rearrange("h v k -> v h k"), stg[:], ["stg"])

    def rw_out(l, N, tiles):
        S.label = 'rwT'
        C = tiles[0][1]
        nq = len(tiles)
        nj = 6 * nq
        PARTS = [(0, 0), (1, 64)]

        def both(fn_):
            if C == 64:
                fn_(slice(0, 128))
            else:
                for par, pb in PARTS:
                    fn_(slice(pb, pb + C))

        def bc(ap, P):
            return ap.unsqueeze(2).broadcast_to([P.stop - P.start, nj, 64])

        both(lambda P: DVE(lambda e: e.tensor_reduce(out=ystat[P, 0, 0:nj], in_=yo[P, 0:nj, :], axis=AX.X, op=ALU.add), ["yo", "ystat"], ["ystat"]))
        both(lambda P: DVE(lambda e: e.tensor_scalar(out=ystat[P, 0, 0:nj], in0=ystat[P, 0, 0:nj], scalar1=1.0 / 64, scalar2=None, op0=ALU.mult),
                           ["ystat"], ["ystat"]))
        both(lambda P: DVE(lambda e: e.tensor_tensor(out=ycen[P, 0:nj, :], in0=yo[P, 0:nj, :], in1=bc(ystat[P, 0, 0:nj], P), op=ALU.subtract),
                           ["yo", "ystat", "ycen"], ["ycen"]))
        both(lambda P: POOL(lambda e: e.tensor_tensor(out=yo[P, 0:nj, :], in0=ycen[P, 0:nj, :], in1=ycen[P, 0:nj, :], op=ALU.mult), ["ycen", "yo"], ["yo"]))
        both(lambda P: DVE(lambda e: e.tensor_reduce(out=ystat[P, 1, 0:nj], in_=yo[P, 0:nj, :], axis=AX.X, op=ALU.add), ["yo", "ystat"], ["ystat"]))
        both(lambda P: ACT(lambda e: e.activation(out=ystat[P, 2, 0:nj], in_=ystat[P, 1, 0:nj], func=AF.Sqrt, bias=RW_GN_EPS, scale=1.0 / 64),
                           ["ystat"], ["ystat"]))
        both(lambda P: DVE(lambda e: e.reciprocal(out=ystat[P, 2, 0:nj], in_=ystat[P, 2, 0:nj]), ["ystat"], ["ystat"]))
        both(lambda P: DVE(lambda e: e.tensor_tensor(out=ycen[P, 0:nj, :], in0=ycen[P, 0:nj, :], in1=bc(ystat[P, 2, 0:nj], P), op=ALU.mult),
                           ["ycen", "ystat"], ["ycen"]))

        def gb(P, a_):
            return rwln[l][P, a_, :].rearrange("p (j d) -> p j d", d=64).unsqueeze(1).broadcast_to([P.stop - P.start, nq, 6, 64])

        def v4(t, P):
            return t[P, 0:nj, :].rearrange("p (q j) d -> p q j d", j=6)

        both(lambda P: DVE(lambda e: e.tensor_tensor(out=v4(ycen, P), in0=v4(ycen, P), in1=gb(P, 0), op=ALU.mult), ["ycen", "rwln%d" % l], ["ycen"]))
        both(lambda P: POOL(lambda e: e.tensor_tensor(out=v4(ycen, P), in0=v4(ycen, P), in1=gb(P, 1), op=ALU.add), ["ycen", "rwln%d" % l], ["ycen"]))
        both(lambda P: DVE(lambda e: e.tensor_tensor(out=ytb[P, 0:nj, :], in0=ycen[P, 0:nj, :], in1=ysq[P, 0:nj, :], op=ALU.add),
                           ["ycen", "ysq", "ytb"], ["ytb"]))

        def fnb(e):
            for q in range(nq):
                for j in range(6):
                    for par, pb in PARTS:
                        ins = e.transpose(out=PSB[pb:pb + 64, par, (q * 6 + j) * 64:(q * 6 + j) * 64 + C], in_=ytb[pb:pb + C, q * 6 + j, :],
                                          identity=ident_b[pb:pb + C, pb:pb + C])
            return ins
        PE(fnb, ["ytb", "ident_b"], ["pb0", "pb1"])
        for par, pb in PARTS:
            for q, (o, C_, sq) in enumerate(tiles):
                DVE(lambda e, par=par, pb=pb, q=q, o=o: e.tensor_tensor(
                    out=yrwT[pb:pb + 64, :, o:o + C],
                    in0=PSB[pb:pb + 64, par, q * 384:(q + 1) * 384].rearrange("p (j t) -> p j t", t=64)[:, :, 0:C],
                    in1=grw[pb:pb + 64, :, o:o + C], op=ALU.mult), ["pb%d" % par, "grw", "yrwT"], ["yrwT"])

    def s5_en(l, N):
        S.label = 's5T'
        C = N
        sl = slice(0, N)
        for ut in range(4):
            ps_, pk = fpair()
            MM(ps_[0:C, 0, :], [(uT[:, ut, sl], bblkB[:, ut, 0:512])], ["uT", "bblkB"], [pk[0]])
            MM(ps_[0:C, 1, :], [(uT[:, ut, sl], bblkB[:, ut, 512:1024])], ["uT", "bblkB"], [pk[1]])
            pvv = ps_[0:C].rearrange("p b (i c q) -> p (b i) c q", i=2, c=2)
            bur, bui = pvv[:, :, 0, :], pvv[:, :, 1, :]
            enr, eni = EnB[0:C, ut * 4:(ut + 1) * 4, 0, :], EnB[0:C, ut * 4:(ut + 1) * 4, 1, :]
            wr_, wi_ = wtok[0:C, ut * 4:(ut + 1) * 4, 0, :], wtok[0:C, ut * 4:(ut + 1) * 4, 1, :]
            ek = ["EnB"]
            DVE(lambda e, bur=bur, enr=enr: e.tensor_tensor(out=s5a[0:C], in0=bur, in1=enr, op=ALU.mult), pk + ek, ["s5a"])
            DVE(lambda e, bui=bui, eni=eni: e.tensor_tensor(out=s5b[0:C], in0=bui, in1=eni, op=ALU.mult), pk + ek, ["s5b"])
            POOL(lambda e, wr_=wr_: e.tensor_tensor(out=wr_, in0=s5a[0:C], in1=s5b[0:C], op=ALU.subtract), ["s5a", "s5b", "wtok"], ["wtok"])
            DVE(lambda e, bur=bur, eni=eni: e.tensor_tensor(out=s5a[0:C], in0=bur, in1=eni, op=ALU.mult), pk + ek + ["s5a"], ["s5a"])
            DVE(lambda e, bui=bui, enr=enr: e.tensor_tensor(out=s5b[0:C], in0=bui, in1=enr, op=ALU.mult), pk + ek + ["s5b"], ["s5b"])
            POOL(lambda e, wi_=wi_: e.tensor_tensor(out=wi_, in0=s5a[0:C], in1=s5b[0:C], op=ALU.add), ["s5a", "s5b", "wtok"], ["wtok"])

    def s5_tile(l, q, o, C, sq, kind, og, last, nq):
        S.label = 's5T'
        samp = kind == "sample"
        hk = "h0%d" % l
        sl = slice(o, o + C)
        if samp:
            for c, srcd in ((0, st_s5re), (1, st_s5im)):
                LOAD(lnx[0:16, c * 128:(c + 1) * 128], srcd[l, sq].rearrange("(i g) p -> i (g p)", g=2), [], ["lnx"])
            for c in range(2):
                ps_, pk = fbank()
                TR(ps_[:, 0:16], lnx[0:16, c * 128:(c + 1) * 128], ident_f[0:16, 0:16], ["lnx", "ident_f"], pk)
                ACT(lambda e, ps_=ps_, c=c: e.copy(out=h0[l][:, c, :], in_=ps_[:, 0:16]), pk + [hk], [hk])
        for c, Gd, gk in ((0, Gr, "Gr"), (1, Gi, "Gi")):
            ps_, pk = fpair()
            def fnc(e, ps_=ps_, c=c):
                for i in range(16):
                    ins = e.matmul(ps_[:, i // 8, (i % 8) * 64:(i % 8) * 64 + C], lhsT=wtok[o:o + C, i, c, :], rhs=tri2[o:o + C, 0:C],
                                   start=True, stop=True)
                return ins
            PE(fnc, ["wtok", "tri2"], pk)
            DVE(lambda e, ps_=ps_, Gd=Gd, c=c: e.tensor_tensor(
                out=Gd[:, :, 0:C].rearrange("p (b i) t -> p b i t", b=2),
                in0=ps_[:, :, :].rearrange("p b (i t) -> p b i t", t=64)[:, :, :, 0:C],
                in1=h0[l][:, c, :].rearrange("p (b i) -> p b i", b=2).unsqueeze(3).broadcast_to([128, 2, 8, C]), op=ALU.add),
                pk + [hk], [gk])
        er, ei = EpB[:, 0, :, 0:C], EpB[:, 1, :, 0:C]
        ek = ["EpB"]
        DVE(lambda e: e.tensor_tensor(out=hA[:, :, 0:C], in0=Gr[:, :, 0:C], in1=er, op=ALU.mult), ["Gr"] + ek, ["hA"])
        DVE(lambda e: e.tensor_tensor(out=hB[:, :, 0:C], in0=Gi[:, :, 0:C], in1=ei, op=ALU.mult), ["Gi"] + ek, ["hB"])
        DVE(lambda e: e.tensor_tensor(out=hre[:, :, sl], in0=hA[:, :, 0:C], in1=hB[:, :, 0:C], op=ALU.subtract), ["hA", "hB", "hre"], ["hre"])
        POOL(lambda e: e.tensor_tensor(out=hC[:, :, 0:C], in0=Gi[:, :, 0:C], in1=er, op=ALU.mult), ["Gi"] + ek, ["hC"])
        POOL(lambda e: e.tensor_tensor(out=hD[:, :, 0:C], in0=Gr[:, :, 0:C], in1=ei, op=ALU.mult), ["Gr"] + ek, ["hD"])
        DVE(lambda e: e.scalar_tensor_tensor(out=himn[:, :, sl], in0=hC[:, :, 0:C], scalar=-1.0, in1=hD[:, :, 0:C],
                                             op0=ALU.mult, op1=ALU.subtract), ["hC", "hD", "himn"], ["himn"])
        DVE(lambda e: e.tensor_tensor(out=h0[l][:, 0, :], in0=hA[:, :, C - 1], in1=hB[:, :, C - 1], op=ALU.subtract),
            ["hA", "hB", hk, "Gr", "Gi"], [hk])
        DVE(lambda e: e.tensor_tensor(out=h0[l][:, 1, :], in0=hC[:, :, C - 1], in1=hD[:, :, C - 1], op=ALU.add), ["hC", "hD", hk], [hk])
        if samp or (last and q == nq - 1):
            b_ = sq if samp else 0
            for c, dd in ((0, o_s5re), (1, o_s5im)):
                ps3, pk3 = fbank()
                TR(ps3[0:16, 0:128], h0[l][:, c, :], ident_f[:], [hk, "ident_f"], pk3)
                ACT(lambda e, ps3=ps3, c=c: e.copy(out=lnx[0:16, c * 128:(c + 1) * 128], in_=ps3[0:16, 0:128]), pk3 + ["lnx"], ["lnx"])
                STORE(dd[og][l, b_].rearrange("(i g) p -> i (g p)", g=2), lnx[0:16, c * 128:(c + 1) * 128], ["lnx"])

    def s5_out(l, N):
        S.label = 's5T'
        C = N
        sl = slice(0, N)
        ps_, pk = fbank()
        def fny(e, ps_=ps_):
            for ut in range(4):
                for hf in range(2):
                    out = ps_[hf * 64:(hf + 1) * 64, ut * 128:ut * 128 + C]
                    n_ = 0
                    for ii in range(2):
                        i = ut * 4 + hf * 2 + ii
                        for c, hsrc in ((0, hre), (1, himn)):
                            ins = e.matmul(out, lhsT=cpadB[:, i, c, :], rhs=hsrc[:, i, 0:C], start=(n_ == 0), stop=(n_ == 3))
                            n_ += 1
            return ins
        PE(fny, ["cpadB", "hre", "himn"], pk)
        for ut in range(4):
            DVE(lambda e, ut=ut, ps_=ps_: e.scalar_tensor_tensor(out=yv[:, ut, 0:C], in0=u32[:, ut, sl], scalar=PPc(l, "s5d", ut),
                                                                 in1=ps_[:, ut * 128:ut * 128 + C], op0=ALU.mult, op1=ALU.add),
                pk + ["u32", "pp%d" % l, "yv"], ["yv"])
        POOL(lambda e: e.tensor_tensor(out=gt[:, :, 0:C], in0=yv[:, :, 0:C], in1=yv[:, :, 0:C], op=ALU.mult), ["yv"], ["gt"])
        DVE(lambda e: e.tensor_scalar(out=gt[:, :, 0:C], in0=gt[:, :, 0:C], scalar1=0.044715, scalar2=1.0, op0=ALU.mult, op1=ALU.add),
            ["gt"], ["gt"])
        DVE(lambda e: e.tensor_tensor(out=gt[:, :, 0:C], in0=gt[:, :, 0:C], in1=yv[:, :, 0:C], op=ALU.mult), ["gt", "yv"], ["gt"])
        ACT(lambda e: e.activation(out=gsg[:, :, 0:C], in_=gt[:, :, 0:C], func=AF.Sigmoid, scale=1.5957691216057308), ["gt"], ["gsg"])
        DVE(lambda e: e.tensor_tensor(out=gl[:, :, 0:C], in0=yv[:, :, 0:C], in1=gsg[:, :, 0:C], op=ALU.mult), ["yv", "gsg"], ["gl"])
        ACT(lambda e: e.copy(out=glb[:, :, 0:C], in_=gl[:, :, 0:C]), ["gl"], ["glb"])
        ps2, pk2 = fbank()
        def fng(e):
            for ct in range(4):
                for kc in range(4):
                    ins = e.matmul(ps2[:, ct * 128:ct * 128 + C], lhsT=wglu[l][:, kc, ct * 128:(ct + 1) * 128], rhs=glb[:, kc, 0:C],
                                   start=(kc == 0), stop=(kc == 3))
            return ins
        PE(fng, ["wglu%d" % l, "glb"], pk2)
        for ct in range(4):
            ACT(lambda e, ct=ct: e.activation(out=sgl[:, ct, 0:C], in_=ps2[:, ct * 128:ct * 128 + C], func=AF.Sigmoid,
                                              bias=PPc(l, "bglu", ct), scale=1.0), pk2 + ["pp%d" % l, "sgl"], ["sgl"])
        POOL(lambda e: e.tensor_tensor(out=gl[:, :, 0:C], in0=gl[:, :, 0:C], in1=sgl[:, :, 0:C], op=ALU.mult), ["gl", "sgl"], ["gl"])
        DVE(lambda e: e.tensor_tensor(out=ys5T[:, :, sl], in0=gl[:, :, 0:C], in1=gs5[:, :, sl], op=ALU.mult),
            ["gl", "gs5", "ys5T"], ["ys5T"])

    def ml_tile(l, q, o, C, sq, kind, og, last, nq):
        S.label = 'mlT'
        samp = kind == "sample"
        ck, mk_ = "cst%d" % l, "mst%d" % l
        sl = slice(o, o + C)
        if samp:
            for kt in range(2):
                LOAD(Cst[l][:, kt, :, 0:192], st_c[l, sq, :, kt * 96:(kt + 1) * 96, :].rearrange("h p v -> p h v"), [ck], [ck])
                LOAD(Cst[l][:, kt, :, 192:193], st_n[l, sq, :, kt * 96:(kt + 1) * 96].rearrange("h (p o) -> p h o", o=1), [ck], [ck], slow=True)
            LOAD(mst[l][:], st_m[l, sq].rearrange("(h o) -> h o", o=1), [], [mk_], slow=True)
        ik = "iftok"
        ACT(lambda e: e.activation(out=lfi[0:C, 4:8], in_=iftok[0:C, q, 4:8], func=AF.Sigmoid), [ik], ["lfi"])
        ACT(lambda e: e.activation(out=lfi[0:C, 4:8], in_=lfi[0:C, 4:8], func=AF.Ln), ["lfi"], ["lfi"])
        ps_, pk = fbank()
        MM(ps_[0:C, 0:4], [(tri_f[0:C, 0:C], lfi[0:C, 4:8])], ["tri_f", "lfi"], pk)
        ps7, pk7 = fbank()
        MM(ps7[0:4, 0:1], [(lfi[0:C, 4:8], ones_f[0:C, 0:1])], ["ones_f", "lfi"], pk7)
        ACT(lambda e: e.copy(out=bcs[0:C, :], in_=ps_[0:C, 0:4]), pk, ["bcs"])
        ACT(lambda e: e.copy(out=bend[:], in_=ps7[0:4, 0:1]), pk7, ["bend"])
        DVE(lambda e: e.tensor_tensor(out=zz[0:C, :], in0=iftok[0:C, q, 0:4], in1=bcs[0:C, :], op=ALU.subtract), [ik, "bcs"], ["zz"])
        ps2, pk2 = fbank()
        TR(ps2[0:4, 0:C], zz[0:C, :], ident_f[0:C, 0:C], ["zz", "ident_f"], pk2)
        DVE(lambda e: e.tensor_reduce(out=zmax[:], in_=ps2[0:4, 0:C], axis=AX.X, op=ALU.max), pk2, ["zmax"])
        DVE(lambda e: e.tensor_tensor(out=mu4[:], in0=zmax[:], in1=mst[l][:], op=ALU.max), ["zmax", mk_], ["mu4"])
        DVE(lambda e: e.tensor_tensor(out=f4[:], in0=mst[l][:], in1=mu4[:], op=ALU.subtract), [mk_, "mu4"], ["f4"])
        ACT(lambda e: e.activation(out=f4[:], in_=f4[:], func=AF.Exp), ["f4"], ["f4"])
        DVE(lambda e: e.tensor_tensor(out=mst[l][:], in0=bend[:], in1=mu4[:], op=ALU.add), ["bend", "mu4", "f4", mk_], [mk_])
        DVE(lambda e: e.tensor_scalar(out=dg[:, 0:4], in0=ident_f[0:4, 0:4], scalar1=mu4[:, 0:1], scalar2=None, op0=ALU.mult),
            ["ident_f", "mu4"], ["dg"])
        DVE(lambda e: e.tensor_scalar(out=dg[:, 4:8], in0=ident_f[0:4, 0:4], scalar1=f4[:, 0:1], scalar2=None, op0=ALU.mult),
            ["ident_f", "f4", "dg"], ["dg"])
        ps3, pk3 = fbank()
        MM(ps3[:, 0:8], [(ones_f[0:4, :], dg[:, :])], ["ones_f", "dg"], pk3)
        ACT(lambda e: e.copy(out=bc8[:], in_=ps3[:, 0:8]), pk3, ["bc8"])
        DVE(lambda e: e.tensor_tensor(out=ee[0:C, :], in0=zz[0:C, :], in1=bc8[0:C, 0:4], op=ALU.subtract), ["zz", "bc8"], ["ee"])
        ACT(lambda e: e.activation(out=ee[0:C, :], in_=ee[0:C, :], func=AF.Exp), ["ee"], ["ee"])
        DVE(lambda e: e.tensor_tensor(out=clampt[0:C, :], in0=bcs[0:C, :], in1=bc8[0:C, 0:4], op=ALU.add), ["bcs", "bc8"], ["clampt"])
        ACT(lambda e: e.activation(out=clampt[0:C, :], in_=clampt[0:C, :], func=AF.Exp, scale=-1.0), ["clampt"], ["clampt"])
        for hd in range(4):
            DVE(lambda e, hd=hd: e.tensor_scalar(out=Cst[l][:, :, hd, :], in0=Cst[l][:, :, hd, :], scalar1=bc8[0:96, 4 + hd:5 + hd],
                                                 scalar2=None, op0=ALU.mult), [ck, "bc8"], [ck])
        ACT(lambda e: e.copy(out=Cstb[:], in_=Cst[l][:]), [ck], ["cstb"])
        ps4, pk4 = fbank()
        def fnsc(e):
            for hd in range(4):
                for kt in range(2):
                    ins = e.matmul(ps4[0:C, hd * 64:hd * 64 + C], lhsT=qkT[:, 8 + 2 * hd + kt, sl], rhs=qkT[:, 2 * hd + kt, sl],
                                   start=(kt == 0), stop=(kt == 1))
            return ins
        PE(fnsc, ["qkT"], pk4)
        p4 = ps4[0:C, 0:256].rearrange("p (h t) -> p h t", t=64)[:, :, 0:C]
        DVE(lambda e: e.tensor_tensor(out=PTs[0:C, :, 0:C], in0=p4, in1=m_incl[0:C, 0:4, 0:C], op=ALU.mult), pk4 + ["m_incl"], ["PTs"])
        DVE(lambda e: e.tensor_tensor(out=PTb[0:C, :, 0:C], in0=PTs[0:C, :, 0:C], in1=ee[0:C, :].unsqueeze(2).broadcast_to([C, 4, C]),
                                      op=ALU.mult), ["PTs", "ee"], ["PTb"])
        ps5, pk5 = fpair()
        def fnnd(e):
            for hd in range(4):
                out = ps5[0:C, hd // 2, (hd % 2) * 193:(hd % 2) * 193 + 193]
                e.matmul(out, lhsT=PTb[0:C, hd, 0:C], rhs=vaug[0:C, q, hd, :], start=True, stop=False)
                e.matmul(out, lhsT=qkT[:, 2 * hd, sl], rhs=Cstb[:, 0, hd, :], start=False, stop=False)
                ins = e.matmul(out, lhsT=qkT[:, 2 * hd + 1, sl], rhs=Cstb[:, 1, hd, :], start=False, stop=True)
            return ins
        PE(fnnd, ["PTb", "vaug", "cstb", "qkT"], pk5)
        nd = ps5[0:C, :, 0:386].rearrange("p b (h d) -> p b h d", d=193)
        hv4 = hst[0:C, 0, :].rearrange("p (b h) -> p b h", b=2)
        ACT(lambda e: e.activation(out=hv4.unsqueeze(3), in_=nd[:, :, :, 192:193], func=AF.Abs), pk5, ["hst"])
        DVE(lambda e: e.tensor_tensor(out=hst[0:C, 0, :], in0=hst[0:C, 0, :], in1=clampt[0:C, :], op=ALU.max),
            ["hst", "clampt"], ["hst"])
        DVE(lambda e: e.reciprocal(out=hst[0:C, 0, :], in_=hst[0:C, 0, :]), ["hst"], ["hst"])
        DVE(lambda e: e.tensor_tensor(out=hh[0:C, 4 * q:4 * q + 4, :].rearrange("p (b h) d -> p b h d", b=2), in0=nd[:, :, :, 0:192],
                                      in1=hv4.unsqueeze(3).broadcast_to([C, 2, 2, 192]), op=ALU.mult), pk5 + ["hst", "hh"], ["hh"])
        pt2, pk2_ = bbank()
        for t in range(8):
            TR(pt2[0:C, t * 96:(t + 1) * 96], qkT[:, 8 + t, sl], ident_b[0:96, 0:96], ["qkT", "ident_b"], pk2_)
        DVE(lambda e, pt2=pt2: e.tensor_tensor(out=khm[0:C], in0=pt2[0:C, 0:768].rearrange("p (h d) -> p h d", d=192),
                                               in1=ee[0:C, :].unsqueeze(2).broadcast_to([C, 4, 192]), op=ALU.mult), pk2_ + ["ee"], ["khm"])
        for kt in range(2):
            ps6, pk6 = fpair()
            def fncu(e, ps6=ps6, kt=kt):
                for hd in range(4):
                    ins = e.matmul(ps6[0:96, hd // 2, (hd % 2) * 193:(hd % 2) * 193 + 193], lhsT=khm[0:C, hd, kt * 96:(kt + 1) * 96],
                                   rhs=vaug[0:C, q, hd, :], start=True, stop=True)
                return ins
            PE(fncu, ["khm", "vaug"], pk6)
            DVE(lambda e, ps6=ps6, kt=kt: e.tensor_tensor(out=Cst[l][:, kt, :, :].rearrange("p (b h) d -> p b h d", b=2),
                                                          in0=Cst[l][:, kt, :, :].rearrange("p (b h) d -> p b h d", b=2),
                                                          in1=ps6[0:96, :, 0:386].rearrange("p b (h d) -> p b h d", d=193), op=ALU.add),
                pk6 + [ck], [ck])
        if samp or (last and q == nq - 1):
            b_ = sq if samp else 0
            for kt in range(2):
                STORE(o_c[og][l, b_, :, kt * 96:(kt + 1) * 96, :].rearrange("h p v -> p h v"), Cst[l][:, kt, :, 0:192], [ck])
                STORE(o_n[og][l, b_, :, kt * 96:(kt + 1) * 96].rearrange("h (p o) -> p h o", o=1), Cst[l][:, kt, :, 192:193], [ck], slow=True)
            STORE(o_m[og][l, b_].rearrange("(h o) -> h o", o=1), mst[l][:], [mk_], slow=True)

    def ml_out(l, N, tiles):
        S.label = 'mlT'
        C = tiles[0][1]
        nh = 4 * len(tiles)
        H = hh[0:C, 0:nh, :]
        DVE(lambda e: e.tensor_reduce(out=hs2[0:C, 0, 0:nh], in_=H, axis=AX.X, op=ALU.add), ["hh", "hs2"], ["hs2"])
        DVE(lambda e: e.tensor_scalar(out=hs2[0:C, 0, 0:nh], in0=hs2[0:C, 0, 0:nh], scalar1=1.0 / 192, scalar2=None, op0=ALU.mult), ["hs2"], ["hs2"])
        DVE(lambda e: e.tensor_tensor(out=hcen[0:C, 0:nh, :], in0=H, in1=hs2[0:C, 0, 0:nh].unsqueeze(2).broadcast_to([C, nh, 192]),
                                      op=ALU.subtract), ["hh", "hs2"], ["hcen"])
        POOL(lambda e: e.tensor_tensor(out=hsq[0:C, 0:nh, :], in0=hcen[0:C, 0:nh, :], in1=hcen[0:C, 0:nh, :], op=ALU.mult), ["hcen"], ["hsq"])
        DVE(lambda e: e.tensor_reduce(out=hs2[0:C, 1, 0:nh], in_=hsq[0:C, 0:nh, :], axis=AX.X, op=ALU.add), ["hsq", "hs2"], ["hs2"])
        ACT(lambda e: e.activation(out=hs2[0:C, 2, 0:nh], in_=hs2[0:C, 1, 0:nh], func=AF.Sqrt, bias=LN_EPS, scale=1.0 / 192), ["hs2"], ["hs2"])
        DVE(lambda e: e.reciprocal(out=hs2[0:C, 2, 0:nh], in_=hs2[0:C, 2, 0:nh]), ["hs2"], ["hs2"])
        DVE(lambda e: e.tensor_tensor(out=hnb[0:C].rearrange("p q (h d) -> p (q h) d", d=192)[:, 0:nh, :], in0=hcen[0:C, 0:nh, :],
                                      in1=hs2[0:C, 2, 0:nh].unsqueeze(2).broadcast_to([C, nh, 192]), op=ALU.mult), ["hcen", "hs2"], ["hnb"])
        pt, pk = bbank()
        for q, (o, C_, sq) in enumerate(tiles):
            for j in range(6):
                TR(pt[:, j * 128 + o:j * 128 + o + C], hnb[0:C, q, j * 128:(j + 1) * 128], ident_b[0:C, 0:C], ["hnb", "ident_b"], pk)
        DVE(lambda e, pt=pt: e.tensor_tensor(out=ymlT[:, :, 0:N], in0=pt[:, 0:768].rearrange("p (j t) -> p j t", t=128)[:, :, 0:N],
                                             in1=ogT[:, :, 0:N], op=ALU.mult), pk + ["ogT", "ymlT"], ["ymlT"])

    def phase_c(kind, N, l, ydst):
        S.label = 'C'
        pl = "pp%d" % l
        for jg in range(2):
            for b in range(3):
                wbm, wkm = load_group(l, "mg%d%d" % (b, jg))
                wbb, wkb = load_group(l, "br%d%d" % (b, jg))
                for jj in range(4):
                    j = jg * 4 + jj
                    ps_, pk = proj(wbm, wkm, jj * 128, 128, N)
                    gi_ = (b + jj) % 2
                    ACT(lambda e, ps_=ps_, gi_=gi_, b=b, j=j: e.activation(out=gate[gi_][:, 0:N], in_=ps_, func=AF.Sigmoid,
                                                                           bias=PPc(l, "bmrg", b * 8 + j), scale=1.0),
                        pk + [pl], ["gate%d" % gi_])
                    ps2, pk2 = fbank()
                    nk = BR_KC[b]
                    MM(ps2[:, 0:N], [(wbb[:, kc, jj * 128:(jj + 1) * 128], BR_T[b][:, kc, 0:N]) for kc in range(nk)],
                       [wkb, BR_K[b]], pk2)
                    if b == 0:
                        DVE(lambda e, ps2=ps2, gi_=gi_, j=j: e.tensor_tensor(out=mrg[:, j, 0:N], in0=ps2[:, 0:N], in1=gate[gi_][:, 0:N],
                                                                             op=ALU.mult), pk2 + ["gate%d" % gi_, "mrg%d" % j], ["mrg%d" % j])
                    else:
                        ctmp = ctmp2[jj % 2]
                        DVE(lambda e, ps2=ps2, gi_=gi_, ctmp=ctmp: e.tensor_tensor(out=ctmp[:, 0:N], in0=ps2[:, 0:N], in1=gate[gi_][:, 0:N],
                                                                        op=ALU.mult), pk2 + ["gate%d" % gi_], ["ctmp%d" % (jj % 2)])
                        if b == 1:
                            POOL(lambda e, j=j, ctmp=ctmp: e.tensor_tensor(out=mrg[:, j, 0:N], in0=mrg[:, j, 0:N], in1=ctmp[:, 0:N], op=ALU.add),
                                 ["mrg%d" % j, "ctmp%d" % (jj % 2)], ["mrg%d" % j])
                        else:
                            POOL(lambda e, j=j, ctmp=ctmp: e.tensor_tensor(out=mrgb[:, j, 0:N], in0=mrg[:, j, 0:N], in1=ctmp[:, 0:N], op=ALU.add),
                                 ["mrg%d" % j, "ctmp%d" % (jj % 2), "mrgb%d" % j], ["mrgb%d" % j])
        load_bc(1 + l)
        wo0, wok0 = load_group(l, "wo0")
        wo1, wok1 = load_group(l, "wo1")
        rows = N
        ps_, pk = fpair()
        MM(ps_[0:rows, 0, :], [(mrgb[:, kc, 0:rows], wo0[:, kc, 0:512]) for kc in range(8)], [wok0] + ["mrgb%d" % j_ for j_ in range(8)], [pk[0]])
        MM(ps_[0:rows, 1, :], [(mrgb[:, kc, 0:rows], wo1[:, kc, 0:512]) for kc in range(8)], [wok1] + ["mrgb%d" % j_ for j_ in range(8)], [pk[1]])
        DVE(lambda e, ps_=ps_: e.scalar_tensor_tensor(
            out=lnx[0:rows, :].rearrange("p (b n) -> p b n", b=2), in0=x_tok[0:rows, 0, :].rearrange("p (b n) -> p b n", b=2),
            scalar=DN_ALPHA, in1=ps_[0:rows, :, :], op0=ALU.mult, op1=ALU.add), pk + ["x_tok", "lnx"], ["lnx"])
        ln_block(lnx, ["lnx"], rows, out_dram=ydst)

    pass
    if stage >= 1000:
        S.limit = stage - 1000
        stage = 3
    try:
        if stage >= 1:
            run_pass("meta", 16, [(0, 16, 0)], meta, None, True, False)
        npp = SEQ // NT
        tl2 = [(0, 64, 0), (64, 64, 1)]
        for p in range(npp):
            if stage >= 2 and (stage >= 99 or p < stage - 1):
                run_pass("prompt", NT, tl2, xp[p * NT:(p + 1) * NT, :], yp[p * NT:(p + 1) * NT, :], False, p == npp - 1)
        for sp_ in range(2):
            if stage >= 99:
                run_pass("sample", NT, [(0, 64, 2 * sp_), (64, 64, 2 * sp_ + 1)], xs[sp_ * NT:(sp_ + 1) * NT, :],
                         ys[sp_ * NT:(sp_ + 1) * NT, :], False, False)


    except StopBuild:
        pass
    pass
    S.emit(final_wait_ops=final_ops)
    es.close()
    return nc


_NC = None


def _get_nc():
    global _NC
    if _NC is None:
        _NC = build()
    return _NC


def _host_inputs(inp, c):
    f = lambda a: np.ascontiguousarray(a, dtype=np.float32)
    p = c % 4
    sl = slice(4 * c, 4 * c + 4)
    m = {}
    m["xp"] = f(inp["x_prompt"][p])
    m["xs"] = f(inp["x_sample"][sl].reshape(NSAMP * 64, D))
    m["meta"] = f(inp["meta"])
    m["st_shift"] = f(inp["state_rwkv_shift"][:, sl])
    m["st_wkv"] = f(inp["state_rwkv_wkv"][:, sl])
    m["st_s5re"] = f(inp["state_s5_re"][:, sl])
    m["st_s5im"] = f(inp["state_s5_im"][:, sl])
    m["st_conv"] = f(inp["state_mlstm_conv"][:, sl])
    m["st_c"] = f(inp["state_mlstm_c"][:, sl])
    m["st_n"] = f(inp["state_mlstm_n"][:, sl])
    m["st_m"] = f(inp["state_mlstm_m"][:, sl])
    for k in ("w_in", "w_br_rw", "w_br_s5", "w_br_ml", "w_out", "s5_w_glu", "rw_w2", "rw_a2"):
        m[k] = f(inp[k])
    return m


def _shared_inputs(inp):
    f32 = np.float32
    pp = np.zeros((DEPTH, 128, NPP), f32)
    pq = np.zeros((DEPTH, 96, 80), f32)

    def cols(v, n):
        return np.asarray(v, f32).reshape(n, 128).T

    for l in range(DEPTH):
        def put(name, arr):
            o, w = PP[name]
            pp[l, :, o:o + w] = arr
        put("mu", cols(inp["rw_mu"][l], 19))
        put("w0", cols(inp["rw_w0"][l], 6))
        put("a0", cols(inp["rw_a0"][l], 6))
        put("kk", cols(inp["rw_kk"][l], 6))
        put("ka", cols(inp["rw_ka"][l], 6))
        put("rk", cols(np.asarray(inp["rw_rk"][l]).reshape(768), 6))
        put("s5d", cols(inp["s5_d"][l], 4))
        put("bglu", cols(inp["s5_b_glu"][l], 4))
        put("mlg", cols(inp["ml_ln_g"][l], 6))
        put("bmrg", cols(inp["b_merge"][l], 24))
        are = np.asarray(inp["s5_a_re"][l], f32).reshape(16, 2, 64).transpose(1, 2, 0).reshape(128, 16)
        aim = np.asarray(inp["s5_a_im"][l], f32).reshape(16, 2, 64).transpose(1, 2, 0).reshape(128, 16)
        ldt = np.repeat(np.asarray(inp["s5_log_dt"][l], f32).reshape(16, 2, 1), 64, axis=2).transpose(1, 2, 0).reshape(128, 16)
        put("are", are)
        put("aim", aim)
        put("ldt", ldt)
        cw = np.asarray(inp["ml_conv_w"][l], f32).reshape(4, 16, 96)
        cb = np.asarray(inp["ml_conv_b"][l], f32).reshape(16, 96)
        pqv = np.zeros((96, 16, 5), f32)
        pqv[:, :, 0:4] = cw.transpose(2, 1, 0)
        pqv[:, :, 4] = cb.T
        pq[l] = pqv.reshape(96, 80)
    bc = np.stack([inp["in_ln_g"], inp["in_ln_b"], inp["ln_g"][0], inp["ln_b"][0], inp["ln_g"][1], inp["ln_b"][1]]).astype(f32)
    rwln = np.stack([np.asarray(inp["rw_ln_g"], f32), np.asarray(inp["rw_ln_b"], f32)], axis=1)
    rwln = np.ascontiguousarray(rwln.reshape(DEPTH, 2, 6, 2, 64).transpose(0, 1, 3, 2, 4).reshape(DEPTH, 2, 2, 384))
    bif = np.asarray(inp["ml_b_if"], f32)
    bblk = np.zeros((DEPTH, 128, 4, 1024), f32)
    cpad = np.zeros((DEPTH, 128, 16, 2, 64), f32)
    for l in range(DEPTH):
        for c, (bk, ck) in enumerate((("s5_b_re", "s5_c_re"), ("s5_b_im", "s5_c_im"))):
            B = np.asarray(inp[bk][l], f32)
            Cm = np.asarray(inp[ck][l], f32)
            for g in range(32):
                i = g // 2
                ut = g // 8
                col0 = (i % 4) * 256 + c * 128 + (g % 2) * 64
                bblk[l, (g % 8) * 16:(g % 8) * 16 + 16, ut, col0:col0 + 64] = B[g].T
                oc = (i % 2) * 32 + (g % 2) * 16
                cpad[l, (g % 2) * 64:(g % 2) * 64 + 64, i, c, oc:oc + 16] = Cm[g].T
    return dict(pp=pp, pq=pq, bc=bc, rwln=rwln, bif=bif, bblk=bblk, cpad=cpad)


def kernel(**inp):
    inp = {k: np.asarray(v) for k, v in inp.items()}
    nc = _get_nc()
    shared = _shared_inputs(inp)
    in_maps = []
    for c in range(8):
        m = _host_inputs(inp, c)
        m.update(shared)
        in_maps.append(m)
    res = run_bass_kernel_spmd(nc, in_maps, core_ids=list(range(8)))
    R = res.results
    y_prompt = np.stack([R[c]["yp"] for c in range(4)], 0)
    y_sample = np.concatenate([R[c]["ys"].reshape(NSAMP, 64, D) for c in range(8)], 0)
    outs = [y_prompt, y_sample]
    for nm in ("shift", "wkv", "s5re", "s5im", "conv", "c", "n", "m"):
        outs.append(np.concatenate([R[c]["p_" + nm] for c in range(4)], 1))
    for nm in ("shift", "wkv", "s5re", "s5im", "conv", "c", "n", "m"):
        outs.append(np.concatenate([R[c]["s_" + nm] for c in range(8)], 1))
    return tuple(np.ascontiguousarray(o, dtype=np.float32) for o in outs)
```

```python
import contextlib
import math
import numpy as np
import concourse.bass as bass
import concourse.mybir as mybir
from concourse.bass_utils import run_bass_kernel_spmd

F32 = mybir.dt.float32
BF16 = mybir.dt.bfloat16
ALU = mybir.AluOpType
AF = mybir.ActivationFunctionType
AX = mybir.AxisListType

ENGS = ("pe", "act", "dve", "pool", "sp")
D = 1024
DEPTH = 2
NT = 128
SEQ = 4096
NMETA = 16
NSAMP = 4
RWW = 768
RWS = 2432
S5W = 512
MLW = 768
INC = 11144
DN_ALPHA = (2 * DEPTH) ** 0.25
LN_EPS = 1e-5
RW_GN_EPS = 64e-5
C_RW = 0
C_RWG = 2432
C_S5U = 3200
C_S5G = 3712
C_MLQK = 4224
C_MLV = 5760
C_MLIF = 6528
C_MLO = 6536
C_MLZ = 7304
C_MRG = 8072
EXPM05 = math.exp(-0.5)


class StopBuild(Exception):
    pass


class _FirstHook:
    def __init__(self, eng, wait):
        self._e = eng
        self._w = wait

    def _wrap(self, f):
        def g(*a, **k):
            ins = f(*a, **k)
            if self._w is not None:
                ins._wait_ge(*self._w)
                self._w = None
            return ins
        return g

    def __getattr__(self, name):
        v = getattr(self._e, name)
        if name in ("matmul", "transpose"):
            return self._wrap(v)
        return v


class Sched:
    limit = None
    resched = True
    def __init__(self, nc, n_dma_sems=12):
        self.nc = nc
        self.ops = []
        self.last_w = {}
        self.readers = {}
        self.n_dma_sems = n_dma_sems
        self.arena = {}

    def xl(self, keys):
        out = []
        for k in keys:
            r = self.arena.get(k)
            if r is None:
                r = self.arena.get(k.rstrip('0123456789'))
            if r is None:
                assert not ('_' in k and k.rsplit('_', 1)[1].isdigit() and k.rsplit('_', 1)[0] in self.arena), k
                out.append(k)
            else:
                out.extend("ar%d" % u for u in range(r[0] // 256, (r[1] + 255) // 256))
        return out

    def op(self, eng, fn, reads=(), writes=(), dma=False, single=False):
        if self.limit is not None and len(self.ops) >= self.limit:
            raise StopBuild()
        reads = self.xl(reads)
        writes = self.xl(writes)
        i = len(self.ops)
        deps = set()
        for k in reads:
            w = self.last_w.get(k)
            if w is not None:
                deps.add(w)
        for k in writes:
            w = self.last_w.get(k)
            if w is not None:
                deps.add(w)
            for r in self.readers.get(k, ()):
                deps.add(r)
        deps.discard(i)
        odeps = set()
        if eng == "pe" and not dma:
            odeps = {d for d in deps if (self.ops[d]["eng"] == "pe" and not self.ops[d]["dma"])}
            deps = deps - odeps
        self.ops.append(dict(eng=eng, fn=fn, deps=deps, odeps=odeps, dma=dma, used=False, single=single, label=getattr(self, 'label', ''),
                             dur=getattr(self, 'dur', None), tbl=getattr(self, 'tbl', None)))
        for d in deps:
            self.ops[d]["used"] = True
        for k in writes:
            self.last_w[k] = i
            self.readers[k] = []
        for k in reads:
            if k not in writes:
                self.readers.setdefault(k, []).append(i)
        return i

    def reschedule(self, final_wait_ops):
        ops = self.ops
        n = len(ops)
        DUR = {"pe": 0.35, "act": 0.3, "dve": 0.3, "pool": 0.45, "sp": 0.2}
        succ = [[] for _ in range(n)]
        indeg = [0] * n
        for i, o in enumerate(ops):
            for d in (o["deps"] | o["odeps"]):
                succ[d].append(i)
                indeg[i] += 1
        lastq = {}
        for i, o in enumerate(ops):
            if o["dma"]:
                q = o["eng"]
                if q in lastq:
                    succ[lastq[q]].append(i)
                    indeg[i] += 1
                lastq[q] = i
        fin = [0.0] * n
        ready_t = [0.0] * n
        cur = {e: 0.0 for e in ENGS}
        ready = {e: [] for e in ENGS}
        tail = [0.0] * n
        for i in range(n - 1, -1, -1):
            o = ops[i]
            d_ = 2.5 if o["dma"] else (o["dur"] or DUR[o["eng"]])
            t_ = 0.0
            for j in succ[i]:
                if tail[j] > t_:
                    t_ = tail[j]
            tail[i] = d_ + 0.15 + t_
        for i in range(n):
            if indeg[i] == 0:
                ready[ops[i]["eng"]].append(i)
        order = []
        acttbl = [None]
        done = 0
        while done < n:
            best = None
            for e in ENGS:
                lst = ready[e]
                if not lst:
                    continue
                bi_, bs_ = None, None
                cands = []
                for i in lst[:32]:
                    st = max(ready_t[i], cur[e])
                    if e == "act" and ops[i]["tbl"] is not None and ops[i]["tbl"] != acttbl[0]:
                        st += 1.3
                    cands.append((st, i))
                smin = min(c[0] for c in cands)
                for st, i in cands:
                    if st <= smin + 0.25:
                        key = (-tail[i], i)
                        if bs_ is None or key < bs_[1:]:
                            bi_, bs_ = i, (st,) + key
                bs_ = (bs_[0], bi_)
                if best is None or bs_ < best[0]:
                    best = (bs_, bi_, e)
            (st, _), i, e = best
            ready[e].remove(i)
            o = ops[i]
            if o["dma"]:
                cur[e] = st + 0.1
                fin[i] = st + 2.5
            else:
                if e == "act" and o["tbl"] is not None:
                    acttbl[0] = o["tbl"]
                fin[i] = st + (o["dur"] or DUR[e])
                cur[e] = fin[i]
            order.append(i)
            done += 1
            for j in succ[i]:
                ready_t[j] = max(ready_t[j], fin[i] + 0.15)
                indeg[j] -= 1
                if indeg[j] == 0:
                    ready[ops[j]["eng"]].append(j)
        assert len(order) == n
        pos = {old: new for new, old in enumerate(order)}
        newops = []
        for old_i in order:
            o = ops[old_i]
            o["deps"] = {pos[d] for d in o["deps"]}
            o["odeps"] = {pos[d] for d in o["odeps"]}
            newops.append(o)
        self.ops = newops
        return [pos[i] for i in final_wait_ops]

    def emit(self, final_wait_ops=()):
        nc = self.nc
        if self.resched:
            final_wait_ops = self.reschedule(list(final_wait_ops))
        ops = self.ops
        for i in final_wait_ops:
            ops[i]["used"] = True
        cnt = {e: 0 for e in ENGS}
        dma_cnt = {}
        dma_rr = {e: 0 for e in ENGS}
        for o in ops:
            if o["dma"]:
                q = o["eng"]
                s = (q, dma_rr[q] % self.n_dma_sems)
                dma_rr[q] += 1
                dma_cnt[s] = dma_cnt.get(s, 0) + 1
                o["sig"] = ("dma", s, 16 * dma_cnt[s])
            elif o["used"]:
                cnt[o["eng"]] += 1
                o["sig"] = ("eng", o["eng"], cnt[o["eng"]])
            else:
                o["sig"] = None
        with contextlib.ExitStack() as st:
            esem = {e: st.enter_context(nc.semaphore("s_" + e)) for e in ENGS}
            dsem = {}
            for s in dma_cnt:
                dsem[s] = st.enter_context(nc.semaphore("d_%s_%d" % s))
            block = st.enter_context(nc.Block())

            def semof(sig):
                return esem[sig[1]] if sig[0] == "eng" else dsem[sig[1]]

            def run(e, engobj):
                seen = {}
                for o in ops:
                    if o["eng"] != e:
                        continue
                    need = {}
                    for d in o["deps"]:
                        sg = ops[d]["sig"]
                        key = (sg[0], sg[1])
                        need[key] = max(need.get(key, 0), sg[2])
                    if o["dma"]:
                        sg = o["sig"]
                        key = (sg[0], sg[1])
                        if sg[2] > 16:
                            need[key] = max(need.get(key, 0), sg[2] - 16)
                    waits = []
                    for key, v in need.items():
                        if seen.get(key, 0) < v:
                            waits.append((esem[key[1]] if key[0] == "eng" else dsem[key[1]], v))
                            seen[key] = v
                    emb = []
                    if not o["dma"] and (o["single"] or e == "pe"):
                        emb = waits[-1:]
                        waits = waits[:-1]
                    for sm, v in waits:
                        engobj.wait_ge(sm, v)
                    if emb and not o["single"]:
                        ins = o["fn"](_FirstHook(engobj, emb[0]))
                    else:
                        ins = o["fn"](engobj)
                        for sm, v in emb:
                            ins._wait_ge(sm, v)
                    sg = o["sig"]
                    if sg is not None:
                        ins.then_inc(semof(sg), 16 if sg[0] == "dma" else 1)
                if e == "sp":
                    for i in final_wait_ops:
                        sg = ops[i]["sig"]
                        engobj.wait_ge(semof(sg), sg[2])

            @block.tensor
            def _(eng):
                run("pe", eng)

            @block.scalar
            def _(eng):
                run("act", eng)

            @block.vector
            def _(eng):
                run("dve", eng)

            @block.gpsimd
            def _(eng):
                run("pool", eng)

            @block.sync
            def _(eng):
                run("sp", eng)


def w_groups():
    g = []
    g.append(("rwx", "w_in", 1024, 2304, 128))
    g.append(("rwk0", "w_in", 1024, 768, 512))
    g.append(("rwk1", "w_in", 1024, 1280, 256))
    g.append(("rwr0", "w_in", 1024, 0, 512))
    g.append(("rwr1", "w_in", 1024, 512, 256))
    g.append(("rwv0", "w_in", 1024, 1536, 512))
    g.append(("rwv1", "w_in", 1024, 2048, 256))
    g.append(("rwg0", "w_in", 1024, C_RWG, 512))
    g.append(("rwg1", "w_in", 1024, C_RWG + 512, 256))
    g.append(("s5u", "w_in", 1024, C_S5U, 512))
    g.append(("s5g", "w_in", 1024, C_S5G, 512))
    for i in range(4):
        g.append(("mlqk%d" % i, "w_in", 1024, C_MLQK + 384 * i, 384))
    g.append(("mlv0", "w_in", 1024, C_MLV, 512))
    g.append(("mlv1", "w_in", 1024, C_MLV + 512, 264))
    g.append(("mlo0", "w_in", 1024, C_MLO, 512))
    g.append(("mlo1", "w_in", 1024, C_MLO + 512, 256))
    g.append(("mlz0", "w_in", 1024, C_MLZ, 512))
    g.append(("mlz1", "w_in", 1024, C_MLZ + 512, 256))
    for jg in range(2):
        for b, (nm, k) in enumerate((("w_br_rw", 768), ("w_br_s5", 512), ("w_br_ml", 768))):
            g.append(("mg%d%d" % (b, jg), "w_in", 1024, C_MRG + b * 1024 + jg * 512, 512))
            g.append(("br%d%d" % (b, jg), nm, k, jg * 512, 512))
    for jg in range(2):
        g.append(("wo%d" % jg, "w_out", 1024, jg * 512, 512))
    return g


GROUPS = w_groups()
GIDX = {g[0]: i for i, g in enumerate(GROUPS)}

PP = {}
_o = 0
for _n, _w in (("mu", 19), ("w0", 6), ("a0", 6), ("kk", 6), ("ka", 6), ("rk", 6), ("s5d", 4), ("bglu", 4),
               ("mlg", 6), ("bmrg", 24), ("are", 16), ("aim", 16), ("ldt", 16)):
    PP[_n] = (_o, _w)
    _o += _w
NPP = _o


def build(stage=99):
    nc = bass.Bass("TRN2", target_bir_lowering=False)
    es = contextlib.ExitStack()
    S = Sched(nc)

    def din(name, shape):
        return nc.dram_tensor(name, list(shape), F32, kind="ExternalInput").ap()

    def dout(name, shape):
        return nc.dram_tensor(name, list(shape), F32, kind="ExternalOutput").ap()

    def dscr(name, shape, dt):
        return nc.dram_tensor(name, list(shape), dt, kind="Internal").ap()

    xp = din("xp", [SEQ, D])
    xs = din("xs", [NSAMP * 64, D])
    meta = din("meta", [NMETA, D])
    st_shift = din("st_shift", [DEPTH, NSAMP, RWS])
    st_wkv = din("st_wkv", [DEPTH, NSAMP, 12, 64, 64])
    st_s5re = din("st_s5re", [DEPTH, NSAMP, 32, 64])
    st_s5im = din("st_s5im", [DEPTH, NSAMP, 32, 64])
    st_conv = din("st_conv", [DEPTH, NSAMP, 3, 1536])
    st_c = din("st_c", [DEPTH, NSAMP, 4, 192, 192])
    st_n = din("st_n", [DEPTH, NSAMP, 4, 192])
    st_m = din("st_m", [DEPTH, NSAMP, 4])
    w_in = din("w_in", [DEPTH, D, INC])
    wsrc = dict(w_in=w_in, w_br_rw=din("w_br_rw", [DEPTH, 768, D]), w_br_s5=din("w_br_s5", [DEPTH, 512, D]),
                w_br_ml=din("w_br_ml", [DEPTH, 768, D]), w_out=din("w_out", [DEPTH, D, D]))
    w_glu = din("s5_w_glu", [DEPTH, 512, 512])
    rw_w2 = din("rw_w2", [DEPTH, 64, 768])
    rw_a2 = din("rw_a2", [DEPTH, 64, 768])
    pp_d = din("pp", [DEPTH, 128, NPP])
    pq_d = din("pq", [DEPTH, 96, 16 * 5])
    bc_d = din("bc", [6, D])
    rwln_d = din("rwln", [DEPTH, 2, 2, 384])
    bif_d = din("bif", [DEPTH, 8])
    bblk_d = din("bblk", [DEPTH, 128, 4, 1024])
    cpad_d = din("cpad", [DEPTH, 128, 16, 2, 64])

    yp = dout("yp", [SEQ, D])
    ys = dout("ys", [NSAMP * 64, D])
    o_shift = {"p": dout("p_shift", [DEPTH, 1, RWS]), "s": dout("s_shift", [DEPTH, NSAMP, RWS])}
    o_wkv = {"p": dout("p_wkv", [DEPTH, 1, 12, 64, 64]), "s": dout("s_wkv", [DEPTH, NSAMP, 12, 64, 64])}
    o_s5re = {"p": dout("p_s5re", [DEPTH, 1, 32, 64]), "s": dout("s_s5re", [DEPTH, NSAMP, 32, 64])}
    o_s5im = {"p": dout("p_s5im", [DEPTH, 1, 32, 64]), "s": dout("s_s5im", [DEPTH, NSAMP, 32, 64])}
    o_conv = {"p": dout("p_conv", [DEPTH, 1, 3, 1536]), "s": dout("s_conv", [DEPTH, NSAMP, 3, 1536])}
    o_c = {"p": dout("p_c", [DEPTH, 1, 4, 192, 192]), "s": dout("s_c", [DEPTH, NSAMP, 4, 192, 192])}
    o_n = {"p": dout("p_n", [DEPTH, 1, 4, 192]), "s": dout("s_n", [DEPTH, NSAMP, 4, 192])}
    o_m = {"p": dout("p_m", [DEPTH, 1, 4]), "s": dout("s_m", [DEPTH, NSAMP, 4])}

    wq = [[dscr("wq%d_%d" % (l, i), [128, g[2] // 128, g[4]], BF16) for i, g in enumerate(GROUPS)]
          for l in range(DEPTH)]

    def sb(name, shape, dt=F32):
        return es.enter_context(nc.sbuf_tensor(name, list(shape), dt))

    def psum(name, shape, dt):
        return es.enter_context(nc.psum_tensor(name, list(shape), dt))

    final_ops = []

    def ACT(fn, r, w):
        tbl = None
        names = fn.__code__.co_names
        for nm_, t_ in (("Sigmoid", "sig"), ("Exp", "exp"), ("Ln", "ln"), ("Sqrt", "sqrt"), ("Silu", "silu"), ("Sin", "silu")):
            if nm_ in names:
                tbl = t_
        S.tbl = tbl
        i_ = S.op("act", fn, r, w, single=True)
        S.tbl = None
        return i_

    def DVE(fn, r, w):
        return S.op("dve", fn, r, w, single=True)

    def POOL(fn, r, w):
        return S.op("pool", fn, r, w, single=True)

    def PE(fn, r, w):
        return S.op("pe", fn, r, w)

    def LOAD(out, in_, r, w, slow=False):
        return S.op("sp", lambda e: e.dma_start(out=out, in_=in_, allow_slow_non_contiguous=slow), r, w, dma=True)

    def STORE(out, in_, r, slow=False):
        i = S.op("pool", lambda e: e.dma_start(out=out, in_=in_, allow_slow_non_contiguous=slow), r, (), dma=True)
        final_ops.append(i)
        return i

    def MM(out, pairs, r, w):
        S.dur = 0.1 + 0.07 * len(pairs)
        def fn(e):
            n = len(pairs)
            for i, (l_, r_) in enumerate(pairs):
                ins = e.matmul(out, lhsT=l_, rhs=r_, start=(i == 0), stop=(i == n - 1))
            return ins
        i_ = PE(fn, r, w)
        S.dur = None
        return i_

    def TR(out, in_, ident, r, w):
        return S.op("pe", lambda e: e.transpose(out=out, in_=in_, identity=ident), r, w, single=True)

    PSF = [psum("psf%d" % i, [128, 2, 512], F32) for i in range(3)]
    PSB = psum("psb", [128, 2, 1024], BF16)
    rr = {"b": 0, "p": 0, "t": 0}

    def fbank():
        i = rr["b"] % 6
        rr["b"] += 1
        return PSF[i // 2][:, i % 2, :], ["pf%d" % i]

    def fpair():
        i = rr["p"] % 3
        rr["p"] += 1
        return PSF[i], ["pf%d" % (2 * i), "pf%d" % (2 * i + 1)]

    def bbank():
        i = rr["t"] % 2
        rr["t"] += 1
        return PSB[:, i, :], ["pb%d" % i]

    ident_b = sb("ident_b", [128, 128], BF16)
    ident_f = sb("ident_f", [128, 128], F32)
    m_strict = sb("m_strict", [128, 6, 64], F32)
    m_incl = sb("m_incl", [128, 6, 64], F32)
    m_lower = sb("m_lower", [128, 6, 64], F32)
    tri_b = sb("tri_b", [64, 64], BF16)
    tri2 = sb("tri2", [128, 64], BF16)
    tri_f = sb("tri_f", [64, 64], F32)
    blk1 = sb("blk1", [128, 128], BF16)
    bsel = sb("bsel", [128, 2], BF16)
    ones_f = sb("ones_f", [128, 128], F32)
    onesb = sb("onesb", [128, 1], BF16)
    scanm = sb("scanm", [128, NT], F32)

    def mk_sel(t, pattern, base, cm, op, key):
        POOL(lambda e: e.memset(t, 1.0), [], [key])
        POOL(lambda e: e.affine_select(out=t, in_=t, pattern=pattern, compare_op=op, fill=0.0, base=base,
                                       channel_multiplier=cm), [key], [key])

    mk_sel(ident_f[:], [[-1, 128]], 0, 1, ALU.is_equal, "ident_f")
    POOL(lambda e: e.tensor_copy(out=ident_b[:], in_=ident_f[:]), ["ident_f"], ["ident_b"])
    for hf_ in range(2):
        ps_ = slice(hf_ * 64, hf_ * 64 + 64)
        mk_sel(m_strict[ps_], [[0, 6], [1, 64]], -1, -1, ALU.is_ge, "m_strict")
        mk_sel(m_incl[ps_], [[0, 6], [1, 64]], 0, -1, ALU.is_ge, "m_incl")
        mk_sel(m_lower[ps_], [[0, 6], [-1, 64]], -1, 1, ALU.is_ge, "m_lower")
    mk_sel(tri_f[:], [[1, 64]], 0, -1, ALU.is_ge, "tri_f")
    POOL(lambda e: e.tensor_copy(out=tri_b[:], in_=tri_f[:]), ["tri_f"], ["tri_b"])
    POOL(lambda e: e.tensor_copy(out=tri2[0:64, :], in_=tri_f[:]), ["tri_f"], ["tri2"])
    POOL(lambda e: e.tensor_copy(out=tri2[64:128, :], in_=m_incl[64:128, 0, :]), ["m_incl", "tri2"], ["tri2"])
    POOL(lambda e: e.memset(ones_f[:], 1.0), [], ["ones_f"])
    POOL(lambda e: e.memset(onesb[:], 1.0), [], ["onesb"])
    POOL(lambda e: e.memset(blk1[:], 0.0), [], ["blk1"])
    POOL(lambda e: e.memset(blk1[0:64, 0:64], 1.0), ["blk1"], ["blk1"])
    POOL(lambda e: e.memset(blk1[64:128, 64:128], 1.0), ["blk1"], ["blk1"])
    POOL(lambda e: e.memset(bsel[:], 0.0), [], ["bsel"])
    POOL(lambda e: e.memset(bsel[0:64, 0:1], 1.0), ["bsel"], ["bsel"])
    POOL(lambda e: e.memset(bsel[64:128, 1:2], 1.0), ["bsel"], ["bsel"])
    POOL(lambda e: e.memset(scanm[:], 1.0), [], ["scanm"])
    POOL(lambda e: e.memset(scanm[:].rearrange("p (c t) -> p c t", t=64)[:, :, 0:1], 0.0), ["scanm"], ["scanm"])

    if stage == -4:
        S.emit(final_wait_ops=final_ops); es.close(); return nc
    for l in range(DEPTH):
        for i, (nm, src, K, c0, wd) in enumerate(GROUPS):
            srcap = wsrc[src][l, :, c0:c0 + wd].rearrange("(kc p) c -> p kc c", p=128)
            S.op("pool", lambda e, o_=wq[l][i], s_=srcap: e.dma_start(out=o_, in_=s_), [], ["wq%d_%d" % (l, i)], dma=True)

    if stage == -3:
        S.emit(final_wait_ops=final_ops); es.close(); return nc
    ARW = 15360
    arena_t = sb("arena", [128, ARW], F32)
    arp = {"o": 0}

    def ar_reset(o=0):
        arp["o"] = o

    def ar(name, shape, dt=F32):
        P = shape[0]
        n = int(np.prod(shape[1:]))
        words = n if dt == F32 else (n + 1) // 2
        words = (words + 15) // 16 * 16
        o = arp["o"]
        assert o + words <= ARW, (name, o, words)
        arp["o"] = o + words
        v = arena_t[0:P, o:o + words]
        if dt != F32:
            v = v.bitcast(BF16)
        v = v[:, 0:n]
        if len(shape) == 3:
            v = v.rearrange("p (a b) -> p a b", b=shape[2])
        elif len(shape) == 4:
            v = v.rearrange("p (a b c) -> p a b c", b=shape[2], c=shape[3])
        S.arena[name] = (o * 4, (o + words) * 4)
        if len(shape) >= 3:
            esz = 4 if dt == F32 else 2
            sub = int(np.prod(shape[2:])) * esz
            for a_ in range(shape[1]):
                S.arena["%s_%d" % (name, a_)] = (o * 4 + a_ * sub, o * 4 + (a_ + 1) * sub)
        return v

    pp = [sb("pp%d" % l, [128, NPP]) for l in range(DEPTH)]
    pq = [sb("pq%d" % l, [96, 16, 5]) for l in range(DEPTH)]
    omka = [sb("omka%d" % l, [128, 6]) for l in range(DEPTH)]
    lora = [sb("lora%d" % l, [128, 768], BF16) for l in range(DEPTH)]
    wglu = [sb("wglu%d" % l, [128, 4, 512], BF16) for l in range(DEPTH)]
    bcg = sb("bcg", [128, 2, D])
    rwln = [sb("rwln%d" % l, [128, 2, 384]) for l in range(DEPTH)]
    bif = [sb("bif%d" % l, [64, 8]) for l in range(DEPTH)]
    EpB = sb("EpB", [128, 2, 16, 64])
    EnB = sb("EnB", [128, 16, 2, 128], BF16)
    bblkB = sb("bblkB", [128, 4, 1024], BF16)
    cpadB = sb("cpadB", [128, 16, 2, 64], BF16)
    S5K = ["EpB", "EnB", "bblkB", "cpadB"]
    epd = [dscr("epd%d" % l, [128, 2, 16, 64], F32) for l in range(DEPTH)]
    end_ = [dscr("end%d" % l, [64, 16, 2, 128], BF16) for l in range(DEPTH)]
    bblkq = [dscr("bblkq%d" % l, [128, 4, 1024], BF16) for l in range(DEPTH)]
    cpadq = [dscr("cpadq%d" % l, [128, 16, 2, 64], BF16) for l in range(DEPTH)]
    for l in range(DEPTH):
        LOAD(pp[l][:], pp_d[l], [], ["pp%d" % l])
        LOAD(pq[l][:], pq_d[l].rearrange("p (t j) -> p t j", j=5), [], ["pq%d" % l])
        for a_ in range(2):
            for r_ in range(2):
                LOAD(rwln[l][r_ * 64:(r_ + 1) * 64, a_, :], rwln_d[l, a_, r_].partition_broadcast(64), ["rwln%d" % l], ["rwln%d" % l])
        LOAD(bif[l][:], bif_d[l].partition_broadcast(64), [], ["bif%d" % l])
        S.op("pool", lambda e, l=l: e.dma_start(out=lora[l][0:64, :], in_=rw_w2[l]), [], ["lora%d" % l], dma=True)
        S.op("pool", lambda e, l=l: e.dma_start(out=lora[l][64:128, :], in_=rw_a2[l]), [], ["lora%da" % l], dma=True)
        S.op("pool", lambda e, l=l: e.dma_start(out=wglu[l][:], in_=w_glu[l].rearrange("(kc p) c -> p kc c", p=128)),
             [], ["wglu%d" % l], dma=True)
        S.op("pool", lambda e, l=l: e.dma_start(out=bblkq[l], in_=bblk_d[l]), [], ["bblkq%d" % l], dma=True)
        S.op("pool", lambda e, l=l: e.dma_start(out=cpadq[l], in_=cpad_d[l]), [], ["cpadq%d" % l], dma=True)
        o_, w_ = PP["ka"]
        DVE(lambda e, l=l, o_=o_: e.tensor_scalar(out=omka[l][:], in0=pp[l][:, o_:o_ + 6], scalar1=-1.0, scalar2=1.0,
                                                  op0=ALU.mult, op1=ALU.add), ["pp%d" % l], ["omka%d" % l])

    if stage == -2:
        S.emit(final_wait_ops=final_ops); es.close(); return nc

    def load_bc(i):
        LOAD(bcg[:].rearrange("p a b -> p (a b)"), bc_d[2 * i:2 * i + 2, :].rearrange("a b -> (a b)").partition_broadcast(128),
             [], ["bcg"])

    def load_s5(l):
        LOAD(EpB[:], epd[l], ["epd%d" % l], ["EpB"])
        LOAD(EnB[0:64], end_[l], ["end%d" % l, "EnB"], ["EnB"])
        LOAD(EnB[64:128], end_[l], ["end%d" % l, "EnB"], ["EnB"])
        LOAD(bblkB[:], bblkq[l], ["bblkq%d" % l], ["bblkB"])
        LOAD(cpadB[:], cpadq[l], ["cpadq%d" % l], ["cpadB"])

    def PPc(l, name, j=None):
        o_, w_ = PP[name]
        if j is None:
            return pp[l][:, o_:o_ + w_]
        return pp[l][:, o_ + j:o_ + j + 1]

    ar_reset()
    zr = ar("zr", [128, 16]); zi = ar("zi", [128, 16]); dtt = ar("dtt", [128, 16])
    t1 = ar("t1", [128, 16]); t2 = ar("t2", [128, 16]); t3 = ar("t3", [128, 16]); mg = ar("mg", [128, 16])
    lr = ar("lr", [128, 16])
    Epf = ar("Epf", [128, 2, 16, 64])
    Enf = [ar("enf%d" % c, [128, 16, 64]) for c in range(2)]
    ta = ar("ta", [128, 16, 32]); tb_ = ar("tb", [128, 16, 32])
    cfr = ar("cfr", [128, 16]); cfi = ar("cfi", [128, 16]); den = ar("den", [128, 16])
    enc = [ar("enc%d" % c, [128, 16, 64]) for c in range(2)]
    Ent = ar("Ent", [64, 16, 2, 128])
    KT = ["zr", "zi", "dtt", "t1", "t2", "t3", "mg", "lr", "Epf", "enf0", "enf1", "ta", "tb", "cfr", "cfi", "den", "enc0", "enc1"]

    def cexp32(sign, outr, outi):
        ACT(lambda e: e.activation(out=mg[:], in_=zr[:], func=AF.Exp, scale=sign / 32.0), KT, KT)
        ACT(lambda e: e.activation(out=t1[:], in_=zi[:], func=AF.Sin, scale=sign / 32.0), KT, KT)
        ACT(lambda e: e.activation(out=t2[:], in_=zi[:], func=AF.Sin, scale=sign / 32.0, bias=hpi[:, 0:1]), KT + ["hpi"], KT)
        DVE(lambda e: e.tensor_tensor(out=outr, in0=mg[:], in1=t2[:], op=ALU.mult), KT, KT)
        DVE(lambda e: e.tensor_tensor(out=outi, in0=mg[:], in1=t1[:], op=ALU.mult), KT, KT)
        for _ in range(5):
            DVE(lambda e: e.tensor_tensor(out=t1[:], in0=outr, in1=outr, op=ALU.mult), KT, KT)
            DVE(lambda e: e.tensor_tensor(out=t2[:], in0=outi, in1=outi, op=ALU.mult), KT, KT)
            DVE(lambda e: e.tensor_tensor(out=t3[:], in0=outr, in1=outi, op=ALU.mult), KT, KT)
            DVE(lambda e: e.tensor_tensor(out=outr, in0=t1[:], in1=t2[:], op=ALU.subtract), KT, KT)
            DVE(lambda e: e.tensor_scalar(out=outi, in0=t3[:], scalar1=2.0, scalar2=None, op0=ALU.mult), KT, KT)

    def powers(tabr, tabi):
        ln_ = 1
        while ln_ < 64:
            Lr = tabr[:, :, ln_ - 1:ln_].broadcast_to([128, 16, ln_])
            Li = tabi[:, :, ln_ - 1:ln_].broadcast_to([128, 16, ln_])
            a_ = ta[:, :, 0:ln_]
            b_ = tb_[:, :, 0:ln_]
            sr = tabr[:, :, 0:ln_]
            si = tabi[:, :, 0:ln_]
            dr = tabr[:, :, ln_:2 * ln_]
            di = tabi[:, :, ln_:2 * ln_]
            DVE(lambda e, a_=a_, sr=sr, Lr=Lr: e.tensor_tensor(out=a_, in0=sr, in1=Lr, op=ALU.mult), KT, KT)
            DVE(lambda e, b_=b_, si=si, Li=Li: e.tensor_tensor(out=b_, in0=si, in1=Li, op=ALU.mult), KT, KT)
            DVE(lambda e, a_=a_, b_=b_, dr=dr: e.tensor_tensor(out=dr, in0=a_, in1=b_, op=ALU.subtract), KT, KT)
            DVE(lambda e, a_=a_, sr=sr, Li=Li: e.tensor_tensor(out=a_, in0=sr, in1=Li, op=ALU.mult), KT, KT)
            DVE(lambda e, b_=b_, si=si, Lr=Lr: e.tensor_tensor(out=b_, in0=si, in1=Lr, op=ALU.mult), KT, KT)
            DVE(lambda e, a_=a_, b_=b_, di=di: e.tensor_tensor(out=di, in0=a_, in1=b_, op=ALU.add), KT, KT)
            ln_ *= 2

    hpi = sb("hpi", [128, 1])
    POOL(lambda e: e.memset(hpi[:], math.pi / 2), [], ["hpi"])
    for l in range(DEPTH):
        KP = ["pp%d" % l]
        ACT(lambda e, l=l: e.activation(out=dtt[:], in_=PPc(l, "ldt"), func=AF.Exp), KT + KP, KT)
        DVE(lambda e, l=l: e.tensor_tensor(out=zr[:], in0=PPc(l, "are"), in1=dtt[:], op=ALU.mult), KT + KP, KT)
        DVE(lambda e, l=l: e.tensor_tensor(out=zi[:], in0=PPc(l, "aim"), in1=dtt[:], op=ALU.mult), KT + KP, KT)

        if stage == -10:
            S.emit(final_wait_ops=final_ops); es.close(); return nc
        cexp32(1.0, Epf[:, 0, :, 0], Epf[:, 1, :, 0])

        if stage == -9:
            S.emit(final_wait_ops=final_ops); es.close(); return nc
        powers(Epf[:, 0], Epf[:, 1])

        if stage == -8:
            S.emit(final_wait_ops=final_ops); es.close(); return nc
        cexp32(-1.0, Enf[0][:, :, 0], Enf[1][:, :, 0])
        powers(Enf[0], Enf[1])

        if stage == -7:
            S.emit(final_wait_ops=final_ops); es.close(); return nc
        DVE(lambda e: e.tensor_scalar(out=lr[:], in0=Epf[:, 0, :, 0], scalar1=-1.0, scalar2=None, op0=ALU.add), KT, KT)
        DVE(lambda e, l=l: e.tensor_tensor(out=t1[:], in0=PPc(l, "are"), in1=PPc(l, "are"), op=ALU.mult), KT + KP, KT)
        DVE(lambda e, l=l: e.tensor_tensor(out=t2[:], in0=PPc(l, "aim"), in1=PPc(l, "aim"), op=ALU.mult), KT + KP, KT)
        DVE(lambda e: e.tensor_tensor(out=den[:], in0=t1[:], in1=t2[:], op=ALU.add), KT, KT)
        DVE(lambda e: e.reciprocal(out=den[:], in_=den[:]), KT, KT)
        DVE(lambda e, l=l: e.tensor_tensor(out=t1[:], in0=lr[:], in1=PPc(l, "are"), op=ALU.mult), KT + KP, KT)
        DVE(lambda e, l=l: e.tensor_tensor(out=t2[:], in0=Epf[:, 1, :, 0], in1=PPc(l, "aim"), op=ALU.mult), KT + KP, KT)
        DVE(lambda e: e.tensor_tensor(out=cfr[:], in0=t1[:], in1=t2[:], op=ALU.add), KT, KT)
        DVE(lambda e: e.tensor_tensor(out=cfr[:], in0=cfr[:], in1=den[:], op=ALU.mult), KT, KT)
        DVE(lambda e, l=l: e.tensor_tensor(out=t1[:], in0=Epf[:, 1, :, 0], in1=PPc(l, "are"), op=ALU.mult), KT + KP, KT)
        DVE(lambda e, l=l: e.tensor_tensor(out=t2[:], in0=lr[:], in1=PPc(l, "aim"), op=ALU.mult), KT + KP, KT)
        DVE(lambda e: e.tensor_tensor(out=cfi[:], in0=t1[:], in1=t2[:], op=ALU.subtract), KT, KT)
        DVE(lambda e: e.tensor_tensor(out=cfi[:], in0=cfi[:], in1=den[:], op=ALU.mult), KT, KT)
        CR = cfr[:].unsqueeze(2).broadcast_to([128, 16, 64])
        CI = cfi[:].unsqueeze(2).broadcast_to([128, 16, 64])
        DVE(lambda e, CR=CR: e.tensor_tensor(out=enc[0][:], in0=Enf[0][:], in1=CR, op=ALU.mult), KT, KT)
        DVE(lambda e, CI=CI: e.tensor_tensor(out=enc[1][:], in0=Enf[1][:], in1=CI, op=ALU.mult), KT, KT)
        DVE(lambda e: e.tensor_tensor(out=enc[0][:], in0=enc[0][:], in1=enc[1][:], op=ALU.subtract), KT, KT)
        DVE(lambda e, CI=CI: e.tensor_tensor(out=enc[1][:], in0=Enf[0][:], in1=CI, op=ALU.mult), KT, KT)
        DVE(lambda e, CR=CR: e.tensor_tensor(out=Enf[0][:], in0=Enf[1][:], in1=CR, op=ALU.mult), KT, KT)
        DVE(lambda e: e.tensor_tensor(out=enc[1][:], in0=enc[1][:], in1=Enf[0][:], op=ALU.add), KT, KT)

        if stage == -6:
            S.emit(final_wait_ops=final_ops); es.close(); return nc
        for c in range(2):
            for i in range(16):
                ps_, pk = fbank()
                TR(ps_[0:64, 0:128], enc[c][:, i, :], ident_f[:], KT + ["ident_f"], pk)
                ACT(lambda e, ps_=ps_, i=i, c=c: e.copy(out=Ent[:, i, c, :], in_=ps_[0:64, 0:128]), pk, ["Ent"])

        if stage == -5:
            S.emit(final_wait_ops=final_ops); es.close(); return nc
        S.op("pool", lambda e, l=l: e.dma_start(out=epd[l], in_=Epf), KT, ["epd%d" % l], dma=True)
        S.op("pool", lambda e, l=l: e.dma_start(out=end_[l], in_=Ent), ["Ent"], ["end%d" % l], dma=True)

    x_tok = sb("x_tok", [128, 1, D])
    xT = sb("xT", [128, 8, NT], BF16)
    wbuf = [sb("wbuf%d" % i, [128, 8, 512], BF16) for i in range(4)]
    wr = {"i": 0}

    def load_group(l, name):
        gi = GIDX[name]
        g = GROUPS[gi]
        bi = wr["i"] % 4
        wr["i"] += 1
        kc = g[2] // 128
        LOAD(wbuf[bi][:, 0:kc, 0:g[4]], wq[l][gi], ["wq%d_%d" % (l, gi)], ["wbuf%d" % bi])
        return wbuf[bi], "wbuf%d" % bi

    shiftst = [sb("shiftst%d" % l, [128, 19]) for l in range(DEPTH)]
    sshift = sb("sshift", [128, 19, 2])
    sshift_o = sb("sshift_o", [128, 19, 2])
    S0T = [sb("s0t%d" % l, [128, 6, 64]) for l in range(DEPTH)]
    S0Tb = sb("s0tb", [128, 6, 64], BF16)
    h0 = [sb("h0%d" % l, [128, 2, 16]) for l in range(DEPTH)]
    convst = [sb("convst%d" % l, [96, 16, 3]) for l in range(DEPTH)]
    sconv = sb("sconv", [96, 16, 2, 3])
    Cst = [sb("cst%d" % l, [96, 2, 4, 193]) for l in range(DEPTH)]
    Cstb = sb("cstb", [96, 2, 4, 193], BF16)
    mst = [sb("mst%d" % l, [4, 1]) for l in range(DEPTH)]
    stg = sb("stg", [64, 12, 64])

    yrwT = sb("yrwT", [128, 6, NT], BF16)
    ys5T = sb("ys5T", [128, 4, NT], BF16)
    ymlT = sb("ymlT", [128, 6, NT], BF16)
    gate = [sb("gate%d" % i, [128, NT], BF16) for i in range(2)]
    mrg = sb("mrg", [128, 8, NT]); mrgb = sb("mrgb", [128, 8, NT], BF16)
    lnt = sb("lnt", [128, D]); lnb = sb("lnb", [128, D], BF16); lnx = sb("lnx", [128, D]); ctmp2 = [sb("ctmp%d" % i, [128, NT]) for i in range(2)]
    lst = sb("lst", [128, 2, 6]); lmv = sb("lmv", [128, 2]); lrs = sb("lrs", [128, 1])
    gC = sb("gC", [128, 6, 2])

    ar_reset()
    U = [ar("U%d" % i, [128, NT + 1]) for i in range(2)]
    dtmp = [ar("dtmp%d" % i, [128, NT]) for i in range(2)]
    xs18 = ar("xs18", [128, NT]); txw = ar("txw", [128, NT], BF16)
    ldec_2 = [ar("ldec%d" % i_, [128, NT]) for i_ in range(2)]; Gc_2 = [ar("Gc%d" % i_, [128, NT]) for i_ in range(2)]; aa_2 = [ar("aa%d" % i_, [128, NT]) for i_ in range(2)]
    eneg_2 = [ar("eneg%d" % i_, [128, NT]) for i_ in range(2)]; eprev_2 = [ar("eprev%d" % i_, [128, NT]) for i_ in range(2)]; ehat_2 = [ar("ehat%d" % i_, [128, NT]) for i_ in range(2)]; epos_2 = [ar("epos%d" % i_, [128, NT]) for i_ in range(2)]
    kx_2 = [ar("kx%d" % i_, [128, NT]) for i_ in range(2)]; kkr_2 = [ar("kkr%d" % i_, [128, NT]) for i_ in range(2)]; kksq_2 = [ar("kksq%d" % i_, [128, NT], BF16) for i_ in range(2)]
    rn_2 = [ar("rn%d" % i_, [128, NT]) for i_ in range(2)]; kkn_2 = [ar("kkn%d" % i_, [128, NT]) for i_ in range(2)]; tk_2 = [ar("tk%d" % i_, [128, NT]) for i_ in range(2)]
    kmod_2 = [ar("kmod%d" % i_, [128, NT]) for i_ in range(2)]; bb_2 = [ar("bb%d" % i_, [128, NT]) for i_ in range(2)]; rx_2 = [ar("rx%d" % i_, [128, NT]) for i_ in range(2)]
    rkp = ar("rkp", [128, 6, NT], BF16)
    rt_ = ar("rt_", [128, 6, NT], BF16); kt_ = ar("kt_", [128, 6, NT], BF16); bt_ = ar("bt_", [128, 6, NT], BF16)
    at_ = ar("at_", [128, 6, NT], BF16); khat = ar("khat", [128, 6, NT], BF16); bhat = ar("bhat", [128, 6, NT], BF16)
    vT = ar("vT", [128, 6, NT], BF16); grw = ar("grw", [128, 6, NT], BF16)
    TB = []
    for pz in ("A", "B"):
        TB.append(dict(
            vtok=ar("vtok" + pz, [128, 6, 64], BF16), khtok=ar("khtok" + pz, [128, 6, 64], BF16), bhtok=ar("bhtok" + pz, [128, 6, 64], BF16),
            Nsb=[ar("Nsb%d%s" % (i, pz), [128, 6, 64], BF16) for i in range(2)],
            NTsb=[ar("NTsb%d%s" % (i, pz), [128, 6, 64], BF16) for i in range(2)],
            Msb=ar("Msb" + pz, [128, 6, 64], BF16), P1sb=ar("P1sb" + pz, [128, 6, 64], BF16), P2sb=ar("P2sb" + pz, [128, 6, 64], BF16), z=pz))
    Ysb = [ar("Ysb%d" % i, [128, 6, 64], BF16) for i in range(2)]
    yo = ar("yo", [128, 12, 64]); ycen = ar("ycen", [128, 12, 64]); ysq = ar("ysq", [128, 12, 64])
    ystat = ar("ystat", [128, 4, 12]); rkd = ar("rkd", [128, 12]); ytb = ar("ytb", [128, 12, 64], BF16)
    RW_END = arp["o"]
    ar_reset()
    uT = ar("uT", [128, 4, NT], BF16); u32 = ar("u32", [128, 4, NT]); gs5 = ar("gs5", [128, 4, NT], BF16)
    wtok = ar("wtok", [128, 16, 2, 128], BF16)
    s5a = ar("s5a", [128, 4, 128]); s5b = ar("s5b", [128, 4, 128])
    Gr = ar("Gr", [128, 16, 64]); Gi = ar("Gi", [128, 16, 64])
    hA = ar("hA", [128, 16, 64]); hB = ar("hB", [128, 16, 64]); hC = ar("hC", [128, 16, 64]); hD = ar("hD", [128, 16, 64])
    hre = ar("hre", [128, 16, NT], BF16); himn = ar("himn", [128, 16, NT], BF16)
    yv = ar("yv", [128, 4, NT]); gt = ar("gt", [128, 4, NT]); gsg = ar("gsg", [128, 4, NT])
    gl = ar("gl", [128, 4, NT]); glb = ar("glb", [128, 4, NT], BF16); sgl = ar("sgl", [128, 4, NT])
    ar_reset()
    qkraw = ar("qkraw", [96, 16, NT + 3]); cvacc = ar("cvacc", [96, 16, NT]); cvtmp = ar("cvtmp", [96, 16, NT])
    hbuf = ar("hbuf0", [96, 16, 6])
    qkT = ar("qkT", [96, 16, NT], BF16)
    vaug = ar("vaug", [64, NT // 64, 4, 193], BF16)
    iftok = ar("iftok", [64, NT // 64, 8])
    zsl_2 = [ar("zsl%d" % i_, [128, NT]) for i_ in range(2)]; ogT = ar("ogT", [128, 6, NT], BF16)
    lfi = ar("lfi", [64, 8]); bcs = ar("bcs", [64, 4]); zz = ar("zz", [64, 4]); ee = ar("ee", [64, 4]); clampt = ar("clampt", [64, 4])
    zmax = ar("zmax", [4, 1]); mu4 = ar("mu4", [4, 1]); f4 = ar("f4", [4, 1]); bend = ar("bend", [4, 1]); dg = ar("dg", [4, 8])
    bc8 = ar("bc8", [128, 8])
    PTs = ar("PTs", [64, 4, 64]); PTb = ar("PTb", [64, 4, 64], BF16)
    hh = ar("hh", [64, 8, 192]); hcen = ar("hcen", [64, 8, 192]); hsq = ar("hsq", [64, 8, 192]); hst = ar("hst", [64, 4, 4]); hs2 = ar("hs2", [64, 3, 8])
    hnb = ar("hnb", [64, 2, 768], BF16); khm = ar("khm", [64, 4, 192], BF16)

    BR_T = {0: yrwT, 1: ys5T, 2: ymlT}
    BR_K = {0: "yrwT", 1: "ys5T", 2: "ymlT"}
    BR_KC = {0: 6, 1: 4, 2: 6}

    def ln_block(src, srckeys, rows, out_dram=None):
        for hf in range(2):
            DVE(lambda e, hf=hf: e.bn_stats(out=lst[0:rows, hf, :], in_=src[0:rows, hf * 512:(hf + 1) * 512]),
                srckeys, ["lst"])
        DVE(lambda e: e.bn_aggr(out=lmv[0:rows, :], in_=lst[0:rows].rearrange("p a b -> p (a b)")), ["lst"], ["lmv"])
        ACT(lambda e: e.activation(out=lrs[0:rows, :], in_=lmv[0:rows, 1:2], func=AF.Sqrt, bias=LN_EPS, scale=1.0),
            ["lmv"], ["lrs"])
        DVE(lambda e: e.reciprocal(out=lrs[0:rows, :], in_=lrs[0:rows, :]), ["lrs"], ["lrs"])
        DVE(lambda e: e.tensor_scalar(out=lnt[0:rows, :], in0=src[0:rows, :], scalar1=lmv[0:rows, 0:1],
                                      scalar2=lrs[0:rows, 0:1], op0=ALU.subtract, op1=ALU.mult),
            srckeys + ["lmv", "lrs"], ["lnt"])
        DVE(lambda e: e.tensor_tensor(out=lnt[0:rows, :], in0=lnt[0:rows, :], in1=bcg[0:rows, 0, :], op=ALU.mult),
            ["lnt", "bcg"], ["lnt"])
        POOL(lambda e: e.tensor_tensor(out=x_tok[0:rows, 0, :], in0=lnt[0:rows, :], in1=bcg[0:rows, 1, :],
                                       op=ALU.add), ["lnt", "bcg"], ["x_tok"])
        if out_dram is not None:
            STORE(out_dram, x_tok[0:rows, 0, :], ["x_tok"])
        ACT(lambda e: e.copy(out=lnb[0:rows, :], in_=x_tok[0:rows, 0, :]), ["x_tok"], ["lnb"])
        pt, pk = bbank()
        for kc in range(8):
            TR(pt[:, kc * 128:kc * 128 + rows], lnb[0:rows, kc * 128:(kc + 1) * 128], ident_b[0:rows, 0:rows],
               ["lnb", "ident_b"], pk)
        DVE(lambda e: e.tensor_copy(out=xT[:, :, 0:rows],
                                    in_=pt.rearrange("p (k t) -> p k t", t=128)[:, :, 0:rows]), pk, ["xT"])

    def proj(wb, wk, c0, M, N):
        ps_, pk = fbank()
        MM(ps_[0:M, 0:N], [(wb[:, kc, c0:c0 + M], xT[:, kc, 0:N]) for kc in range(8)], [wk, "xT"], pk)
        return ps_[0:M, 0:N], pk

    s5cur = {"l": None}

    def ensure_s5(l):
        if s5cur["l"] != l:
            load_s5(l)
            s5cur["l"] = l

    def run_pass(kind, N, tiles, xsrc, ydst, first, last):
        og = "p" if kind != "sample" else "s"
        load_bc(0)
        LOAD(lnx[0:N, :], xsrc, [], ["lnx"])
        ln_block(lnx, ["lnx"], N)
        for l in range(DEPTH):
            if first:
                for t_, k_ in ((shiftst[l], "shiftst%d" % l), (S0T[l], "s0t%d" % l), (h0[l], "h0%d" % l),
                               (convst[l], "convst%d" % l), (Cst[l], "cst%d" % l), (mst[l], "mst%d" % l)):
                    POOL(lambda e, t_=t_: e.memset(t_[:], 0.0), [], [k_])
            layer(kind, N, tiles, l, og, last)
            phase_c(kind, N, l, ydst if l == DEPTH - 1 else None)

    def layer(kind, N, tiles, l, og, last):
        pl = "pp%d" % l
        samp = kind == "sample"
        nq = len(tiles)
        if samp:
            for q, (o, C, sq) in enumerate(tiles):
                LOAD(lnx[0:19, 0:128], st_shift[l, sq].rearrange("(t p) -> t p", p=128), [], ["lnx"])
                ps_, pk = fbank()
                TR(ps_[:, 0:19], lnx[0:19, 0:128], ident_f[0:19, 0:19], ["lnx", "ident_f"], pk)
                ACT(lambda e, ps_=ps_, q=q: e.copy(out=sshift[:, :, q], in_=ps_[:, 0:19]), pk + ["sshift"], ["sshift"])
                LOAD(lnx[0:48, 128:224], st_conv[l, sq].rearrange("j (t p) -> (j t) p", p=96), [], ["lnx"])
                ps_, pk = fbank()
                TR(ps_[0:96, 0:48], lnx[0:48, 128:224], ident_f[0:48, 0:48], ["lnx", "ident_f"], pk)
                ACT(lambda e, ps_=ps_, q=q: e.copy(out=sconv[:, :, q, :], in_=ps_[0:96, 0:48].rearrange("p (j t) -> p t j", j=3)),
                    pk + ["sconv"], ["sconv"])

        def shift_tile(ps_, pk, ct, out_ap, outkeys, ui):
            Ub = U[ui]
            uk = "U%d" % ui
            ACT(lambda e: e.copy(out=Ub[:, 1:N + 1], in_=ps_), pk, [uk])
            if not samp:
                ACT(lambda e: e.copy(out=Ub[:, 0:1], in_=shiftst[l][:, ct:ct + 1]), ["shiftst%d" % l, uk], [uk])
            else:
                ACT(lambda e: e.copy(out=Ub[:, 0:1], in_=sshift[:, ct, 0:1]), ["sshift", uk], [uk])
            DVE(lambda e: e.tensor_tensor(out=dtmp[ui][:, 0:N], in0=Ub[:, 0:N], in1=Ub[:, 1:N + 1], op=ALU.subtract),
                [uk], ["dtmp%d" % ui])
            DVE(lambda e: e.scalar_tensor_tensor(out=out_ap, in0=dtmp[ui][:, 0:N], scalar=PPc(l, "mu", ct),
                                                 in1=Ub[:, 1:N + 1], op0=ALU.mult, op1=ALU.add),
                ["dtmp%d" % ui, uk, pl], outkeys)
            if samp:
                DVE(lambda e: e.tensor_tensor(out=dtmp[ui][:, 0:1], in0=sshift[:, ct, 1:2], in1=Ub[:, 65:66], op=ALU.subtract),
                    [uk, "sshift", "dtmp%d" % ui], ["dtmp%d" % ui])
                DVE(lambda e: e.scalar_tensor_tensor(out=out_ap[:, 64:65], in0=dtmp[ui][:, 0:1], scalar=PPc(l, "mu", ct),
                                                     in1=Ub[:, 65:66], op0=ALU.mult, op1=ALU.add),
                    ["dtmp%d" % ui, uk, pl] + outkeys, outkeys)
                ACT(lambda e: e.copy(out=sshift_o[:, ct, 0:1], in_=Ub[:, 64:65]), [uk], ["sshift_o"])
                ACT(lambda e: e.copy(out=sshift_o[:, ct, 1:2], in_=Ub[:, 128:129]), [uk, "sshift_o"], ["sshift_o"])
            else:
                ACT(lambda e: e.copy(out=shiftst[l][:, ct:ct + 1], in_=Ub[:, N:N + 1]), [uk], ["shiftst%d" % l])

        ui = [0]

        def nui():
            ui[0] ^= 1
            return ui[0]

        S.label = 'rwA'
        wb, wk = load_group(l, "rwx")
        ps_, pk = proj(wb, wk, 0, 128, N)
        shift_tile(ps_, pk, 18, xs18[:, 0:N], ["xs18"], nui())
        ACT(lambda e: e.activation(out=txw[0:64, 0:N], in_=xs18[0:64, 0:N], func=AF.Tanh), ["xs18"], ["txw"])
        ACT(lambda e: e.copy(out=txw[64:128, 0:N], in_=xs18[64:128, 0:N]), ["xs18", "txw"], ["txw"])
        lk = ["lora%d" % l, "lora%da" % l]

        def jtile(j, wbk, wkk, wbr, wkr, wbv, wkv, c0):
            jp = j % 2
            ldec = ldec_2[jp]
            Gc = Gc_2[jp]
            aa = aa_2[jp]
            eneg = eneg_2[jp]
            eprev = eprev_2[jp]
            ehat = ehat_2[jp]
            epos = epos_2[jp]
            kx = kx_2[jp]
            kkr = kkr_2[jp]
            rn = rn_2[jp]
            kkn = kkn_2[jp]
            tk = tk_2[jp]
            kmod = kmod_2[jp]
            bb = bb_2[jp]
            rx = rx_2[jp]
            kksq = kksq_2[jp]
            ps_, pk = fbank()
            MM(ps_[:, 0:N], [(lora[l][0:64, j * 128:(j + 1) * 128], txw[0:64, 0:N])], lk + ["txw"], pk)
            ACT(lambda e, ps_=ps_: e.activation(out=ldec[:, 0:N], in_=ps_[:, 0:N], func=AF.Sigmoid, bias=PPc(l, "w0", j), scale=1.0),
                pk + [pl], ["ldec%d" % jp])
            POOL(lambda e: e.tensor_scalar(out=ldec[:, 0:N], in0=ldec[:, 0:N], scalar1=-EXPM05, scalar2=None, op0=ALU.mult),
                 ["ldec%d" % jp], ["ldec%d" % jp])
            ps2, pk2 = fbank()
            MM(ps2[:, 0:N], [(lora[l][64:128, j * 128:(j + 1) * 128], txw[64:128, 0:N])], lk + ["txw"], pk2)
            ACT(lambda e, ps2=ps2: e.activation(out=aa[:, 0:N], in_=ps2[:, 0:N], func=AF.Sigmoid, bias=PPc(l, "a0", j), scale=1.0),
                pk2 + [pl], ["aa%d" % jp])
            DVE(lambda e: e.tensor_tensor_scan(out=Gc[:, 0:N], data0=scanm[:, 0:N], data1=ldec[:, 0:N], initial=0.0,
                                               op0=ALU.mult, op1=ALU.add), ["ldec%d" % jp, "scanm"], ["Gc%d" % jp])
            for q, (o, C, sq) in enumerate(tiles):
                ACT(lambda e, q=q, o=o, C=C: e.activation(out=gC[:, j, q:q + 1], in_=Gc[:, o + C - 1:o + C], func=AF.Exp),
                    ["Gc%d" % jp, "gC"], ["gC"])
            ps_, pk = proj(wbk, wkk, c0, 128, N)
            shift_tile(ps_, pk, 6 + j, kx[:, 0:N], ["kx%d" % jp], nui())
            ACT(lambda e: e.activation(out=eneg[:, 0:N], in_=Gc[:, 0:N], func=AF.Exp, scale=-1.0), ["Gc%d" % jp], ["eneg%d" % jp])
            DVE(lambda e: e.tensor_tensor(out=eprev[:, 0:N], in0=Gc[:, 0:N], in1=ldec[:, 0:N], op=ALU.subtract),
                ["Gc%d" % jp, "ldec%d" % jp], ["eprev%d" % jp])
            ACT(lambda e: e.activation(out=eprev[:, 0:N], in_=eprev[:, 0:N], func=AF.Exp), ["eprev%d" % jp], ["eprev%d" % jp])
            for q, (o, C, sq) in enumerate(tiles):
                DVE(lambda e, o=o, C=C, q=q: e.tensor_scalar(out=ehat[:, o:o + C], in0=eneg[:, o:o + C],
                                                             scalar1=gC[:, j, q:q + 1], scalar2=None, op0=ALU.mult),
                    ["eneg%d" % jp, "ehat%d" % jp, "gC"], ["ehat%d" % jp])
            DVE(lambda e: e.tensor_scalar(out=kkr[:, 0:N], in0=kx[:, 0:N], scalar1=PPc(l, "kk", j), scalar2=None,
                                          op0=ALU.mult), ["kx%d" % jp, pl], ["kkr%d" % jp])
            POOL(lambda e: e.tensor_tensor(out=kksq[:, 0:N], in0=kkr[:, 0:N], in1=kkr[:, 0:N], op=ALU.mult), ["kkr%d" % jp], ["kksq%d" % jp])
            ps2, pk2 = fbank()
            MM(ps2[:, 0:N], [(blk1[:], kksq[:, 0:N])], ["blk1", "kksq%d" % jp], pk2)
            ACT(lambda e, ps2=ps2: e.activation(out=rn[:, 0:N], in_=ps2[:, 0:N], func=AF.Sqrt, bias=1e-12, scale=1.0), pk2, ["rn%d" % jp])
            DVE(lambda e: e.reciprocal(out=rn[:, 0:N], in_=rn[:, 0:N]), ["rn%d" % jp], ["rn%d" % jp])
            DVE(lambda e: e.tensor_tensor(out=kkn[:, 0:N], in0=kkr[:, 0:N], in1=rn[:, 0:N], op=ALU.mult), ["kkr%d" % jp, "rn%d" % jp], ["kkn%d" % jp])
            DVE(lambda e: e.tensor_scalar(out=tk[:, 0:N], in0=aa[:, 0:N], scalar1=PPc(l, "ka", j),
                                          scalar2=omka[l][:, j:j + 1], op0=ALU.mult, op1=ALU.add),
                ["aa%d" % jp, pl, "omka%d" % l], ["tk%d" % jp])
            DVE(lambda e: e.tensor_tensor(out=kmod[:, 0:N], in0=kx[:, 0:N], in1=tk[:, 0:N], op=ALU.mult), ["kx%d" % jp, "tk%d" % jp], ["kmod%d" % jp])
            POOL(lambda e: e.tensor_tensor(out=bb[:, 0:N], in0=kkn[:, 0:N], in1=aa[:, 0:N], op=ALU.mult), ["kkn%d" % jp, "aa%d" % jp], ["bb%d" % jp])
            DVE(lambda e: e.tensor_tensor(out=kt_[:, j, 0:N], in0=kmod[:, 0:N], in1=eneg[:, 0:N], op=ALU.mult),
                ["kmod%d" % jp, "eneg%d" % jp, "kt__%d" % j], ["kt__%d" % j])
            POOL(lambda e: e.tensor_tensor(out=bt_[:, j, 0:N], in0=bb[:, 0:N], in1=eneg[:, 0:N], op=ALU.mult),
                 ["bb%d" % jp, "eneg%d" % jp, "bt__%d" % j], ["bt__%d" % j])
            DVE(lambda e: e.scalar_tensor_tensor(out=at_[:, j, 0:N], in0=kkn[:, 0:N], scalar=-1.0, in1=eprev[:, 0:N],
                                                 op0=ALU.mult, op1=ALU.mult), ["kkn%d" % jp, "eprev%d" % jp, "at__%d" % j], ["at__%d" % j])
            POOL(lambda e: e.tensor_tensor(out=khat[:, j, 0:N], in0=kmod[:, 0:N], in1=ehat[:, 0:N], op=ALU.mult),
                 ["kmod%d" % jp, "ehat%d" % jp, "khat_%d" % j], ["khat_%d" % j])
            DVE(lambda e: e.tensor_tensor(out=bhat[:, j, 0:N], in0=bb[:, 0:N], in1=ehat[:, 0:N], op=ALU.mult),
                ["bb%d" % jp, "ehat%d" % jp, "bhat_%d" % j], ["bhat_%d" % j])
            ps_, pk = proj(wbr, wkr, c0, 128, N)
            shift_tile(ps_, pk, j, rx[:, 0:N], ["rx%d" % jp], nui())
            ACT(lambda e: e.activation(out=epos[:, 0:N], in_=Gc[:, 0:N], func=AF.Exp), ["Gc%d" % jp], ["epos%d" % jp])
            DVE(lambda e: e.tensor_tensor(out=rt_[:, j, 0:N], in0=rx[:, 0:N], in1=epos[:, 0:N], op=ALU.mult),
                ["rx%d" % jp, "epos%d" % jp, "rt__%d" % j], ["rt__%d" % j])
            DVE(lambda e: e.scalar_tensor_tensor(out=rkp[:, j, 0:N], in0=rx[:, 0:N], scalar=PPc(l, "rk", j),
                                                 in1=kmod[:, 0:N], op0=ALU.mult, op1=ALU.mult),
                ["rx%d" % jp, pl, "kmod%d" % jp, "rkp_%d" % j], ["rkp_%d" % j])
            ps_, pk = proj(wbv, wkv, c0, 128, N)
            shift_tile(ps_, pk, 12 + j, vT[:, j, 0:N], ["vT_%d" % j], nui())

        for part, js in (("0", range(4)), ("1", range(4, 6))):
            wbk, wkk = load_group(l, "rwk" + part)
            wbr, wkr = load_group(l, "rwr" + part)
            wbv, wkv = load_group(l, "rwv" + part)
            for j in js:
                jtile(j, wbk, wkk, wbr, wkr, wbv, wkv, (j % 4) * 128)

        for gn, js in (("rwg0", range(4)), ("rwg1", range(4, 6))):
            wb, wk = load_group(l, gn)
            for j in js:
                ps_, pk = proj(wb, wk, (j % 4) * 128, 128, N)
                ACT(lambda e, ps_=ps_, j=j: e.activation(out=grw[:, j, 0:N], in_=ps_, func=AF.Silu), pk + ["grw_%d" % j], ["grw_%d" % j])

        for q, (o, C, sq) in enumerate(tiles):
            rwkv_tile(l, q, o, C, sq, kind, og, last, nq)
        rw_out(l, N, tiles)
        if samp:
            for q, (o, C, sq) in enumerate(tiles):
                ps_, pk = fbank()
                TR(ps_[0:19, 0:128], sshift_o[:, :, q], ident_f[:], ["sshift_o", "ident_f"], pk)
                ACT(lambda e, ps_=ps_: e.copy(out=lnx[0:19, 0:128], in_=ps_[0:19, 0:128]), pk + ["lnx"], ["lnx"])
                STORE(o_shift["s"][l, sq].rearrange("(t p) -> t p", p=128), lnx[0:19, 0:128], ["lnx"])
        elif last:
            ps_, pk = fbank()
            TR(ps_[0:19, 0:128], shiftst[l][:, :], ident_f[:], ["shiftst%d" % l, "ident_f"], pk)
            ACT(lambda e, ps_=ps_: e.copy(out=lnx[0:19, 0:128], in_=ps_[0:19, 0:128]), pk + ["lnx"], ["lnx"])
            STORE(o_shift["p"][l, 0].rearrange("(t p) -> t p", p=128), lnx[0:19, 0:128], ["lnx"])

        S.label = 's5A'
        ensure_s5(l)
        wb, wk = load_group(l, "s5u")
        for j in range(4):
            ps_, pk = proj(wb, wk, j * 128, 128, N)
            ACT(lambda e, ps_=ps_, j=j: e.copy(out=u32[:, j, 0:N], in_=ps_), pk + ["u32_%d" % j], ["u32_%d" % j])
            DVE(lambda e, j=j: e.tensor_copy(out=uT[:, j, 0:N], in_=u32[:, j, 0:N]), ["u32_%d" % j, "uT_%d" % j], ["uT_%d" % j])
        wb, wk = load_group(l, "s5g")
        for j in range(4):
            ps_, pk = proj(wb, wk, j * 128, 128, N)
            ACT(lambda e, ps_=ps_, j=j: e.activation(out=gs5[:, j, 0:N], in_=ps_, func=AF.Silu), pk + ["gs5_%d" % j], ["gs5_%d" % j])
        s5_en(l, N)
        for q, (o, C, sq) in enumerate(tiles):
            s5_tile(l, q, o, C, sq, kind, og, last, nq)
        s5_out(l, N)
        ensure_s5(1 - l)

        S.label = 'mlA'
        S.label = 'mlA'
        for gi_ in range(4):
            wb, wk = load_group(l, "mlqk%d" % gi_)
            for jj in range(4):
                t = gi_ * 4 + jj
                ps_, pk = proj(wb, wk, jj * 96, 96, N)
                ACT(lambda e, ps_=ps_, t=t: e.copy(out=qkraw[:, t, 3:N + 3], in_=ps_), pk + ["qkraw_%d" % t], ["qkraw_%d" % t])
        qk = ["qkraw"]
        if not samp:
            ACT(lambda e: e.copy(out=qkraw[:, :, 0:3], in_=convst[l][:, :, :]), ["convst%d" % l] + qk, qk)
        else:
            ACT(lambda e: e.copy(out=qkraw[:, :, 0:3], in_=sconv[:, :, 0, :]), ["sconv"] + qk, qk)

        def wbc(jt, n_):
            return pq[l][:, :, jt:jt + 1].broadcast_to([96, 16, n_])

        pqk = "pq%d" % l
        DVE(lambda e: e.tensor_tensor(out=cvacc[:, :, 0:N], in0=qkraw[:, :, 0:N], in1=wbc(0, N), op=ALU.mult), qk + [pqk], ["cvacc"])
        POOL(lambda e: e.tensor_tensor(out=cvacc[:, :, 0:N], in0=cvacc[:, :, 0:N], in1=wbc(4, N), op=ALU.add), ["cvacc", pqk], ["cvacc"])
        for jt in range(1, 4):
            POOL(lambda e, jt=jt: e.tensor_tensor(out=cvtmp[:, :, 0:N], in0=qkraw[:, :, jt:N + jt], in1=wbc(jt, N), op=ALU.mult),
                 qk + [pqk, "cvtmp"], ["cvtmp"])
            DVE(lambda e: e.tensor_tensor(out=cvacc[:, :, 0:N], in0=cvacc[:, :, 0:N], in1=cvtmp[:, :, 0:N], op=ALU.add),
                ["cvacc", "cvtmp"], ["cvacc"])
        if samp:
            hk = "hbuf0"
            hb = hbuf
            POOL(lambda e: e.tensor_copy(out=hb[:, :, 0:3], in_=sconv[:, :, 1, :]), ["sconv", hk], [hk])
            POOL(lambda e: e.tensor_copy(out=hb[:, :, 3:6], in_=qkraw[:, :, 67:70]), qk + [hk], [hk])
            f3 = cvacc[:, :, 64:67]
            DVE(lambda e: e.tensor_tensor(out=f3, in0=hb[:, :, 0:3], in1=wbc(0, 3), op=ALU.mult), [hk, pqk, "cvacc"], ["cvacc"])
            DVE(lambda e: e.tensor_tensor(out=f3, in0=f3, in1=wbc(4, 3), op=ALU.add), [pqk, "cvacc"], ["cvacc"])
            for jt in range(1, 4):
                DVE(lambda e, jt=jt: e.tensor_tensor(out=cvtmp[:, :, 0:3], in0=hb[:, :, jt:jt + 3], in1=wbc(jt, 3), op=ALU.mult),
                    [hk, pqk, "cvtmp"], ["cvtmp"])
                DVE(lambda e: e.tensor_tensor(out=f3, in0=f3, in1=cvtmp[:, :, 0:3], op=ALU.add), ["cvacc", "cvtmp"], ["cvacc"])
        ACT(lambda e: e.activation(out=qkT[:, :, 0:N], in_=cvacc[:, :, 0:N], func=AF.Silu), ["cvacc", "qkT"], ["qkT"])
        POOL(lambda e: e.tensor_scalar(out=qkT[:, 8:16, 0:N], in0=qkT[:, 8:16, 0:N], scalar1=1.0 / math.sqrt(192.0), scalar2=None,
                                       op0=ALU.mult), ["qkT"], ["qkT"])
        if not samp:
            ACT(lambda e: e.copy(out=convst[l][:, :, :], in_=qkraw[:, :, N:N + 3]), qk + ["convst%d" % l], ["convst%d" % l])
        conv_out(l, N, tiles, kind, og, last)
        wb0, wk0 = load_group(l, "mlv0")
        wb1, wk1 = load_group(l, "mlv1")
        for q, (o, C, sq) in enumerate(tiles):
            ps_, pk = fpair()
            MM(ps_[0:C, 0, 0:512], [(xT[:, kc, o:o + C], wb0[:, kc, 0:512]) for kc in range(8)], [wk0, "xT"], [pk[0]])
            MM(ps_[0:C, 1, 0:264], [(xT[:, kc, o:o + C], wb1[:, kc, 0:264]) for kc in range(8)], [wk1, "xT"], [pk[1]])
            vk = ["vaug"]
            ACT(lambda e, ps_=ps_, q=q, C=C: e.copy(out=vaug[0:C, q, 0:2, 0:192],
                                                    in_=ps_[0:C, 0, 0:384].rearrange("p (h d) -> p h d", d=192)), [pk[0]] + vk, vk)
            ACT(lambda e, ps_=ps_, q=q, C=C: e.copy(out=vaug[0:C, q, 2, 0:128], in_=ps_[0:C, 0, 384:512]), [pk[0]] + vk, vk)
            DVE(lambda e, ps_=ps_, q=q, C=C: e.tensor_copy(out=vaug[0:C, q, 2, 128:192], in_=ps_[0:C, 1, 0:64]), [pk[1]] + vk, vk)
            DVE(lambda e, ps_=ps_, q=q, C=C: e.tensor_copy(out=vaug[0:C, q, 3, 0:192], in_=ps_[0:C, 1, 64:256]), [pk[1]] + vk, vk)
            POOL(lambda e, q=q, C=C: e.memset(vaug[0:C, q, :, 192:193], 1.0), vk, vk)
            DVE(lambda e, ps_=ps_, q=q, C=C: e.tensor_tensor(out=iftok[0:C, q, :], in0=ps_[0:C, 1, 256:264], in1=bif[l][0:C, :],
                                                             op=ALU.add), [pk[1], "bif%d" % l, "iftok"], ["iftok"])
        wbo0, wko0 = load_group(l, "mlo0")
        wbo1, wko1 = load_group(l, "mlo1")
        for j in range(6):
            wb, wk = (wbo0, wko0) if j < 4 else (wbo1, wko1)
            ps_, pk = proj(wb, wk, (j % 4) * 128, 128, N)
            ACT(lambda e, ps_=ps_, j=j: e.activation(out=ogT[:, j, 0:N], in_=ps_, func=AF.Sigmoid), pk + ["ogT_%d" % j], ["ogT_%d" % j])
        wbz0, wkz0 = load_group(l, "mlz0")
        wbz1, wkz1 = load_group(l, "mlz1")
        for j in range(6):
            wb, wk = (wbz0, wkz0) if j < 4 else (wbz1, wkz1)
            ps_, pk = proj(wb, wk, (j % 4) * 128, 128, N)
            zsl = zsl_2[j % 2]
            ACT(lambda e, ps_=ps_, zsl=zsl: e.activation(out=zsl[:, 0:N], in_=ps_, func=AF.Silu), pk, ["zsl%d" % (j % 2)])
            DVE(lambda e, j=j, zsl=zsl: e.scalar_tensor_tensor(out=ogT[:, j, 0:N], in0=ogT[:, j, 0:N], scalar=PPc(l, "mlg", j),
                                                               in1=zsl[:, 0:N], op0=ALU.mult, op1=ALU.mult),
                ["ogT_%d" % j, "zsl%d" % (j % 2), pl], ["ogT_%d" % j])
        for q, (o, C, sq) in enumerate(tiles):
            ml_tile(l, q, o, C, sq, kind, og, last, nq)
        ml_out(l, N, tiles)

    def conv_out(l, N, tiles, kind, og, last):
        if kind == "sample":
            ends = [(o + C - 3, sq) for (o, C, sq) in tiles]
        elif last:
            ends = [(N - 3, 0)]
        else:
            return
        for (e0, sq) in ends:
            for hf in range(2):
                for i2 in range(2):
                    i = hf * 2 + i2
                    wbi, wki = load_group(l, "mlqk%d" % i)
                    ps_, pk = fbank()
                    MM(ps_[0:3, 0:384], [(xT[:, kc, e0:e0 + 3], wbi[:, kc, 0:384]) for kc in range(8)], [wki, "xT"], pk)
                    ACT(lambda e, i2=i2, ps_=ps_: e.copy(out=lnt[0:3, i2 * 384:(i2 + 1) * 384], in_=ps_[0:3, 0:384]), pk + ["lnt"], ["lnt"])
                STORE(o_conv[og][l, sq, :, hf * 768:(hf + 1) * 768], lnt[0:3, 0:768], ["lnt"])

    def rwkv_tile(l, q, o, C, sq, kind, og, last, nq):
        S.label = 'rwT'
        tb_ = TB[q % 2]
        pz = tb_['z']
        vtok, khtok, bhtok, Nsb, NTsb, Msb, P1sb, P2sb = (tb_[k_] for k_ in ('vtok', 'khtok', 'bhtok', 'Nsb', 'NTsb', 'Msb', 'P1sb', 'P2sb'))
        samp = kind == "sample"
        sk = "s0t%d" % l
        sl = slice(o, o + C)
        PARTS = [(0, 0), (1, 64)]
        if samp:
            LOAD(stg[:], st_wkv[l, sq].rearrange("h v k -> v h k"), [], ["stg"])
            for j in range(6):
                ps_, pk = fbank()
                TR(ps_[:, 0:64], stg[:, 2 * j:2 * j + 2, :].rearrange("p a b -> p (a b)"), ident_f[0:64, 0:64], ["stg", "ident_f"], pk)
                ACT(lambda e, ps_=ps_, j=j: e.copy(out=S0T[l][:, j, :], in_=ps_[:, 0:64]), pk + [sk], [sk])
        ACT(lambda e: e.copy(out=S0Tb[:], in_=S0T[l][:]), [sk], ["s0tb"])

        def both(fn_):
            if C == 64:
                fn_(slice(0, 128))
            else:
                for par, pb in PARTS:
                    fn_(slice(pb, pb + C))

        for src, srck, dst, dk in ((vT, "vT", vtok, "vtok" + pz), (khat, "khat", khtok, "khtok" + pz), (bhat, "bhat", bhtok, "bhtok" + pz)):
            def fnt(e, src=src):
                for j in range(6):
                    for par, pb in PARTS:
                        ins = e.transpose(out=PSB[pb:pb + C, par, j * 64:(j + 1) * 64], in_=src[pb:pb + 64, j, sl],
                                          identity=ident_b[pb:pb + 64, pb:pb + 64])
                return ins
            PE(fnt, [srck, "ident_b"], ["pb0", "pb1"])
            for par, pb in PARTS:
                ACT(lambda e, dst=dst, par=par, pb=pb: e.copy(out=dst[pb:pb + C, :, :],
                                                                in_=PSB[pb:pb + C, par, 0:384].rearrange("p (j d) -> p j d", d=64)),
                    ["pb%d" % par, dk], [dk])

        def score(lt, lk_, rt2, rk_, mask, mk, dst, dk):
            ps_, pk = fpair()
            def fn(e, ps_=ps_):
                for j in range(6):
                    for par, pb in PARTS:
                        ins = e.matmul(ps_[pb:pb + C, par, j * 64:j * 64 + C], lhsT=lt[pb:pb + 64, j, sl],
                                       rhs=rt2[pb:pb + 64, j, sl], start=True, stop=True)
                return ins
            PE(fn, [lk_, rk_], pk)
            for par, pb in PARTS:
                DVE(lambda e, ps_=ps_, par=par, pb=pb: e.tensor_tensor(
                    out=dst[pb:pb + C, :, 0:C], in0=ps_[pb:pb + C, par, 0:384].rearrange("p (j t) -> p j t", t=64)[:, :, 0:C],
                    in1=mask[pb:pb + C, :, 0:C], op=ALU.mult), [pk[par], mk, dk], [dk])

        def evac(ps_, pk, dst, dk, eng=ACT):
            for par, pb in PARTS:
                if par == 0:
                    ACT(lambda e, par=par, pb=pb: e.copy(out=dst[pb:pb + C, :, :],
                                                         in_=ps_[pb:pb + C, par, 0:384].rearrange("p (j d) -> p j d", d=64)),
                        [pk[par], dk], [dk])
                else:
                    DVE(lambda e, par=par, pb=pb: e.tensor_copy(out=dst[pb:pb + C, :, :],
                                                                in_=ps_[pb:pb + C, par, 0:384].rearrange("p (j d) -> p j d", d=64)),
                        [pk[par], dk], [dk])

        def evac_sq(ps_, pk, dst, dk):
            for par, pb in PARTS:
                DVE(lambda e, par=par, pb=pb: e.tensor_copy(
                    out=dst[pb:pb + C, :, 0:C], in_=ps_[pb:pb + C, par, 0:384].rearrange("p (j t) -> p j t", t=64)[:, :, 0:C]),
                    [pk[par], dk], [dk])

        score(bt_, "bt_", at_, "at_", m_strict, "m_strict", Nsb[0], "Nsb0" + pz)
        score(at_, "at_", bt_, "bt_", m_lower, "m_lower", NTsb[0], "NTsb0" + pz)
        score(kt_, "kt_", at_, "at_", m_strict, "m_strict", Msb, "Msb" + pz)

        ps_, pk = fpair()
        def fnx(e, ps_=ps_):
            for j in range(6):
                for par, pb in PARTS:
                    out = ps_[pb:pb + C, par, j * 64:(j + 1) * 64]
                    e.matmul(out, lhsT=at_[pb:pb + 64, j, sl], rhs=S0Tb[pb:pb + 64, j, :], start=True, stop=False)
                    ins = e.matmul(out, lhsT=Msb[pb:pb + C, j, 0:C], rhs=vtok[pb:pb + C, j, :], start=False, stop=True)
            return ins
        PE(fnx, ["at_", "s0tb", "Msb" + pz, "vtok" + pz], pk)
        evac(ps_, pk, Ysb[0], "Ysb0")
        lev = int(round(math.log2(C)))
        cur = 0
        for lv in range(lev):
            Pc, PTc, Yc = Nsb[cur], NTsb[cur], Ysb[cur]
            Pn, PTn, Yn = Nsb[1 - cur], NTsb[1 - cur], Ysb[1 - cur]
            ps_, pk = fpair()
            def fny(e, ps_=ps_, Pc=Pc, Yc=Yc):
                for j in range(6):
                    for par, pb in PARTS:
                        out = ps_[pb:pb + C, par, j * 64:(j + 1) * 64]
                        e.matmul(out, lhsT=ident_b[pb:pb + C, pb:pb + C], rhs=Yc[pb:pb + C, j, :], start=True, stop=False)
                        ins = e.matmul(out, lhsT=Pc[pb:pb + C, j, 0:C], rhs=Yc[pb:pb + C, j, :], start=False, stop=True)
                return ins
            PE(fny, ["Nsb%d" % cur + pz, "Ysb%d" % cur, "ident_b"], pk)
            evac(ps_, pk, Yn, "Ysb%d" % (1 - cur))
            if lv < lev - 1:
                ps2, pk2 = fpair()
                def fnp(e, ps2=ps2, Pc=Pc, PTc=PTc):
                    for j in range(6):
                        for par, pb in PARTS:
                            ins = e.matmul(ps2[pb:pb + C, par, j * 64:j * 64 + C], lhsT=PTc[pb:pb + C, j, 0:C],
                                           rhs=Pc[pb:pb + C, j, 0:C], start=True, stop=True)
                    return ins
                PE(fnp, ["Nsb%d" % cur + pz, "NTsb%d" % cur + pz], pk2)
                evac_sq(ps2, pk2, Pn, "Nsb%d" % (1 - cur) + pz)
                if lv < lev - 2:
                    ps3, pk3 = fpair()
                    def fnq(e, ps3=ps3, Pc=Pc, PTc=PTc):
                        for j in range(6):
                            for par, pb in PARTS:
                                ins = e.matmul(ps3[pb:pb + C, par, j * 64:j * 64 + C], lhsT=Pc[pb:pb + C, j, 0:C],
                                               rhs=PTc[pb:pb + C, j, 0:C], start=True, stop=True)
                        return ins
                    PE(fnq, ["Nsb%d" % cur + pz, "NTsb%d" % cur + pz], pk3)
                    evac_sq(ps3, pk3, PTn, "NTsb%d" % (1 - cur) + pz)
            cur = 1 - cur
        UT = Ysb[cur]
        uk = "Ysb%d" % cur
        score(kt_, "kt_", rt_, "rt_", m_incl, "m_incl", P1sb, "P1sb" + pz)
        score(bt_, "bt_", rt_, "rt_", m_incl, "m_incl", P2sb, "P2sb" + pz)
        ps_, pk = fpair()
        def fno(e, ps_=ps_, UT=UT):
            for j in range(6):
                for par, pb in PARTS:
                    out = ps_[pb:pb + C, par, j * 64:(j + 1) * 64]
                    e.matmul(out, lhsT=rt_[pb:pb + 64, j, sl], rhs=S0Tb[pb:pb + 64, j, :], start=True, stop=False)
                    e.matmul(out, lhsT=P1sb[pb:pb + C, j, 0:C], rhs=vtok[pb:pb + C, j, :], start=False, stop=False)
                    ins = e.matmul(out, lhsT=P2sb[pb:pb + C, j, 0:C], rhs=UT[pb:pb + C, j, :], start=False, stop=True)
            return ins
        PE(fno, ["rt_", "s0tb", "P1sb" + pz, "P2sb" + pz, "vtok" + pz, uk], pk)
        evac(ps_, pk, yo[:, 6 * q:6 * q + 6, :], "yo")
        psr, pkr = fpair()
        def fnr(e, psr=psr):
            for j in range(6):
                for par, pb in PARTS:
                    ins = e.matmul(psr[pb:pb + C, par, j:j + 1], lhsT=rkp[pb:pb + 64, j, sl], rhs=onesb[pb:pb + 64, 0:1],
                                   start=True, stop=True)
            return ins
        PE(fnr, ["rkp", "onesb"], pkr)
        for par, pb in PARTS:
            ACT(lambda e, par=par, pb=pb, psr=psr: e.copy(out=rkd[pb:pb + C, 6 * q:6 * q + 6], in_=psr[pb:pb + C, par, 0:6]), [pkr[par], "rkd"], ["rkd"])
        both(lambda P: POOL(lambda e: e.tensor_tensor(out=ysq[P, 6 * q:6 * q + 6, :], in0=vtok[P],
                                                      in1=rkd[P, 6 * q:6 * q + 6].unsqueeze(2).broadcast_to([P.stop - P.start, 6, 64]),
                                                      op=ALU.mult), ["vtok" + pz, "rkd", "ysq"], ["ysq"]))
        ps_, pk = fpair()
        def fns(e, ps_=ps_, UT=UT):
            for j in range(6):
                for par, pb in PARTS:
                    out = ps_[pb:pb + 64, par, j * 64:(j + 1) * 64]
                    e.matmul(out, lhsT=khtok[pb:pb + C, j, :], rhs=vtok[pb:pb + C, j, :], start=True, stop=False)
                    ins = e.matmul(out, lhsT=bhtok[pb:pb + C, j, :], rhs=UT[pb:pb + C, j, :], start=False, stop=True)
            return ins
        PE(fns, ["khtok" + pz, "bhtok" + pz, "vtok" + pz, uk], pk)
        for j in range(6):
            for par, pb in PARTS:
                DVE(lambda e, j=j, ps_=ps_, par=par, pb=pb: e.scalar_tensor_tensor(
                    out=S0T[l][pb:pb + 64, j, :], in0=S0T[l][pb:pb + 64, j, :], scalar=gC[pb:pb + 64, j, q:q + 1],
                    in1=ps_[pb:pb + 64, par, j * 64:(j + 1) * 64], op0=ALU.mult, op1=ALU.add), [pk[par], sk, "gC"], [sk])
        if samp or (last and q == nq - 1):
            b_ = sq if samp else 0
            for j in range(6):
                ps2, pk2 = fbank()
                TR(ps2[0:64, 0:128], S0T[l][:, j, :], ident_f[:], [sk, "ident_f"], pk2)
                ACT(lambda e, ps2=ps2, j=j: e.copy(out=stg[:, 2 * j:2 * j + 2, :].rearrange("p a b -> p (a b)"), in_=ps2[0:64, 0:128]),
                    pk2 + ["stg"], ["stg"])
            STORE(o_wkv[og][l, b_].rearrange("h v k -> v h k"), stg[:], ["stg"])

    def rw_out(l, N, tiles):
        S.label = 'rwT'
        C = tiles[0][1]
        nq = len(tiles)
        nj = 6 * nq
        PARTS = [(0, 0), (1, 64)]

        def both(fn_):
            if C == 64:
                fn_(slice(0, 128))
            else:
                for par, pb in PARTS:
                    fn_(slice(pb, pb + C))

        def bc(ap, P):
            return ap.unsqueeze(2).broadcast_to([P.stop - P.start, nj, 64])

        both(lambda P: DVE(lambda e: e.tensor_reduce(out=ystat[P, 0, 0:nj], in_=yo[P, 0:nj, :], axis=AX.X, op=ALU.add), ["yo", "ystat"], ["ystat"]))
        both(lambda P: DVE(lambda e: e.tensor_scalar(out=ystat[P, 0, 0:nj], in0=ystat[P, 0, 0:nj], scalar1=1.0 / 64, scalar2=None, op0=ALU.mult),
                           ["ystat"], ["ystat"]))
        both(lambda P: DVE(lambda e: e.tensor_tensor(out=ycen[P, 0:nj, :], in0=yo[P, 0:nj, :], in1=bc(ystat[P, 0, 0:nj], P), op=ALU.subtract),
                           ["yo", "ystat", "ycen"], ["ycen"]))
        both(lambda P: POOL(lambda e: e.tensor_tensor(out=yo[P, 0:nj, :], in0=ycen[P, 0:nj, :], in1=ycen[P, 0:nj, :], op=ALU.mult), ["ycen", "yo"], ["yo"]))
        both(lambda P: DVE(lambda e: e.tensor_reduce(out=ystat[P, 1, 0:nj], in_=yo[P, 0:nj, :], axis=AX.X, op=ALU.add), ["yo", "ystat"], ["ystat"]))
        both(lambda P: ACT(lambda e: e.activation(out=ystat[P, 2, 0:nj], in_=ystat[P, 1, 0:nj], func=AF.Sqrt, bias=RW_GN_EPS, scale=1.0 / 64),
                           ["ystat"], ["ystat"]))
        both(lambda P: DVE(lambda e: e.reciprocal(out=ystat[P, 2, 0:nj], in_=ystat[P, 2, 0:nj]), ["ystat"], ["ystat"]))
        both(lambda P: DVE(lambda e: e.tensor_tensor(out=ycen[P, 0:nj, :], in0=ycen[P, 0:nj, :], in1=bc(ystat[P, 2, 0:nj], P), op=ALU.mult),
                           ["ycen", "ystat"], ["ycen"]))

        def gb(P, a_):
            return rwln[l][P, a_, :].rearrange("p (j d) -> p j d", d=64).unsqueeze(1).broadcast_to([P.stop - P.start, nq, 6, 64])

        def v4(t, P):
            return t[P, 0:nj, :].rearrange("p (q j) d -> p q j d", j=6)

        both(lambda P: DVE(lambda e: e.tensor_tensor(out=v4(ycen, P), in0=v4(ycen, P), in1=gb(P, 0), op=ALU.mult), ["ycen", "rwln%d" % l], ["ycen"]))
        both(lambda P: POOL(lambda e: e.tensor_tensor(out=v4(ycen, P), in0=v4(ycen, P), in1=gb(P, 1), op=ALU.add), ["ycen", "rwln%d" % l], ["ycen"]))
        both(lambda P: DVE(lambda e: e.tensor_tensor(out=ytb[P, 0:nj, :], in0=ycen[P, 0:nj, :], in1=ysq[P, 0:nj, :], op=ALU.add),
                           ["ycen", "ysq", "ytb"], ["ytb"]))

        def fnb(e):
            for q in range(nq):
                for j in range(6):
                    for par, pb in PARTS:
                        ins = e.transpose(out=PSB[pb:pb + 64, par, (q * 6 + j) * 64:(q * 6 + j) * 64 + C], in_=ytb[pb:pb + C, q * 6 + j, :],
                                          identity=ident_b[pb:pb + C, pb:pb + C])
            return ins
        PE(fnb, ["ytb", "ident_b"], ["pb0", "pb1"])
        for par, pb in PARTS:
            for q, (o, C_, sq) in enumerate(tiles):
                DVE(lambda e, par=par, pb=pb, q=q, o=o: e.tensor_tensor(
                    out=yrwT[pb:pb + 64, :, o:o + C],
                    in0=PSB[pb:pb + 64, par, q * 384:(q + 1) * 384].rearrange("p (j t) -> p j t", t=64)[:, :, 0:C],
                    in1=grw[pb:pb + 64, :, o:o + C], op=ALU.mult), ["pb%d" % par, "grw", "yrwT"], ["yrwT"])

    def s5_en(l, N):
        S.label = 's5T'
        C = N
        sl = slice(0, N)
        for ut in range(4):
            ps_, pk = fpair()
            MM(ps_[0:C, 0, :], [(uT[:, ut, sl], bblkB[:, ut, 0:512])], ["uT", "bblkB"], [pk[0]])
            MM(ps_[0:C, 1, :], [(uT[:, ut, sl], bblkB[:, ut, 512:1024])], ["uT", "bblkB"], [pk[1]])
            pvv = ps_[0:C].rearrange("p b (i c q) -> p (b i) c q", i=2, c=2)
            bur, bui = pvv[:, :, 0, :], pvv[:, :, 1, :]
            enr, eni = EnB[0:C, ut * 4:(ut + 1) * 4, 0, :], EnB[0:C, ut * 4:(ut + 1) * 4, 1, :]
            wr_, wi_ = wtok[0:C, ut * 4:(ut + 1) * 4, 0, :], wtok[0:C, ut * 4:(ut + 1) * 4, 1, :]
            ek = ["EnB"]
            DVE(lambda e, bur=bur, enr=enr: e.tensor_tensor(out=s5a[0:C], in0=bur, in1=enr, op=ALU.mult), pk + ek, ["s5a"])
            DVE(lambda e, bui=bui, eni=eni: e.tensor_tensor(out=s5b[0:C], in0=bui, in1=eni, op=ALU.mult), pk + ek, ["s5b"])
            POOL(lambda e, wr_=wr_: e.tensor_tensor(out=wr_, in0=s5a[0:C], in1=s5b[0:C], op=ALU.subtract), ["s5a", "s5b", "wtok"], ["wtok"])
            DVE(lambda e, bur=bur, eni=eni: e.tensor_tensor(out=s5a[0:C], in0=bur, in1=eni, op=ALU.mult), pk + ek + ["s5a"], ["s5a"])
            DVE(lambda e, bui=bui, enr=enr: e.tensor_tensor(out=s5b[0:C], in0=bui, in1=enr, op=ALU.mult), pk + ek + ["s5b"], ["s5b"])
            POOL(lambda e, wi_=wi_: e.tensor_tensor(out=wi_, in0=s5a[0:C], in1=s5b[0:C], op=ALU.add), ["s5a", "s5b", "wtok"], ["wtok"])

    def s5_tile(l, q, o, C, sq, kind, og, last, nq):
        S.label = 's5T'
        samp = kind == "sample"
        hk = "h0%d" % l
        sl = slice(o, o + C)
        if samp:
            for c, srcd in ((0, st_s5re), (1, st_s5im)):
                LOAD(lnx[0:16, c * 128:(c + 1) * 128], srcd[l, sq].rearrange("(i g) p -> i (g p)", g=2), [], ["lnx"])
            for c in range(2):
                ps_, pk = fbank()
                TR(ps_[:, 0:16], lnx[0:16, c * 128:(c + 1) * 128], ident_f[0:16, 0:16], ["lnx", "ident_f"], pk)
                ACT(lambda e, ps_=ps_, c=c: e.copy(out=h0[l][:, c, :], in_=ps_[:, 0:16]), pk + [hk], [hk])
        for c, Gd, gk in ((0, Gr, "Gr"), (1, Gi, "Gi")):
            ps_, pk = fpair()
            def fnc(e, ps_=ps_, c=c):
                for i in range(16):
                    ins = e.matmul(ps_[:, i // 8, (i % 8) * 64:(i % 8) * 64 + C], lhsT=wtok[o:o + C, i, c, :], rhs=tri2[o:o + C, 0:C],
                                   start=True, stop=True)
                return ins
            PE(fnc, ["wtok", "tri2"], pk)
            DVE(lambda e, ps_=ps_, Gd=Gd, c=c: e.tensor_tensor(
                out=Gd[:, :, 0:C].rearrange("p (b i) t -> p b i t", b=2),
                in0=ps_[:, :, :].rearrange("p b (i t) -> p b i t", t=64)[:, :, :, 0:C],
                in1=h0[l][:, c, :].rearrange("p (b i) -> p b i", b=2).unsqueeze(3).broadcast_to([128, 2, 8, C]), op=ALU.add),
                pk + [hk], [gk])
        er, ei = EpB[:, 0, :, 0:C], EpB[:, 1, :, 0:C]
        ek = ["EpB"]
        DVE(lambda e: e.tensor_tensor(out=hA[:, :, 0:C], in0=Gr[:, :, 0:C], in1=er, op=ALU.mult), ["Gr"] + ek, ["hA"])
        DVE(lambda e: e.tensor_tensor(out=hB[:, :, 0:C], in0=Gi[:, :, 0:C], in1=ei, op=ALU.mult), ["Gi"] + ek, ["hB"])
        DVE(lambda e: e.tensor_tensor(out=hre[:, :, sl], in0=hA[:, :, 0:C], in1=hB[:, :, 0:C], op=ALU.subtract), ["hA", "hB", "hre"], ["hre"])
        POOL(lambda e: e.tensor_tensor(out=hC[:, :, 0:C], in0=Gi[:, :, 0:C], in1=er, op=ALU.mult), ["Gi"] + ek, ["hC"])
        POOL(lambda e: e.tensor_tensor(out=hD[:, :, 0:C], in0=Gr[:, :, 0:C], in1=ei, op=ALU.mult), ["Gr"] + ek, ["hD"])
        DVE(lambda e: e.scalar_tensor_tensor(out=himn[:, :, sl], in0=hC[:, :, 0:C], scalar=-1.0, in1=hD[:, :, 0:C],
                                             op0=ALU.mult, op1=ALU.subtract), ["hC", "hD", "himn"], ["himn"])
        DVE(lambda e: e.tensor_tensor(out=h0[l][:, 0, :], in0=hA[:, :, C - 1], in1=hB[:, :, C - 1], op=ALU.subtract),
            ["hA", "hB", hk, "Gr", "Gi"], [hk])
        DVE(lambda e: e.tensor_tensor(out=h0[l][:, 1, :], in0=hC[:, :, C - 1], in1=hD[:, :, C - 1], op=ALU.add), ["hC", "hD", hk], [hk])
        if samp or (last and q == nq - 1):
            b_ = sq if samp else 0
            for c, dd in ((0, o_s5re), (1, o_s5im)):
                ps3, pk3 = fbank()
                TR(ps3[0:16, 0:128], h0[l][:, c, :], ident_f[:], [hk, "ident_f"], pk3)
                ACT(lambda e, ps3=ps3, c=c: e.copy(out=lnx[0:16, c * 128:(c + 1) * 128], in_=ps3[0:16, 0:128]), pk3 + ["lnx"], ["lnx"])
                STORE(dd[og][l, b_].rearrange("(i g) p -> i (g p)", g=2), lnx[0:16, c * 128:(c + 1) * 128], ["lnx"])

    def s5_out(l, N):
        S.label = 's5T'
        C = N
        sl = slice(0, N)
        ps_, pk = fbank()
        def fny(e, ps_=ps_):
            for ut in range(4):
                for hf in range(2):
                    out = ps_[hf * 64:(hf + 1) * 64, ut * 128:ut * 128 + C]
                    n_ = 0
                    for ii in range(2):
                        i = ut * 4 + hf * 2 + ii
                        for c, hsrc in ((0, hre), (1, himn)):
                            ins = e.matmul(out, lhsT=cpadB[:, i, c, :], rhs=hsrc[:, i, 0:C], start=(n_ == 0), stop=(n_ == 3))
                            n_ += 1
            return ins
        PE(fny, ["cpadB", "hre", "himn"], pk)
        for ut in range(4):
            DVE(lambda e, ut=ut, ps_=ps_: e.scalar_tensor_tensor(out=yv[:, ut, 0:C], in0=u32[:, ut, sl], scalar=PPc(l, "s5d", ut),
                                                                 in1=ps_[:, ut * 128:ut * 128 + C], op0=ALU.mult, op1=ALU.add),
                pk + ["u32", "pp%d" % l, "yv"], ["yv"])
        POOL(lambda e: e.tensor_tensor(out=gt[:, :, 0:C], in0=yv[:, :, 0:C], in1=yv[:, :, 0:C], op=ALU.mult), ["yv"], ["gt"])
        DVE(lambda e: e.tensor_scalar(out=gt[:, :, 0:C], in0=gt[:, :, 0:C], scalar1=0.044715, scalar2=1.0, op0=ALU.mult, op1=ALU.add),
            ["gt"], ["gt"])
        DVE(lambda e: e.tensor_tensor(out=gt[:, :, 0:C], in0=gt[:, :, 0:C], in1=yv[:, :, 0:C], op=ALU.mult), ["gt", "yv"], ["gt"])
        ACT(lambda e: e.activation(out=gsg[:, :, 0:C], in_=gt[:, :, 0:C], func=AF.Sigmoid, scale=1.5957691216057308), ["gt"], ["gsg"])
        DVE(lambda e: e.tensor_tensor(out=gl[:, :, 0:C], in0=yv[:, :, 0:C], in1=gsg[:, :, 0:C], op=ALU.mult), ["yv", "gsg"], ["gl"])
        ACT(lambda e: e.copy(out=glb[:, :, 0:C], in_=gl[:, :, 0:C]), ["gl"], ["glb"])
        ps2, pk2 = fbank()
        def fng(e):
            for ct in range(4):
                for kc in range(4):
                    ins = e.matmul(ps2[:, ct * 128:ct * 128 + C], lhsT=wglu[l][:, kc, ct * 128:(ct + 1) * 128], rhs=glb[:, kc, 0:C],
                                   start=(kc == 0), stop=(kc == 3))
            return ins
        PE(fng, ["wglu%d" % l, "glb"], pk2)
        for ct in range(4):
            ACT(lambda e, ct=ct: e.activation(out=sgl[:, ct, 0:C], in_=ps2[:, ct * 128:ct * 128 + C], func=AF.Sigmoid,
                                              bias=PPc(l, "bglu", ct), scale=1.0), pk2 + ["pp%d" % l, "sgl"], ["sgl"])
        POOL(lambda e: e.tensor_tensor(out=gl[:, :, 0:C], in0=gl[:, :, 0:C], in1=sgl[:, :, 0:C], op=ALU.mult), ["gl", "sgl"], ["gl"])
        DVE(lambda e: e.tensor_tensor(out=ys5T[:, :, sl], in0=gl[:, :, 0:C], in1=gs5[:, :, sl], op=ALU.mult),
            ["gl", "gs5", "ys5T"], ["ys5T"])

    def ml_tile(l, q, o, C, sq, kind, og, last, nq):
        S.label = 'mlT'
        samp = kind == "sample"
        ck, mk_ = "cst%d" % l, "mst%d" % l
        sl = slice(o, o + C)
        if samp:
            for kt in range(2):
                LOAD(Cst[l][:, kt, :, 0:192], st_c[l, sq, :, kt * 96:(kt + 1) * 96, :].rearrange("h p v -> p h v"), [ck], [ck])
                LOAD(Cst[l][:, kt, :, 192:193], st_n[l, sq, :, kt * 96:(kt + 1) * 96].rearrange("h (p o) -> p h o", o=1), [ck], [ck], slow=True)
            LOAD(mst[l][:], st_m[l, sq].rearrange("(h o) -> h o", o=1), [], [mk_], slow=True)
        ik = "iftok"
        ACT(lambda e: e.activation(out=lfi[0:C, 4:8], in_=iftok[0:C, q, 4:8], func=AF.Sigmoid), [ik], ["lfi"])
        ACT(lambda e: e.activation(out=lfi[0:C, 4:8], in_=lfi[0:C, 4:8], func=AF.Ln), ["lfi"], ["lfi"])
        ps_, pk = fbank()
        MM(ps_[0:C, 0:4], [(tri_f[0:C, 0:C], lfi[0:C, 4:8])], ["tri_f", "lfi"], pk)
        ps7, pk7 = fbank()
        MM(ps7[0:4, 0:1], [(lfi[0:C, 4:8], ones_f[0:C, 0:1])], ["ones_f", "lfi"], pk7)
        ACT(lambda e: e.copy(out=bcs[0:C, :], in_=ps_[0:C, 0:4]), pk, ["bcs"])
        ACT(lambda e: e.copy(out=bend[:], in_=ps7[0:4, 0:1]), pk7, ["bend"])
        DVE(lambda e: e.tensor_tensor(out=zz[0:C, :], in0=iftok[0:C, q, 0:4], in1=bcs[0:C, :], op=ALU.subtract), [ik, "bcs"], ["zz"])
        ps2, pk2 = fbank()
        TR(ps2[0:4, 0:C], zz[0:C, :], ident_f[0:C, 0:C], ["zz", "ident_f"], pk2)
        DVE(lambda e: e.tensor_reduce(out=zmax[:], in_=ps2[0:4, 0:C], axis=AX.X, op=ALU.max), pk2, ["zmax"])
        DVE(lambda e: e.tensor_tensor(out=mu4[:], in0=zmax[:], in1=mst[l][:], op=ALU.max), ["zmax", mk_], ["mu4"])
        DVE(lambda e: e.tensor_tensor(out=f4[:], in0=mst[l][:], in1=mu4[:], op=ALU.subtract), [mk_, "mu4"], ["f4"])
        ACT(lambda e: e.activation(out=f4[:], in_=f4[:], func=AF.Exp), ["f4"], ["f4"])
        DVE(lambda e: e.tensor_tensor(out=mst[l][:], in0=bend[:], in1=mu4[:], op=ALU.add), ["bend", "mu4", "f4", mk_], [mk_])
        DVE(lambda e: e.tensor_scalar(out=dg[:, 0:4], in0=ident_f[0:4, 0:4], scalar1=mu4[:, 0:1], scalar2=None, op0=ALU.mult),
            ["ident_f", "mu4"], ["dg"])
        DVE(lambda e: e.tensor_scalar(out=dg[:, 4:8], in0=ident_f[0:4, 0:4], scalar1=f4[:, 0:1], scalar2=None, op0=ALU.mult),
            ["ident_f", "f4", "dg"], ["dg"])
        ps3, pk3 = fbank()
        MM(ps3[:, 0:8], [(ones_f[0:4, :], dg[:, :])], ["ones_f", "dg"], pk3)
        ACT(lambda e: e.copy(out=bc8[:], in_=ps3[:, 0:8]), pk3, ["bc8"])
        DVE(lambda e: e.tensor_tensor(out=ee[0:C, :], in0=zz[0:C, :], in1=bc8[0:C, 0:4], op=ALU.subtract), ["zz", "bc8"], ["ee"])
        ACT(lambda e: e.activation(out=ee[0:C, :], in_=ee[0:C, :], func=AF.Exp), ["ee"], ["ee"])
        DVE(lambda e: e.tensor_tensor(out=clampt[0:C, :], in0=bcs[0:C, :], in1=bc8[0:C, 0:4], op=ALU.add), ["bcs", "bc8"], ["clampt"])
        ACT(lambda e: e.activation(out=clampt[0:C, :], in_=clampt[0:C, :], func=AF.Exp, scale=-1.0), ["clampt"], ["clampt"])
        for hd in range(4):
            DVE(lambda e, hd=hd: e.tensor_scalar(out=Cst[l][:, :, hd, :], in0=Cst[l][:, :, hd, :], scalar1=bc8[0:96, 4 + hd:5 + hd],
                                                 scalar2=None, op0=ALU.mult), [ck, "bc8"], [ck])
        ACT(lambda e: e.copy(out=Cstb[:], in_=Cst[l][:]), [ck], ["cstb"])
        ps4, pk4 = fbank()
        def fnsc(e):
            for hd in range(4):
                for kt in range(2):
                    ins = e.matmul(ps4[0:C, hd * 64:hd * 64 + C], lhsT=qkT[:, 8 + 2 * hd + kt, sl], rhs=qkT[:, 2 * hd + kt, sl],
                                   start=(kt == 0), stop=(kt == 1))
            return ins
        PE(fnsc, ["qkT"], pk4)
        p4 = ps4[0:C, 0:256].rearrange("p (h t) -> p h t", t=64)[:, :, 0:C]
        DVE(lambda e: e.tensor_tensor(out=PTs[0:C, :, 0:C], in0=p4, in1=m_incl[0:C, 0:4, 0:C], op=ALU.mult), pk4 + ["m_incl"], ["PTs"])
        DVE(lambda e: e.tensor_tensor(out=PTb[0:C, :, 0:C], in0=PTs[0:C, :, 0:C], in1=ee[0:C, :].unsqueeze(2).broadcast_to([C, 4, C]),
                                      op=ALU.mult), ["PTs", "ee"], ["PTb"])
        ps5, pk5 = fpair()
        def fnnd(e):
            for hd in range(4):
                out = ps5[0:C, hd // 2, (hd % 2) * 193:(hd % 2) * 193 + 193]
                e.matmul(out, lhsT=PTb[0:C, hd, 0:C], rhs=vaug[0:C, q, hd, :], start=True, stop=False)
                e.matmul(out, lhsT=qkT[:, 2 * hd, sl], rhs=Cstb[:, 0, hd, :], start=False, stop=False)
                ins = e.matmul(out, lhsT=qkT[:, 2 * hd + 1, sl], rhs=Cstb[:, 1, hd, :], start=False, stop=True)
            return ins
        PE(fnnd, ["PTb", "vaug", "cstb", "qkT"], pk5)
        nd = ps5[0:C, :, 0:386].rearrange("p b (h d) -> p b h d", d=193)
        hv4 = hst[0:C, 0, :].rearrange("p (b h) -> p b h", b=2)
        ACT(lambda e: e.activation(out=hv4.unsqueeze(3), in_=nd[:, :, :, 192:193], func=AF.Abs), pk5, ["hst"])
        DVE(lambda e: e.tensor_tensor(out=hst[0:C, 0, :], in0=hst[0:C, 0, :], in1=clampt[0:C, :], op=ALU.max),
            ["hst", "clampt"], ["hst"])
        DVE(lambda e: e.reciprocal(out=hst[0:C, 0, :], in_=hst[0:C, 0, :]), ["hst"], ["hst"])
        DVE(lambda e: e.tensor_tensor(out=hh[0:C, 4 * q:4 * q + 4, :].rearrange("p (b h) d -> p b h d", b=2), in0=nd[:, :, :, 0:192],
                                      in1=hv4.unsqueeze(3).broadcast_to([C, 2, 2, 192]), op=ALU.mult), pk5 + ["hst", "hh"], ["hh"])
        pt2, pk2_ = bbank()
        for t in range(8):
            TR(pt2[0:C, t * 96:(t + 1) * 96], qkT[:, 8 + t, sl], ident_b[0:96, 0:96], ["qkT", "ident_b"], pk2_)
        DVE(lambda e, pt2=pt2: e.tensor_tensor(out=khm[0:C], in0=pt2[0:C, 0:768].rearrange("p (h d) -> p h d", d=192),
                                               in1=ee[0:C, :].unsqueeze(2).broadcast_to([C, 4, 192]), op=ALU.mult), pk2_ + ["ee"], ["khm"])
        for kt in range(2):
            ps6, pk6 = fpair()
            def fncu(e, ps6=ps6, kt=kt):
                for hd in range(4):
                    ins = e.matmul(ps6[0:96, hd // 2, (hd % 2) * 193:(hd % 2) * 193 + 193], lhsT=khm[0:C, hd, kt * 96:(kt + 1) * 96],
                                   rhs=vaug[0:C, q, hd, :], start=True, stop=True)
                return ins
            PE(fncu, ["khm", "vaug"], pk6)
            DVE(lambda e, ps6=ps6, kt=kt: e.tensor_tensor(out=Cst[l][:, kt, :, :].rearrange("p (b h) d -> p b h d", b=2),
                                                          in0=Cst[l][:, kt, :, :].rearrange("p (b h) d -> p b h d", b=2),
                                                          in1=ps6[0:96, :, 0:386].rearrange("p b (h d) -> p b h d", d=193), op=ALU.add),
                pk6 + [ck], [ck])
        if samp or (last and q == nq - 1):
            b_ = sq if samp else 0
            for kt in range(2):
                STORE(o_c[og][l, b_, :, kt * 96:(kt + 1) * 96, :].rearrange("h p v -> p h v"), Cst[l][:, kt, :, 0:192], [ck])
                STORE(o_n[og][l, b_, :, kt * 96:(kt + 1) * 96].rearrange("h (p o) -> p h o", o=1), Cst[l][:, kt, :, 192:193], [ck], slow=True)
            STORE(o_m[og][l, b_].rearrange("(h o) -> h o", o=1), mst[l][:], [mk_], slow=True)

    def ml_out(l, N, tiles):
        S.label = 'mlT'
        C = tiles[0][1]
        nh = 4 * len(tiles)
        H = hh[0:C, 0:nh, :]
        DVE(lambda e: e.tensor_reduce(out=hs2[0:C, 0, 0:nh], in_=H, axis=AX.X, op=ALU.add), ["hh", "hs2"], ["hs2"])
        DVE(lambda e: e.tensor_scalar(out=hs2[0:C, 0, 0:nh], in0=hs2[0:C, 0, 0:nh], scalar1=1.0 / 192, scalar2=None, op0=ALU.mult), ["hs2"], ["hs2"])
        DVE(lambda e: e.tensor_tensor(out=hcen[0:C, 0:nh, :], in0=H, in1=hs2[0:C, 0, 0:nh].unsqueeze(2).broadcast_to([C, nh, 192]),
                                      op=ALU.subtract), ["hh", "hs2"], ["hcen"])
        POOL(lambda e: e.tensor_tensor(out=hsq[0:C, 0:nh, :], in0=hcen[0:C, 0:nh, :], in1=hcen[0:C, 0:nh, :], op=ALU.mult), ["hcen"], ["hsq"])
        DVE(lambda e: e.tensor_reduce(out=hs2[0:C, 1, 0:nh], in_=hsq[0:C, 0:nh, :], axis=AX.X, op=ALU.add), ["hsq", "hs2"], ["hs2"])
        ACT(lambda e: e.activation(out=hs2[0:C, 2, 0:nh], in_=hs2[0:C, 1, 0:nh], func=AF.Sqrt, bias=LN_EPS, scale=1.0 / 192), ["hs2"], ["hs2"])
        DVE(lambda e: e.reciprocal(out=hs2[0:C, 2, 0:nh], in_=hs2[0:C, 2, 0:nh]), ["hs2"], ["hs2"])
        DVE(lambda e: e.tensor_tensor(out=hnb[0:C].rearrange("p q (h d) -> p (q h) d", d=192)[:, 0:nh, :], in0=hcen[0:C, 0:nh, :],
                                      in1=hs2[0:C, 2, 0:nh].unsqueeze(2).broadcast_to([C, nh, 192]), op=ALU.mult), ["hcen", "hs2"], ["hnb"])
        pt, pk = bbank()
        for q, (o, C_, sq) in enumerate(tiles):
            for j in range(6):
                TR(pt[:, j * 128 + o:j * 128 + o + C], hnb[0:C, q, j * 128:(j + 1) * 128], ident_b[0:C, 0:C], ["hnb", "ident_b"], pk)
        DVE(lambda e, pt=pt: e.tensor_tensor(out=ymlT[:, :, 0:N], in0=pt[:, 0:768].rearrange("p (j t) -> p j t", t=128)[:, :, 0:N],
                                             in1=ogT[:, :, 0:N], op=ALU.mult), pk + ["ogT", "ymlT"], ["ymlT"])

    def phase_c(kind, N, l, ydst):
        S.label = 'C'
        pl = "pp%d" % l
        for jg in range(2):
            for b in range(3):
                wbm, wkm = load_group(l, "mg%d%d" % (b, jg))
                wbb, wkb = load_group(l, "br%d%d" % (b, jg))
                for jj in range(4):
                    j = jg * 4 + jj
                    ps_, pk = proj(wbm, wkm, jj * 128, 128, N)
                    gi_ = (b + jj) % 2
                    ACT(lambda e, ps_=ps_, gi_=gi_, b=b, j=j: e.activation(out=gate[gi_][:, 0:N], in_=ps_, func=AF.Sigmoid,
                                                                           bias=PPc(l, "bmrg", b * 8 + j), scale=1.0),
                        pk + [pl], ["gate%d" % gi_])
                    ps2, pk2 = fbank()
                    nk = BR_KC[b]
                    MM(ps2[:, 0:N], [(wbb[:, kc, jj * 128:(jj + 1) * 128], BR_T[b][:, kc, 0:N]) for kc in range(nk)],
                       [wkb, BR_K[b]], pk2)
                    if b == 0:
                        DVE(lambda e, ps2=ps2, gi_=gi_, j=j: e.tensor_tensor(out=mrg[:, j, 0:N], in0=ps2[:, 0:N], in1=gate[gi_][:, 0:N],
                                                                             op=ALU.mult), pk2 + ["gate%d" % gi_, "mrg%d" % j], ["mrg%d" % j])
                    else:
                        ctmp = ctmp2[jj % 2]
                        DVE(lambda e, ps2=ps2, gi_=gi_, ctmp=ctmp: e.tensor_tensor(out=ctmp[:, 0:N], in0=ps2[:, 0:N], in1=gate[gi_][:, 0:N],
                                                                        op=ALU.mult), pk2 + ["gate%d" % gi_], ["ctmp%d" % (jj % 2)])
                        if b == 1:
                            POOL(lambda e, j=j, ctmp=ctmp: e.tensor_tensor(out=mrg[:, j, 0:N], in0=mrg[:, j, 0:N], in1=ctmp[:, 0:N], op=ALU.add),
                                 ["mrg%d" % j, "ctmp%d" % (jj % 2)], ["mrg%d" % j])
                        else:
                            POOL(lambda e, j=j, ctmp=ctmp: e.tensor_tensor(out=mrgb[:, j, 0:N], in0=mrg[:, j, 0:N], in1=ctmp[:, 0:N], op=ALU.add),
                                 ["mrg%d" % j, "ctmp%d" % (jj % 2), "mrgb%d" % j], ["mrgb%d" % j])
        load_bc(1 + l)
        wo0, wok0 = load_group(l, "wo0")
        wo1, wok1 = load_group(l, "wo1")
        rows = N
        ps_, pk = fpair()
        MM(ps_[0:rows, 0, :], [(mrgb[:, kc, 0:rows], wo0[:, kc, 0:512]) for kc in range(8)], [wok0] + ["mrgb%d" % j_ for j_ in range(8)], [pk[0]])
        MM(ps_[0:rows, 1, :], [(mrgb[:, kc, 0:rows], wo1[:, kc, 0:512]) for kc in range(8)], [wok1] + ["mrgb%d" % j_ for j_ in range(8)], [pk[1]])
        DVE(lambda e, ps_=ps_: e.scalar_tensor_tensor(
            out=lnx[0:rows, :].rearrange("p (b n) -> p b n", b=2), in0=x_tok[0:rows, 0, :].rearrange("p (b n) -> p b n", b=2),
            scalar=DN_ALPHA, in1=ps_[0:rows, :, :], op0=ALU.mult, op1=ALU.add), pk + ["x_tok", "lnx"], ["lnx"])
        ln_block(lnx, ["lnx"], rows, out_dram=ydst)

    pass
    if stage >= 1000:
        S.limit = stage - 1000
        stage = 3
    try:
        if stage >= 1:
            run_pass("meta", 16, [(0, 16, 0)], meta, None, True, False)
        npp = SEQ // NT
        tl2 = [(0, 64, 0), (64, 64, 1)]
        for p in range(npp):
            if stage >= 2 and (stage >= 99 or p < stage - 1):
                run_pass("prompt", NT, tl2, xp[p * NT:(p + 1) * NT, :], yp[p * NT:(p + 1) * NT, :], False, p == npp - 1)
        for sp_ in range(2):
            if stage >= 99:
                run_pass("sample", NT, [(0, 64, 2 * sp_), (64, 64, 2 * sp_ + 1)], xs[sp_ * NT:(sp_ + 1) * NT, :],
                         ys[sp_ * NT:(sp_ + 1) * NT, :], False, False)


    except StopBuild:
        pass
    pass
    S.emit(final_wait_ops=final_ops)
    es.close()
    return nc


_NC = None


def _get_nc():
    global _NC
    if _NC is None:
        _NC = build()
    return _NC


def _host_inputs(inp, c):
    f = lambda a: np.ascontiguousarray(a, dtype=np.float32)
    p = c % 4
    sl = slice(4 * c, 4 * c + 4)
    m = {}
    m["xp"] = f(inp["x_prompt"][p])
    m["xs"] = f(inp["x_sample"][sl].reshape(NSAMP * 64, D))
    m["meta"] = f(inp["meta"])
    m["st_shift"] = f(inp["state_rwkv_shift"][:, sl])
    m["st_wkv"] = f(inp["state_rwkv_wkv"][:, sl])
    m["st_s5re"] = f(inp["state_s5_re"][:, sl])
    m["st_s5im"] = f(inp["state_s5_im"][:, sl])
    m["st_conv"] = f(inp["state_mlstm_conv"][:, sl])
    m["st_c"] = f(inp["state_mlstm_c"][:, sl])
    m["st_n"] = f(inp["state_mlstm_n"][:, sl])
    m["st_m"] = f(inp["state_mlstm_m"][:, sl])
    for k in ("w_in", "w_br_rw", "w_br_s5", "w_br_ml", "w_out", "s5_w_glu", "rw_w2", "rw_a2"):
        m[k] = f(inp[k])
    return m


def _shared_inputs(inp):
    f32 = np.float32
    pp = np.zeros((DEPTH, 128, NPP), f32)
    pq = np.zeros((DEPTH, 96, 80), f32)

    def cols(v, n):
        return np.asarray(v, f32).reshape(n, 128).T

    for l in range(DEPTH):
        def put(name, arr):
            o, w = PP[name]
            pp[l, :, o:o + w] = arr
        put("mu", cols(inp["rw_mu"][l], 19))
        put("w0", cols(inp["rw_w0"][l], 6))
        put("a0", cols(inp["rw_a0"][l], 6))
        put("kk", cols(inp["rw_kk"][l], 6))
        put("ka", cols(inp["rw_ka"][l], 6))
        put("rk", cols(np.asarray(inp["rw_rk"][l]).reshape(768), 6))
        put("s5d", cols(inp["s5_d"][l], 4))
        put("bglu", cols(inp["s5_b_glu"][l], 4))
        put("mlg", cols(inp["ml_ln_g"][l], 6))
        put("bmrg", cols(inp["b_merge"][l], 24))
        are = np.asarray(inp["s5_a_re"][l], f32).reshape(16, 2, 64).transpose(1, 2, 0).reshape(128, 16)
        aim = np.asarray(inp["s5_a_im"][l], f32).reshape(16, 2, 64).transpose(1, 2, 0).reshape(128, 16)
        ldt = np.repeat(np.asarray(inp["s5_log_dt"][l], f32).reshape(16, 2, 1), 64, axis=2).transpose(1, 2, 0).reshape(128, 16)
        put("are", are)
        put("aim", aim)
        put("ldt", ldt)
        cw = np.asarray(inp["ml_conv_w"][l], f32).reshape(4, 16, 96)
        cb = np.asarray(inp["ml_conv_b"][l], f32).reshape(16, 96)
        pqv = np.zeros((96, 16, 5), f32)
        pqv[:, :, 0:4] = cw.transpose(2, 1, 0)
        pqv[:, :, 4] = cb.T
        pq[l] = pqv.reshape(96, 80)
    bc = np.stack([inp["in_ln_g"], inp["in_ln_b"], inp["ln_g"][0], inp["ln_b"][0], inp["ln_g"][1], inp["ln_b"][1]]).astype(f32)
    rwln = np.stack([np.asarray(inp["rw_ln_g"], f32), np.asarray(inp["rw_ln_b"], f32)], axis=1)
    rwln = np.ascontiguousarray(rwln.reshape(DEPTH, 2, 6, 2, 64).transpose(0, 1, 3, 2, 4).reshape(DEPTH, 2, 2, 384))
    bif = np.asarray(inp["ml_b_if"], f32)
    bblk = np.zeros((DEPTH, 128, 4, 1024), f32)
    cpad = np.zeros((DEPTH, 128, 16, 2, 64), f32)
    for l in range(DEPTH):
        for c, (bk, ck) in enumerate((("s5_b_re", "s5_c_re"), ("s5_b_im", "s5_c_im"))):
            B = np.asarray(inp[bk][l], f32)
            Cm = np.asarray(inp[ck][l], f32)
            for g in range(32):
                i = g // 2
                ut = g // 8
                col0 = (i % 4) * 256 + c * 128 + (g % 2) * 64
                bblk[l, (g % 8) * 16:(g % 8) * 16 + 16, ut, col0:col0 + 64] = B[g].T
                oc = (i % 2) * 32 + (g % 2) * 16
                cpad[l, (g % 2) * 64:(g % 2) * 64 + 64, i, c, oc:oc + 16] = Cm[g].T
    return dict(pp=pp, pq=pq, bc=bc, rwln=rwln, bif=bif, bblk=bblk, cpad=cpad)


def kernel(**inp):
    inp = {k: np.asarray(v) for k, v in inp.items()}
    nc = _get_nc()
    shared = _shared_inputs(inp)
    in_maps = []
    for c in range(8):
        m = _host_inputs(inp, c)
        m.update(shared)
        in_maps.append(m)
    res = run_bass_kernel_spmd(nc, in_maps, core_ids=list(range(8)))
    R = res.results
    y_prompt = np.stack([R[c]["yp"] for c in range(4)], 0)
    y_sample = np.concatenate([R[c]["ys"].reshape(NSAMP, 64, D) for c in range(8)], 0)
    outs = [y_prompt, y_sample]
    for nm in ("shift", "wkv", "s5re", "s5im", "conv", "c", "n", "m"):
        outs.append(np.concatenate([R[c]["p_" + nm] for c in range(4)], 1))
    for nm in ("shift", "wkv", "s5re", "s5im", "conv", "c", "n", "m"):
        outs.append(np.concatenate([R[c]["s_" + nm] for c in range(8)], 1))
    return tuple(np.ascontiguousarray(o, dtype=np.float32) for o in outs)
```
